# Optimizing a Trainium2 kernel written in Bass

```python
import math
import jax
import jax.numpy as jnp
from jax import lax
import numpy as np

D_MODEL = 1024
BATCH = 4
SEQ = 4096
DEPTH = 2

GRID_W = 64
CTX_LEN = 256
N_BRANCH = 4
RMS_EPS = 1e-6
CONV_K = 3

MLSTM_HEADS = 4
MLSTM_HEAD_DIM = 128
MLSTM_WIDTH = MLSTM_HEADS * MLSTM_HEAD_DIM
MLSTM_CHUNK = 128
SGU_GROUPS = 4
SGU_GROUP_DIM = 128
SGU_WIDTH = SGU_GROUPS * SGU_GROUP_DIM
SGU_CHUNK = 128
S5_GROUP_DIM = 16
S5_GROUPS = 24
S5_WIDTH = S5_GROUPS * S5_GROUP_DIM
S5_STATE = 64
S5_DT_MIN = 1e-3
S5_DT_MAX = 1e-1
SCONV_WIDTH = 512
FFN_HIDDEN = ((8 * D_MODEL + 3 * 256 - 1) // (3 * 256)) * 256

STATE_WIDTHS = (MLSTM_WIDTH, MLSTM_WIDTH, MLSTM_WIDTH, 4 * MLSTM_HEADS, S5_WIDTH)
OUT_WIDTHS = (MLSTM_WIDTH, SGU_WIDTH, SGU_WIDTH, SCONV_WIDTH, SCONV_WIDTH, SCONV_WIDTH, N_BRANCH * D_MODEL)
STATE_DIM = sum(STATE_WIDTHS)
IN_DIM = STATE_DIM + sum(OUT_WIDTHS)

kernel_name = 'hybrid_mlstm_sgu_s5_conv_dit_trunk'


def rmsnorm(x, g):
    xf = x.astype(jnp.float32)
    y = xf * lax.rsqrt(jnp.mean(xf * xf, axis=-1, keepdims=True) + RMS_EPS)
    return (y * g.astype(jnp.float32)).astype(x.dtype)


def modulate(x, g, shift, scale):
    return rmsnorm(x, g) * (1 + scale) + shift


def swiglu(h, w_gate, w_up, w_down):
    return (jax.nn.silu(h @ w_gate) * (h @ w_up)) @ w_down


def split_cols(z, widths):
    parts, off = [], 0
    for w in widths:
        parts.append(z[..., off:off + w])
        off += w
    return parts


def conv_seq(x, w):
    t = x.shape[1]
    pad = CONV_K // 2
    xp = jnp.pad(x, ((0, 0), (pad, pad), (0, 0)))
    out = xp[:, 0:t] * w[0]
    for j in range(1, CONV_K):
        out = out + xp[:, j:j + t] * w[j]
    return out


def conv_grid_rows(x, w):
    b, t, ch = x.shape
    rows = t // GRID_W
    return conv_seq(x.reshape(b * rows, GRID_W, ch), w).reshape(b, t, ch)


def mlstm_zero_state(b):
    return (jnp.zeros((b, MLSTM_HEADS, MLSTM_HEAD_DIM, MLSTM_HEAD_DIM), jnp.float32),
            jnp.zeros((b, MLSTM_HEADS, MLSTM_HEAD_DIM), jnp.float32),
            jnp.zeros((b, MLSTM_HEADS), jnp.float32))


def mlstm_inputs(q_raw, k_raw, v_raw, gate_raw, w_conv_qk, b_gates, conv_fn):
    b, t, _ = q_raw.shape

    def heads(a):
        return a.astype(jnp.float32).reshape(b, t, MLSTM_HEADS, MLSTM_HEAD_DIM).transpose(0, 2, 1, 3)
    q = heads(jax.nn.silu(conv_fn(q_raw, w_conv_qk[0])))
    k = heads(jax.nn.silu(conv_fn(k_raw, w_conv_qk[1]))) * (MLSTM_HEAD_DIM ** -0.5)
    v = heads(v_raw)
    g = (gate_raw + b_gates).astype(jnp.float32).reshape(b, t, 4, MLSTM_HEADS).transpose(2, 0, 3, 1)
    return q, k, v, g


def mlstm_chunked(q, k, v, i_pre, f_pre, state, with_out):
    bsz, nh, t, dh = q.shape
    lc = MLSTM_CHUNK
    nc = t // lc

    def chunks(a):
        return jnp.moveaxis(a.reshape(bsz, nh, nc, lc, *a.shape[3:]), 2, 0)
    xs = (chunks(q), chunks(k), chunks(v), chunks(i_pre), chunks(jax.nn.log_sigmoid(f_pre)))
    causal = jnp.tril(jnp.ones((lc, lc), dtype=bool))

    def step(carry, inp):
        c_mat, n_vec, m = carry
        qc, kc, vc, ic, lfc = inp
        bcum = jnp.cumsum(lfc, axis=-1)
        b_last = bcum[..., -1]
        log_s = b_last[..., None] - bcum + ic
        m_new = jnp.maximum(b_last + m, jnp.max(log_s, axis=-1))
        w_s = jnp.exp(log_s - m_new[..., None])
        decay = jnp.exp(b_last + m - m_new)
        c_new = decay[..., None, None] * c_mat + jnp.einsum('bhs,bhsd,bhse->bhde', w_s, kc, vc)
        n_new = decay[..., None] * n_vec + jnp.einsum('bhs,bhsd->bhd', w_s, kc)
        if not with_out:
            return (c_new, n_new, m_new), None
        log_w = jnp.where(causal, bcum[..., :, None] - bcum[..., None, :] + ic[..., None, :], -jnp.inf)
        log_inter = bcum + m[..., None]
        m_t = jnp.maximum(log_inter, jnp.max(log_w, axis=-1))
        inter = jnp.exp(log_inter - m_t)
        s = jnp.einsum('bhtd,bhsd->bhts', qc, kc) * jnp.exp(log_w - m_t[..., None])
        num = inter[..., None] * jnp.einsum('bhtd,bhde->bhte', qc, c_mat) + jnp.einsum('bhts,bhse->bhte', s, vc)
        den = inter * jnp.einsum('bhtd,bhd->bht', qc, n_vec) + jnp.sum(s, axis=-1)
        h_t = num / jnp.maximum(jnp.abs(den), jnp.exp(-m_t))[..., None]
        return (c_new, n_new, m_new), h_t

    state, hs = lax.scan(step, state, xs)
    if not with_out:
        return None, state
    return jnp.moveaxis(hs, 0, 2).reshape(bsz, nh, t, dh), state


def mlstm_bidir(q, k, v, g, init, with_out):
    def fl(a):
        return jnp.flip(a, axis=2)
    h_f, st_f = mlstm_chunked(q, k, v, g[0], g[1], init[0], with_out)
    h_b, st_b = mlstm_chunked(fl(q), fl(k), fl(v), fl(g[2]), fl(g[3]), init[1], with_out)
    h = h_f + fl(h_b) if with_out else None
    return h, (st_f, st_b)


def spatial_gating(u, v, g, w_s, b_s):
    bsz, t, wd = v.shape
    vf = v.astype(jnp.float32)
    mu = jnp.mean(vf, axis=-1, keepdims=True)
    var = jnp.mean(jnp.square(vf - mu), axis=-1, keepdims=True)
    vn = ((vf - mu) * lax.rsqrt(var + RMS_EPS) * g.astype(jnp.float32)).astype(v.dtype)
    vg = vn.reshape(bsz, t // SGU_CHUNK, SGU_CHUNK, SGU_GROUPS, SGU_GROUP_DIM)
    mixed = jnp.einsum('gts,bnsgd->bntgd', w_s, vg) + jnp.swapaxes(b_s, 0, 1)[None, None, :, :, None]
    return u * mixed.reshape(bsz, t, wd)


def s5_discretise(a_re, a_im, log_dt, b_re, b_im):
    a_re = a_re.astype(jnp.float32)
    a_im = a_im.astype(jnp.float32)
    dt = jnp.exp(log_dt.astype(jnp.float32))[:, None]
    la_re, la_im = dt * a_re, dt * a_im
    mag = jnp.exp(la_re)
    ab_re, ab_im = mag * jnp.cos(la_im), mag * jnp.sin(la_im)
    nr, ni = ab_re - 1.0, ab_im
    den = a_re * a_re + a_im * a_im
    f_re = ((nr * a_re + ni * a_im) / den)[..., None]
    f_im = ((ni * a_re - nr * a_im) / den)[..., None]
    b_re = b_re.astype(jnp.float32)
    b_im = b_im.astype(jnp.float32)
    bb_re = f_re * b_re - f_im * b_im
    bb_im = f_re * b_im + f_im * b_re
    return la_re, la_im, ab_re, ab_im, bb_re, bb_im


def s5_drive(u, bb_re, bb_im):
    bsz, t, _ = u.shape
    ug = u.reshape(bsz, t, S5_GROUPS, S5_GROUP_DIM)
    return jnp.einsum('btgj,gpj->btgp', ug, bb_re), jnp.einsum('btgj,gpj->btgp', ug, bb_im)


def s5_combine(e1, e2):
    a1r, a1i, b1r, b1i = e1
    a2r, a2i, b2r, b2i = e2
    return (a2r * a1r - a2i * a1i, a2r * a1i + a2i * a1r,
            a2r * b1r - a2i * b1i + b2r, a2r * b1i + a2i * b1r + b2i)


def s5_states(u, disc, x0, reverse):
    _, _, ab_re, ab_im, bb_re, bb_im = disc
    bu_re, bu_im = s5_drive(u, bb_re, bb_im)
    if reverse:
        bu_re, bu_im = jnp.flip(bu_re, axis=1), jnp.flip(bu_im, axis=1)
    if x0 is not None:
        x0_re, x0_im = x0
        bu_re = bu_re.at[:, 0].add(ab_re * x0_re - ab_im * x0_im)
        bu_im = bu_im.at[:, 0].add(ab_re * x0_im + ab_im * x0_re)
    t = u.shape[1]
    a_re_t = jnp.broadcast_to(ab_re, (1, t) + ab_re.shape)
    a_im_t = jnp.broadcast_to(ab_im, (1, t) + ab_im.shape)
    _, _, xr, xi = lax.associative_scan(s5_combine, (a_re_t, a_im_t, bu_re, bu_im), axis=1)
    if reverse:
        xr, xi = jnp.flip(xr, axis=1), jnp.flip(xi, axis=1)
    return xr, xi


def s5_final_state(u, disc, reverse):
    la_re, la_im, _, _, bb_re, bb_im = disc
    bu_re, bu_im = s5_drive(u, bb_re, bb_im)
    t = u.shape[1]
    pos = jnp.arange(t, dtype=jnp.float32)
    steps = pos if reverse else (t - 1) - pos
    mag = jnp.exp(steps[:, None, None] * la_re)
    ang = steps[:, None, None] * la_im
    p_re, p_im = mag * jnp.cos(ang), mag * jnp.sin(ang)
    x_re = jnp.einsum('tgp,btgp->bgp', p_re, bu_re) - jnp.einsum('tgp,btgp->bgp', p_im, bu_im)
    x_im = jnp.einsum('tgp,btgp->bgp', p_re, bu_im) + jnp.einsum('tgp,btgp->bgp', p_im, bu_re)
    return x_re, x_im


def s5_out(xr, xi, u, c_re, c_im, d, w_glu, b_glu, dtype):
    bsz, t, _ = u.shape
    y = (jnp.einsum('gjp,btgp->btgj', c_re.astype(jnp.float32), xr)
         - jnp.einsum('gjp,btgp->btgj', c_im.astype(jnp.float32), xi))
    y = y.reshape(bsz, t, S5_WIDTH) + d.astype(jnp.float32) * u
    y = jax.nn.gelu(y).astype(dtype)
    return y * jax.nn.sigmoid(y @ w_glu + b_glu)


def merge_branches(o_cols, h_mlstm, y_s5, conv_fn, g_mh, g_sgu, w_sgu, b_sgu, w_sconv,
                   w_up_mlstm, w_up_sgu, w_up_s5, w_up_sconv, w_out):
    o_raw, su, sv, cb, cc, cx, gate_raw = o_cols
    bsz, t, _ = o_raw.shape
    hm = jnp.transpose(h_mlstm, (0, 2, 1, 3))
    hm = hm * lax.rsqrt(jnp.mean(hm * hm, axis=-1, keepdims=True) + RMS_EPS)
    hm = (hm.reshape(bsz, t, MLSTM_WIDTH) * g_mh.astype(jnp.float32)).astype(o_raw.dtype)
    y_a = hm * jax.nn.sigmoid(o_raw)
    y_b = spatial_gating(jax.nn.gelu(su), jax.nn.gelu(sv), g_sgu, w_sgu, b_sgu)
    y_d = cb * conv_fn(cc * cx, w_sconv)
    gates = jax.nn.sigmoid(gate_raw.reshape(bsz, t, N_BRANCH, D_MODEL))
    merged = (gates[:, :, 0] * (y_a @ w_up_mlstm) + gates[:, :, 1] * (y_b @ w_up_sgu)
              + gates[:, :, 2] * (y_s5 @ w_up_s5) + gates[:, :, 3] * (y_d @ w_up_sconv))
    return merged @ w_out


def token_mixer(hx, hc, need_ctx, w_in, b_gates, w_conv_qk, g_mh, g_sgu, w_sgu, b_sgu,
                s5_a_re, s5_a_im, s5_log_dt, s5_b_re, s5_b_im, s5_c_re, s5_c_im, s5_d,
                w_glu, b_glu, w_sconv, w_up_mlstm, w_up_sgu, w_up_s5, w_up_sconv, w_out):
    bsz = hx.shape[0]
    zx = hx @ w_in
    zc = hc @ (w_in if need_ctx else w_in[:, :STATE_DIM])
    qx, kx, vx, gx, ux = split_cols(zx[..., :STATE_DIM], STATE_WIDTHS)
    qc, kc, vc, gc, uc = split_cols(zc[..., :STATE_DIM], STATE_WIDTHS)

    mc = mlstm_inputs(qc, kc, vc, gc, w_conv_qk, b_gates, conv_seq)
    h_mc, st_ctx = mlstm_bidir(*mc, (mlstm_zero_state(bsz), mlstm_zero_state(bsz)), need_ctx)
    mx = mlstm_inputs(qx, kx, vx, gx, w_conv_qk, b_gates, conv_grid_rows)
    h_mx, _ = mlstm_bidir(*mx, st_ctx, True)

    discs = [s5_discretise(s5_a_re[d], s5_a_im[d], s5_log_dt[d], s5_b_re, s5_b_im) for d in range(2)]
    uc32 = uc.astype(jnp.float32)
    ux32 = ux.astype(jnp.float32)
    if need_ctx:
        sc_f = s5_states(uc32, discs[0], None, False)
        sc_b = s5_states(uc32, discs[1], None, True)
        x0s = ((sc_f[0][:, -1], sc_f[1][:, -1]), (sc_b[0][:, 0], sc_b[1][:, 0]))
        y_s5c = s5_out(sc_f[0] + sc_b[0], sc_f[1] + sc_b[1], uc32, s5_c_re, s5_c_im, s5_d, w_glu, b_glu, hc.dtype)
    else:
        x0s = (s5_final_state(uc32, discs[0], False), s5_final_state(uc32, discs[1], True))
    sx_f = s5_states(ux32, discs[0], x0s[0], False)
    sx_b = s5_states(ux32, discs[1], x0s[1], True)
    y_s5x = s5_out(sx_f[0] + sx_b[0], sx_f[1] + sx_b[1], ux32, s5_c_re, s5_c_im, s5_d, w_glu, b_glu, hx.dtype)

    shared = (g_mh, g_sgu, w_sgu, b_sgu, w_sconv, w_up_mlstm, w_up_sgu, w_up_s5, w_up_sconv, w_out)
    yx = merge_branches(split_cols(zx[..., STATE_DIM:], OUT_WIDTHS), h_mx, y_s5x, conv_grid_rows, *shared)
    yc = None
    if need_ctx:
        yc = merge_branches(split_cols(zc[..., STATE_DIM:], OUT_WIDTHS), h_mc, y_s5c, conv_seq, *shared)
    return yx, yc


def setup_inputs(seed: int = 0) -> dict:
    key = jax.random.key(seed)
    ks = iter(jax.random.split(key, 48))

    def nrm(shape, scale):
        return jax.random.normal(next(ks), shape, jnp.float32) * scale
    L, D = DEPTH, D_MODEL
    f_bias = jnp.linspace(3.0, 6.0, MLSTM_HEADS, dtype=jnp.float32)
    i_bias = jnp.zeros((MLSTM_HEADS,), jnp.float32)
    gate_bias = jnp.concatenate([i_bias, f_bias, i_bias, f_bias])
    n_idx = jnp.arange(S5_STATE, dtype=jnp.float32)
    return {
        'x': nrm((BATCH, SEQ, D), 1.0),
        'c': nrm((BATCH, D), 1.0),
        'ctx': nrm((BATCH, CTX_LEN, D), 1.0),
        'c_ctx': nrm((D,), 1.0),
        'w_mod': nrm((L, D, 6 * D), 0.5 * D ** -0.5),
        'b_mod': nrm((L, 6 * D), 0.02),
        'g_norm_mix': 1.0 + nrm((L, D), 0.02),
        'g_norm_ffn': 1.0 + nrm((L, D), 0.02),
        'w_in': nrm((L, D, IN_DIM), D ** -0.5),
        'b_gates': gate_bias[None, :] + nrm((L, 4 * MLSTM_HEADS), 0.1),
        'w_conv_qk': nrm((L, 2, CONV_K, MLSTM_WIDTH), CONV_K ** -0.5),
        'g_mh': 1.0 + nrm((L, MLSTM_WIDTH), 0.02),
        'g_sgu': 1.0 + nrm((L, SGU_WIDTH), 0.02),
        'w_sgu': nrm((L, SGU_GROUPS, SGU_CHUNK, SGU_CHUNK), SGU_CHUNK ** -0.5),
        'b_sgu': 1.0 + nrm((L, SGU_GROUPS, SGU_CHUNK), 0.02),
        's5_a_re': -0.5 + nrm((L, 2, S5_GROUPS, S5_STATE), 0.01),
        's5_a_im': math.pi * n_idx + nrm((L, 2, S5_GROUPS, S5_STATE), 0.01),
        's5_log_dt': jax.random.uniform(next(ks), (L, 2, S5_GROUPS), jnp.float32,
                                        math.log(S5_DT_MIN), math.log(S5_DT_MAX)),
        's5_b_re': nrm((L, S5_GROUPS, S5_STATE, S5_GROUP_DIM), (2 * S5_GROUP_DIM) ** -0.5),
        's5_b_im': nrm((L, S5_GROUPS, S5_STATE, S5_GROUP_DIM), (2 * S5_GROUP_DIM) ** -0.5),
        's5_c_re': nrm((L, S5_GROUPS, S5_GROUP_DIM, S5_STATE), S5_STATE ** -0.5),
        's5_c_im': nrm((L, S5_GROUPS, S5_GROUP_DIM, S5_STATE), S5_STATE ** -0.5),
        's5_d': nrm((L, S5_WIDTH), 1.0),
        'w_glu': nrm((L, S5_WIDTH, S5_WIDTH), S5_WIDTH ** -0.5),
        'b_glu': nrm((L, S5_WIDTH), 0.02),
        'w_sconv': nrm((L, CONV_K, SCONV_WIDTH), CONV_K ** -0.5),
        'w_up_mlstm': nrm((L, MLSTM_WIDTH, D), MLSTM_WIDTH ** -0.5),
        'w_up_sgu': nrm((L, SGU_WIDTH, D), SGU_WIDTH ** -0.5),
        'w_up_s5': nrm((L, S5_WIDTH, D), S5_WIDTH ** -0.5),
        'w_up_sconv': nrm((L, SCONV_WIDTH, D), SCONV_WIDTH ** -0.5),
        'w_out': nrm((L, D, D), D ** -0.5),
        'w_ffn_gate': nrm((L, D, FFN_HIDDEN), D ** -0.5),
        'w_ffn_up': nrm((L, D, FFN_HIDDEN), D ** -0.5),
        'w_ffn_down': nrm((L, FFN_HIDDEN, D), FFN_HIDDEN ** -0.5),
        'g_final': 1.0 + nrm((D,), 0.02),
    }


def reference(x, c, ctx, c_ctx, w_mod, b_mod, g_norm_mix, g_norm_ffn, w_in, b_gates,
              w_conv_qk, g_mh, g_sgu, w_sgu, b_sgu, s5_a_re, s5_a_im, s5_log_dt,
              s5_b_re, s5_b_im, s5_c_re, s5_c_im, s5_d, w_glu, b_glu, w_sconv,
              w_up_mlstm, w_up_sgu, w_up_s5, w_up_sconv, w_out,
              w_ffn_gate, w_ffn_up, w_ffn_down, g_final):
    D = D_MODEL
    xc = ctx
    sc_x = jax.nn.silu(c)
    sc_c = jax.nn.silu(c_ctx)
    for l in range(DEPTH):
        last = l == DEPTH - 1
        mod_x = jnp.split((sc_x @ w_mod[l] + b_mod[l])[:, None, :], 6, axis=-1)
        n_mod = 2 if last else 6
        mod_c = jnp.split((sc_c @ w_mod[l][:, :n_mod * D] + b_mod[l][:n_mod * D])[None, None, :], n_mod, axis=-1)
        hx = modulate(x, g_norm_mix[l], mod_x[0], mod_x[1])
        hc = modulate(xc, g_norm_mix[l], mod_c[0], mod_c[1])
        yx, yc = token_mixer(hx, hc, not last, w_in[l], b_gates[l], w_conv_qk[l], g_mh[l], g_sgu[l],
                             w_sgu[l], b_sgu[l], s5_a_re[l], s5_a_im[l], s5_log_dt[l], s5_b_re[l],
                             s5_b_im[l], s5_c_re[l], s5_c_im[l], s5_d[l], w_glu[l], b_glu[l],
                             w_sconv[l], w_up_mlstm[l], w_up_sgu[l], w_up_s5[l], w_up_sconv[l], w_out[l])
        x = x + mod_x[2] * yx
        x = x + mod_x[5] * swiglu(modulate(x, g_norm_ffn[l], mod_x[3], mod_x[4]),
                                  w_ffn_gate[l], w_ffn_up[l], w_ffn_down[l])
        if not last:
            xc = xc + mod_c[2] * yc
            xc = xc + mod_c[5] * swiglu(modulate(xc, g_norm_ffn[l], mod_c[3], mod_c[4]),
                                        w_ffn_gate[l], w_ffn_up[l], w_ffn_down[l])
    return rmsnorm(x, g_final)
```

```python
import contextlib
import numpy as np
import concourse.bass as bass
import concourse.mybir as mybir
from concourse.bass_utils import run_bass_kernel_spmd

F32 = mybir.dt.float32
I32 = mybir.dt.int32
BF16 = mybir.dt.bfloat16
ALU = mybir.AluOpType
AF = mybir.ActivationFunctionType
AX = mybir.AxisListType

ENG_NAMES = ("pe", "act", "dve", "pool", "sp")
N_DMA_SEMS = 6


class Prog:
    def __init__(self, nc, same_eng_sync=True):
        self.nc = nc
        self.same_eng_sync = same_eng_sync
        self.q = {e: [] for e in ENG_NAMES}
        self.cnt = {e: 0 for e in ENG_NAMES}
        self.waited = {e: {} for e in ENG_NAMES}
        self.last_w = {}
        self.readers = {}
        self.dma_rr = {e: 0 for e in ENG_NAMES}
        self.dma_cnt = {}
        self.sem_handles = {}
        self.pending = {e: {} for e in ENG_NAMES}
        self.n_instr = 0

    def _need(self, eng, needs, semkey, val):
        if semkey == ("c", "pe") and eng == "pe":
            return
        if (not self.same_eng_sync) and semkey == ("c", eng):
            return
        if self.waited[eng].get(semkey, 0) >= val:
            return
        if needs.get(semkey, 0) < val:
            needs[semkey] = val

    @staticmethod
    def _k(k):
        if isinstance(k, (str, int)):
            return k
        if isinstance(k, tuple):
            return tuple(Prog._k(x) for x in k)
        return k.name

    def barrier(self):
        snap = {("c", e): self.cnt[e] for e in ENG_NAMES if self.cnt[e]}
        snap.update(self.dma_cnt)
        for e in ENG_NAMES:
            for sk, v in snap.items():
                if self.pending[e].get(sk, 0) < v:
                    self.pending[e][sk] = v

    def _deps(self, eng, reads, writes):
        needs = {}
        if self.pending[eng]:
            for sk, v in self.pending[eng].items():
                self._need(eng, needs, sk, v)
            self.pending[eng] = {}
        for k in reads:
            lw = self.last_w.get(k)
            if lw:
                self._need(eng, needs, lw[0], lw[1])
        for k in writes:
            lw = self.last_w.get(k)
            if lw:
                self._need(eng, needs, lw[0], lw[1])
            for sk, v in self.readers.get(k, {}).items():
                self._need(eng, needs, sk, v)
        for sk, v in needs.items():
            self.waited[eng][sk] = v
        return list(needs.items())

    def _commit(self, semkey, val, reads, writes):
        for k in reads:
            self.readers.setdefault(k, {})[semkey] = val
        for k in writes:
            self.last_w[k] = (semkey, val)
            self.readers[k] = {}

    def op(self, eng, fn, reads=(), writes=()):
        reads = [self._k(k) for k in reads]
        writes = [self._k(k) for k in writes]
        waits = self._deps(eng, reads, writes)
        self.cnt[eng] += 1
        semkey = ("c", eng)
        self._commit(semkey, self.cnt[eng], reads, writes)
        self.q[eng].append((waits, fn, semkey, 1))
        self.n_instr += 1

    def dma(self, eng, out, in_, reads=(), writes=(), **kw):
        reads = [self._k(k) for k in reads]
        writes = [self._k(k) for k in writes]
        slot = self.dma_rr[eng]
        self.dma_rr[eng] = (slot + 1) % N_DMA_SEMS
        semkey = ("d", eng, slot)
        prev = self.dma_cnt.get(semkey, 0)
        waits = self._deps(eng, reads, writes)
        if prev and self.waited[eng].get(semkey, 0) < prev:
            waits.append((semkey, prev))
            self.waited[eng][semkey] = prev
        val = prev + 16
        self.dma_cnt[semkey] = val
        self._commit(semkey, val, reads, writes)

        def fn(e, out=out, in_=in_, kw=kw):
            return e.dma_start(out=out, in_=in_, **kw)
        self.q[eng].append((waits, fn, semkey, 16))
        self.n_instr += 1

    def final_wait(self, eng, keys):
        keys = [self._k(k) for k in keys]
        waits = self._deps(eng, keys, ())
        self.q[eng].append((waits, None, None, 0))

    def emit(self):
        nc = self.nc
        semkeys = set()
        for e in ENG_NAMES:
            for waits, fn, sk, inc in self.q[e]:
                if sk is not None:
                    semkeys.add(sk)
                for w, _ in waits:
                    semkeys.add(w)
        semkeys = sorted(semkeys, key=str)
        with contextlib.ExitStack() as st:
            for sk in semkeys:
                self.sem_handles[sk] = st.enter_context(nc.semaphore("s_" + "_".join(map(str, sk))))
            block = st.enter_context(nc.Block())

            def run(e_name):
                def body(eng):
                    for waits, fn, sk, inc in self.q[e_name]:
                        for wsk, v in waits:
                            eng.wait_ge(self.sem_handles[wsk], v)
                        if fn is not None:
                            fn(eng).then_inc(self.sem_handles[sk], inc)
                return body
            block.tensor(run("pe"))
            block.scalar(run("act"))
            block.vector(run("dve"))
            block.gpsimd(run("pool"))
            block.sync(run("sp"))


D = 1024
KT = 8
NT = 512
H = 4
CTX = 256
L = 2
FH = 2816
FT = 22
IN_DIM = 9104
COL_Q, COL_K, COL_V, COL_G, COL_U = 0, 512, 1024, 1536, 1552
COL_O, COL_SU, COL_SV, COL_CB, COL_CC, COL_CX, COL_GATE = 1936, 2448, 2960, 3472, 3984, 4496, 5008
EPS = 1e-6
BIG = 1.0e30
SLABF = 1408
NSLAB = 6
TQ, TK, TV, TG, TU, TO, TSU, TSV, TCB, TCC, TCX, TGATE = 0, 4, 8, 12, 13, 16, 20, 24, 28, 32, 36, 40
NWT = 72
PI = float(np.pi)
TWO_PI = float(2 * np.pi)

def input_shapes(T):
    return {
        "xT": [D, T], "cT": [D, CTX], "sc": [128, KT, 2], "cst": [128, 7, 128],
        "wmod": [L, 128, KT, 6 * D], "bmod": [L, 128, 48], "gnm": [L, 128, KT], "gnf": [L, 128, KT], "gfin": [128, KT],
        "win": [L, 128, NWT, KT, 128], "bg": [L, 1, 16], "wconv": [L, 128, 24], "gmh": [L, 1, 512], "gsgu": [L, 1, 512],
        "wsT": [L, 128, 4, 128], "bs": [L, 1, 512], "are": [L, 2, 128, 12], "aim": [L, 2, 128, 12], "ldt": [L, 2, 128, 12],
        "Bre": [L, 128, 12, 128], "Bim": [L, 128, 12, 128], "Cre": [L, 128, 12, 128], "Cim": [L, 128, 12, 128],
        "s5d": [L, 128, 3], "bglu": [L, 128, 3], "wglu": [L, 128, 3, 384], "wsconv": [L, 128, 12],
        "wupm": [L, 128, 8, 4, 128], "wups": [L, 128, 8, 4, 128], "wup5": [L, 128, 8, 3, 128], "wupc": [L, 128, 8, 4, 128],
        "wout": [L, 128, 8, KT, 128], "wfg": [L, 128, FT, KT, 128], "wfu": [L, 128, FT, KT, 128], "wfd": [L, 128, 8, FT, 128],
    }


def build_program(T, n_layers=L, dbg=()):
    nc = bass.Bass("TRN2", target_bir_lowering=False)
    din = {n: nc.dram_tensor(n, s, F32, kind="ExternalInput").ap() for n, s in input_shapes(T).items()}
    outT = nc.dram_tensor("outT", [D, T], F32, kind="ExternalOutput").ap()
    dbg_out = {}
    TS = {"lat": T, "ctx": CTX}
    xs = nc.dram_tensor("xs", [D, T], F32).ap()
    csd = nc.dram_tensor("csd", [D, CTX], F32).ap()
    qd = {sg: nc.dram_tensor("qd_" + sg, [128, 4, TS[sg]], F32).ap() for sg in TS}
    kd = {sg: nc.dram_tensor("kd_" + sg, [128, 4, TS[sg]], F32).ap() for sg in TS}
    vd = {sg: nc.dram_tensor("vd_" + sg, [TS[sg], 512], F32).ap() for sg in TS}
    gd = {sg: nc.dram_tensor("gd_" + sg, [TS[sg], 16], F32).ap() for sg in TS}
    ud = {sg: nc.dram_tensor("ud_" + sg, [128, 3, TS[sg]], F32).ap() for sg in TS}
    hd = {(d, sg): nc.dram_tensor(f"hd{d}_{sg}", [TS[sg], 512], F32).ap() for sg in TS for d in (0, 1)}
    yd_ = {(d, sg): nc.dram_tensor(f"yd{d}_{sg}", [128, 3, TS[sg]], F32).ap() for sg in TS for d in (0, 1)}
    BIGW = ("win", "wupm", "wups", "wup5", "wupc", "wout", "wfg", "wfu", "wfd")
    wb = {n: nc.dram_tensor("wb_" + n, input_shapes(T)[n], BF16).ap() for n in BIGW}

    with contextlib.ExitStack() as st:
        def sb(name, shape, dt=F32):
            return st.enter_context(nc.sbuf_tensor("sb_" + name, shape, dt))
        P = Prog(nc)
        cst = sb("cst", [128, 7, 128])
        ident, ones, Umat, Lmat, biasU, biasL, iota1 = (cst[:, i, :] for i in range(7))
        sc = sb("sc", [128, KT, 2]); modv = sb("modv", [128, 48, 2]); lcs = sb("lcs", [128, 2, 2, KT])
        gnm = sb("gnm", [128, KT]); gnf = sb("gnf", [128, KT]); gfin = sb("gfin", [128, KT]); bmod = sb("bmod", [128, 48])
        bg_row = sb("bg_row", [128, 16]); wconv = sb("wconv", [128, 24]); wsconv = sb("wsconv", [128, 12])
        s5d = sb("s5d", [128, 3]); bglu = sb("bglu", [128, 3])
        rhod = [sb(f"rho{d}", [128, 12]) for d in range(2)]
        card = [sb(f"car{d}", [128, 2, 12]) for d in range(2)]
        Caugd = [sb(f"Caug{d}", [128, H, 129]) for d in range(2)]
        mstd = [sb(f"mst{d}", [128, H]) for d in range(2)]
        s5p = sb("s5p", [128, 16, 12])
        NA = 50000
        ARENA = sb("ARENA", [128, NA])
        TABF = 5 * 12 * 128
        tabd = [ARENA[:, NA - (2 - d) * TABF:NA - (1 - d) * TABF].rearrange("p (t a b) -> p t a b", t=5, a=12) for d in range(2)]
        TABK = ["tab0", "tab1"]
        pbs = [st.enter_context(nc.psum_tensor(f"pb{i}", [128, 512], F32)) for i in range(8)]
        ps_rr = [0]

        def PS():
            t = pbs[ps_rr[0] % 8]
            ps_rr[0] += 1
            return t

        class Carver:
            def __init__(self, lim=None):
                self.off = 0
                self.lim = NA if lim is None else lim

            def f(self, n, pat=None, **kw):
                ap = ARENA[:, self.off:self.off + n]
                self.off += n
                assert self.off <= self.lim, self.off
                return ap.rearrange(pat, **kw) if pat else ap

            def b(self, n, pat=None, **kw):
                nf = (n + 1) // 2
                ap = ARENA[:, self.off:self.off + nf].bitcast(BF16)
                self.off += nf
                assert self.off <= self.lim, self.off
                return ap.rearrange(pat, **kw) if pat else ap

        def tt(eng, out, a, b, op, r, w):
            P.op(eng, lambda e: e.tensor_tensor(out, a, b, op), reads=r, writes=w)

        def ts(eng, out, a, s1, s2, op0, op1, r, w):
            if s2 is None:
                P.op(eng, lambda e: e.tensor_scalar(out, a, s1, None, op0=op0), reads=r, writes=w)
            else:
                P.op(eng, lambda e: e.tensor_scalar(out, a, s1, s2, op0=op0, op1=op1), reads=r, writes=w)

        def stt(eng, out, in0, scalar, in1, op0, op1, r, w):
            P.op(eng, lambda e: e.scalar_tensor_tensor(out, in0, scalar, in1, op0=op0, op1=op1), reads=r, writes=w)

        def act(out, in_, func, r, w, bias=0.0, scale=1.0):
            P.op("act", lambda e: e.activation(out, in_, func, bias=bias, scale=scale), reads=r, writes=w)

        def mm(ps, lhsT, rhs, start, stop, r, w):
            P.op("pe", lambda e: e.matmul(ps, lhsT, rhs, start=start, stop=stop), reads=r, writes=w)

        def tr(ps, in_, r, w):
            P.op("pe", lambda e: e.transpose(ps, in_, ident), reads=r + [cst], writes=w)

        def red(eng, out, in_, op, r, w):
            P.op(eng, lambda e: e.tensor_reduce(out, in_, axis=AX.X, op=op), reads=r, writes=w)

        def recip(out, in_, r, w):
            P.op("dve", lambda e: e.reciprocal(out, in_), reads=r, writes=w)

        def cp(eng, out, in_, r, w):
            P.op(eng, lambda e: e.tensor_copy(out, in_), reads=r, writes=w)

        def mset(eng, out, val, w):
            P.op(eng, lambda e: e.memset(out, val), writes=w)

        def dma(out, in_, r, w, eng="sp"):
            P.dma(eng, out, in_, reads=r, writes=w)

        def DBG(name, ap, shape, keys):
            if name in dbg and name not in dbg_out:
                t = nc.dram_tensor("dbg_" + name, shape, F32, kind="ExternalOutput").ap()
                dbg_out[name] = t
                dma(t, ap, keys, ["dbg_" + name])

        def cast_weights(l):
            for n in BIGW:
                src = din[n][l]; dstw = wb[n][l]
                nd = len(src.shape)
                letters = "abcd"[:nd - 1]
                pat = "p " + " ".join(letters) + " -> p (" + " ".join(letters) + ")"
                s2 = src.rearrange(pat); d2 = dstw.rearrange(pat)
                N = s2.shape[1]
                for c0 in range(0, N, 8192):
                    c1 = min(N, c0 + 8192)
                    dma(d2[:, c0:c1], s2[:, c0:c1], [], [("wb", n, l, c0)], eng="pool")

        def range_reduce(out, in_, ti, tm, keys_in, keys_out, kti, ktm):
            ts("dve", tm, in_, 1.0 / TWO_PI, None, ALU.mult, None, keys_in, [ktm])
            cp("dve", ti, tm, [ktm], [kti])
            cp("dve", tm, ti, [kti], [ktm])
            stt("dve", out, tm, -TWO_PI, in_, ALU.mult, ALU.add, [ktm] + keys_in, keys_out)
            P.op("dve", lambda e: e.tensor_single_scalar(tm, out, PI, op=ALU.is_gt), reads=keys_out, writes=[ktm])
            stt("dve", out, tm, -TWO_PI, out, ALU.mult, ALU.add, [ktm] + keys_out, keys_out)
            P.op("dve", lambda e: e.tensor_single_scalar(tm, out, -PI, op=ALU.is_lt), reads=keys_out, writes=[ktm])
            stt("dve", out, tm, TWO_PI, out, ALU.mult, ALU.add, [ktm] + keys_out, keys_out)
            ts("dve", out, out, PI, -PI, ALU.min, ALU.max, keys_out, keys_out)

        dma(cst[:], din["cst"], [], [cst])
        dma(sc[:], din["sc"], [], [sc])
        dma(gfin[:], din["gfin"], [], [gfin])
        act(sc[:], sc[:], AF.Silu, [sc], [sc])

        def layer_prep(l):
            P.barrier()
            cv = Carver(NA - 2 * TABF)
            wm = [cv.f(1024, "p (k c) -> p k c", k=KT) for _ in range(2)]
            ang = cv.f(1536, "p (a b) -> p a b", a=12); tsn = cv.f(1536, "p (a b) -> p a b", a=12); tcs = cv.f(1536, "p (a b) -> p a b", a=12)
            tmf = cv.f(1536); ti32 = cv.f(1536).bitcast(I32)
            for nm, t in (("gnm", gnm), ("gnf", gnf), ("bmod", bmod), ("wconv", wconv), ("wsconv", wsconv),
                          ("s5d", s5d), ("bglu", bglu)):
                dma(t[:], din[nm][l], [], [t])
            dma(bg_row[:], din["bg"][l].partition_broadcast(128), [], [bg_row])
            for ft in range(48):
                s = "wm%d" % (ft % 2)
                v = wm[ft % 2]
                dma(v, din["wmod"][l][:, :, ft * 128:(ft + 1) * 128], [], [s])
                ps = PS()
                for k in range(KT):
                    mm(ps[:, 0:2], v[:, k, :], sc[:, k, :], k == 0, k == KT - 1, [s, sc], [ps])
                ts("dve", modv[:, ft, :], ps[:, 0:2], bmod[:, ft:ft + 1], None, ALU.add, None, [ps, bmod], [modv])
            for sg in range(2):
                stt("dve", lcs[:, sg, 0, :], modv[:, 8:16, sg], 1.0, gnm[:], ALU.add, ALU.mult, [modv, gnm], [lcs])
                stt("dve", lcs[:, sg, 1, :], modv[:, 32:40, sg], 1.0, gnf[:], ALU.add, ALU.mult, [modv, gnf], [lcs])
            K = ["prep"]
            for d in range(2):
                tab = tabd[d]; rho = rhod[d]; TK_ = TABK[d]
                are = s5p[:, 0, :]; aim = s5p[:, 1, :]; ldt = s5p[:, 2, :]
                dma(are, din["are"][l, d], [], K); dma(aim, din["aim"][l, d], [], K); dma(ldt, din["ldt"][l, d], [], K)
                dt_ = s5p[:, 3, :]; th = s5p[:, 4, :]; lar = s5p[:, 5, :]; sn = s5p[:, 6, :]; cs_ = s5p[:, 7, :]
                t0 = s5p[:, 8, :]; t1 = s5p[:, 9, :]; den = s5p[:, 10, :]; fr = s5p[:, 11, :]; fi = s5p[:, 12, :]
                nr = s5p[:, 13, :]; ni = s5p[:, 14, :]; t2 = s5p[:, 15, :]
                act(dt_, ldt, AF.Exp, K, K)
                tt("dve", th, dt_, aim, ALU.mult, K, K)
                tt("dve", lar, dt_, are, ALU.mult, K, K)
                act(rho[:], lar, AF.Exp, K, [rho] + K)
                range_reduce(t0, th, ti32[:, 0:12], t2, K, K, "ti32", "prep")
                act(sn, t0, AF.Sin, K, K)
                ts("dve", t1, th, PI / 2, None, ALU.add, None, K, K)
                range_reduce(t0, t1, ti32[:, 0:12], t2, K, K, "ti32", "prep")
                act(cs_, t0, AF.Sin, K, K)
                tt("dve", nr, rho[:], cs_, ALU.mult, [rho] + K, K)
                ts("dve", nr, nr, -1.0, None, ALU.add, None, K, K)
                tt("dve", ni, rho[:], sn, ALU.mult, [rho] + K, K)
                tt("dve", den, are, are, ALU.mult, K, K)
                tt("dve", t0, aim, aim, ALU.mult, K, K)
                tt("dve", den, den, t0, ALU.add, K, K)
                recip(den, den, K, K)
                tt("dve", t0, nr, are, ALU.mult, K, K); tt("dve", t1, ni, aim, ALU.mult, K, K)
                tt("dve", t0, t0, t1, ALU.add, K, K); tt("dve", fr, t0, den, ALU.mult, K, K)
                tt("dve", t0, ni, are, ALU.mult, K, K); tt("dve", t1, nr, aim, ALU.mult, K, K)
                tt("dve", t0, t0, t1, ALU.subtract, K, K); tt("dve", fi, t0, den, ALU.mult, K, K)
                thb = th.unsqueeze(2).to_broadcast([128, 12, 128])
                io = iota1.unsqueeze(1).to_broadcast([128, 12, 128])
                tt("dve", ang, io, thb, ALU.mult, [cst] + K, K)
                angf = ang.rearrange("p a b -> p (a b)")
                range_reduce(tsn.rearrange("p a b -> p (a b)"), angf, ti32, tmf, K, K, "ti32", "prep")
                act(tab[:, 3, :, :], tsn, AF.Sin, K, [TK_])
                ts("dve", angf, angf, PI / 2, None, ALU.add, None, K, K)
                range_reduce(tcs.rearrange("p a b -> p (a b)"), angf, ti32, tmf, K, K, "ti32", "prep")
                act(tab[:, 2, :, :], tcs, AF.Sin, K, [TK_])
                frb = fr.unsqueeze(2).to_broadcast([128, 12, 128]); fib = fi.unsqueeze(2).to_broadcast([128, 12, 128])
                tt("dve", ang, tab[:, 2, :, :], frb, ALU.mult, [TK_] + K, K)
                tt("dve", tsn, tab[:, 3, :, :], fib, ALU.mult, [TK_] + K, K)
                tt("dve", tab[:, 0, :, :], ang, tsn, ALU.add, K, [TK_])
                tt("dve", ang, tab[:, 2, :, :], fib, ALU.mult, [TK_] + K, K)
                tt("dve", tsn, tab[:, 3, :, :], frb, ALU.mult, [TK_] + K, K)
                tt("dve", tab[:, 1, :, :], ang, tsn, ALU.subtract, K, [TK_])
                cp("dve", tab[:, 4, :, :], rho[:].unsqueeze(2).to_broadcast([128, 12, 128]), [rho], [TK_])
                mset("dve", tab[:, 4, :, 0:1], 0.0, [TK_])
            P.barrier()

        def make_rowlocal(cv, nslab):
            V = {"nt": NT}
            V["xt_"] = cv.f(KT * NT, "p (a b) -> p a b", a=KT); V["sqt_"] = cv.f(KT * NT, "p (a b) -> p a b", a=KT)
            V["hT_"] = cv.b(KT * NT, "p (a b) -> p a b", a=KT); V["rs_"] = cv.f(NT)
            V["slabs"] = [cv.b(SLABF) for _ in range(nslab)]
            set_nt(V, NT)
            return V

        def set_nt(V, nt):
            V["nt"] = nt
            V["xt"] = V["xt_"][:, :, 0:nt]; V["sqt"] = V["sqt_"][:, :, 0:nt]; V["hT"] = V["hT_"][:, :, 0:nt]; V["rs"] = V["rs_"][:, 0:nt]

        slab_rr = [0]

        def load_slab(V, w_ap, nk):
            i = slab_rr[0] % len(V["slabs"])
            slab_rr[0] += 1
            s = V["slabs"][i]
            key = "slab%d" % i
            v = s[:, 0:nk * 128].rearrange("p (k c) -> p k c", k=nk)
            dma(v, w_ap, [], [key])
            return v, key

        def norm(V, scale_fn, shift_fn, extra_r, dst, dkey):
            xt, sqt, rs = V["xt"], V["sqt"], V["rs"]
            NT = V["nt"]
            tt("pool", sqt, xt, xt, ALU.mult, ["xt"], ["sqt"])
            ps = PS()
            for k in range(KT):
                mm(ps[:, 0:NT], ones, sqt[:, k, :], k == 0, k == KT - 1, [cst, "sqt"], [ps])
            act(rs, ps[:, 0:NT], AF.Sqrt, [ps], ["rs"], bias=EPS, scale=1.0 / D)
            recip(rs, rs, ["rs"], ["rs"])
            tt("dve", sqt, xt, rs.unsqueeze(1).to_broadcast([128, KT, NT]), ALU.mult, ["xt", "rs"], ["sqt"])
            for k in range(KT):
                sh = shift_fn(k)
                if sh is None:
                    act(dst[:, k, :], sqt[:, k, :], AF.Identity, ["sqt"] + extra_r, [dkey], scale=scale_fn(k))
                else:
                    act(dst[:, k, :], sqt[:, k, :], AF.Identity, ["sqt"] + extra_r, [dkey], bias=sh, scale=scale_fn(k))

        def proj_fm(V, w_l, c0, ntiles, nk, rhs_t, rhs_key, evac):
            NT = V["nt"]
            for j in range(ntiles):
                v, s = load_slab(V, w_l[:, c0 + j], nk)
                ps = PS()
                for k in range(nk):
                    mm(ps[:, 0:NT], v[:, k, :], rhs_t[:, k, :], k == 0, k == nk - 1, [s, rhs_key], [ps])
                evac(j, ps)

        def proj_tm(V, w_l, c0, ntiles, evac, ncols=128):
            hT = V["hT"]
            NCH = V["nt"] // 128
            for j in range(ntiles):
                v, s = load_slab(V, w_l[:, c0 + j], KT)
                for c in range(NCH):
                    ps = PS()
                    for k in range(KT):
                        mm(ps[:, 0:ncols], hT[:, k, c * 128:(c + 1) * 128], v[:, k, 0:ncols], k == 0, k == KT - 1, [s, "hT"], [ps])
                    evac(j, c, ps)

        def conv3(dst, src, wv, woff, rowlen, r, w, NT):
            nr_ = NT // rowlen
            for j in range(4):
                w0 = wv[:, woff + 0 + j:woff + 1 + j]; w1 = wv[:, woff + 4 + j:woff + 5 + j]; w2 = wv[:, woff + 8 + j:woff + 9 + j]
                d3 = dst[:, j, :].rearrange("p (r c) -> p r c", r=nr_); s3 = src[:, j, :].rearrange("p (r c) -> p r c", r=nr_)
                act(dst[:, j, :], src[:, j, :], AF.Identity, r, w, scale=w1)
                stt("dve", d3[:, :, 1:rowlen], s3[:, :, 0:rowlen - 1], w0, d3[:, :, 1:rowlen], ALU.mult, ALU.add, r + w, w)
                stt("dve", d3[:, :, 0:rowlen - 1], s3[:, :, 1:rowlen], w2, d3[:, :, 0:rowlen - 1], ALU.mult, ALU.add, r + w, w)

        def phase1(l, src_of):
            P.barrier()
            cv = Carver()
            V = make_rowlocal(cv, NSLAB)
            NTF = 512; NCF = 4
            raw_ = cv.f(4 * NTF, "p (a b) -> p a b", a=4); qT_ = cv.f(4 * NTF, "p (a b) -> p a b", a=4); kT_ = cv.f(4 * NTF, "p (a b) -> p a b", a=4)
            vtm_ = cv.f(NCF * 512, "p (c e) -> p c e", c=NCF); uT_ = cv.f(3 * NTF, "p (a b) -> p a b", a=3); graw_ = cv.f(NCF * 16, "p (c e) -> p c e", c=NCF)
            w_l = wb["win"][l]
            units = [("ctx", 0, CTX)] + [("lat", u, NTF) for u in range(T // NTF)]
            for seg, u, NT in units:
                NCH = NT // 128
                set_nt(V, NT)
                xt, hT = V["xt"], V["hT"]
                raw = raw_[:, :, 0:NT]; qT = qT_[:, :, 0:NT]; kT = kT_[:, :, 0:NT]; vtm = vtm_[:, 0:NCH, :]; uT = uT_[:, :, 0:NT]; graw = graw_[:, 0:NCH, :]
                sg = 1 if seg == "ctx" else 0
                t0 = u * NTF
                rowlen = NT if seg == "ctx" else 64
                dma(xt, src_of[seg][:, t0:t0 + NT].rearrange("(k p) t -> p k t", p=128), [("x", seg, u)], ["xt"])
                norm(V, lambda k: lcs[:, sg, 0, k:k + 1], lambda k: modv[:, k, sg:sg + 1], [lcs, modv], hT, "hT")
                proj_fm(V, w_l, TQ, 4, KT, hT, "hT", lambda j, ps: act(raw[:, j, :], ps[:, 0:NT], AF.Copy, [ps], ["raw"]))
                conv3(qT, raw, wconv, 0, rowlen, ["raw", wconv], ["qT"], NT)
                act(qT, qT, AF.Silu, ["qT"], ["qT"])
                dma(qd[seg][:, :, t0:t0 + NT], qT, ["qT"], [("q", seg, u)])
                proj_fm(V, w_l, TK, 4, KT, hT, "hT", lambda j, ps: act(raw[:, j, :], ps[:, 0:NT], AF.Copy, [ps], ["raw"]))
                conv3(kT, raw, wconv, 12, rowlen, ["raw", wconv], ["kT"], NT)
                act(kT, kT, AF.Silu, ["kT"], ["kT"])
                ts("pool", kT, kT, float(128 ** -0.5), None, ALU.mult, None, ["kT"], ["kT"])
                dma(kd[seg][:, :, t0:t0 + NT], kT, ["kT"], [("k", seg, u)])
                proj_tm(V, w_l, TV, 4, lambda j, c, ps: act(vtm[:, c, j * 128:(j + 1) * 128], ps[:, 0:128], AF.Copy, [ps], ["vtm"]))
                dma(vd[seg][t0:t0 + NT, :].rearrange("(c p) e -> p c e", p=128), vtm, ["vtm"], [("v", seg, u)])
                proj_tm(V, w_l, TG, 1, lambda j, c, ps: tt("dve", graw[:, c, :], ps[:, 0:16], bg_row[:], ALU.add, [ps, bg_row], ["graw"]), ncols=16)
                dma(gd[seg][t0:t0 + NT, :].rearrange("(c p) e -> p c e", p=128), graw, ["graw"], [("g", seg, u)])
                proj_fm(V, w_l, TU, 3, KT, hT, "hT", lambda j, ps: act(uT[:, j, :], ps[:, 0:NT], AF.Copy, [ps], ["uT"]))
                dma(ud[seg][:, :, t0:t0 + NT], uT, ["uT"], [("u", seg, u)])

        def phase2(l):
            P.barrier()
            cv = Carver(NA - 2 * TABF)
            nlat = T // 128
            chains = {0: [("ctx", c) for c in range(CTX // 128)] + [("lat", c) for c in range(nlat)],
                      1: [("ctx", c) for c in range(CTX // 128 - 1, -1, -1)] + [("lat", c) for c in range(nlat - 1, -1, -1)]}
            for d in range(2):
                mset("pool", Caugd[d][:], 0.0, [Caugd[d]]); mset("pool", mstd[d][:], 0.0, [mstd[d]]); mset("pool", card[d][:], 0.0, [card[d]])

            def unit_of(seg, c):
                return (seg, 0)

            Bre = cv.f(1536, "p (a b) -> p a b", a=12); Bim = cv.f(1536, "p (a b) -> p a b", a=12)
            Cre = cv.f(1536, "p (a b) -> p a b", a=12); Cimn = cv.f(1536, "p (a b) -> p a b", a=12)
            dma(Bre, din["Bre"][l], [], ["Bre"]); dma(Bim, din["Bim"][l], [], ["Bim"]); dma(Cre, din["Cre"][l], [], ["Cre"]); dma(Cimn, din["Cim"][l], [], ["Cimn"])
            ts("dve", Cimn, Cimn, -1.0, None, ALU.mult, None, ["Cimn"], ["Cimn"])

            def mlstm_stream(d):
                bwd = d == 1
                nm = "m%d" % d
                qc = [cv.f(512, "p (a b) -> p a b", a=4) for _ in range(2)]; kc = [cv.f(512, "p (a b) -> p a b", a=4) for _ in range(2)]
                va = [cv.f(516, "p (a b) -> p a b", a=4) for _ in range(2)]; gc = [cv.f(16) for _ in range(2)]
                gsm = cv.f(64, "p (a b) -> p a b", a=16); ex = cv.f(20, "p (a b) -> p a b", a=5)
                gdt = cv.f(512, "p (a b) -> p a b", a=4); tmt = cv.f(512, "p (a b) -> p a b", a=4)
                vw = cv.f(516, "p (a b) -> p a b", a=4); PTa = cv.f(512, "p (a b) -> p a b", a=4); ktm = cv.f(512, "p (a b) -> p a b", a=4)
                r1 = cv.f(258, "p (a b) -> p a b", a=2); tmx = cv.f(258, "p (a b) -> p a b", a=2); rra = cv.f(516, "p (a b) -> p a b", a=4)
                dn = cv.f(16, "p (a b) -> p a b", a=4); hdir = cv.f(512, "p (a b) -> p a b", a=4); Ctmp = cv.f(516, "p (a b) -> p a b", a=4)
                b0 = pbs[2 * d]; b1 = pbs[2 * d + 1]
                Caug = Caugd[d]; mst = mstd[d]
                io, fo = (8, 12) if bwd else (0, 4)
                Tri = Lmat if bwd else Umat
                biasM = biasU if bwd else biasL
                K_ = lambda s_: nm + s_
                chain = chains[d]
                for b_ in range(2):
                    mset("pool", va[b_][:, :, 128:129], 1.0, [K_("va%d" % b_)])

                def load(i):
                    seg, c = chain[i]
                    b_ = i % 2
                    tok = c * 128
                    su = unit_of(seg, c)
                    dma(qc[b_], qd[seg][:, :, tok:tok + 128], [("q",) + su], [K_("qc%d" % b_)])
                    dma(kc[b_], kd[seg][:, :, tok:tok + 128], [("k",) + su], [K_("kc%d" % b_)])
                    dma(va[b_][:, :, 0:128], vd[seg][tok:tok + 128, :].rearrange("p (h e) -> p h e", h=4), [("v",) + su], [K_("va%d" % b_)])
                    dma(gc[b_], gd[seg][tok:tok + 128, :], [("g",) + su], [K_("gc%d" % b_)])
                load(0)
                for i, (seg, c) in enumerate(chain):
                    b_ = i % 2
                    if i + 1 < len(chain):
                        load(i + 1)
                    q_, k_, v_, g_ = qc[b_], kc[b_], va[b_], gc[b_]
                    kq, kk_, kv, kg = K_("qc%d" % b_), K_("kc%d" % b_), K_("va%d" % b_), K_("gc%d" % b_)
                    G = [K_("gsm")]; EX = [K_("ex")]
                    e1 = gsm[:, 0, :]; sp = gsm[:, 1, :]; g = gsm[:, 2, :]; gmax = gsm[:, 3, :]; cmax = gsm[:, 4, :]
                    Mx = gsm[:, 5, :]; M = gsm[:, 6, :]
                    act(e1, g_[:, fo:fo + 4], AF.Exp, [kg], G, scale=-1.0)
                    act(sp, e1, AF.Ln, G, G, bias=1.0)
                    mm(b0[:, 0:4], Tri, sp, True, True, [cst] + G, [b0])
                    mm(b0[:, 4:8], ones, sp, True, True, [cst] + G, [b0])
                    yield
                    tt("dve", g, g_[:, io:io + 4], b0[:, 0:4], ALU.add, [kg, b0], G)
                    tt("dve", gdt, ident.unsqueeze(1).to_broadcast([128, 4, 128]), g.unsqueeze(2).to_broadcast([128, 4, 128]), ALU.mult, [cst] + G, [K_("gdt")])
                    mm(b1[:, :], ones, gdt.rearrange("p a b -> p (a b)"), True, True, [cst, K_("gdt")], [b1])
                    yield
                    b1v = b1[:, :].rearrange("p (a b) -> p a b", a=4)
                    red("dve", gmax, b1v, ALU.max, [b1], G)
                    tt("dve", tmt, b1v, biasM.unsqueeze(1).to_broadcast([128, 4, 128]), ALU.add, [b1, cst], [K_("tmt")])
                    red("dve", cmax, tmt, ALU.max, [K_("tmt")], G)
                    tt("dve", Mx, mst[:], gmax, ALU.max, [mst] + G, G)
                    tt("dve", M, mst[:], cmax, ALU.max, [mst] + G, G)
                    tt("dve", ex[:, 0, :], g, Mx, ALU.subtract, G, EX)
                    tt("dve", ex[:, 1, :], Mx, M, ALU.subtract, G, EX)
                    tt("dve", ex[:, 2, :], mst[:], M, ALU.subtract, [mst] + G, EX)
                    tt("dve", ex[:, 3, :], mst[:], Mx, ALU.subtract, [mst] + G, EX)
                    tt("dve", ex[:, 4, :], b0[:, 0:4], M, ALU.subtract, [b0] + G, EX)
                    act(ex, ex, AF.Exp, EX, EX)
                    tt("dve", mst[:], Mx, b0[:, 4:8], ALU.subtract, G + [b0], [mst])
                    yield
                    tt("dve", vw, v_, ex[:, 0, :].unsqueeze(2).to_broadcast([128, 4, 129]), ALU.mult, [kv] + EX, [K_("vw")])
                    for h in range(H):
                        mm(b0[:, h * 128:(h + 1) * 128], k_[:, h, :], q_[:, h, :], True, True, [kk_, kq], [b0])
                    for h in range(H):
                        tr(b1[:, h * 128:(h + 1) * 128], k_[:, h, :], [kk_], [b1])
                    yield
                    maskPT = (Lmat if bwd else Umat).unsqueeze(1).to_broadcast([128, 4, 128])
                    tt("dve", PTa, b0[:, :].rearrange("p (a b) -> p a b", a=4), maskPT, ALU.mult, [b0, cst], [K_("PTa")])
                    act(ktm, b1[:, :].rearrange("p (a b) -> p a b", a=4), AF.Copy, [b1], [K_("ktm")])
                    yield
                    for pr in range(2):
                        for hh in range(2):
                            h = 2 * pr + hh
                            mm(b0[:, hh * 129:(hh + 1) * 129], PTa[:, h, :], vw[:, h, :], True, True, [K_("PTa"), K_("vw")], [b0])
                            mm(b1[:, hh * 129:(hh + 1) * 129], q_[:, h, :], Caug[:, h, :], True, True, [kq, Caug], [b1])
                        yield
                        hs = slice(2 * pr, 2 * pr + 2)
                        b0p = b0[:, 0:258].rearrange("p (a b) -> p a b", a=2); b1p = b1[:, 0:258].rearrange("p (a b) -> p a b", a=2)
                        tt("dve", r1, b1p, ex[:, 2, hs].unsqueeze(2).to_broadcast([128, 2, 129]), ALU.mult, [b1] + EX, [K_("r1")])
                        tt("dve", tmx, b0p, ex[:, 1, hs].unsqueeze(2).to_broadcast([128, 2, 129]), ALU.mult, [b0] + EX, [K_("tmx")])
                        tt("dve", rra[:, hs, :], tmx, r1, ALU.add, [K_("tmx"), K_("r1")], [K_("rra")])
                        yield
                    den = rra[:, :, 128]
                    ts("dve", dn[:, 0, :], den, -1.0, None, ALU.mult, None, [K_("rra")], [K_("dn")])
                    tt("dve", dn[:, 0, :], dn[:, 0, :], den, ALU.max, [K_("dn"), K_("rra")], [K_("dn")])
                    tt("dve", dn[:, 0, :], dn[:, 0, :], ex[:, 4, :], ALU.max, [K_("dn")] + EX, [K_("dn")])
                    recip(dn[:, 1, :], dn[:, 0, :], [K_("dn")], [K_("dn")])
                    tt("dve", hdir, rra[:, :, 0:128], dn[:, 1, :].unsqueeze(2).to_broadcast([128, 4, 128]), ALU.mult, [K_("rra"), K_("dn")], [K_("hdir")])
                    dma(hd[(d, seg)][c * 128:(c + 1) * 128, :].rearrange("p (h e) -> p h e", h=4), hdir, [K_("hdir")], [("h", d, seg, c)])
                    yield
                    for h in range(H):
                        bb = b0 if h < 2 else b1
                        hh = h % 2
                        mm(bb[:, hh * 129:(hh + 1) * 129], ktm[:, h, :], vw[:, h, :], True, True, [K_("ktm"), K_("vw")], [bb])
                    tt("dve", Ctmp, Caug[:], ex[:, 3, :].unsqueeze(2).to_broadcast([128, 4, 129]), ALU.mult, [Caug] + EX, [K_("Ctmp")])
                    yield
                    tt("dve", Caug[:, 0:2, :], Ctmp[:, 0:2, :], b0[:, 0:258].rearrange("p (a b) -> p a b", a=2), ALU.add, [K_("Ctmp"), b0], [Caug])
                    tt("dve", Caug[:, 2:4, :], Ctmp[:, 2:4, :], b1[:, 0:258].rearrange("p (a b) -> p a b", a=2), ALU.add, [K_("Ctmp"), b1], [Caug])
                    yield

            def s5_stream(d):
                bwd = d == 1
                nm = "s%d" % d
                K_ = lambda s_: nm + s_
                uc = [cv.f(384, "p (a b) -> p a b", a=3) for _ in range(2)]
                B2 = cv.f(1024, "p (r a b) -> p r a b", r=2, a=4); S2 = cv.f(1024, "p (r a b) -> p r a b", r=2, a=4)
                m1 = cv.f(1024, "p (r a b) -> p r a b", r=2, a=4); m2 = cv.f(1024, "p (r a b) -> p r a b", r=2, a=4)
                ycb = cv.f(384, "p (a b) -> p a b", a=3)
                bA = pbs[4 + 2 * d]; bB = pbs[5 + 2 * d]
                tab = tabd[d]; rho = rhod[d]; car = card[d]; TKd = TABK[d]
                chain = chains[d]
                last = 0 if bwd else 127

                def R3(ap):
                    return ap[:, :, ::-1] if bwd else ap

                def R4(ap):
                    return ap[:, :, :, ::-1] if bwd else ap

                def load(i):
                    seg, c = chain[i]
                    b_ = i % 2
                    tok = c * 128
                    dma(uc[b_], ud[seg][:, :, tok:tok + 128], [("u",) + unit_of(seg, c)], [K_("uc%d" % b_)])
                load(0)
                for i, (seg, c) in enumerate(chain):
                    b_ = i % 2
                    if i + 1 < len(chain):
                        load(i + 1)
                    u_ = uc[b_]; ku = K_("uc%d" % b_)
                    for j in range(3):
                        ks = slice(4 * j, 4 * j + 4)
                        for kk in range(4):
                            k = 4 * j + kk
                            mm(bA[:, kk * 128:(kk + 1) * 128], Bre[:, k, :], u_[:, j, :], True, True, ["Bre", ku], [bA])
                            mm(bB[:, kk * 128:(kk + 1) * 128], Bim[:, k, :], u_[:, j, :], True, True, ["Bim", ku], [bB])
                        yield
                        bAv = R3(bA[:, :].rearrange("p (a b) -> p a b", a=4)); bBv = R3(bB[:, :].rearrange("p (a b) -> p a b", a=4))
                        act(B2[:, 0], bAv, AF.Copy, [bA], [K_("B2")])
                        act(B2[:, 1], bBv, AF.Copy, [bB], [K_("B2")])
                        act(S2[:, 0], bBv, AF.Copy, [bB], [K_("S2")], scale=-1.0)
                        act(S2[:, 1], bAv, AF.Copy, [bA], [K_("S2")])
                        yield
                        bc = lambda t_: t_.unsqueeze(1).to_broadcast([128, 2, 4, 128])
                        tt("dve", m1, B2, bc(tab[:, 0, ks, :]), ALU.mult, [K_("B2"), TKd], [K_("m1")])
                        tt("dve", m2, S2, bc(tab[:, 1, ks, :]), ALU.mult, [K_("S2"), TKd], [K_("m2")])
                        tt("dve", m1, m1, m2, ALU.add, [K_("m1"), K_("m2")], [K_("m1")])
                        tt("dve", m1[:, :, :, 0], m1[:, :, :, 0], car[:, :, ks], ALU.add, [K_("m1"), car], [K_("m1")])
                        yield
                        rf = tab[:, 4, ks, :].rearrange("p a b -> p (a b)")
                        for r_ in range(2):
                            P.op("dve", lambda e, r_=r_, rf=rf: e.tensor_tensor_scan(B2[:, r_].rearrange("p a b -> p (a b)"), rf,
                                                                                     m1[:, r_].rearrange("p a b -> p (a b)"), 0.0, ALU.mult, ALU.add),
                                 reads=[TKd, K_("m1")], writes=[K_("B2")])
                        yield
                        act(S2[:, 0], B2[:, 1], AF.Copy, [K_("B2")], [K_("S2")], scale=-1.0)
                        act(S2[:, 1], B2[:, 0], AF.Copy, [K_("B2")], [K_("S2")])
                        tt("dve", m1, B2, bc(tab[:, 2, ks, :]), ALU.mult, [K_("B2"), TKd], [K_("m1")])
                        yield
                        tt("dve", m2, S2, bc(tab[:, 3, ks, :]), ALU.mult, [K_("S2"), TKd], [K_("m2")])
                        tt("dve", R4(B2), m1, m2, ALU.add, [K_("m1"), K_("m2")], [K_("B2")])
                        tt("dve", car[:, :, ks], B2[:, :, :, last], rho[:, ks].unsqueeze(1).to_broadcast([128, 2, 4]), ALU.mult, [K_("B2"), rho], [car])
                        yield
                        for kk in range(4):
                            k = 4 * j + kk
                            mm(bA[:, 0:128], Cre[:, k, :], B2[:, 0, kk, :], kk == 0, False, ["Cre", K_("B2")], [bA])
                            mm(bA[:, 0:128], Cimn[:, k, :], B2[:, 1, kk, :], False, kk == 3, ["Cimn", K_("B2")], [bA])
                        act(ycb[:, j, :], bA[:, 0:128], AF.Copy, [bA], [K_("ycb")])
                        yield
                    dma(yd_[(d, seg)][:, :, c * 128:(c + 1) * 128], ycb, [K_("ycb")], [("y", d, seg, c)])

            gens = [mlstm_stream(0), mlstm_stream(1), s5_stream(0), s5_stream(1)]
            alive = list(gens)
            while alive:
                for g_ in list(alive):
                    try:
                        next(g_)
                    except StopIteration:
                        alive.remove(g_)

        def phase3(l, src_of, dst_of, is_last):
            P.barrier()
            cv = Carver()
            N3 = 256; NC3 = 2
            sqt = cv.f(KT * N3, "p (a b) -> p a b", a=KT); rs = cv.f(N3)
            slabs3 = [cv.b(SLABF) for _ in range(6)]
            gmh_row = cv.f(512); gsgu_row = cv.f(512); wsT = cv.f(512, "p (a b) -> p a b", a=4); bs_row = cv.f(512)
            wglu = cv.f(3 * 384, "p (a b) -> p a b", a=3)
            CK = ["p3c"]
            dma(gmh_row, din["gmh"][l].partition_broadcast(128), [], CK); dma(gsgu_row, din["gsgu"][l].partition_broadcast(128), [], CK)
            dma(bs_row, din["bs"][l].partition_broadcast(128), [], CK); dma(wsT, din["wsT"][l], [], CK); dma(wglu, din["wglu"][l], [], CK)
            w_l = wb["win"][l]

            def alloc_set(tag):
                S = {"tag": tag}
                S["xt"] = cv.f(KT * N3, "p (a b) -> p a b", a=KT); S["hT"] = cv.b(KT * N3, "p (a b) -> p a b", a=KT)
                S["merged"] = cv.b(KT * N3, "p (a b) -> p a b", a=KT)
                S["tmpA"] = cv.f(N3); S["tmpB"] = cv.f(N3)
                S["sigo"] = cv.f(NC3 * 512, "p (c e) -> p c e", c=NC3); S["vg"] = cv.f(NC3 * 512, "p (c e) -> p c e", c=NC3)
                for n in ("gsu", "cb", "cc", "yd"):
                    S[n] = cv.f(4 * N3, "p (a b) -> p a b", a=4)
                for n in ("ydb", "yaT", "ybT"):
                    S[n] = cv.b(4 * N3, "p (a b) -> p a b", a=4)
                S["ysum"] = cv.f(3 * N3, "p (a b) -> p a b", a=3); S["ys5T"] = cv.b(3 * N3, "p (a b) -> p a b", a=3); S["uT"] = cv.f(3 * N3, "p (a b) -> p a b", a=3)
                S["a_t"] = cv.b(FT * N3, "p (a b) -> p a b", a=FT)
                S["hdir"] = cv.f(512); S["hbt"] = cv.f(512); S["hn"] = cv.f(512)
                S["ycb"] = cv.f(3 * N3, "p (a b) -> p a b", a=3); S["sm4"] = cv.f(8)
                return S
            sets = [alloc_set("A"), alloc_set("B")]
            srr = [0]

            def lslab(w_ap, nk):
                i = srr[0] % 6
                srr[0] += 1
                key = "s3slab%d" % i
                v = slabs3[i][:, 0:nk * 128].rearrange("p (k c) -> p k c", k=nk)
                dma(v, w_ap, [], [key])
                return v, key

            def unit_gen(seg, u, S):
                NT = N3
                tg = S["tag"]
                K_ = lambda n: tg + n
                xt, hT, merged, tmpA, tmpB = S["xt"], S["hT"], S["merged"], S["tmpA"], S["tmpB"]
                sigo, vg, gsu, cb, cc, yd, ydb = S["sigo"], S["vg"], S["gsu"], S["cb"], S["cc"], S["yd"], S["ydb"]
                yaT, ybT, ysum, ys5T, uT, a_t = S["yaT"], S["ybT"], S["ysum"], S["ys5T"], S["uT"], S["a_t"]
                hdir, hbt, hn, ycb, sm4 = S["hdir"], S["hbt"], S["hn"], S["ycb"], S["sm4"]
                sg = 1 if seg == "ctx" else 0
                t0 = u * NT
                rowlen = NT if seg == "ctx" else 64

                def norm3(scale_fn, shift_fn, extra_r, dst, dkey):
                    tt("pool", sqt, xt, xt, ALU.mult, [K_("xt")], ["sqt3"])
                    ps = PS()
                    for k in range(KT):
                        mm(ps[:, 0:NT], ones, sqt[:, k, :], k == 0, k == KT - 1, [cst, "sqt3"], [ps])
                    act(rs, ps[:, 0:NT], AF.Sqrt, [ps], ["rs3"], bias=EPS, scale=1.0 / D)
                    recip(rs, rs, ["rs3"], ["rs3"])
                    tt("dve", sqt, xt, rs.unsqueeze(1).to_broadcast([128, KT, NT]), ALU.mult, [K_("xt"), "rs3"], ["sqt3"])
                    for k in range(KT):
                        sh = shift_fn(k)
                        if sh is None:
                            act(dst[:, k, :], sqt[:, k, :], AF.Identity, ["sqt3"] + extra_r, [dkey], scale=scale_fn(k))
                        else:
                            act(dst[:, k, :], sqt[:, k, :], AF.Identity, ["sqt3"] + extra_r, [dkey], bias=sh, scale=scale_fn(k))

                def pfm(c0, ntiles, evac):
                    for j in range(ntiles):
                        v, s_ = lslab(w_l[:, c0 + j], KT)
                        ps = PS()
                        for k in range(KT):
                            mm(ps[:, 0:NT], v[:, k, :], hT[:, k, :], k == 0, k == KT - 1, [s_, K_("hT")], [ps])
                        evac(j, ps)
                        yield "y"

                def ptm(c0, ntiles, evac):
                    for j in range(ntiles):
                        v, s_ = lslab(w_l[:, c0 + j], KT)
                        for c in range(NC3):
                            ps = PS()
                            for k in range(KT):
                                mm(ps[:, 0:128], hT[:, k, c * 128:(c + 1) * 128], v[:, k, :], k == 0, k == KT - 1, [s_, K_("hT")], [ps])
                            evac(j, c, ps)
                        yield "y"

                dma(xt, src_of[seg][:, t0:t0 + NT].rearrange("(k p) t -> p k t", p=128), [], [K_("xt")])
                dma(uT, ud[seg][:, :, t0:t0 + NT], [], [K_("uT")])
                dma(ysum, yd_[(0, seg)][:, :, t0:t0 + NT], [], [K_("ysum")])
                dma(ycb, yd_[(1, seg)][:, :, t0:t0 + NT], [], [K_("ycb")])
                yield "y"
                norm3(lambda k: lcs[:, sg, 0, k:k + 1], lambda k: modv[:, k, sg:sg + 1], [lcs, modv], hT, K_("hT"))
                yield "y"
                tt("dve", ysum, ysum, ycb, ALU.add, [K_("ysum"), K_("ycb")], [K_("ysum")])
                for j in range(3):
                    stt("dve", ysum[:, j, :], uT[:, j, :], s5d[:, j:j + 1], ysum[:, j, :], ALU.mult, ALU.add, [K_("uT"), s5d, K_("ysum")], [K_("ysum")])
                act(ysum, ysum, AF.Gelu_apprx_tanh, [K_("ysum")], [K_("ysum")])
                yield "S"
                yield from ptm(TO, 4, lambda j, c, ps: act(sigo[:, c, j * 128:(j + 1) * 128], ps[:, 0:128], AF.Sigmoid, [ps], [K_("sigo")]))
                yield from ptm(TSV, 4, lambda j, c, ps: act(vg[:, c, j * 128:(j + 1) * 128], ps[:, 0:128], AF.Gelu_apprx_tanh, [ps], [K_("vg")]))
                yield from pfm(TSU, 4, lambda j, ps: act(gsu[:, j, :], ps[:, 0:NT], AF.Gelu_apprx_tanh, [ps], [K_("gsu")]))
                yield from pfm(TCB, 4, lambda j, ps: act(cb[:, j, :], ps[:, 0:NT], AF.Copy, [ps], [K_("cb")]))
                yield from pfm(TCC, 4, lambda j, ps: act(cc[:, j, :], ps[:, 0:NT], AF.Copy, [ps], [K_("cc")]))
                yield from pfm(TCX, 4, lambda j, ps: tt("dve", cc[:, j, :], cc[:, j, :], ps[:, 0:NT], ALU.mult, [K_("cc"), ps], [K_("cc")]))
                for j in range(3):
                    ps = PS()
                    for k in range(3):
                        mm(ps[:, 0:NT], wglu[:, k, j * 128:(j + 1) * 128], ysum[:, k, :], k == 0, k == 2, CK + [K_("ysum")], [ps])
                    act(tmpA, ps[:, 0:NT], AF.Sigmoid, [ps, bglu], [K_("tmpA")], bias=bglu[:, j:j + 1])
                    tt("dve", ys5T[:, j, :], ysum[:, j, :], tmpA, ALU.mult, [K_("ysum"), K_("tmpA")], [K_("ys5T")])
                    yield "y"
                yield "S"
                conv3(yd, cc, wsconv, 0, rowlen, [K_("cc"), wsconv], [K_("yd")], NT)
                tt("pool", ydb, yd, cb, ALU.mult, [K_("yd"), K_("cb")], [K_("ydb")])
                yield "y"
                hn4 = hn.rearrange("p (a b) -> p a b", a=4); hd4 = hdir.rearrange("p (a b) -> p a b", a=4)
                for c in range(NC3):
                    cs = slice(c * 128, (c + 1) * 128)
                    tok0 = t0 + c * 128
                    dma(hdir, hd[(0, seg)][tok0:tok0 + 128, :], [], [K_("hdir")])
                    dma(hbt, hd[(1, seg)][tok0:tok0 + 128, :], [], [K_("hbt")])
                    tt("dve", hdir, hdir, hbt, ALU.add, [K_("hdir"), K_("hbt")], [K_("hdir")])
                    act(hn, hdir, AF.Square, [K_("hdir")], [K_("hn")])
                    red("dve", sm4[:, 2:6], hn4, ALU.add, [K_("hn")], [K_("sm4")])
                    act(sm4[:, 2:6], sm4[:, 2:6], AF.Sqrt, [K_("sm4")], [K_("sm4")], bias=EPS, scale=1.0 / 128)
                    recip(sm4[:, 2:6], sm4[:, 2:6], [K_("sm4")], [K_("sm4")])
                    yield "y"
                    tt("dve", hn4, hd4, sm4[:, 2:6].unsqueeze(2).to_broadcast([128, 4, 128]), ALU.mult, [K_("hdir"), K_("sm4")], [K_("hn")])
                    tt("dve", hn, hn, gmh_row, ALU.mult, [K_("hn")] + CK, [K_("hn")])
                    tt("dve", hn, hn, sigo[:, c, :], ALU.mult, [K_("hn"), K_("sigo")], [K_("hn")])
                    red("dve", sm4[:, 6:7], vg[:, c, :], ALU.add, [K_("vg")], [K_("sm4")])
                    ts("dve", sm4[:, 6:7], sm4[:, 6:7], 1.0 / 512, None, ALU.mult, None, [K_("sm4")], [K_("sm4")])
                    ts("dve", vg[:, c, :], vg[:, c, :], sm4[:, 6:7], None, ALU.subtract, None, [K_("vg"), K_("sm4")], [K_("vg")])
                    act(hbt, vg[:, c, :], AF.Square, [K_("vg")], [K_("hbt")])
                    yield "y"
                    pT_ = PS()
                    for h in range(H):
                        tr(pT_[:, h * 128:(h + 1) * 128], hn[:, h * 128:(h + 1) * 128], [K_("hn")], [pT_])
                    act(yaT[:, :, cs], pT_[:, :].rearrange("p (a b) -> p a b", a=4), AF.Copy, [pT_], [K_("yaT")])
                    red("dve", sm4[:, 7:8], hbt, ALU.add, [K_("hbt")], [K_("sm4")])
                    act(sm4[:, 7:8], sm4[:, 7:8], AF.Sqrt, [K_("sm4")], [K_("sm4")], bias=EPS, scale=1.0 / 512)
                    recip(sm4[:, 7:8], sm4[:, 7:8], [K_("sm4")], [K_("sm4")])
                    stt("dve", vg[:, c, :], vg[:, c, :], sm4[:, 7:8], gsgu_row, ALU.mult, ALU.mult, [K_("vg"), K_("sm4")] + CK, [K_("vg")])
                    yield "y"
                    pM = PS()
                    for gi in range(4):
                        mm(pM[:, gi * 128:(gi + 1) * 128], vg[:, c, gi * 128:(gi + 1) * 128], wsT[:, gi, :], True, True, [K_("vg")] + CK, [pM])
                    tt("dve", hn4, pM[:, :].rearrange("p (a b) -> p a b", a=4), bs_row.rearrange("p (a b) -> p a b", a=4), ALU.add, [pM] + CK, [K_("hn")])
                    tt("dve", ybT[:, :, cs], hn4, gsu[:, :, cs], ALU.mult, [K_("hn"), K_("gsu")], [K_("ybT")])
                    yield "y"
                yield "S"
                branches = [(yaT, K_("yaT"), 4, wb["wupm"][l]), (ybT, K_("ybT"), 4, wb["wups"][l]), (ys5T, K_("ys5T"), 3, wb["wup5"][l]), (ydb, K_("ydb"), 4, wb["wupc"][l])]
                for jd in range(KT):
                    for b, (ybr, ykey, nkb, wup) in enumerate(branches):
                        vgt, sgt = lslab(w_l[:, TGATE + b * 8 + jd], KT)
                        pg = PS()
                        for k in range(KT):
                            mm(pg[:, 0:NT], vgt[:, k, :], hT[:, k, :], k == 0, k == KT - 1, [sgt, K_("hT")], [pg])
                        act(tmpA, pg[:, 0:NT], AF.Sigmoid, [pg], [K_("tmpA")])
                        vu, su_ = lslab(wup[:, jd], nkb)
                        pu = PS()
                        for k in range(nkb):
                            mm(pu[:, 0:NT], vu[:, k, :], ybr[:, k, :], k == 0, k == nkb - 1, [su_, ykey], [pu])
                        if b == 0:
                            tt("dve", tmpB, tmpA, pu[:, 0:NT], ALU.mult, [K_("tmpA"), pu], [K_("tmpB")])
                        else:
                            tt("dve", tmpA, tmpA, pu[:, 0:NT], ALU.mult, [K_("tmpA"), pu], [K_("tmpA")])
                            if b < 3:
                                tt("dve", tmpB, tmpB, tmpA, ALU.add, [K_("tmpA"), K_("tmpB")], [K_("tmpB")])
                            else:
                                tt("dve", merged[:, jd, :], tmpB, tmpA, ALU.add, [K_("tmpA"), K_("tmpB")], [K_("merged")])
                        yield "y"
                yield "S"
                for j in range(KT):
                    v, s_ = lslab(wb["wout"][l][:, j], KT)
                    ps = PS()
                    for k in range(KT):
                        mm(ps[:, 0:NT], v[:, k, :], merged[:, k, :], k == 0, k == KT - 1, [s_, K_("merged")], [ps])
                    stt("dve", xt[:, j, :], ps[:, 0:NT], modv[:, 16 + j, sg:sg + 1], xt[:, j, :], ALU.mult, ALU.add, [ps, modv, K_("xt")], [K_("xt")])
                    yield "y"
                norm3(lambda k: lcs[:, sg, 1, k:k + 1], lambda k: modv[:, 24 + k, sg:sg + 1], [lcs, modv], hT, K_("hT"))
                yield "S"
                for i in range(FT):
                    v1, s1 = lslab(wb["wfg"][l][:, i], KT)
                    pg = PS()
                    for k in range(KT):
                        mm(pg[:, 0:NT], v1[:, k, :], hT[:, k, :], k == 0, k == KT - 1, [s1, K_("hT")], [pg])
                    act(tmpA, pg[:, 0:NT], AF.Silu, [pg], [K_("tmpA")])
                    v2, s2 = lslab(wb["wfu"][l][:, i], KT)
                    pu = PS()
                    for k in range(KT):
                        mm(pu[:, 0:NT], v2[:, k, :], hT[:, k, :], k == 0, k == KT - 1, [s2, K_("hT")], [pu])
                    tt("dve", a_t[:, i, :], tmpA, pu[:, 0:NT], ALU.mult, [K_("tmpA"), pu], [(tg + "a", i)])
                    yield "y"
                for jd in range(KT):
                    ps = PS()
                    for hf in range(2):
                        v3, s3 = lslab(wb["wfd"][l][:, jd, hf * 11:(hf + 1) * 11], 11)
                        for k in range(11):
                            i = hf * 11 + k
                            mm(ps[:, 0:NT], v3[:, k, :], a_t[:, i, :], i == 0, i == FT - 1, [s3, (tg + "a", i)], [ps])
                    stt("dve", xt[:, jd, :], ps[:, 0:NT], modv[:, 40 + jd, sg:sg + 1], xt[:, jd, :], ALU.mult, ALU.add, [ps, modv, K_("xt")], [K_("xt")])
                    yield "y"
                if is_last and seg == "lat":
                    norm3(lambda k: gfin[:, k:k + 1], lambda k: None, [gfin], sqt, "sqt3")
                    dma(outT[:, t0:t0 + NT].rearrange("(k p) t -> p k t", p=128), sqt, ["sqt3"], [("out", u)])
                else:
                    dma(dst_of[seg][:, t0:t0 + NT].rearrange("(k p) t -> p k t", p=128), xt, [K_("xt")], [("x2", seg, u)])
                yield "S"

            units = ([] if is_last else [("ctx", 0)]) + [("lat", u) for u in range(T // N3)]
            gens = [unit_gen(seg, u, sets[i % 2]) for i, (seg, u) in enumerate(units)]

            def to_stage(g):
                while next(g) != "S":
                    pass

            def weave(ga, gb):
                da = db = False
                while not (da and db):
                    if not da:
                        da = next(ga) == "S"
                    if not db:
                        db = next(gb) == "S"
            for _ in range(3):
                to_stage(gens[0])
            for i in range(len(gens)):
                A = gens[i]
                Bn = gens[i + 1] if i + 1 < len(gens) else None
                for _ in range(3):
                    if Bn is None:
                        to_stage(A)
                    else:
                        weave(A, Bn)

        NU = T // 256
        cast_weights(0)
        for l in range(n_layers):
            last = l == n_layers - 1
            layer_prep(l)
            if l + 1 < n_layers:
                cast_weights(l + 1)
            src_of = {"lat": din["xT"] if l == 0 else xs, "ctx": din["cT"] if l == 0 else csd}
            dst_of = {"lat": xs, "ctx": csd}
            phase1(l, src_of)
            phase2(l)
            phase3(l, src_of, dst_of, last)
        P.barrier()
        P.final_wait("sp", [("out", u) for u in range(NU)] + ["dbg_" + n for n in dbg_out])
        P.emit()
    return nc, dbg_out, P


def _kt(w):
    K_, C = w.shape
    return np.ascontiguousarray(w.reshape(K_ // 128, 128, C).transpose(1, 0, 2))


def _tile(w):
    K_, C = w.shape
    return np.ascontiguousarray(w.reshape(K_ // 128, 128, C // 128, 128).transpose(1, 2, 0, 3))


def _pcol(v):
    return np.ascontiguousarray(v.reshape(-1, 128).T)


def make_consts():
    c = np.zeros((128, 7, 128), np.float32)
    r = np.arange(128)[:, None]; cidx = np.arange(128)[None, :]
    c[:, 0] = (r == cidx); c[:, 1] = 1.0
    c[:, 2] = (r <= cidx); c[:, 3] = (r >= cidx)
    c[:, 4] = np.where(r <= cidx, 0.0, -BIG); c[:, 5] = np.where(r >= cidx, 0.0, -BIG)
    c[:, 6] = (cidx + 1)
    return c


def shared_inputs(inp):
    f = lambda a: np.asarray(a, np.float32)
    o = {"cst": make_consts()}
    o["wmod"] = np.stack([_kt(f(inp["w_mod"][l])) for l in range(L)])
    o["bmod"] = np.stack([_pcol(f(inp["b_mod"][l])) for l in range(L)])
    o["gnm"] = np.stack([_pcol(f(inp["g_norm_mix"][l])) for l in range(L)])
    o["gnf"] = np.stack([_pcol(f(inp["g_norm_ffn"][l])) for l in range(L)])
    o["gfin"] = _pcol(f(inp["g_final"]))
    def win_tiles(w):
        cols = []
        for c0, n in ((COL_Q, 4), (COL_K, 4), (COL_V, 4)):
            cols += [w[:, c0 + i * 128:c0 + (i + 1) * 128] for i in range(n)]
        gt = np.zeros((D, 128), np.float32); gt[:, :16] = w[:, COL_G:COL_G + 16]; cols.append(gt)
        for c0, n in ((COL_U, 3), (COL_O, 4), (COL_SU, 4), (COL_SV, 4), (COL_CB, 4), (COL_CC, 4), (COL_CX, 4), (COL_GATE, 32)):
            cols += [w[:, c0 + i * 128:c0 + (i + 1) * 128] for i in range(n)]
        t = np.stack(cols, 0)
        return np.ascontiguousarray(t.reshape(NWT, KT, 128, 128).transpose(2, 0, 1, 3))
    o["win"] = np.stack([win_tiles(f(inp["w_in"][l])) for l in range(L)])
    o["bg"] = f(inp["b_gates"]).reshape(L, 1, 16)
    wc = f(inp["w_conv_qk"])
    o["wconv"] = np.ascontiguousarray(wc.reshape(L, 2, 3, 4, 128).transpose(0, 4, 1, 2, 3).reshape(L, 128, 24))
    o["gmh"] = f(inp["g_mh"]).reshape(L, 1, 512)
    o["gsgu"] = f(inp["g_sgu"]).reshape(L, 1, 512)
    o["wsT"] = np.ascontiguousarray(f(inp["w_sgu"]).transpose(0, 3, 1, 2))
    o["bs"] = f(inp["b_sgu"]).reshape(L, 1, 512)

    def ptile(a):
        return np.ascontiguousarray(a.reshape(L, 2, 12, 128).transpose(0, 1, 3, 2))
    o["are"] = ptile(f(inp["s5_a_re"])); o["aim"] = ptile(f(inp["s5_a_im"]))
    o["ldt"] = ptile(np.repeat(f(inp["s5_log_dt"])[..., None], 64, axis=-1))

    def bpad(B):
        out = np.zeros((L, 128, 12, 128), np.float32)
        for g in range(24):
            k, gg = g // 2, g % 2
            r0 = (g % 8) * 16
            out[:, r0:r0 + 16, k, gg * 64:(gg + 1) * 64] = B[:, g].transpose(0, 2, 1)
        return out

    def cpad(C):
        out = np.zeros((L, 128, 12, 128), np.float32)
        for g in range(24):
            k, gg = g // 2, g % 2
            c0 = (g % 8) * 16
            out[:, gg * 64:(gg + 1) * 64, k, c0:c0 + 16] = C[:, g].transpose(0, 2, 1)
        return out
    o["Bre"] = bpad(f(inp["s5_b_re"])); o["Bim"] = bpad(f(inp["s5_b_im"]))
    o["Cre"] = cpad(f(inp["s5_c_re"])); o["Cim"] = cpad(f(inp["s5_c_im"]))
    o["s5d"] = np.stack([_pcol(f(inp["s5_d"][l])) for l in range(L)])
    o["bglu"] = np.stack([_pcol(f(inp["b_glu"][l])) for l in range(L)])
    o["wglu"] = np.stack([_kt(f(inp["w_glu"][l])) for l in range(L)])
    ws = f(inp["w_sconv"])
    o["wsconv"] = np.ascontiguousarray(ws.reshape(L, 3, 4, 128).transpose(0, 3, 1, 2).reshape(L, 128, 12))
    for nm, key in (("wupm", "w_up_mlstm"), ("wups", "w_up_sgu"), ("wup5", "w_up_s5"), ("wupc", "w_up_sconv"), ("wout", "w_out"),
                    ("wfg", "w_ffn_gate"), ("wfu", "w_ffn_up"), ("wfd", "w_ffn_down")):
        o[nm] = np.stack([_tile(f(inp[key][l])) for l in range(L)])
    return o


def core_inputs(inp, b, T):
    f = lambda a: np.asarray(a, np.float32)
    o = {}
    o["xT"] = np.ascontiguousarray(f(inp["x"][b, :T]).T)
    o["cT"] = np.ascontiguousarray(f(inp["ctx"][b]).T)
    sc = np.stack([_pcol(f(inp["c"][b])), _pcol(f(inp["c_ctx"]))], axis=-1)
    o["sc"] = np.ascontiguousarray(sc)
    return o


_CACHE = {}


def kernel(**inputs):
    B, T = inputs["x"].shape[0], inputs["x"].shape[1]
    if T not in _CACHE:
        _CACHE[T] = build_program(T)[0]
    nc = _CACHE[T]
    sh = shared_inputs(inputs)
    in_maps = []
    for b in range(B):
        m = dict(sh)
        m.update(core_inputs(inputs, b, T))
        in_maps.append(m)
    res = run_bass_kernel_spmd(nc, in_maps, core_ids=list(range(B)))
    out = np.stack([np.ascontiguousarray(r["outT"].T) for r in res.results], axis=0)
    return out.astype(np.float32)
```

```python
import contextlib
import numpy as np
import concourse.bass as bass
import concourse.mybir as mybir
from concourse.bass_utils import run_bass_kernel_spmd

F32 = mybir.dt.float32
I32 = mybir.dt.int32
BF16 = mybir.dt.bfloat16
ALU = mybir.AluOpType
AF = mybir.ActivationFunctionType
AX = mybir.AxisListType

ENG_NAMES = ("pe", "act", "dve", "pool", "sp")
N_DMA_SEMS = 6


class Prog:
    def __init__(self, nc, same_eng_sync=True):
        self.nc = nc
        self.same_eng_sync = same_eng_sync
        self.q = {e: [] for e in ENG_NAMES}
        self.cnt = {e: 0 for e in ENG_NAMES}
        self.waited = {e: {} for e in ENG_NAMES}
        self.last_w = {}
        self.readers = {}
        self.dma_rr = {e: 0 for e in ENG_NAMES}
        self.dma_cnt = {}
        self.sem_handles = {}
        self.pending = {e: {} for e in ENG_NAMES}
        self.n_instr = 0

    def _need(self, eng, needs, semkey, val):
        if semkey == ("c", "pe") and eng == "pe":
            return
        if (not self.same_eng_sync) and semkey == ("c", eng):
            return
        if self.waited[eng].get(semkey, 0) >= val:
            return
        if needs.get(semkey, 0) < val:
            needs[semkey] = val

    @staticmethod
    def _k(k):
        if isinstance(k, (str, int)):
            return k
        if isinstance(k, tuple):
            return tuple(Prog._k(x) for x in k)
        return k.name

    def barrier(self):
        snap = {("c", e): self.cnt[e] for e in ENG_NAMES if self.cnt[e]}
        snap.update(self.dma_cnt)
        for e in ENG_NAMES:
            for sk, v in snap.items():
                if self.pending[e].get(sk, 0) < v:
                    self.pending[e][sk] = v

    def _deps(self, eng, reads, writes):
        needs = {}
        if self.pending[eng]:
            for sk, v in self.pending[eng].items():
                self._need(eng, needs, sk, v)
            self.pending[eng] = {}
        for k in reads:
            lw = self.last_w.get(k)
            if lw:
                self._need(eng, needs, lw[0], lw[1])
        for k in writes:
            lw = self.last_w.get(k)
            if lw:
                self._need(eng, needs, lw[0], lw[1])
            for sk, v in self.readers.get(k, {}).items():
                self._need(eng, needs, sk, v)
        for sk, v in needs.items():
            self.waited[eng][sk] = v
        return list(needs.items())

    def _commit(self, semkey, val, reads, writes):
        for k in reads:
            self.readers.setdefault(k, {})[semkey] = val
        for k in writes:
            self.last_w[k] = (semkey, val)
            self.readers[k] = {}

    def op(self, eng, fn, reads=(), writes=()):
        reads = [self._k(k) for k in reads]
        writes = [self._k(k) for k in writes]
        waits = self._deps(eng, reads, writes)
        self.cnt[eng] += 1
        semkey = ("c", eng)
        self._commit(semkey, self.cnt[eng], reads, writes)
        self.q[eng].append((waits, fn, semkey, 1))
        self.n_instr += 1

    def dma(self, eng, out, in_, reads=(), writes=(), **kw):
        reads = [self._k(k) for k in reads]
        writes = [self._k(k) for k in writes]
        slot = self.dma_rr[eng]
        self.dma_rr[eng] = (slot + 1) % N_DMA_SEMS
        semkey = ("d", eng, slot)
        prev = self.dma_cnt.get(semkey, 0)
        waits = self._deps(eng, reads, writes)
        if prev and self.waited[eng].get(semkey, 0) < prev:
            waits.append((semkey, prev))
            self.waited[eng][semkey] = prev
        val = prev + 16
        self.dma_cnt[semkey] = val
        self._commit(semkey, val, reads, writes)

        def fn(e, out=out, in_=in_, kw=kw):
            return e.dma_start(out=out, in_=in_, **kw)
        self.q[eng].append((waits, fn, semkey, 16))
        self.n_instr += 1

    def final_wait(self, eng, keys):
        keys = [self._k(k) for k in keys]
        waits = self._deps(eng, keys, ())
        self.q[eng].append((waits, None, None, 0))

    def emit(self):
        nc = self.nc
        semkeys = set()
        for e in ENG_NAMES:
            for waits, fn, sk, inc in self.q[e]:
                if sk is not None:
                    semkeys.add(sk)
                for w, _ in waits:
                    semkeys.add(w)
        semkeys = sorted(semkeys, key=str)
        with contextlib.ExitStack() as st:
            for sk in semkeys:
                self.sem_handles[sk] = st.enter_context(nc.semaphore("s_" + "_".join(map(str, sk))))
            block = st.enter_context(nc.Block())

            def run(e_name):
                def body(eng):
                    for waits, fn, sk, inc in self.q[e_name]:
                        for wsk, v in waits:
                            eng.wait_ge(self.sem_handles[wsk], v)
                        if fn is not None:
                            fn(eng).then_inc(self.sem_handles[sk], inc)
                return body
            block.tensor(run("pe"))
            block.scalar(run("act"))
            block.vector(run("dve"))
            block.gpsimd(run("pool"))
            block.sync(run("sp"))


D = 1024
KT = 8
NT = 512
H = 4
CTX = 256
L = 2
FH = 2816
FT = 22
IN_DIM = 9104
COL_Q, COL_K, COL_V, COL_G, COL_U = 0, 512, 1024, 1536, 1552
COL_O, COL_SU, COL_SV, COL_CB, COL_CC, COL_CX, COL_GATE = 1936, 2448, 2960, 3472, 3984, 4496, 5008
EPS = 1e-6
BIG = 1.0e30
SLABF = 1408
NSLAB = 6
TQ, TK, TV, TG, TU, TO, TSU, TSV, TCB, TCC, TCX, TGATE = 0, 4, 8, 12, 13, 16, 20, 24, 28, 32, 36, 40
NWT = 72
PI = float(np.pi)
TWO_PI = float(2 * np.pi)

def input_shapes(T):
    return {
        "xT": [D, T], "cT": [D, CTX], "sc": [128, KT, 2], "cst": [128, 7, 128],
        "wmod": [L, 128, KT, 6 * D], "bmod": [L, 128, 48], "gnm": [L, 128, KT], "gnf": [L, 128, KT], "gfin": [128, KT],
        "win": [L, 128, NWT, KT, 128], "bg": [L, 1, 16], "wconv": [L, 128, 24], "gmh": [L, 1, 512], "gsgu": [L, 1, 512],
        "wsT": [L, 128, 4, 128], "bs": [L, 1, 512], "are": [L, 2, 128, 12], "aim": [L, 2, 128, 12], "ldt": [L, 2, 128, 12],
        "Bre": [L, 128, 12, 128], "Bim": [L, 128, 12, 128], "Cre": [L, 128, 12, 128], "Cim": [L, 128, 12, 128],
        "s5d": [L, 128, 3], "bglu": [L, 128, 3], "wglu": [L, 128, 3, 384], "wsconv": [L, 128, 12],
        "wupm": [L, 128, 8, 4, 128], "wups": [L, 128, 8, 4, 128], "wup5": [L, 128, 8, 3, 128], "wupc": [L, 128, 8, 4, 128],
        "wout": [L, 128, 8, KT, 128], "wfg": [L, 128, FT, KT, 128], "wfu": [L, 128, FT, KT, 128], "wfd": [L, 128, 8, FT, 128],
    }


def build_program(T, n_layers=L, dbg=()):
    nc = bass.Bass("TRN2", target_bir_lowering=False)
    din = {n: nc.dram_tensor(n, s, F32, kind="ExternalInput").ap() for n, s in input_shapes(T).items()}
    outT = nc.dram_tensor("outT", [D, T], F32, kind="ExternalOutput").ap()
    dbg_out = {}
    TS = {"lat": T, "ctx": CTX}
    xs = nc.dram_tensor("xs", [D, T], F32).ap()
    csd = nc.dram_tensor("csd", [D, CTX], F32).ap()
    qd = {sg: nc.dram_tensor("qd_" + sg, [128, 4, TS[sg]], F32).ap() for sg in TS}
    kd = {sg: nc.dram_tensor("kd_" + sg, [128, 4, TS[sg]], F32).ap() for sg in TS}
    vd = {sg: nc.dram_tensor("vd_" + sg, [TS[sg], 512], F32).ap() for sg in TS}
    gd = {sg: nc.dram_tensor("gd_" + sg, [TS[sg], 16], F32).ap() for sg in TS}
    ud = {sg: nc.dram_tensor("ud_" + sg, [128, 3, TS[sg]], F32).ap() for sg in TS}
    x1d = {sg: nc.dram_tensor("x1d_" + sg, [D, TS[sg]], F32).ap() for sg in TS}
    hd = {(d, sg): nc.dram_tensor(f"hd{d}_{sg}", [TS[sg], 512], F32).ap() for sg in TS for d in (0, 1)}
    yd_ = {(d, sg): nc.dram_tensor(f"yd{d}_{sg}", [128, 3, TS[sg]], F32).ap() for sg in TS for d in (0, 1)}
    BIGW = ("win", "wupm", "wups", "wup5", "wupc", "wout", "wfg", "wfu", "wfd")
    wb = {n: nc.dram_tensor("wb_" + n, input_shapes(T)[n], BF16).ap() for n in BIGW}

    with contextlib.ExitStack() as st:
        def sb(name, shape, dt=F32):
            return st.enter_context(nc.sbuf_tensor("sb_" + name, shape, dt))
        P = Prog(nc)
        cst = sb("cst", [128, 7, 128])
        ident, ones, Umat, Lmat, biasU, biasL, iota1 = (cst[:, i, :] for i in range(7))
        sc = sb("sc", [128, KT, 2]); modv = sb("modv", [128, 48, 2]); lcs = sb("lcs", [128, 2, 2, KT])
        gnm = sb("gnm", [128, KT]); gnf = sb("gnf", [128, KT]); gfin = sb("gfin", [128, KT]); bmod = sb("bmod", [128, 48])
        bg_row = sb("bg_row", [128, 16]); wconv = sb("wconv", [128, 24]); wsconv = sb("wsconv", [128, 12])
        s5d = sb("s5d", [128, 3]); bglu = sb("bglu", [128, 3])
        rhod = [sb(f"rho{d}", [128, 12]) for d in range(2)]
        card = [sb(f"car{d}", [128, 2, 12]) for d in range(2)]
        Caugd = [sb(f"Caug{d}", [128, H, 129]) for d in range(2)]
        mstd = [sb(f"mst{d}", [128, H]) for d in range(2)]
        s5p = sb("s5p", [128, 16, 12])
        NA = 50000
        ARENA = sb("ARENA", [128, NA])
        TABF = 5 * 12 * 128
        tabd = [ARENA[:, NA - (2 - d) * TABF:NA - (1 - d) * TABF].rearrange("p (t a b) -> p t a b", t=5, a=12) for d in range(2)]
        TABK = ["tab0", "tab1"]
        pbs = [st.enter_context(nc.psum_tensor(f"pb{i}", [128, 512], F32)) for i in range(8)]
        ps_rr = [0]

        def PS():
            t = pbs[ps_rr[0] % 8]
            ps_rr[0] += 1
            return t

        class Carver:
            def __init__(self, lim=None):
                self.off = 0
                self.lim = NA if lim is None else lim

            def f(self, n, pat=None, **kw):
                ap = ARENA[:, self.off:self.off + n]
                self.off += n
                assert self.off <= self.lim, self.off
                return ap.rearrange(pat, **kw) if pat else ap

            def b(self, n, pat=None, **kw):
                nf = (n + 1) // 2
                ap = ARENA[:, self.off:self.off + nf].bitcast(BF16)
                self.off += nf
                assert self.off <= self.lim, self.off
                return ap.rearrange(pat, **kw) if pat else ap

        def tt(eng, out, a, b, op, r, w):
            P.op(eng, lambda e: e.tensor_tensor(out, a, b, op), reads=r, writes=w)

        def ts(eng, out, a, s1, s2, op0, op1, r, w):
            if s2 is None:
                P.op(eng, lambda e: e.tensor_scalar(out, a, s1, None, op0=op0), reads=r, writes=w)
            else:
                P.op(eng, lambda e: e.tensor_scalar(out, a, s1, s2, op0=op0, op1=op1), reads=r, writes=w)

        def stt(eng, out, in0, scalar, in1, op0, op1, r, w):
            P.op(eng, lambda e: e.scalar_tensor_tensor(out, in0, scalar, in1, op0=op0, op1=op1), reads=r, writes=w)

        def act(out, in_, func, r, w, bias=0.0, scale=1.0):
            P.op("act", lambda e: e.activation(out, in_, func, bias=bias, scale=scale), reads=r, writes=w)

        def mm(ps, lhsT, rhs, start, stop, r, w):
            P.op("pe", lambda e: e.matmul(ps, lhsT, rhs, start=start, stop=stop), reads=r, writes=w)

        def tr(ps, in_, r, w):
            P.op("pe", lambda e: e.transpose(ps, in_, ident), reads=r + [cst], writes=w)

        def red(eng, out, in_, op, r, w):
            P.op(eng, lambda e: e.tensor_reduce(out, in_, axis=AX.X, op=op), reads=r, writes=w)

        def recip(out, in_, r, w):
            P.op("dve", lambda e: e.reciprocal(out, in_), reads=r, writes=w)

        def cp(eng, out, in_, r, w):
            P.op(eng, lambda e: e.tensor_copy(out, in_), reads=r, writes=w)

        def mset(eng, out, val, w):
            P.op(eng, lambda e: e.memset(out, val), writes=w)

        def dma(out, in_, r, w, eng="sp"):
            P.dma(eng, out, in_, reads=r, writes=w)

        def DBG(name, ap, shape, keys):
            if name in dbg and name not in dbg_out:
                t = nc.dram_tensor("dbg_" + name, shape, F32, kind="ExternalOutput").ap()
                dbg_out[name] = t
                dma(t, ap, keys, ["dbg_" + name])

        def cast_weights(l):
            for n in BIGW:
                src = din[n][l]; dstw = wb[n][l]
                nd = len(src.shape)
                letters = "abcd"[:nd - 1]
                pat = "p " + " ".join(letters) + " -> p (" + " ".join(letters) + ")"
                s2 = src.rearrange(pat); d2 = dstw.rearrange(pat)
                N = s2.shape[1]
                for c0 in range(0, N, 8192):
                    c1 = min(N, c0 + 8192)
                    dma(d2[:, c0:c1], s2[:, c0:c1], [], [("wb", n, l, c0)], eng="pool")

        def range_reduce(out, in_, ti, tm, keys_in, keys_out, kti, ktm):
            ts("dve", tm, in_, 1.0 / TWO_PI, None, ALU.mult, None, keys_in, [ktm])
            cp("dve", ti, tm, [ktm], [kti])
            cp("dve", tm, ti, [kti], [ktm])
            stt("dve", out, tm, -TWO_PI, in_, ALU.mult, ALU.add, [ktm] + keys_in, keys_out)
            P.op("dve", lambda e: e.tensor_single_scalar(tm, out, PI, op=ALU.is_gt), reads=keys_out, writes=[ktm])
            stt("dve", out, tm, -TWO_PI, out, ALU.mult, ALU.add, [ktm] + keys_out, keys_out)
            P.op("dve", lambda e: e.tensor_single_scalar(tm, out, -PI, op=ALU.is_lt), reads=keys_out, writes=[ktm])
            stt("dve", out, tm, TWO_PI, out, ALU.mult, ALU.add, [ktm] + keys_out, keys_out)
            ts("dve", out, out, PI, -PI, ALU.min, ALU.max, keys_out, keys_out)

        dma(cst[:], din["cst"], [], [cst])
        dma(sc[:], din["sc"], [], [sc])
        dma(gfin[:], din["gfin"], [], [gfin])
        act(sc[:], sc[:], AF.Silu, [sc], [sc])

        def layer_prep(l):
            P.barrier()
            cv = Carver(NA - 2 * TABF)
            wm = [cv.f(1024, "p (k c) -> p k c", k=KT) for _ in range(2)]
            ang = cv.f(1536, "p (a b) -> p a b", a=12); tsn = cv.f(1536, "p (a b) -> p a b", a=12); tcs = cv.f(1536, "p (a b) -> p a b", a=12)
            tmf = cv.f(1536); ti32 = cv.f(1536).bitcast(I32)
            for nm, t in (("gnm", gnm), ("gnf", gnf), ("bmod", bmod), ("wconv", wconv), ("wsconv", wsconv),
                          ("s5d", s5d), ("bglu", bglu)):
                dma(t[:], din[nm][l], [], [t])
            dma(bg_row[:], din["bg"][l].partition_broadcast(128), [], [bg_row])
            for ft in range(48):
                s = "wm%d" % (ft % 2)
                v = wm[ft % 2]
                dma(v, din["wmod"][l][:, :, ft * 128:(ft + 1) * 128], [], [s])
                ps = PS()
                for k in range(KT):
                    mm(ps[:, 0:2], v[:, k, :], sc[:, k, :], k == 0, k == KT - 1, [s, sc], [ps])
                ts("dve", modv[:, ft, :], ps[:, 0:2], bmod[:, ft:ft + 1], None, ALU.add, None, [ps, bmod], [modv])
            for sg in range(2):
                stt("dve", lcs[:, sg, 0, :], modv[:, 8:16, sg], 1.0, gnm[:], ALU.add, ALU.mult, [modv, gnm], [lcs])
                stt("dve", lcs[:, sg, 1, :], modv[:, 32:40, sg], 1.0, gnf[:], ALU.add, ALU.mult, [modv, gnf], [lcs])
            K = ["prep"]
            for d in range(2):
                tab = tabd[d]; rho = rhod[d]; TK_ = TABK[d]
                are = s5p[:, 0, :]; aim = s5p[:, 1, :]; ldt = s5p[:, 2, :]
                dma(are, din["are"][l, d], [], K); dma(aim, din["aim"][l, d], [], K); dma(ldt, din["ldt"][l, d], [], K)
                dt_ = s5p[:, 3, :]; th = s5p[:, 4, :]; lar = s5p[:, 5, :]; sn = s5p[:, 6, :]; cs_ = s5p[:, 7, :]
                t0 = s5p[:, 8, :]; t1 = s5p[:, 9, :]; den = s5p[:, 10, :]; fr = s5p[:, 11, :]; fi = s5p[:, 12, :]
                nr = s5p[:, 13, :]; ni = s5p[:, 14, :]; t2 = s5p[:, 15, :]
                act(dt_, ldt, AF.Exp, K, K)
                tt("dve", th, dt_, aim, ALU.mult, K, K)
                tt("dve", lar, dt_, are, ALU.mult, K, K)
                act(rho[:], lar, AF.Exp, K, [rho] + K)
                range_reduce(t0, th, ti32[:, 0:12], t2, K, K, "ti32", "prep")
                act(sn, t0, AF.Sin, K, K)
                ts("dve", t1, th, PI / 2, None, ALU.add, None, K, K)
                range_reduce(t0, t1, ti32[:, 0:12], t2, K, K, "ti32", "prep")
                act(cs_, t0, AF.Sin, K, K)
                tt("dve", nr, rho[:], cs_, ALU.mult, [rho] + K, K)
                ts("dve", nr, nr, -1.0, None, ALU.add, None, K, K)
                tt("dve", ni, rho[:], sn, ALU.mult, [rho] + K, K)
                tt("dve", den, are, are, ALU.mult, K, K)
                tt("dve", t0, aim, aim, ALU.mult, K, K)
                tt("dve", den, den, t0, ALU.add, K, K)
                recip(den, den, K, K)
                tt("dve", t0, nr, are, ALU.mult, K, K); tt("dve", t1, ni, aim, ALU.mult, K, K)
                tt("dve", t0, t0, t1, ALU.add, K, K); tt("dve", fr, t0, den, ALU.mult, K, K)
                tt("dve", t0, ni, are, ALU.mult, K, K); tt("dve", t1, nr, aim, ALU.mult, K, K)
                tt("dve", t0, t0, t1, ALU.subtract, K, K); tt("dve", fi, t0, den, ALU.mult, K, K)
                thb = th.unsqueeze(2).to_broadcast([128, 12, 128])
                io = iota1.unsqueeze(1).to_broadcast([128, 12, 128])
                tt("dve", ang, io, thb, ALU.mult, [cst] + K, K)
                angf = ang.rearrange("p a b -> p (a b)")
                range_reduce(tsn.rearrange("p a b -> p (a b)"), angf, ti32, tmf, K, K, "ti32", "prep")
                act(tab[:, 3, :, :], tsn, AF.Sin, K, [TK_])
                ts("dve", angf, angf, PI / 2, None, ALU.add, None, K, K)
                range_reduce(tcs.rearrange("p a b -> p (a b)"), angf, ti32, tmf, K, K, "ti32", "prep")
                act(tab[:, 2, :, :], tcs, AF.Sin, K, [TK_])
                frb = fr.unsqueeze(2).to_broadcast([128, 12, 128]); fib = fi.unsqueeze(2).to_broadcast([128, 12, 128])
                tt("dve", ang, tab[:, 2, :, :], frb, ALU.mult, [TK_] + K, K)
                tt("dve", tsn, tab[:, 3, :, :], fib, ALU.mult, [TK_] + K, K)
                tt("dve", tab[:, 0, :, :], ang, tsn, ALU.add, K, [TK_])
                tt("dve", ang, tab[:, 2, :, :], fib, ALU.mult, [TK_] + K, K)
                tt("dve", tsn, tab[:, 3, :, :], frb, ALU.mult, [TK_] + K, K)
                tt("dve", tab[:, 1, :, :], ang, tsn, ALU.subtract, K, [TK_])
                cp("dve", tab[:, 4, :, :], rho[:].unsqueeze(2).to_broadcast([128, 12, 128]), [rho], [TK_])
                mset("dve", tab[:, 4, :, 0:1], 0.0, [TK_])
            P.barrier()

        def make_rowlocal(cv, nslab):
            V = {"nt": NT}
            V["xt_"] = cv.f(KT * NT, "p (a b) -> p a b", a=KT); V["sqt_"] = cv.f(KT * NT, "p (a b) -> p a b", a=KT)
            V["hT_"] = cv.b(KT * NT, "p (a b) -> p a b", a=KT); V["rs_"] = cv.f(NT)
            V["slabs"] = [cv.b(SLABF) for _ in range(nslab)]
            set_nt(V, NT)
            return V

        def set_nt(V, nt):
            V["nt"] = nt
            V["xt"] = V["xt_"][:, :, 0:nt]; V["sqt"] = V["sqt_"][:, :, 0:nt]; V["hT"] = V["hT_"][:, :, 0:nt]; V["rs"] = V["rs_"][:, 0:nt]

        slab_rr = [0]

        def load_slab(V, w_ap, nk):
            i = slab_rr[0] % len(V["slabs"])
            slab_rr[0] += 1
            s = V["slabs"][i]
            key = "slab%d" % i
            v = s[:, 0:nk * 128].rearrange("p (k c) -> p k c", k=nk)
            dma(v, w_ap, [], [key])
            return v, key

        def norm(V, scale_fn, shift_fn, extra_r, dst, dkey):
            xt, sqt, rs = V["xt"], V["sqt"], V["rs"]
            NT = V["nt"]
            tt("pool", sqt, xt, xt, ALU.mult, ["xt"], ["sqt"])
            ps = PS()
            for k in range(KT):
                mm(ps[:, 0:NT], ones, sqt[:, k, :], k == 0, k == KT - 1, [cst, "sqt"], [ps])
            act(rs, ps[:, 0:NT], AF.Sqrt, [ps], ["rs"], bias=EPS, scale=1.0 / D)
            recip(rs, rs, ["rs"], ["rs"])
            tt("dve", sqt, xt, rs.unsqueeze(1).to_broadcast([128, KT, NT]), ALU.mult, ["xt", "rs"], ["sqt"])
            for k in range(KT):
                sh = shift_fn(k)
                if sh is None:
                    act(dst[:, k, :], sqt[:, k, :], AF.Identity, ["sqt"] + extra_r, [dkey], scale=scale_fn(k))
                else:
                    act(dst[:, k, :], sqt[:, k, :], AF.Identity, ["sqt"] + extra_r, [dkey], bias=sh, scale=scale_fn(k))

        def proj_fm(V, w_l, c0, ntiles, nk, rhs_t, rhs_key, evac):
            NT = V["nt"]
            for j in range(ntiles):
                v, s = load_slab(V, w_l[:, c0 + j], nk)
                ps = PS()
                for k in range(nk):
                    mm(ps[:, 0:NT], v[:, k, :], rhs_t[:, k, :], k == 0, k == nk - 1, [s, rhs_key], [ps])
                evac(j, ps)

        def proj_tm(V, w_l, c0, ntiles, evac, ncols=128):
            hT = V["hT"]
            NCH = V["nt"] // 128
            for j in range(ntiles):
                v, s = load_slab(V, w_l[:, c0 + j], KT)
                for c in range(NCH):
                    ps = PS()
                    for k in range(KT):
                        mm(ps[:, 0:ncols], hT[:, k, c * 128:(c + 1) * 128], v[:, k, 0:ncols], k == 0, k == KT - 1, [s, "hT"], [ps])
                    evac(j, c, ps)

        def conv3(dst, src, wv, woff, rowlen, r, w, NT):
            nr_ = NT // rowlen
            for j in range(4):
                w0 = wv[:, woff + 0 + j:woff + 1 + j]; w1 = wv[:, woff + 4 + j:woff + 5 + j]; w2 = wv[:, woff + 8 + j:woff + 9 + j]
                d3 = dst[:, j, :].rearrange("p (r c) -> p r c", r=nr_); s3 = src[:, j, :].rearrange("p (r c) -> p r c", r=nr_)
                act(dst[:, j, :], src[:, j, :], AF.Identity, r, w, scale=w1)
                stt("dve", d3[:, :, 1:rowlen], s3[:, :, 0:rowlen - 1], w0, d3[:, :, 1:rowlen], ALU.mult, ALU.add, r + w, w)
                stt("dve", d3[:, :, 0:rowlen - 1], s3[:, :, 1:rowlen], w2, d3[:, :, 0:rowlen - 1], ALU.mult, ALU.add, r + w, w)

        def phase1(l, src_of):
            P.barrier()
            cv = Carver(NA - 2 * TABF)
            NTF = 512; NCF = 4
            sqt_ = cv.f(KT * NTF, "p (a b) -> p a b", a=KT); rs_ = cv.f(NTF)
            slabs1 = [cv.b(SLABF) for _ in range(6)]
            rawq_ = cv.f(4 * NTF, "p (a b) -> p a b", a=4); rawk_ = cv.f(4 * NTF, "p (a b) -> p a b", a=4)
            qT_ = cv.f(4 * NTF, "p (a b) -> p a b", a=4); kT_ = cv.f(4 * NTF, "p (a b) -> p a b", a=4)
            vtm_ = cv.f(NCF * 512, "p (c e) -> p c e", c=NCF); uT_ = cv.f(3 * NTF, "p (a b) -> p a b", a=3); graw_ = cv.f(NCF * 16, "p (c e) -> p c e", c=NCF)
            sets = [{"tag": tg, "xt": cv.f(KT * NTF, "p (a b) -> p a b", a=KT), "hT": cv.b(KT * NTF, "p (a b) -> p a b", a=KT)} for tg in ("A", "B")]
            w_l = wb["win"][l]
            srr = [0]

            def lslab(w_ap, nk):
                i = srr[0] % 6
                srr[0] += 1
                key = "s1slab%d" % i
                v = slabs1[i][:, 0:nk * 128].rearrange("p (k c) -> p k c", k=nk)
                dma(v, w_ap, [], [key])
                return v, key

            def unit_gen(seg, u, NT, S):
                NCH = NT // 128
                tg = S["tag"]
                kx, kh = "p1xt" + tg, "p1hT" + tg
                xt, hT = S["xt"][:, :, 0:NT], S["hT"][:, :, 0:NT]
                sqt = sqt_[:, :, 0:NT]; rs = rs_[:, 0:NT]
                rawq = rawq_[:, :, 0:NT]; rawk = rawk_[:, :, 0:NT]; qT = qT_[:, :, 0:NT]; kT = kT_[:, :, 0:NT]
                vtm = vtm_[:, 0:NCH, :]; uT = uT_[:, :, 0:NT]; graw = graw_[:, 0:NCH, :]
                sg = 1 if seg == "ctx" else 0
                t0 = u * NTF
                rowlen = NT if seg == "ctx" else 64
                dma(xt, src_of[seg][:, t0:t0 + NT].rearrange("(k p) t -> p k t", p=128), [], [kx])
                yield "y"
                act(sqt, xt, AF.Square, [kx], ["p1sqt"])
                ps = PS()
                for k in range(KT):
                    mm(ps[:, 0:NT], ones, sqt[:, k, :], k == 0, k == KT - 1, [cst, "p1sqt"], [ps])
                act(rs, ps[:, 0:NT], AF.Sqrt, [ps], ["p1rs"], bias=EPS, scale=1.0 / D)
                recip(rs, rs, ["p1rs"], ["p1rs"])
                yield "y"
                tt("dve", sqt, xt, rs.unsqueeze(1).to_broadcast([128, KT, NT]), ALU.mult, [kx, "p1rs"], ["p1sqt"])
                for k in range(KT):
                    act(hT[:, k, :], sqt[:, k, :], AF.Identity, ["p1sqt", lcs, modv], [kh], bias=modv[:, k, sg:sg + 1], scale=lcs[:, sg, 0, k:k + 1])
                yield "S"

                def pfm(c0, ntiles, dst, dkey):
                    for j in range(ntiles):
                        v, s_ = lslab(w_l[:, c0 + j], KT)
                        ps = PS()
                        for k in range(KT):
                            mm(ps[:, 0:NT], v[:, k, :], hT[:, k, :], k == 0, k == KT - 1, [s_, kh], [ps])
                        act(dst[:, j, :], ps[:, 0:NT], AF.Copy, [ps], [dkey])
                        yield "y"

                def ptm(c0, ntiles, evac, ncols=128):
                    for j in range(ntiles):
                        v, s_ = lslab(w_l[:, c0 + j], KT)
                        for c in range(NCH):
                            ps = PS()
                            for k in range(KT):
                                mm(ps[:, 0:ncols], hT[:, k, c * 128:(c + 1) * 128], v[:, k, 0:ncols], k == 0, k == KT - 1, [s_, kh], [ps])
                            evac(j, c, ps)
                        yield "y"
                yield from pfm(TQ, 4, rawq, "p1rawq")
                yield from pfm(TK, 4, rawk, "p1rawk")
                conv3(qT, rawq, wconv, 0, rowlen, ["p1rawq", wconv], ["p1qT"], NT)
                act(qT, qT, AF.Silu, ["p1qT"], ["p1qT"])
                dma(qd[seg][:, :, t0:t0 + NT], qT, ["p1qT"], [("q", seg, u)])
                yield "y"
                yield from ptm(TV, 4, lambda j, c, ps: act(vtm[:, c, j * 128:(j + 1) * 128], ps[:, 0:128], AF.Copy, [ps], ["p1vtm"]))
                dma(vd[seg][t0:t0 + NT, :].rearrange("(c p) e -> p c e", p=128), vtm, ["p1vtm"], [("v", seg, u)])
                conv3(kT, rawk, wconv, 12, rowlen, ["p1rawk", wconv], ["p1kT"], NT)
                act(kT, kT, AF.Silu, ["p1kT"], ["p1kT"], )
                ts("dve", kT, kT, float(128 ** -0.5), None, ALU.mult, None, ["p1kT"], ["p1kT"])
                dma(kd[seg][:, :, t0:t0 + NT], kT, ["p1kT"], [("k", seg, u)])
                yield "y"
                yield from ptm(TG, 1, lambda j, c, ps: tt("dve", graw[:, c, :], ps[:, 0:16], bg_row[:], ALU.add, [ps, bg_row], ["p1graw"]), ncols=16)
                dma(gd[seg][t0:t0 + NT, :].rearrange("(c p) e -> p c e", p=128), graw, ["p1graw"], [("g", seg, u)])
                yield from pfm(TU, 3, uT, "p1uT")
                dma(ud[seg][:, :, t0:t0 + NT], uT, ["p1uT"], [("u", seg, u)])
                yield "S"

            units = [("ctx", 0, CTX)] + [("lat", u, NTF) for u in range(T // NTF)]
            gens = [unit_gen(seg, u, nt_, sets[i % 2]) for i, (seg, u, nt_) in enumerate(units)]

            def to_stage(g):
                while next(g) != "S":
                    pass

            def weave(ga, gb):
                da = db = False
                while not (da and db):
                    if not da:
                        da = next(ga) == "S"
                    if not db:
                        db = next(gb) == "S"
            to_stage(gens[0])
            for i in range(len(gens)):
                if i + 1 < len(gens):
                    weave(gens[i], gens[i + 1])
                else:
                    to_stage(gens[i])

        def phase2(l):
            P.barrier()
            cv = Carver(NA - 2 * TABF)
            nlat = T // 128
            chains = {0: [("ctx", c) for c in range(CTX // 128)] + [("lat", c) for c in range(nlat)],
                      1: [("ctx", c) for c in range(CTX // 128 - 1, -1, -1)] + [("lat", c) for c in range(nlat - 1, -1, -1)]}
            for d in range(2):
                mset("pool", Caugd[d][:], 0.0, [Caugd[d]]); mset("pool", mstd[d][:], 0.0, [mstd[d]]); mset("pool", card[d][:], 0.0, [card[d]])

            def unit_of(seg, c):
                return (seg, 0)

            Bre = cv.f(1536, "p (a b) -> p a b", a=12); Bim = cv.f(1536, "p (a b) -> p a b", a=12)
            Cre = cv.f(1536, "p (a b) -> p a b", a=12); Cimn = cv.f(1536, "p (a b) -> p a b", a=12)
            dma(Bre, din["Bre"][l], [], ["Bre"]); dma(Bim, din["Bim"][l], [], ["Bim"]); dma(Cre, din["Cre"][l], [], ["Cre"]); dma(Cimn, din["Cim"][l], [], ["Cimn"])
            ts("dve", Cimn, Cimn, -1.0, None, ALU.mult, None, ["Cimn"], ["Cimn"])

            def mlstm_stream(d):
                bwd = d == 1
                nm = "m%d" % d
                qc = [cv.f(512, "p (a b) -> p a b", a=4) for _ in range(2)]; kc = [cv.f(512, "p (a b) -> p a b", a=4) for _ in range(2)]
                va = [cv.f(516, "p (a b) -> p a b", a=4) for _ in range(2)]; gc = [cv.f(16) for _ in range(2)]
                gsm = cv.f(64, "p (a b) -> p a b", a=16); ex = cv.f(20, "p (a b) -> p a b", a=5)
                gdt = cv.f(512, "p (a b) -> p a b", a=4); tmt = cv.f(512, "p (a b) -> p a b", a=4)
                vw = cv.f(516, "p (a b) -> p a b", a=4); PTa = cv.f(512, "p (a b) -> p a b", a=4); ktm = cv.f(512, "p (a b) -> p a b", a=4)
                r1 = cv.f(258, "p (a b) -> p a b", a=2); tmx = cv.f(258, "p (a b) -> p a b", a=2); rra = cv.f(516, "p (a b) -> p a b", a=4)
                dn = cv.f(16, "p (a b) -> p a b", a=4); hdir = cv.f(512, "p (a b) -> p a b", a=4); Ctmp = cv.f(516, "p (a b) -> p a b", a=4)
                b0 = pbs[2 * d]; b1 = pbs[2 * d + 1]
                Caug = Caugd[d]; mst = mstd[d]
                io, fo = (8, 12) if bwd else (0, 4)
                Tri = Lmat if bwd else Umat
                biasM = biasU if bwd else biasL
                K_ = lambda s_: nm + s_
                chain = chains[d]
                for b_ in range(2):
                    mset("pool", va[b_][:, :, 128:129], 1.0, [K_("va%d" % b_)])

                def load(i):
                    seg, c = chain[i]
                    b_ = i % 2
                    tok = c * 128
                    su = unit_of(seg, c)
                    dma(qc[b_], qd[seg][:, :, tok:tok + 128], [("q",) + su], [K_("qc%d" % b_)])
                    dma(kc[b_], kd[seg][:, :, tok:tok + 128], [("k",) + su], [K_("kc%d" % b_)])
                    dma(va[b_][:, :, 0:128], vd[seg][tok:tok + 128, :].rearrange("p (h e) -> p h e", h=4), [("v",) + su], [K_("va%d" % b_)])
                    dma(gc[b_], gd[seg][tok:tok + 128, :], [("g",) + su], [K_("gc%d" % b_)])
                load(0)
                for i, (seg, c) in enumerate(chain):
                    b_ = i % 2
                    if i + 1 < len(chain):
                        load(i + 1)
                    q_, k_, v_, g_ = qc[b_], kc[b_], va[b_], gc[b_]
                    kq, kk_, kv, kg = K_("qc%d" % b_), K_("kc%d" % b_), K_("va%d" % b_), K_("gc%d" % b_)
                    G = [K_("gsm")]; EX = [K_("ex")]
                    e1 = gsm[:, 0, :]; sp = gsm[:, 1, :]; g = gsm[:, 2, :]; gmax = gsm[:, 3, :]; cmax = gsm[:, 4, :]
                    Mx = gsm[:, 5, :]; M = gsm[:, 6, :]
                    act(e1, g_[:, fo:fo + 4], AF.Exp, [kg], G, scale=-1.0)
                    act(sp, e1, AF.Ln, G, G, bias=1.0)
                    mm(b0[:, 0:4], Tri, sp, True, True, [cst] + G, [b0])
                    mm(b0[:, 4:8], ones, sp, True, True, [cst] + G, [b0])
                    yield
                    tt("dve", g, g_[:, io:io + 4], b0[:, 0:4], ALU.add, [kg, b0], G)
                    tt("dve", gdt, ident.unsqueeze(1).to_broadcast([128, 4, 128]), g.unsqueeze(2).to_broadcast([128, 4, 128]), ALU.mult, [cst] + G, [K_("gdt")])
                    mm(b1[:, :], ones, gdt.rearrange("p a b -> p (a b)"), True, True, [cst, K_("gdt")], [b1])
                    yield
                    b1v = b1[:, :].rearrange("p (a b) -> p a b", a=4)
                    red("dve", gmax, b1v, ALU.max, [b1], G)
                    tt("dve", tmt, b1v, biasM.unsqueeze(1).to_broadcast([128, 4, 128]), ALU.add, [b1, cst], [K_("tmt")])
                    red("dve", cmax, tmt, ALU.max, [K_("tmt")], G)
                    tt("dve", Mx, mst[:], gmax, ALU.max, [mst] + G, G)
                    tt("dve", M, mst[:], cmax, ALU.max, [mst] + G, G)
                    tt("dve", ex[:, 0, :], g, Mx, ALU.subtract, G, EX)
                    tt("dve", ex[:, 1, :], Mx, M, ALU.subtract, G, EX)
                    tt("dve", ex[:, 2, :], mst[:], M, ALU.subtract, [mst] + G, EX)
                    tt("dve", ex[:, 3, :], mst[:], Mx, ALU.subtract, [mst] + G, EX)
                    tt("dve", ex[:, 4, :], b0[:, 0:4], M, ALU.subtract, [b0] + G, EX)
                    act(ex, ex, AF.Exp, EX, EX)
                    tt("dve", mst[:], Mx, b0[:, 4:8], ALU.subtract, G + [b0], [mst])
                    yield
                    tt("dve", vw, v_, ex[:, 0, :].unsqueeze(2).to_broadcast([128, 4, 129]), ALU.mult, [kv] + EX, [K_("vw")])
                    for h in range(H):
                        mm(b0[:, h * 128:(h + 1) * 128], k_[:, h, :], q_[:, h, :], True, True, [kk_, kq], [b0])
                    for h in range(H):
                        tr(b1[:, h * 128:(h + 1) * 128], k_[:, h, :], [kk_], [b1])
                    yield
                    maskPT = (Lmat if bwd else Umat).unsqueeze(1).to_broadcast([128, 4, 128])
                    tt("dve", PTa, b0[:, :].rearrange("p (a b) -> p a b", a=4), maskPT, ALU.mult, [b0, cst], [K_("PTa")])
                    act(ktm, b1[:, :].rearrange("p (a b) -> p a b", a=4), AF.Copy, [b1], [K_("ktm")])
                    yield
                    for pr in range(2):
                        for hh in range(2):
                            h = 2 * pr + hh
                            mm(b0[:, hh * 129:(hh + 1) * 129], PTa[:, h, :], vw[:, h, :], True, True, [K_("PTa"), K_("vw")], [b0])
                            mm(b1[:, hh * 129:(hh + 1) * 129], q_[:, h, :], Caug[:, h, :], True, True, [kq, Caug], [b1])
                        yield
                        hs = slice(2 * pr, 2 * pr + 2)
                        b0p = b0[:, 0:258].rearrange("p (a b) -> p a b", a=2); b1p = b1[:, 0:258].rearrange("p (a b) -> p a b", a=2)
                        tt("dve", r1, b1p, ex[:, 2, hs].unsqueeze(2).to_broadcast([128, 2, 129]), ALU.mult, [b1] + EX, [K_("r1")])
                        tt("dve", tmx, b0p, ex[:, 1, hs].unsqueeze(2).to_broadcast([128, 2, 129]), ALU.mult, [b0] + EX, [K_("tmx")])
                        tt("dve", rra[:, hs, :], tmx, r1, ALU.add, [K_("tmx"), K_("r1")], [K_("rra")])
                        yield
                    den = rra[:, :, 128]
                    ts("dve", dn[:, 0, :], den, -1.0, None, ALU.mult, None, [K_("rra")], [K_("dn")])
                    tt("dve", dn[:, 0, :], dn[:, 0, :], den, ALU.max, [K_("dn"), K_("rra")], [K_("dn")])
                    tt("dve", dn[:, 0, :], dn[:, 0, :], ex[:, 4, :], ALU.max, [K_("dn")] + EX, [K_("dn")])
                    recip(dn[:, 1, :], dn[:, 0, :], [K_("dn")], [K_("dn")])
                    tt("dve", hdir, rra[:, :, 0:128], dn[:, 1, :].unsqueeze(2).to_broadcast([128, 4, 128]), ALU.mult, [K_("rra"), K_("dn")], [K_("hdir")])
                    dma(hd[(d, seg)][c * 128:(c + 1) * 128, :].rearrange("p (h e) -> p h e", h=4), hdir, [K_("hdir")], [("h", d, seg, c)])
                    yield
                    for h in range(H):
                        bb = b0 if h < 2 else b1
                        hh = h % 2
                        mm(bb[:, hh * 129:(hh + 1) * 129], ktm[:, h, :], vw[:, h, :], True, True, [K_("ktm"), K_("vw")], [bb])
                    tt("dve", Ctmp, Caug[:], ex[:, 3, :].unsqueeze(2).to_broadcast([128, 4, 129]), ALU.mult, [Caug] + EX, [K_("Ctmp")])
                    yield
                    tt("dve", Caug[:, 0:2, :], Ctmp[:, 0:2, :], b0[:, 0:258].rearrange("p (a b) -> p a b", a=2), ALU.add, [K_("Ctmp"), b0], [Caug])
                    tt("dve", Caug[:, 2:4, :], Ctmp[:, 2:4, :], b1[:, 0:258].rearrange("p (a b) -> p a b", a=2), ALU.add, [K_("Ctmp"), b1], [Caug])
                    yield

            def s5_stream(d):
                bwd = d == 1
                nm = "s%d" % d
                K_ = lambda s_: nm + s_
                uc = [cv.f(384, "p (a b) -> p a b", a=3) for _ in range(2)]
                B2 = cv.f(1024, "p (r a b) -> p r a b", r=2, a=4); S2 = cv.f(1024, "p (r a b) -> p r a b", r=2, a=4)
                m1 = cv.f(1024, "p (r a b) -> p r a b", r=2, a=4); m2 = cv.f(1024, "p (r a b) -> p r a b", r=2, a=4)
                ycb = cv.f(384, "p (a b) -> p a b", a=3)
                bA = pbs[4 + 2 * d]; bB = pbs[5 + 2 * d]
                tab = tabd[d]; rho = rhod[d]; car = card[d]; TKd = TABK[d]
                chain = chains[d]
                last = 0 if bwd else 127

                def R3(ap):
                    return ap[:, :, ::-1] if bwd else ap

                def R4(ap):
                    return ap[:, :, :, ::-1] if bwd else ap

                def load(i):
                    seg, c = chain[i]
                    b_ = i % 2
                    tok = c * 128
                    dma(uc[b_], ud[seg][:, :, tok:tok + 128], [("u",) + unit_of(seg, c)], [K_("uc%d" % b_)])
                load(0)
                for i, (seg, c) in enumerate(chain):
                    b_ = i % 2
                    if i + 1 < len(chain):
                        load(i + 1)
                    u_ = uc[b_]; ku = K_("uc%d" % b_)
                    for j in range(3):
                        ks = slice(4 * j, 4 * j + 4)
                        for kk in range(4):
                            k = 4 * j + kk
                            mm(bA[:, kk * 128:(kk + 1) * 128], Bre[:, k, :], u_[:, j, :], True, True, ["Bre", ku], [bA])
                            mm(bB[:, kk * 128:(kk + 1) * 128], Bim[:, k, :], u_[:, j, :], True, True, ["Bim", ku], [bB])
                        yield
                        bAv = R3(bA[:, :].rearrange("p (a b) -> p a b", a=4)); bBv = R3(bB[:, :].rearrange("p (a b) -> p a b", a=4))
                        act(B2[:, 0], bAv, AF.Copy, [bA], [K_("B2")])
                        act(B2[:, 1], bBv, AF.Copy, [bB], [K_("B2")])
                        act(S2[:, 0], bBv, AF.Copy, [bB], [K_("S2")], scale=-1.0)
                        act(S2[:, 1], bAv, AF.Copy, [bA], [K_("S2")])
                        yield
                        bc = lambda t_: t_.unsqueeze(1).to_broadcast([128, 2, 4, 128])
                        tt("dve", m1, B2, bc(tab[:, 0, ks, :]), ALU.mult, [K_("B2"), TKd], [K_("m1")])
                        tt("dve", m2, S2, bc(tab[:, 1, ks, :]), ALU.mult, [K_("S2"), TKd], [K_("m2")])
                        tt("dve", m1, m1, m2, ALU.add, [K_("m1"), K_("m2")], [K_("m1")])
                        tt("dve", m1[:, :, :, 0], m1[:, :, :, 0], car[:, :, ks], ALU.add, [K_("m1"), car], [K_("m1")])
                        yield
                        rf = tab[:, 4, ks, :].rearrange("p a b -> p (a b)")
                        for r_ in range(2):
                            P.op("dve", lambda e, r_=r_, rf=rf: e.tensor_tensor_scan(B2[:, r_].rearrange("p a b -> p (a b)"), rf,
                                                                                     m1[:, r_].rearrange("p a b -> p (a b)"), 0.0, ALU.mult, ALU.add),
                                 reads=[TKd, K_("m1")], writes=[K_("B2")])
                        yield
                        act(S2[:, 0], B2[:, 1], AF.Copy, [K_("B2")], [K_("S2")], scale=-1.0)
                        act(S2[:, 1], B2[:, 0], AF.Copy, [K_("B2")], [K_("S2")])
                        tt("dve", m1, B2, bc(tab[:, 2, ks, :]), ALU.mult, [K_("B2"), TKd], [K_("m1")])
                        yield
                        tt("dve", m2, S2, bc(tab[:, 3, ks, :]), ALU.mult, [K_("S2"), TKd], [K_("m2")])
                        tt("dve", R4(B2), m1, m2, ALU.add, [K_("m1"), K_("m2")], [K_("B2")])
                        tt("dve", car[:, :, ks], B2[:, :, :, last], rho[:, ks].unsqueeze(1).to_broadcast([128, 2, 4]), ALU.mult, [K_("B2"), rho], [car])
                        yield
                        for kk in range(4):
                            k = 4 * j + kk
                            mm(bA[:, 0:128], Cre[:, k, :], B2[:, 0, kk, :], kk == 0, False, ["Cre", K_("B2")], [bA])
                            mm(bA[:, 0:128], Cimn[:, k, :], B2[:, 1, kk, :], False, kk == 3, ["Cimn", K_("B2")], [bA])
                        act(ycb[:, j, :], bA[:, 0:128], AF.Copy, [bA], [K_("ycb")])
                        yield
                    dma(yd_[(d, seg)][:, :, c * 128:(c + 1) * 128], ycb, [K_("ycb")], [("y", d, seg, c)])

            gens = [mlstm_stream(0), mlstm_stream(1), s5_stream(0), s5_stream(1)]
            alive = list(gens)
            while alive:
                for g_ in list(alive):
                    try:
                        next(g_)
                    except StopIteration:
                        alive.remove(g_)

        def phase3b(l, dst_of, is_last, units):
            P.barrier()
            cv = Carver()
            N3 = 512
            sqt_ = cv.f(KT * N3, "p (a b) -> p a b", a=KT); rs_ = cv.f(N3)
            slabs3 = [cv.b(SLABF) for _ in range(6)]
            a_t_ = cv.b(FT * N3, "p (a b) -> p a b", a=FT)
            sets = []
            for tag in ("A", "B"):
                sets.append({"tag": tag, "xt": cv.f(KT * N3, "p (a b) -> p a b", a=KT), "hT": cv.b(KT * N3, "p (a b) -> p a b", a=KT), "tmpA": cv.f(N3)})
            srr = [0]

            def lslab(w_ap, nk):
                i = srr[0] % 6
                srr[0] += 1
                key = "s3bslab%d" % i
                v = slabs3[i][:, 0:nk * 128].rearrange("p (k c) -> p k c", k=nk)
                dma(v, w_ap, [], [key])
                return v, key

            def unit_gen(seg, u, NT, S):
                tg = S["tag"]
                K_ = lambda n: "b" + tg + n
                xt, hT, tmpA = S["xt"][:, :, 0:NT], S["hT"][:, :, 0:NT], S["tmpA"][:, 0:NT]
                sqt = sqt_[:, :, 0:NT]; rs = rs_[:, 0:NT]; a_t = a_t_[:, :, 0:NT]
                sg = 1 if seg == "ctx" else 0
                t0 = u * N3

                def norm3(scale_fn, shift_fn, extra_r, dst, dkey):
                    act(sqt, xt, AF.Square, [K_("xt")], ["sqt3b"])
                    ps = PS()
                    for k in range(KT):
                        mm(ps[:, 0:NT], ones, sqt[:, k, :], k == 0, k == KT - 1, [cst, "sqt3b"], [ps])
                    act(rs, ps[:, 0:NT], AF.Sqrt, [ps], ["rs3b"], bias=EPS, scale=1.0 / D)
                    recip(rs, rs, ["rs3b"], ["rs3b"])
                    tt("dve", sqt, xt, rs.unsqueeze(1).to_broadcast([128, KT, NT]), ALU.mult, [K_("xt"), "rs3b"], ["sqt3b"])
                    for k in range(KT):
                        sh = shift_fn(k)
                        if sh is None:
                            act(dst[:, k, :], sqt[:, k, :], AF.Identity, ["sqt3b"] + extra_r, [dkey], scale=scale_fn(k))
                        else:
                            act(dst[:, k, :], sqt[:, k, :], AF.Identity, ["sqt3b"] + extra_r, [dkey], bias=sh, scale=scale_fn(k))
                dma(xt, x1d[seg][:, t0:t0 + NT].rearrange("(k p) t -> p k t", p=128), [], [K_("xt")])
                yield "y"
                norm3(lambda k: lcs[:, sg, 1, k:k + 1], lambda k: modv[:, 24 + k, sg:sg + 1], [lcs, modv], hT, K_("hT"))
                yield "S"
                for i in range(FT):
                    v1, s1 = lslab(wb["wfg"][l][:, i], KT)
                    pg = PS()
                    for k in range(KT):
                        mm(pg[:, 0:NT], v1[:, k, :], hT[:, k, :], k == 0, k == KT - 1, [s1, K_("hT")], [pg])
                    act(tmpA, pg[:, 0:NT], AF.Silu, [pg], [K_("tmpA")])
                    v2, s2 = lslab(wb["wfu"][l][:, i], KT)
                    pu = PS()
                    for k in range(KT):
                        mm(pu[:, 0:NT], v2[:, k, :], hT[:, k, :], k == 0, k == KT - 1, [s2, K_("hT")], [pu])
                    tt("dve", a_t[:, i, :], tmpA, pu[:, 0:NT], ALU.mult, [K_("tmpA"), pu], [("ba", i)])
                    yield "y"
                for jd in range(KT):
                    ps = PS()
                    for hf in range(2):
                        v3, s3 = lslab(wb["wfd"][l][:, jd, hf * 11:(hf + 1) * 11], 11)
                        for k in range(11):
                            i = hf * 11 + k
                            mm(ps[:, 0:NT], v3[:, k, :], a_t[:, i, :], i == 0, i == FT - 1, [s3, ("ba", i)], [ps])
                    stt("dve", xt[:, jd, :], ps[:, 0:NT], modv[:, 40 + jd, sg:sg + 1], xt[:, jd, :], ALU.mult, ALU.add, [ps, modv, K_("xt")], [K_("xt")])
                    yield "y"
                if is_last and seg == "lat":
                    norm3(lambda k: gfin[:, k:k + 1], lambda k: None, [gfin], sqt, "sqt3b")
                    dma(outT[:, t0:t0 + NT].rearrange("(k p) t -> p k t", p=128), sqt, ["sqt3b"], [("out", u)])
                else:
                    dma(dst_of[seg][:, t0:t0 + NT].rearrange("(k p) t -> p k t", p=128), xt, [K_("xt")], [("x2", seg, u)])
                yield "S"

            gens = [unit_gen(seg, u, nt_, sets[i % 2]) for i, (seg, u, nt_) in enumerate(units)]

            def to_stage(g):
                while next(g) != "S":
                    pass

            def weave(ga, gb):
                da = db = False
                while not (da and db):
                    if not da:
                        da = next(ga) == "S"
                    if not db:
                        db = next(gb) == "S"
            to_stage(gens[0])
            for i in range(len(gens)):
                if i + 1 < len(gens):
                    weave(gens[i], gens[i + 1])
                else:
                    to_stage(gens[i])

        def phase3(l, src_of, dst_of, is_last):
            P.barrier()
            cv = Carver()
            N3 = 512; NC3 = 4
            sqt_ = cv.f(KT * N3, "p (a b) -> p a b", a=KT); rs_ = cv.f(N3)
            slabs3 = [cv.b(SLABF) for _ in range(5)]
            gmh_row = cv.f(512); gsgu_row = cv.f(512); wsT = cv.f(512, "p (a b) -> p a b", a=4); bs_row = cv.f(512)
            wglu = cv.f(3 * 384, "p (a b) -> p a b", a=3)
            CK = ["p3c"]
            dma(gmh_row, din["gmh"][l].partition_broadcast(128), [], CK); dma(gsgu_row, din["gsgu"][l].partition_broadcast(128), [], CK)
            dma(bs_row, din["bs"][l].partition_broadcast(128), [], CK); dma(wsT, din["wsT"][l], [], CK); dma(wglu, din["wglu"][l], [], CK)
            w_l = wb["win"][l]

            SH = {}
            SH["merged"] = cv.b(KT * N3, "p (a b) -> p a b", a=KT); SH["tmpB"] = cv.f(N3)
            SH["sigo"] = cv.b(NC3 * 512, "p (c e) -> p c e", c=NC3); SH["vg"] = cv.f(NC3 * 512, "p (c e) -> p c e", c=NC3)
            SH["gsu"] = cv.b(4 * N3, "p (a b) -> p a b", a=4); SH["cb"] = cv.b(4 * N3, "p (a b) -> p a b", a=4)
            for n in ("cc", "yd"):
                SH[n] = cv.f(4 * N3, "p (a b) -> p a b", a=4)
            for n in ("ydb", "yaT", "ybT"):
                SH[n] = cv.b(4 * N3, "p (a b) -> p a b", a=4)
            SH["ysum"] = cv.f(3 * N3, "p (a b) -> p a b", a=3); SH["uT"] = cv.f(3 * N3, "p (a b) -> p a b", a=3)
            SH["hdir"] = cv.f(512); SH["hbt"] = cv.f(512); SH["hn"] = cv.f(512)
            SH["ycb"] = cv.f(3 * N3, "p (a b) -> p a b", a=3); SH["sm4"] = cv.f(8)

            def alloc_set(tag):
                S = dict(SH)
                S["tag"] = tag
                S["xt"] = cv.f(KT * N3, "p (a b) -> p a b", a=KT); S["hT"] = cv.b(KT * N3, "p (a b) -> p a b", a=KT)
                S["tmpA"] = cv.f(N3); S["ys5T"] = cv.b(3 * N3, "p (a b) -> p a b", a=3)
                return S
            sets = [alloc_set("A"), alloc_set("B")]
            srr = [0]

            def lslab(w_ap, nk):
                i = srr[0] % 5
                srr[0] += 1
                key = "s3slab%d" % i
                v = slabs3[i][:, 0:nk * 128].rearrange("p (k c) -> p k c", k=nk)
                dma(v, w_ap, [], [key])
                return v, key

            def unit_gen(seg, u, NT, S):
                NCH_ = NT // 128
                tg = S["tag"]
                K_ = lambda n: (tg + n) if n in ("xt", "hT", "tmpA", "ys5T") else ("p3" + n)
                xt, hT, merged, tmpA, tmpB = S["xt"][:, :, 0:NT], S["hT"][:, :, 0:NT], S["merged"][:, :, 0:NT], S["tmpA"][:, 0:NT], S["tmpB"][:, 0:NT]
                sigo, vg = S["sigo"][:, 0:NCH_, :], S["vg"][:, 0:NCH_, :]
                gsu, cb, cc, yd, ydb = (S[n_][:, :, 0:NT] for n_ in ("gsu", "cb", "cc", "yd", "ydb"))
                yaT, ybT, ysum, ys5T, uT = (S[n_][:, :, 0:NT] for n_ in ("yaT", "ybT", "ysum", "ys5T", "uT"))
                hdir, hbt, hn, ycb, sm4 = S["hdir"], S["hbt"], S["hn"], S["ycb"][:, :, 0:NT], S["sm4"]
                sqt = sqt_[:, :, 0:NT]; rs = rs_[:, 0:NT]
                sg = 1 if seg == "ctx" else 0
                t0 = u * N3
                rowlen = NT if seg == "ctx" else 64

                def norm3(scale_fn, shift_fn, extra_r, dst, dkey):
                    tt("pool", sqt, xt, xt, ALU.mult, [K_("xt")], ["sqt3"])
                    ps = PS()
                    for k in range(KT):
                        mm(ps[:, 0:NT], ones, sqt[:, k, :], k == 0, k == KT - 1, [cst, "sqt3"], [ps])
                    act(rs, ps[:, 0:NT], AF.Sqrt, [ps], ["rs3"], bias=EPS, scale=1.0 / D)
                    recip(rs, rs, ["rs3"], ["rs3"])
                    tt("dve", sqt, xt, rs.unsqueeze(1).to_broadcast([128, KT, NT]), ALU.mult, [K_("xt"), "rs3"], ["sqt3"])
                    for k in range(KT):
                        sh = shift_fn(k)
                        if sh is None:
                            act(dst[:, k, :], sqt[:, k, :], AF.Identity, ["sqt3"] + extra_r, [dkey], scale=scale_fn(k))
                        else:
                            act(dst[:, k, :], sqt[:, k, :], AF.Identity, ["sqt3"] + extra_r, [dkey], bias=sh, scale=scale_fn(k))

                def pfm(c0, ntiles, evac):
                    for j in range(ntiles):
                        v, s_ = lslab(w_l[:, c0 + j], KT)
                        ps = PS()
                        for k in range(KT):
                            mm(ps[:, 0:NT], v[:, k, :], hT[:, k, :], k == 0, k == KT - 1, [s_, K_("hT")], [ps])
                        evac(j, ps)
                        yield "y"

                def ptm(c0, ntiles, evac):
                    for j in range(ntiles):
                        v, s_ = lslab(w_l[:, c0 + j], KT)
                        for c in range(NCH_):
                            ps = PS()
                            for k in range(KT):
                                mm(ps[:, 0:128], hT[:, k, c * 128:(c + 1) * 128], v[:, k, :], k == 0, k == KT - 1, [s_, K_("hT")], [ps])
                            evac(j, c, ps)
                        yield "y"

                dma(xt, src_of[seg][:, t0:t0 + NT].rearrange("(k p) t -> p k t", p=128), [], [K_("xt")])
                dma(uT, ud[seg][:, :, t0:t0 + NT], [], [K_("uT")])
                dma(ysum, yd_[(0, seg)][:, :, t0:t0 + NT], [], [K_("ysum")])
                dma(ycb, yd_[(1, seg)][:, :, t0:t0 + NT], [], [K_("ycb")])
                yield "y"
                norm3(lambda k: lcs[:, sg, 0, k:k + 1], lambda k: modv[:, k, sg:sg + 1], [lcs, modv], hT, K_("hT"))
                yield "y"
                tt("dve", ysum, ysum, ycb, ALU.add, [K_("ysum"), K_("ycb")], [K_("ysum")])
                for j in range(3):
                    stt("dve", ysum[:, j, :], uT[:, j, :], s5d[:, j:j + 1], ysum[:, j, :], ALU.mult, ALU.add, [K_("uT"), s5d, K_("ysum")], [K_("ysum")])
                act(ysum, ysum, AF.Gelu_apprx_tanh, [K_("ysum")], [K_("ysum")])
                yield "S"
                yield from ptm(TO, 4, lambda j, c, ps: act(sigo[:, c, j * 128:(j + 1) * 128], ps[:, 0:128], AF.Sigmoid, [ps], [K_("sigo")]))
                yield from ptm(TSV, 4, lambda j, c, ps: act(vg[:, c, j * 128:(j + 1) * 128], ps[:, 0:128], AF.Gelu_apprx_tanh, [ps], [K_("vg")]))
                yield from pfm(TSU, 4, lambda j, ps: act(gsu[:, j, :], ps[:, 0:NT], AF.Gelu_apprx_tanh, [ps], [K_("gsu")]))
                yield from pfm(TCB, 4, lambda j, ps: act(cb[:, j, :], ps[:, 0:NT], AF.Copy, [ps], [K_("cb")]))
                yield from pfm(TCC, 4, lambda j, ps: act(cc[:, j, :], ps[:, 0:NT], AF.Copy, [ps], [K_("cc")]))
                yield from pfm(TCX, 4, lambda j, ps: tt("dve", cc[:, j, :], cc[:, j, :], ps[:, 0:NT], ALU.mult, [K_("cc"), ps], [K_("cc")]))
                for j in range(3):
                    ps = PS()
                    for k in range(3):
                        mm(ps[:, 0:NT], wglu[:, k, j * 128:(j + 1) * 128], ysum[:, k, :], k == 0, k == 2, CK + [K_("ysum")], [ps])
                    act(tmpA, ps[:, 0:NT], AF.Sigmoid, [ps, bglu], [K_("tmpA")], bias=bglu[:, j:j + 1])
                    tt("dve", ys5T[:, j, :], ysum[:, j, :], tmpA, ALU.mult, [K_("ysum"), K_("tmpA")], [K_("ys5T")])
                    yield "y"
                yield "S"
                conv3(yd, cc, wsconv, 0, rowlen, [K_("cc"), wsconv], [K_("yd")], NT)
                tt("pool", ydb, yd, cb, ALU.mult, [K_("yd"), K_("cb")], [K_("ydb")])
                yield "y"
                hn4 = hn.rearrange("p (a b) -> p a b", a=4); hd4 = hdir.rearrange("p (a b) -> p a b", a=4)
                for c in range(NCH_):
                    cs = slice(c * 128, (c + 1) * 128)
                    tok0 = t0 + c * 128
                    dma(hdir, hd[(0, seg)][tok0:tok0 + 128, :], [], [K_("hdir")])
                    dma(hbt, hd[(1, seg)][tok0:tok0 + 128, :], [], [K_("hbt")])
                    tt("dve", hdir, hdir, hbt, ALU.add, [K_("hdir"), K_("hbt")], [K_("hdir")])
                    act(hn, hdir, AF.Square, [K_("hdir")], [K_("hn")])
                    red("dve", sm4[:, 2:6], hn4, ALU.add, [K_("hn")], [K_("sm4")])
                    act(sm4[:, 2:6], sm4[:, 2:6], AF.Sqrt, [K_("sm4")], [K_("sm4")], bias=EPS, scale=1.0 / 128)
                    recip(sm4[:, 2:6], sm4[:, 2:6], [K_("sm4")], [K_("sm4")])
                    yield "y"
                    tt("dve", hn4, hd4, sm4[:, 2:6].unsqueeze(2).to_broadcast([128, 4, 128]), ALU.mult, [K_("hdir"), K_("sm4")], [K_("hn")])
                    tt("dve", hn, hn, gmh_row, ALU.mult, [K_("hn")] + CK, [K_("hn")])
                    tt("dve", hn, hn, sigo[:, c, :], ALU.mult, [K_("hn"), K_("sigo")], [K_("hn")])
                    red("dve", sm4[:, 6:7], vg[:, c, :], ALU.add, [K_("vg")], [K_("sm4")])
                    ts("dve", sm4[:, 6:7], sm4[:, 6:7], 1.0 / 512, None, ALU.mult, None, [K_("sm4")], [K_("sm4")])
                    ts("dve", vg[:, c, :], vg[:, c, :], sm4[:, 6:7], None, ALU.subtract, None, [K_("vg"), K_("sm4")], [K_("vg")])
                    act(hbt, vg[:, c, :], AF.Square, [K_("vg")], [K_("hbt")])
                    yield "y"
                    pT_ = PS()
                    for h in range(H):
                        tr(pT_[:, h * 128:(h + 1) * 128], hn[:, h * 128:(h + 1) * 128], [K_("hn")], [pT_])
                    act(yaT[:, :, cs], pT_[:, :].rearrange("p (a b) -> p a b", a=4), AF.Copy, [pT_], [K_("yaT")])
                    red("dve", sm4[:, 7:8], hbt, ALU.add, [K_("hbt")], [K_("sm4")])
                    act(sm4[:, 7:8], sm4[:, 7:8], AF.Sqrt, [K_("sm4")], [K_("sm4")], bias=EPS, scale=1.0 / 512)
                    recip(sm4[:, 7:8], sm4[:, 7:8], [K_("sm4")], [K_("sm4")])
                    stt("dve", vg[:, c, :], vg[:, c, :], sm4[:, 7:8], gsgu_row, ALU.mult, ALU.mult, [K_("vg"), K_("sm4")] + CK, [K_("vg")])
                    yield "y"
                    pM = PS()
                    for gi in range(4):
                        mm(pM[:, gi * 128:(gi + 1) * 128], vg[:, c, gi * 128:(gi + 1) * 128], wsT[:, gi, :], True, True, [K_("vg")] + CK, [pM])
                    tt("dve", hn4, pM[:, :].rearrange("p (a b) -> p a b", a=4), bs_row.rearrange("p (a b) -> p a b", a=4), ALU.add, [pM] + CK, [K_("hn")])
                    tt("dve", ybT[:, :, cs], hn4, gsu[:, :, cs], ALU.mult, [K_("hn"), K_("gsu")], [K_("ybT")])
                    yield "y"
                yield "S"
                branches = [(yaT, K_("yaT"), 4, wb["wupm"][l]), (ybT, K_("ybT"), 4, wb["wups"][l]), (ys5T, K_("ys5T"), 3, wb["wup5"][l]), (ydb, K_("ydb"), 4, wb["wupc"][l])]
                for jd in range(KT):
                    for b, (ybr, ykey, nkb, wup) in enumerate(branches):
                        vgt, sgt = lslab(w_l[:, TGATE + b * 8 + jd], KT)
                        pg = PS()
                        for k in range(KT):
                            mm(pg[:, 0:NT], vgt[:, k, :], hT[:, k, :], k == 0, k == KT - 1, [sgt, K_("hT")], [pg])
                        act(tmpA, pg[:, 0:NT], AF.Sigmoid, [pg], [K_("tmpA")])
                        vu, su_ = lslab(wup[:, jd], nkb)
                        pu = PS()
                        for k in range(nkb):
                            mm(pu[:, 0:NT], vu[:, k, :], ybr[:, k, :], k == 0, k == nkb - 1, [su_, ykey], [pu])
                        if b == 0:
                            tt("dve", tmpB, tmpA, pu[:, 0:NT], ALU.mult, [K_("tmpA"), pu], [K_("tmpB")])
                        else:
                            tt("dve", tmpA, tmpA, pu[:, 0:NT], ALU.mult, [K_("tmpA"), pu], [K_("tmpA")])
                            if b < 3:
                                tt("dve", tmpB, tmpB, tmpA, ALU.add, [K_("tmpA"), K_("tmpB")], [K_("tmpB")])
                            else:
                                tt("dve", merged[:, jd, :], tmpB, tmpA, ALU.add, [K_("tmpA"), K_("tmpB")], [K_("merged")])
                        yield "y"
                yield "S"
                for j in range(KT):
                    v, s_ = lslab(wb["wout"][l][:, j], KT)
                    ps = PS()
                    for k in range(KT):
                        mm(ps[:, 0:NT], v[:, k, :], merged[:, k, :], k == 0, k == KT - 1, [s_, K_("merged")], [ps])
                    stt("dve", xt[:, j, :], ps[:, 0:NT], modv[:, 16 + j, sg:sg + 1], xt[:, j, :], ALU.mult, ALU.add, [ps, modv, K_("xt")], [K_("xt")])
                    yield "y"
                dma(x1d[seg][:, t0:t0 + NT].rearrange("(k p) t -> p k t", p=128), xt, [K_("xt")], [("x1", seg, u)])
                yield "S"

            units = ([] if is_last else [("ctx", 0, CTX)]) + [("lat", u, N3) for u in range(T // N3)]
            gens = [unit_gen(seg, u, nt_, sets[i % 2]) for i, (seg, u, nt_) in enumerate(units)]

            def to_stage(g):
                while next(g) != "S":
                    pass

            def weave(ga, gb):
                da = db = False
                while not (da and db):
                    if not da:
                        da = next(ga) == "S"
                    if not db:
                        db = next(gb) == "S"
            for _ in range(2):
                to_stage(gens[0])
            for i in range(len(gens)):
                A = gens[i]
                Bn = gens[i + 1] if i + 1 < len(gens) else None
                for _ in range(2):
                    if Bn is None:
                        to_stage(A)
                    else:
                        weave(A, Bn)
                to_stage(A)
            phase3b(l, dst_of, is_last, units)

        NU = T // 512
        cast_weights(0)
        for l in range(n_layers):
            last = l == n_layers - 1
            layer_prep(l)
            if l + 1 < n_layers:
                cast_weights(l + 1)
            src_of = {"lat": din["xT"] if l == 0 else xs, "ctx": din["cT"] if l == 0 else csd}
            dst_of = {"lat": xs, "ctx": csd}
            phase1(l, src_of)
            phase2(l)
            phase3(l, src_of, dst_of, last)
        P.barrier()
        P.final_wait("sp", [("out", u) for u in range(NU)] + ["dbg_" + n for n in dbg_out])
        P.emit()
    return nc, dbg_out, P


def _kt(w):
    K_, C = w.shape
    return np.ascontiguousarray(w.reshape(K_ // 128, 128, C).transpose(1, 0, 2))


def _tile(w):
    K_, C = w.shape
    return np.ascontiguousarray(w.reshape(K_ // 128, 128, C // 128, 128).transpose(1, 2, 0, 3))


def _pcol(v):
    return np.ascontiguousarray(v.reshape(-1, 128).T)


def make_consts():
    c = np.zeros((128, 7, 128), np.float32)
    r = np.arange(128)[:, None]; cidx = np.arange(128)[None, :]
    c[:, 0] = (r == cidx); c[:, 1] = 1.0
    c[:, 2] = (r <= cidx); c[:, 3] = (r >= cidx)
    c[:, 4] = np.where(r <= cidx, 0.0, -BIG); c[:, 5] = np.where(r >= cidx, 0.0, -BIG)
    c[:, 6] = (cidx + 1)
    return c


def shared_inputs(inp):
    f = lambda a: np.asarray(a, np.float32)
    o = {"cst": make_consts()}
    o["wmod"] = np.stack([_kt(f(inp["w_mod"][l])) for l in range(L)])
    o["bmod"] = np.stack([_pcol(f(inp["b_mod"][l])) for l in range(L)])
    o["gnm"] = np.stack([_pcol(f(inp["g_norm_mix"][l])) for l in range(L)])
    o["gnf"] = np.stack([_pcol(f(inp["g_norm_ffn"][l])) for l in range(L)])
    o["gfin"] = _pcol(f(inp["g_final"]))
    def win_tiles(w):
        cols = []
        for c0, n in ((COL_Q, 4), (COL_K, 4), (COL_V, 4)):
            cols += [w[:, c0 + i * 128:c0 + (i + 1) * 128] for i in range(n)]
        gt = np.zeros((D, 128), np.float32); gt[:, :16] = w[:, COL_G:COL_G + 16]; cols.append(gt)
        for c0, n in ((COL_U, 3), (COL_O, 4), (COL_SU, 4), (COL_SV, 4), (COL_CB, 4), (COL_CC, 4), (COL_CX, 4), (COL_GATE, 32)):
            cols += [w[:, c0 + i * 128:c0 + (i + 1) * 128] for i in range(n)]
        t = np.stack(cols, 0)
        return np.ascontiguousarray(t.reshape(NWT, KT, 128, 128).transpose(2, 0, 1, 3))
    o["win"] = np.stack([win_tiles(f(inp["w_in"][l])) for l in range(L)])
    o["bg"] = f(inp["b_gates"]).reshape(L, 1, 16)
    wc = f(inp["w_conv_qk"])
    o["wconv"] = np.ascontiguousarray(wc.reshape(L, 2, 3, 4, 128).transpose(0, 4, 1, 2, 3).reshape(L, 128, 24))
    o["gmh"] = f(inp["g_mh"]).reshape(L, 1, 512)
    o["gsgu"] = f(inp["g_sgu"]).reshape(L, 1, 512)
    o["wsT"] = np.ascontiguousarray(f(inp["w_sgu"]).transpose(0, 3, 1, 2))
    o["bs"] = f(inp["b_sgu"]).reshape(L, 1, 512)

    def ptile(a):
        return np.ascontiguousarray(a.reshape(L, 2, 12, 128).transpose(0, 1, 3, 2))
    o["are"] = ptile(f(inp["s5_a_re"])); o["aim"] = ptile(f(inp["s5_a_im"]))
    o["ldt"] = ptile(np.repeat(f(inp["s5_log_dt"])[..., None], 64, axis=-1))

    def bpad(B):
        out = np.zeros((L, 128, 12, 128), np.float32)
        for g in range(24):
            k, gg = g // 2, g % 2
            r0 = (g % 8) * 16
            out[:, r0:r0 + 16, k, gg * 64:(gg + 1) * 64] = B[:, g].transpose(0, 2, 1)
        return out

    def cpad(C):
        out = np.zeros((L, 128, 12, 128), np.float32)
        for g in range(24):
            k, gg = g // 2, g % 2
            c0 = (g % 8) * 16
            out[:, gg * 64:(gg + 1) * 64, k, c0:c0 + 16] = C[:, g].transpose(0, 2, 1)
        return out
    o["Bre"] = bpad(f(inp["s5_b_re"])); o["Bim"] = bpad(f(inp["s5_b_im"]))
    o["Cre"] = cpad(f(inp["s5_c_re"])); o["Cim"] = cpad(f(inp["s5_c_im"]))
    o["s5d"] = np.stack([_pcol(f(inp["s5_d"][l])) for l in range(L)])
    o["bglu"] = np.stack([_pcol(f(inp["b_glu"][l])) for l in range(L)])
    o["wglu"] = np.stack([_kt(f(inp["w_glu"][l])) for l in range(L)])
    ws = f(inp["w_sconv"])
    o["wsconv"] = np.ascontiguousarray(ws.reshape(L, 3, 4, 128).transpose(0, 3, 1, 2).reshape(L, 128, 12))
    for nm, key in (("wupm", "w_up_mlstm"), ("wups", "w_up_sgu"), ("wup5", "w_up_s5"), ("wupc", "w_up_sconv"), ("wout", "w_out"),
                    ("wfg", "w_ffn_gate"), ("wfu", "w_ffn_up"), ("wfd", "w_ffn_down")):
        o[nm] = np.stack([_tile(f(inp[key][l])) for l in range(L)])
    return o


def core_inputs(inp, b, T):
    f = lambda a: np.asarray(a, np.float32)
    o = {}
    o["xT"] = np.ascontiguousarray(f(inp["x"][b, :T]).T)
    o["cT"] = np.ascontiguousarray(f(inp["ctx"][b]).T)
    sc = np.stack([_pcol(f(inp["c"][b])), _pcol(f(inp["c_ctx"]))], axis=-1)
    o["sc"] = np.ascontiguousarray(sc)
    return o


_CACHE = {}


def kernel(**inputs):
    B, T = inputs["x"].shape[0], inputs["x"].shape[1]
    if T not in _CACHE:
        _CACHE[T] = build_program(T)[0]
    nc = _CACHE[T]
    sh = shared_inputs(inputs)
    in_maps = []
    for b in range(B):
        m = dict(sh)
        m.update(core_inputs(inputs, b, T))
        in_maps.append(m)
    res = run_bass_kernel_spmd(nc, in_maps, core_ids=list(range(B)))
    out = np.stack([np.ascontiguousarray(r["outT"].T) for r in res.results], axis=0)
    return out.astype(np.float32)
```

```python
import contextlib
import numpy as np
import concourse.bass as bass
import concourse.mybir as mybir
from concourse.bass_utils import run_bass_kernel_spmd

F32 = mybir.dt.float32
I32 = mybir.dt.int32
BF16 = mybir.dt.bfloat16
ALU = mybir.AluOpType
AF = mybir.ActivationFunctionType
AX = mybir.AxisListType

ENG_NAMES = ("pe", "act", "dve", "pool", "sp")
N_DMA_SEMS = 6


class Prog:
    def __init__(self, nc, same_eng_sync=True):
        self.nc = nc
        self.same_eng_sync = same_eng_sync
        self.q = {e: [] for e in ENG_NAMES}
        self.cnt = {e: 0 for e in ENG_NAMES}
        self.waited = {e: {} for e in ENG_NAMES}
        self.last_w = {}
        self.readers = {}
        self.dma_rr = {e: 0 for e in ENG_NAMES}
        self.dma_cnt = {}
        self.sem_handles = {}
        self.pending = {e: {} for e in ENG_NAMES}
        self.n_instr = 0

    def _need(self, eng, needs, semkey, val):
        if semkey == ("c", "pe") and eng == "pe":
            return
        if (not self.same_eng_sync) and semkey == ("c", eng):
            return
        if self.waited[eng].get(semkey, 0) >= val:
            return
        if needs.get(semkey, 0) < val:
            needs[semkey] = val

    @staticmethod
    def _k(k):
        if isinstance(k, (str, int)):
            return k
        if isinstance(k, tuple):
            return tuple(Prog._k(x) for x in k)
        return k.name

    def barrier(self):
        snap = {("c", e): self.cnt[e] for e in ENG_NAMES if self.cnt[e]}
        snap.update(self.dma_cnt)
        for e in ENG_NAMES:
            for sk, v in snap.items():
                if self.pending[e].get(sk, 0) < v:
                    self.pending[e][sk] = v

    def _deps(self, eng, reads, writes):
        needs = {}
        if self.pending[eng]:
            for sk, v in self.pending[eng].items():
                self._need(eng, needs, sk, v)
            self.pending[eng] = {}
        for k in reads:
            lw = self.last_w.get(k)
            if lw:
                self._need(eng, needs, lw[0], lw[1])
        for k in writes:
            lw = self.last_w.get(k)
            if lw:
                self._need(eng, needs, lw[0], lw[1])
            for sk, v in self.readers.get(k, {}).items():
                self._need(eng, needs, sk, v)
        for sk, v in needs.items():
            self.waited[eng][sk] = v
        return list(needs.items())

    def _commit(self, semkey, val, reads, writes):
        for k in reads:
            self.readers.setdefault(k, {})[semkey] = val
        for k in writes:
            self.last_w[k] = (semkey, val)
            self.readers[k] = {}

    def op(self, eng, fn, reads=(), writes=()):
        reads = [self._k(k) for k in reads]
        writes = [self._k(k) for k in writes]
        waits = self._deps(eng, reads, writes)
        self.cnt[eng] += 1
        semkey = ("c", eng)
        self._commit(semkey, self.cnt[eng], reads, writes)
        self.q[eng].append((waits, fn, semkey, 1))
        self.n_instr += 1

    def dma(self, eng, out, in_, reads=(), writes=(), **kw):
        reads = [self._k(k) for k in reads]
        writes = [self._k(k) for k in writes]
        slot = self.dma_rr[eng]
        self.dma_rr[eng] = (slot + 1) % N_DMA_SEMS
        semkey = ("d", eng, slot)
        prev = self.dma_cnt.get(semkey, 0)
        waits = self._deps(eng, reads, writes)
        if prev and self.waited[eng].get(semkey, 0) < prev:
            waits.append((semkey, prev))
            self.waited[eng][semkey] = prev
        val = prev + 16
        self.dma_cnt[semkey] = val
        self._commit(semkey, val, reads, writes)

        def fn(e, out=out, in_=in_, kw=kw):
            return e.dma_start(out=out, in_=in_, **kw)
        self.q[eng].append((waits, fn, semkey, 16))
        self.n_instr += 1

    def final_wait(self, eng, keys):
        keys = [self._k(k) for k in keys]
        waits = self._deps(eng, keys, ())
        self.q[eng].append((waits, None, None, 0))

    def emit(self):
        nc = self.nc
        semkeys = set()
        for e in ENG_NAMES:
            for waits, fn, sk, inc in self.q[e]:
                if sk is not None:
                    semkeys.add(sk)
                for w, _ in waits:
                    semkeys.add(w)
        semkeys = sorted(semkeys, key=str)
        with contextlib.ExitStack() as st:
            for sk in semkeys:
                self.sem_handles[sk] = st.enter_context(nc.semaphore("s_" + "_".join(map(str, sk))))
            block = st.enter_context(nc.Block())

            def run(e_name):
                def body(eng):
                    for waits, fn, sk, inc in self.q[e_name]:
                        for wsk, v in waits:
                            eng.wait_ge(self.sem_handles[wsk], v)
                        if fn is not None:
                            fn(eng).then_inc(self.sem_handles[sk], inc)
                return body
            block.tensor(run("pe"))
            block.scalar(run("act"))
            block.vector(run("dve"))
            block.gpsimd(run("pool"))
            block.sync(run("sp"))


D = 1024
KT = 8
NT = 512
H = 4
CTX = 256
L = 2
FH = 2816
FT = 22
IN_DIM = 9104
COL_Q, COL_K, COL_V, COL_G, COL_U = 0, 512, 1024, 1536, 1552
COL_O, COL_SU, COL_SV, COL_CB, COL_CC, COL_CX, COL_GATE = 1936, 2448, 2960, 3472, 3984, 4496, 5008
EPS = 1e-6
BIG = 1.0e30
SLABF = 1408
NSLAB = 6
TQ, TK, TV, TG, TU, TO, TSU, TSV, TCB, TCC, TCX, TGATE = 0, 4, 8, 12, 13, 16, 20, 24, 28, 32, 36, 40
NWT = 72
PI = float(np.pi)
TWO_PI = float(2 * np.pi)

def input_shapes(T):
    return {
        "xT": [D, T], "cT": [D, CTX], "sc": [128, KT, 2], "cst": [128, 7, 128],
        "wmod": [L, 128, KT, 6 * D], "bmod": [L, 128, 48], "gnm": [L, 128, KT], "gnf": [L, 128, KT], "gfin": [128, KT],
        "win": [L, 128, NWT, KT, 128], "bg": [L, 1, 16], "wconv": [L, 128, 24], "gmh": [L, 1, 512], "gsgu": [L, 1, 512],
        "wsT": [L, 128, 4, 128], "bs": [L, 1, 512], "are": [L, 2, 128, 12], "aim": [L, 2, 128, 12], "ldt": [L, 2, 128, 12],
        "Bre": [L, 128, 12, 128], "Bim": [L, 128, 12, 128], "Cre": [L, 128, 12, 128], "Cim": [L, 128, 12, 128],
        "s5d": [L, 128, 3], "bglu": [L, 128, 3], "wglu": [L, 128, 3, 384], "wsconv": [L, 128, 12],
        "wupm": [L, 128, 8, 4, 128], "wups": [L, 128, 8, 4, 128], "wup5": [L, 128, 8, 3, 128], "wupc": [L, 128, 8, 4, 128],
        "wout": [L, 128, 8, KT, 128], "wfg": [L, 128, FT, KT, 128], "wfu": [L, 128, FT, KT, 128], "wfd": [L, 128, 8, FT, 128],
    }


def build_program(T, n_layers=L, dbg=()):
    nc = bass.Bass("TRN2", target_bir_lowering=False)
    din = {n: nc.dram_tensor(n, s, F32, kind="ExternalInput").ap() for n, s in input_shapes(T).items()}
    outT = nc.dram_tensor("outT", [D, T], F32, kind="ExternalOutput").ap()
    dbg_out = {}
    TS = {"lat": T, "ctx": CTX}
    xs = nc.dram_tensor("xs", [D, T], F32).ap()
    csd = nc.dram_tensor("csd", [D, CTX], F32).ap()
    qd = {sg: nc.dram_tensor("qd_" + sg, [128, 4, TS[sg]], F32).ap() for sg in TS}
    kd = {sg: nc.dram_tensor("kd_" + sg, [128, 4, TS[sg]], F32).ap() for sg in TS}
    vd = {sg: nc.dram_tensor("vd_" + sg, [TS[sg], 512], F32).ap() for sg in TS}
    gd = {sg: nc.dram_tensor("gd_" + sg, [TS[sg], 16], F32).ap() for sg in TS}
    ud = {sg: nc.dram_tensor("ud_" + sg, [128, 3, TS[sg]], F32).ap() for sg in TS}
    x1d = {sg: nc.dram_tensor("x1d_" + sg, [D, TS[sg]], F32).ap() for sg in TS}
    hd = {(d, sg): nc.dram_tensor(f"hd{d}_{sg}", [TS[sg], 512], F32).ap() for sg in TS for d in (0, 1)}
    yd_ = {(d, sg): nc.dram_tensor(f"yd{d}_{sg}", [128, 3, TS[sg]], F32).ap() for sg in TS for d in (0, 1)}
    BIGW = ("win", "wupm", "wups", "wup5", "wupc", "wout", "wfg", "wfu", "wfd")
    wb = {n: nc.dram_tensor("wb_" + n, input_shapes(T)[n], BF16).ap() for n in BIGW}

    with contextlib.ExitStack() as st:
        def sb(name, shape, dt=F32):
            return st.enter_context(nc.sbuf_tensor("sb_" + name, shape, dt))
        P = Prog(nc)
        cst = sb("cst", [128, 7, 128])
        ident, ones, Umat, Lmat, biasU, biasL, iota1 = (cst[:, i, :] for i in range(7))
        sc = sb("sc", [128, KT, 2]); modv = sb("modv", [128, 48, 2]); lcs = sb("lcs", [128, 2, 2, KT])
        gnm = sb("gnm", [128, KT]); gnf = sb("gnf", [128, KT]); gfin = sb("gfin", [128, KT]); bmod = sb("bmod", [128, 48])
        bg_row = sb("bg_row", [128, 16]); wconv = sb("wconv", [128, 24]); wsconv = sb("wsconv", [128, 12])
        s5d = sb("s5d", [128, 3]); bglu = sb("bglu", [128, 3])
        rhod = [sb(f"rho{d}", [128, 12]) for d in range(2)]
        card = [sb(f"car{d}", [128, 2, 12]) for d in range(2)]
        Caugd = [sb(f"Caug{d}", [128, H, 129]) for d in range(2)]
        mstd = [sb(f"mst{d}", [128, H]) for d in range(2)]
        s5p = sb("s5p", [128, 16, 12])
        NA = 50000
        ARENA = sb("ARENA", [128, NA])
        TABF = 5 * 12 * 128
        tabd = [ARENA[:, NA - (2 - d) * TABF:NA - (1 - d) * TABF].rearrange("p (t a b) -> p t a b", t=5, a=12) for d in range(2)]
        TABK = ["tab0", "tab1"]
        pbs = [st.enter_context(nc.psum_tensor(f"pb{i}", [128, 512], F32)) for i in range(8)]
        ps_rr = [0]

        def PS():
            t = pbs[ps_rr[0] % 8]
            ps_rr[0] += 1
            return t

        class Carver:
            def __init__(self, lim=None):
                self.off = 0
                self.lim = NA if lim is None else lim

            def f(self, n, pat=None, **kw):
                ap = ARENA[:, self.off:self.off + n]
                self.off += n
                assert self.off <= self.lim, self.off
                return ap.rearrange(pat, **kw) if pat else ap

            def b(self, n, pat=None, **kw):
                nf = (n + 1) // 2
                ap = ARENA[:, self.off:self.off + nf].bitcast(BF16)
                self.off += nf
                assert self.off <= self.lim, self.off
                return ap.rearrange(pat, **kw) if pat else ap

        def tt(eng, out, a, b, op, r, w):
            P.op(eng, lambda e: e.tensor_tensor(out, a, b, op), reads=r, writes=w)

        def ts(eng, out, a, s1, s2, op0, op1, r, w):
            if s2 is None:
                P.op(eng, lambda e: e.tensor_scalar(out, a, s1, None, op0=op0), reads=r, writes=w)
            else:
                P.op(eng, lambda e: e.tensor_scalar(out, a, s1, s2, op0=op0, op1=op1), reads=r, writes=w)

        def stt(eng, out, in0, scalar, in1, op0, op1, r, w):
            P.op(eng, lambda e: e.scalar_tensor_tensor(out, in0, scalar, in1, op0=op0, op1=op1), reads=r, writes=w)

        def act(out, in_, func, r, w, bias=0.0, scale=1.0):
            P.op("act", lambda e: e.activation(out, in_, func, bias=bias, scale=scale), reads=r, writes=w)

        def mm(ps, lhsT, rhs, start, stop, r, w):
            P.op("pe", lambda e: e.matmul(ps, lhsT, rhs, start=start, stop=stop), reads=r, writes=w)

        def tr(ps, in_, r, w):
            P.op("pe", lambda e: e.transpose(ps, in_, ident), reads=r + [cst], writes=w)

        def red(eng, out, in_, op, r, w):
            P.op(eng, lambda e: e.tensor_reduce(out, in_, axis=AX.X, op=op), reads=r, writes=w)

        def recip(out, in_, r, w):
            P.op("dve", lambda e: e.reciprocal(out, in_), reads=r, writes=w)

        def cp(eng, out, in_, r, w):
            P.op(eng, lambda e: e.tensor_copy(out, in_), reads=r, writes=w)

        def mset(eng, out, val, w):
            P.op(eng, lambda e: e.memset(out, val), writes=w)

        def dma(out, in_, r, w, eng="sp"):
            P.dma(eng, out, in_, reads=r, writes=w)

        def DBG(name, ap, shape, keys):
            if name in dbg and name not in dbg_out:
                t = nc.dram_tensor("dbg_" + name, shape, F32, kind="ExternalOutput").ap()
                dbg_out[name] = t
                dma(t, ap, keys, ["dbg_" + name])

        def cast_weights(l):
            for n in BIGW:
                src = din[n][l]; dstw = wb[n][l]
                nd = len(src.shape)
                letters = "abcd"[:nd - 1]
                pat = "p " + " ".join(letters) + " -> p (" + " ".join(letters) + ")"
                s2 = src.rearrange(pat); d2 = dstw.rearrange(pat)
                N = s2.shape[1]
                for c0 in range(0, N, 8192):
                    c1 = min(N, c0 + 8192)
                    dma(d2[:, c0:c1], s2[:, c0:c1], [], [("wb", n, l, c0)], eng="pool")

        def range_reduce(out, in_, ti, tm, keys_in, keys_out, kti, ktm):
            ts("dve", tm, in_, 1.0 / TWO_PI, None, ALU.mult, None, keys_in, [ktm])
            cp("dve", ti, tm, [ktm], [kti])
            cp("dve", tm, ti, [kti], [ktm])
            stt("dve", out, tm, -TWO_PI, in_, ALU.mult, ALU.add, [ktm] + keys_in, keys_out)
            P.op("dve", lambda e: e.tensor_single_scalar(tm, out, PI, op=ALU.is_gt), reads=keys_out, writes=[ktm])
            stt("dve", out, tm, -TWO_PI, out, ALU.mult, ALU.add, [ktm] + keys_out, keys_out)
            P.op("dve", lambda e: e.tensor_single_scalar(tm, out, -PI, op=ALU.is_lt), reads=keys_out, writes=[ktm])
            stt("dve", out, tm, TWO_PI, out, ALU.mult, ALU.add, [ktm] + keys_out, keys_out)
            ts("dve", out, out, PI, -PI, ALU.min, ALU.max, keys_out, keys_out)

        dma(cst[:], din["cst"], [], [cst])
        dma(sc[:], din["sc"], [], [sc])
        dma(gfin[:], din["gfin"], [], [gfin])
        act(sc[:], sc[:], AF.Silu, [sc], [sc])

        def layer_prep(l, first=False):
            if not first:
                P.barrier()
            cv = Carver(NA - 2 * TABF)
            wm = [cv.f(1024, "p (k c) -> p k c", k=KT) for _ in range(2)]
            ang = cv.f(1536, "p (a b) -> p a b", a=12); tsn = cv.f(1536, "p (a b) -> p a b", a=12); tcs = cv.f(1536, "p (a b) -> p a b", a=12)
            tmf = cv.f(1536); ti32 = cv.f(1536).bitcast(I32)
            for nm, t in (("gnm", gnm), ("gnf", gnf), ("bmod", bmod), ("wconv", wconv), ("wsconv", wsconv),
                          ("s5d", s5d), ("bglu", bglu)):
                dma(t[:], din[nm][l], [], [t])
            dma(bg_row[:], din["bg"][l].partition_broadcast(128), [], [bg_row])
            for ft in range(48):
                s = "wm%d" % (ft % 2)
                v = wm[ft % 2]
                dma(v, din["wmod"][l][:, :, ft * 128:(ft + 1) * 128], [], [s])
                ps = PS()
                for k in range(KT):
                    mm(ps[:, 0:2], v[:, k, :], sc[:, k, :], k == 0, k == KT - 1, [s, sc], [ps])
                ts("dve", modv[:, ft, :], ps[:, 0:2], bmod[:, ft:ft + 1], None, ALU.add, None, [ps, bmod], [modv])
            for sg in range(2):
                stt("dve", lcs[:, sg, 0, :], modv[:, 8:16, sg], 1.0, gnm[:], ALU.add, ALU.mult, [modv, gnm], [lcs])
                stt("dve", lcs[:, sg, 1, :], modv[:, 32:40, sg], 1.0, gnf[:], ALU.add, ALU.mult, [modv, gnf], [lcs])
            K = ["prep"]
            for d in range(2):
                tab = tabd[d]; rho = rhod[d]; TK_ = TABK[d]
                are = s5p[:, 0, :]; aim = s5p[:, 1, :]; ldt = s5p[:, 2, :]
                dma(are, din["are"][l, d], [], K); dma(aim, din["aim"][l, d], [], K); dma(ldt, din["ldt"][l, d], [], K)
                dt_ = s5p[:, 3, :]; th = s5p[:, 4, :]; lar = s5p[:, 5, :]; sn = s5p[:, 6, :]; cs_ = s5p[:, 7, :]
                t0 = s5p[:, 8, :]; t1 = s5p[:, 9, :]; den = s5p[:, 10, :]; fr = s5p[:, 11, :]; fi = s5p[:, 12, :]
                nr = s5p[:, 13, :]; ni = s5p[:, 14, :]; t2 = s5p[:, 15, :]
                act(dt_, ldt, AF.Exp, K, K)
                tt("dve", th, dt_, aim, ALU.mult, K, K)
                tt("dve", lar, dt_, are, ALU.mult, K, K)
                act(rho[:], lar, AF.Exp, K, [rho] + K)
                range_reduce(t0, th, ti32[:, 0:12], t2, K, K, "ti32", "prep")
                act(sn, t0, AF.Sin, K, K)
                ts("dve", t1, th, PI / 2, None, ALU.add, None, K, K)
                range_reduce(t0, t1, ti32[:, 0:12], t2, K, K, "ti32", "prep")
                act(cs_, t0, AF.Sin, K, K)
                tt("dve", nr, rho[:], cs_, ALU.mult, [rho] + K, K)
                ts("dve", nr, nr, -1.0, None, ALU.add, None, K, K)
                tt("dve", ni, rho[:], sn, ALU.mult, [rho] + K, K)
                tt("dve", den, are, are, ALU.mult, K, K)
                tt("dve", t0, aim, aim, ALU.mult, K, K)
                tt("dve", den, den, t0, ALU.add, K, K)
                recip(den, den, K, K)
                tt("dve", t0, nr, are, ALU.mult, K, K); tt("dve", t1, ni, aim, ALU.mult, K, K)
                tt("dve", t0, t0, t1, ALU.add, K, K); tt("dve", fr, t0, den, ALU.mult, K, K)
                tt("dve", t0, ni, are, ALU.mult, K, K); tt("dve", t1, nr, aim, ALU.mult, K, K)
                tt("dve", t0, t0, t1, ALU.subtract, K, K); tt("dve", fi, t0, den, ALU.mult, K, K)
                thb = th.unsqueeze(2).to_broadcast([128, 12, 128])
                io = iota1.unsqueeze(1).to_broadcast([128, 12, 128])
                tt("dve", ang, io, thb, ALU.mult, [cst] + K, K)
                angf = ang.rearrange("p a b -> p (a b)")
                range_reduce(tsn.rearrange("p a b -> p (a b)"), angf, ti32, tmf, K, K, "ti32", "prep")
                act(tab[:, 3, :, :], tsn, AF.Sin, K, [TK_])
                ts("dve", angf, angf, PI / 2, None, ALU.add, None, K, K)
                range_reduce(tcs.rearrange("p a b -> p (a b)"), angf, ti32, tmf, K, K, "ti32", "prep")
                act(tab[:, 2, :, :], tcs, AF.Sin, K, [TK_])
                frb = fr.unsqueeze(2).to_broadcast([128, 12, 128]); fib = fi.unsqueeze(2).to_broadcast([128, 12, 128])
                tt("dve", ang, tab[:, 2, :, :], frb, ALU.mult, [TK_] + K, K)
                tt("dve", tsn, tab[:, 3, :, :], fib, ALU.mult, [TK_] + K, K)
                tt("dve", tab[:, 0, :, :], ang, tsn, ALU.add, K, [TK_])
                tt("dve", ang, tab[:, 2, :, :], fib, ALU.mult, [TK_] + K, K)
                tt("dve", tsn, tab[:, 3, :, :], frb, ALU.mult, [TK_] + K, K)
                tt("dve", tab[:, 1, :, :], ang, tsn, ALU.subtract, K, [TK_])
                cp("dve", tab[:, 4, :, :], rho[:].unsqueeze(2).to_broadcast([128, 12, 128]), [rho], [TK_])
                mset("dve", tab[:, 4, :, 0:1], 0.0, [TK_])
            P.barrier()

        def make_rowlocal(cv, nslab):
            V = {"nt": NT}
            V["xt_"] = cv.f(KT * NT, "p (a b) -> p a b", a=KT); V["sqt_"] = cv.f(KT * NT, "p (a b) -> p a b", a=KT)
            V["hT_"] = cv.b(KT * NT, "p (a b) -> p a b", a=KT); V["rs_"] = cv.f(NT)
            V["slabs"] = [cv.b(SLABF) for _ in range(nslab)]
            set_nt(V, NT)
            return V

        def set_nt(V, nt):
            V["nt"] = nt
            V["xt"] = V["xt_"][:, :, 0:nt]; V["sqt"] = V["sqt_"][:, :, 0:nt]; V["hT"] = V["hT_"][:, :, 0:nt]; V["rs"] = V["rs_"][:, 0:nt]

        slab_rr = [0]

        def load_slab(V, w_ap, nk):
            i = slab_rr[0] % len(V["slabs"])
            slab_rr[0] += 1
            s = V["slabs"][i]
            key = "slab%d" % i
            v = s[:, 0:nk * 128].rearrange("p (k c) -> p k c", k=nk)
            dma(v, w_ap, [], [key])
            return v, key

        def norm(V, scale_fn, shift_fn, extra_r, dst, dkey):
            xt, sqt, rs = V["xt"], V["sqt"], V["rs"]
            NT = V["nt"]
            tt("pool", sqt, xt, xt, ALU.mult, ["xt"], ["sqt"])
            ps = PS()
            for k in range(KT):
                mm(ps[:, 0:NT], ones, sqt[:, k, :], k == 0, k == KT - 1, [cst, "sqt"], [ps])
            act(rs, ps[:, 0:NT], AF.Sqrt, [ps], ["rs"], bias=EPS, scale=1.0 / D)
            recip(rs, rs, ["rs"], ["rs"])
            tt("dve", sqt, xt, rs.unsqueeze(1).to_broadcast([128, KT, NT]), ALU.mult, ["xt", "rs"], ["sqt"])
            for k in range(KT):
                sh = shift_fn(k)
                if sh is None:
                    act(dst[:, k, :], sqt[:, k, :], AF.Identity, ["sqt"] + extra_r, [dkey], scale=scale_fn(k))
                else:
                    act(dst[:, k, :], sqt[:, k, :], AF.Identity, ["sqt"] + extra_r, [dkey], bias=sh, scale=scale_fn(k))

        def proj_fm(V, w_l, c0, ntiles, nk, rhs_t, rhs_key, evac):
            NT = V["nt"]
            for j in range(ntiles):
                v, s = load_slab(V, w_l[:, c0 + j], nk)
                ps = PS()
                for k in range(nk):
                    mm(ps[:, 0:NT], v[:, k, :], rhs_t[:, k, :], k == 0, k == nk - 1, [s, rhs_key], [ps])
                evac(j, ps)

        def proj_tm(V, w_l, c0, ntiles, evac, ncols=128):
            hT = V["hT"]
            NCH = V["nt"] // 128
            for j in range(ntiles):
                v, s = load_slab(V, w_l[:, c0 + j], KT)
                for c in range(NCH):
                    ps = PS()
                    for k in range(KT):
                        mm(ps[:, 0:ncols], hT[:, k, c * 128:(c + 1) * 128], v[:, k, 0:ncols], k == 0, k == KT - 1, [s, "hT"], [ps])
                    evac(j, c, ps)

        def conv3(dst, src, wv, woff, rowlen, r, w, NT):
            nr_ = NT // rowlen
            for j in range(4):
                w0 = wv[:, woff + 0 + j:woff + 1 + j]; w1 = wv[:, woff + 4 + j:woff + 5 + j]; w2 = wv[:, woff + 8 + j:woff + 9 + j]
                d3 = dst[:, j, :].rearrange("p (r c) -> p r c", r=nr_); s3 = src[:, j, :].rearrange("p (r c) -> p r c", r=nr_)
                act(dst[:, j, :], src[:, j, :], AF.Identity, r, w, scale=w1)
                stt("dve", d3[:, :, 1:rowlen], s3[:, :, 0:rowlen - 1], w0, d3[:, :, 1:rowlen], ALU.mult, ALU.add, r + w, w)
                stt("dve", d3[:, :, 0:rowlen - 1], s3[:, :, 1:rowlen], w2, d3[:, :, 0:rowlen - 1], ALU.mult, ALU.add, r + w, w)

        def phase1(l, src_of):
            P.barrier()
            cv = Carver(NA - 2 * TABF)
            NTF = 512; NCF = 4
            sqt_ = cv.f(KT * NTF, "p (a b) -> p a b", a=KT); rs_ = cv.f(NTF)
            slabs1 = [cv.b(SLABF) for _ in range(6)]
            rawq_ = cv.f(4 * NTF, "p (a b) -> p a b", a=4); rawk_ = cv.f(4 * NTF, "p (a b) -> p a b", a=4)
            qT_ = cv.f(4 * NTF, "p (a b) -> p a b", a=4); kT_ = cv.f(4 * NTF, "p (a b) -> p a b", a=4)
            vtm_ = cv.f(NCF * 512, "p (c e) -> p c e", c=NCF); uT_ = cv.f(3 * NTF, "p (a b) -> p a b", a=3); graw_ = cv.f(NCF * 16, "p (c e) -> p c e", c=NCF)
            sets = [{"tag": tg, "xt": cv.f(KT * NTF, "p (a b) -> p a b", a=KT), "hT": cv.b(KT * NTF, "p (a b) -> p a b", a=KT)} for tg in ("A", "B")]
            w_l = wb["win"][l]
            srr = [0]

            def lslab(w_ap, nk):
                i = srr[0] % 6
                srr[0] += 1
                key = "s1slab%d" % i
                v = slabs1[i][:, 0:nk * 128].rearrange("p (k c) -> p k c", k=nk)
                dma(v, w_ap, [], [key])
                return v, key

            def unit_gen(seg, u, NT, S):
                NCH = NT // 128
                tg = S["tag"]
                kx, kh = "p1xt" + tg, "p1hT" + tg
                xt, hT = S["xt"][:, :, 0:NT], S["hT"][:, :, 0:NT]
                sqt = sqt_[:, :, 0:NT]; rs = rs_[:, 0:NT]
                rawq = rawq_[:, :, 0:NT]; rawk = rawk_[:, :, 0:NT]; qT = qT_[:, :, 0:NT]; kT = kT_[:, :, 0:NT]
                vtm = vtm_[:, 0:NCH, :]; uT = uT_[:, :, 0:NT]; graw = graw_[:, 0:NCH, :]
                sg = 1 if seg == "ctx" else 0
                t0 = u * NTF
                rowlen = NT if seg == "ctx" else 64
                dma(xt, src_of[seg][:, t0:t0 + NT].rearrange("(k p) t -> p k t", p=128), [], [kx])
                yield "y"
                act(sqt, xt, AF.Square, [kx], ["p1sqt"])
                ps = PS()
                for k in range(KT):
                    mm(ps[:, 0:NT], ones, sqt[:, k, :], k == 0, k == KT - 1, [cst, "p1sqt"], [ps])
                act(rs, ps[:, 0:NT], AF.Sqrt, [ps], ["p1rs"], bias=EPS, scale=1.0 / D)
                recip(rs, rs, ["p1rs"], ["p1rs"])
                yield "y"
                tt("dve", sqt, xt, rs.unsqueeze(1).to_broadcast([128, KT, NT]), ALU.mult, [kx, "p1rs"], ["p1sqt"])
                for k in range(KT):
                    act(hT[:, k, :], sqt[:, k, :], AF.Identity, ["p1sqt", lcs, modv], [kh], bias=modv[:, k, sg:sg + 1], scale=lcs[:, sg, 0, k:k + 1])
                yield "S"

                def pfm(c0, ntiles, dst, dkey):
                    for j in range(ntiles):
                        v, s_ = lslab(w_l[:, c0 + j], KT)
                        ps = PS()
                        for k in range(KT):
                            mm(ps[:, 0:NT], v[:, k, :], hT[:, k, :], k == 0, k == KT - 1, [s_, kh], [ps])
                        act(dst[:, j, :], ps[:, 0:NT], AF.Copy, [ps], [dkey])
                        yield "y"

                def ptm(c0, ntiles, evac, ncols=128):
                    for j in range(ntiles):
                        v, s_ = lslab(w_l[:, c0 + j], KT)
                        for c in range(NCH):
                            ps = PS()
                            for k in range(KT):
                                mm(ps[:, 0:ncols], hT[:, k, c * 128:(c + 1) * 128], v[:, k, 0:ncols], k == 0, k == KT - 1, [s_, kh], [ps])
                            evac(j, c, ps)
                        yield "y"
                yield from pfm(TQ, 4, rawq, "p1rawq")
                yield from pfm(TK, 4, rawk, "p1rawk")
                conv3(qT, rawq, wconv, 0, rowlen, ["p1rawq", wconv], ["p1qT"], NT)
                act(qT, qT, AF.Silu, ["p1qT"], ["p1qT"])
                dma(qd[seg][:, :, t0:t0 + NT], qT, ["p1qT"], [("q", seg, u)])
                yield "y"
                yield from ptm(TV, 4, lambda j, c, ps: act(vtm[:, c, j * 128:(j + 1) * 128], ps[:, 0:128], AF.Copy, [ps], ["p1vtm"]))
                dma(vd[seg][t0:t0 + NT, :].rearrange("(c p) e -> p c e", p=128), vtm, ["p1vtm"], [("v", seg, u)])
                conv3(kT, rawk, wconv, 12, rowlen, ["p1rawk", wconv], ["p1kT"], NT)
                act(kT, kT, AF.Silu, ["p1kT"], ["p1kT"], )
                ts("dve", kT, kT, float(128 ** -0.5), None, ALU.mult, None, ["p1kT"], ["p1kT"])
                dma(kd[seg][:, :, t0:t0 + NT], kT, ["p1kT"], [("k", seg, u)])
                yield "y"
                yield from ptm(TG, 1, lambda j, c, ps: tt("dve", graw[:, c, :], ps[:, 0:16], bg_row[:], ALU.add, [ps, bg_row], ["p1graw"]), ncols=16)
                dma(gd[seg][t0:t0 + NT, :].rearrange("(c p) e -> p c e", p=128), graw, ["p1graw"], [("g", seg, u)])
                yield from pfm(TU, 3, uT, "p1uT")
                dma(ud[seg][:, :, t0:t0 + NT], uT, ["p1uT"], [("u", seg, u)])
                yield "S"

            units = [("ctx", 0, CTX)] + [("lat", u, NTF) for u in range(T // NTF)]
            gens = [unit_gen(seg, u, nt_, sets[i % 2]) for i, (seg, u, nt_) in enumerate(units)]

            def to_stage(g):
                while next(g) != "S":
                    pass

            def weave(ga, gb):
                da = db = False
                while not (da and db):
                    if not da:
                        da = next(ga) == "S"
                    if not db:
                        db = next(gb) == "S"
            to_stage(gens[0])
            for i in range(len(gens)):
                if i + 1 < len(gens):
                    weave(gens[i], gens[i + 1])
                else:
                    to_stage(gens[i])

        def phase2(l):
            P.barrier()
            cv = Carver(NA - 2 * TABF)
            nlat = T // 128
            chains = {0: [("ctx", c) for c in range(CTX // 128)] + [("lat", c) for c in range(nlat)],
                      1: [("ctx", c) for c in range(CTX // 128 - 1, -1, -1)] + [("lat", c) for c in range(nlat - 1, -1, -1)]}
            for d in range(2):
                mset("pool", Caugd[d][:], 0.0, [Caugd[d]]); mset("pool", mstd[d][:], 0.0, [mstd[d]]); mset("pool", card[d][:], 0.0, [card[d]])

            def unit_of(seg, c):
                return (seg, 0)

            Bre = cv.f(1536, "p (a b) -> p a b", a=12); Bim = cv.f(1536, "p (a b) -> p a b", a=12)
            Cre = cv.f(1536, "p (a b) -> p a b", a=12); Cimn = cv.f(1536, "p (a b) -> p a b", a=12)
            dma(Bre, din["Bre"][l], [], ["Bre"]); dma(Bim, din["Bim"][l], [], ["Bim"]); dma(Cre, din["Cre"][l], [], ["Cre"]); dma(Cimn, din["Cim"][l], [], ["Cimn"])
            ts("dve", Cimn, Cimn, -1.0, None, ALU.mult, None, ["Cimn"], ["Cimn"])

            def mlstm_stream(d):
                bwd = d == 1
                nm = "m%d" % d
                qc = [cv.f(512, "p (a b) -> p a b", a=4) for _ in range(2)]; kc = [cv.f(512, "p (a b) -> p a b", a=4) for _ in range(2)]
                va = [cv.f(516, "p (a b) -> p a b", a=4) for _ in range(2)]; gc = [cv.f(16) for _ in range(2)]
                gsm = cv.f(64, "p (a b) -> p a b", a=16); ex = cv.f(20, "p (a b) -> p a b", a=5)
                gdt = cv.f(512, "p (a b) -> p a b", a=4); tmt = cv.f(512, "p (a b) -> p a b", a=4)
                vw = cv.f(516, "p (a b) -> p a b", a=4); PTa = cv.f(512, "p (a b) -> p a b", a=4); ktm = cv.f(512, "p (a b) -> p a b", a=4)
                r1 = cv.f(258, "p (a b) -> p a b", a=2); tmx = cv.f(258, "p (a b) -> p a b", a=2); rra = cv.f(516, "p (a b) -> p a b", a=4)
                dn = cv.f(16, "p (a b) -> p a b", a=4); hdir = cv.f(512, "p (a b) -> p a b", a=4); Ctmp = cv.f(516, "p (a b) -> p a b", a=4)
                b0 = pbs[2 * d]; b1 = pbs[2 * d + 1]
                Caug = Caugd[d]; mst = mstd[d]
                io, fo = (8, 12) if bwd else (0, 4)
                Tri = Lmat if bwd else Umat
                biasM = biasU if bwd else biasL
                K_ = lambda s_: nm + s_
                chain = chains[d]
                for b_ in range(2):
                    mset("pool", va[b_][:, :, 128:129], 1.0, [K_("va%d" % b_)])

                def load(i):
                    seg, c = chain[i]
                    b_ = i % 2
                    tok = c * 128
                    su = unit_of(seg, c)
                    dma(qc[b_], qd[seg][:, :, tok:tok + 128], [("q",) + su], [K_("qc%d" % b_)])
                    dma(kc[b_], kd[seg][:, :, tok:tok + 128], [("k",) + su], [K_("kc%d" % b_)])
                    dma(va[b_][:, :, 0:128], vd[seg][tok:tok + 128, :].rearrange("p (h e) -> p h e", h=4), [("v",) + su], [K_("va%d" % b_)])
                    dma(gc[b_], gd[seg][tok:tok + 128, :], [("g",) + su], [K_("gc%d" % b_)])
                load(0)
                for i, (seg, c) in enumerate(chain):
                    b_ = i % 2
                    if i + 1 < len(chain):
                        load(i + 1)
                    q_, k_, v_, g_ = qc[b_], kc[b_], va[b_], gc[b_]
                    kq, kk_, kv, kg = K_("qc%d" % b_), K_("kc%d" % b_), K_("va%d" % b_), K_("gc%d" % b_)
                    G = [K_("gsm")]; EX = [K_("ex")]
                    e1 = gsm[:, 0, :]; sp = gsm[:, 1, :]; g = gsm[:, 2, :]; gmax = gsm[:, 3, :]; cmax = gsm[:, 4, :]
                    Mx = gsm[:, 5, :]; M = gsm[:, 6, :]
                    act(e1, g_[:, fo:fo + 4], AF.Exp, [kg], G, scale=-1.0)
                    act(sp, e1, AF.Ln, G, G, bias=1.0)
                    mm(b0[:, 0:4], Tri, sp, True, True, [cst] + G, [b0])
                    mm(b0[:, 4:8], ones, sp, True, True, [cst] + G, [b0])
                    yield
                    tt("dve", g, g_[:, io:io + 4], b0[:, 0:4], ALU.add, [kg, b0], G)
                    tt("dve", gdt, ident.unsqueeze(1).to_broadcast([128, 4, 128]), g.unsqueeze(2).to_broadcast([128, 4, 128]), ALU.mult, [cst] + G, [K_("gdt")])
                    mm(b1[:, :], ones, gdt.rearrange("p a b -> p (a b)"), True, True, [cst, K_("gdt")], [b1])
                    yield
                    b1v = b1[:, :].rearrange("p (a b) -> p a b", a=4)
                    red("dve", gmax, b1v, ALU.max, [b1], G)
                    tt("dve", tmt, b1v, biasM.unsqueeze(1).to_broadcast([128, 4, 128]), ALU.add, [b1, cst], [K_("tmt")])
                    red("dve", cmax, tmt, ALU.max, [K_("tmt")], G)
                    tt("dve", Mx, mst[:], gmax, ALU.max, [mst] + G, G)
                    tt("dve", M, mst[:], cmax, ALU.max, [mst] + G, G)
                    tt("dve", ex[:, 0, :], g, Mx, ALU.subtract, G, EX)
                    tt("dve", ex[:, 1, :], Mx, M, ALU.subtract, G, EX)
                    tt("dve", ex[:, 2, :], mst[:], M, ALU.subtract, [mst] + G, EX)
                    tt("dve", ex[:, 3, :], mst[:], Mx, ALU.subtract, [mst] + G, EX)
                    tt("dve", ex[:, 4, :], b0[:, 0:4], M, ALU.subtract, [b0] + G, EX)
                    act(ex, ex, AF.Exp, EX, EX)
                    tt("dve", mst[:], Mx, b0[:, 4:8], ALU.subtract, G + [b0], [mst])
                    yield
                    tt("dve", vw, v_, ex[:, 0, :].unsqueeze(2).to_broadcast([128, 4, 129]), ALU.mult, [kv] + EX, [K_("vw")])
                    for h in range(H):
                        mm(b0[:, h * 128:(h + 1) * 128], k_[:, h, :], q_[:, h, :], True, True, [kk_, kq], [b0])
                    for h in range(H):
                        tr(b1[:, h * 128:(h + 1) * 128], k_[:, h, :], [kk_], [b1])
                    yield
                    maskPT = (Lmat if bwd else Umat).unsqueeze(1).to_broadcast([128, 4, 128])
                    tt("dve", PTa, b0[:, :].rearrange("p (a b) -> p a b", a=4), maskPT, ALU.mult, [b0, cst], [K_("PTa")])
                    act(ktm, b1[:, :].rearrange("p (a b) -> p a b", a=4), AF.Copy, [b1], [K_("ktm")])
                    yield
                    for pr in range(2):
                        for hh in range(2):
                            h = 2 * pr + hh
                            mm(b0[:, hh * 129:(hh + 1) * 129], PTa[:, h, :], vw[:, h, :], True, True, [K_("PTa"), K_("vw")], [b0])
                            mm(b1[:, hh * 129:(hh + 1) * 129], q_[:, h, :], Caug[:, h, :], True, True, [kq, Caug], [b1])
                        yield
                        hs = slice(2 * pr, 2 * pr + 2)
                        b0p = b0[:, 0:258].rearrange("p (a b) -> p a b", a=2); b1p = b1[:, 0:258].rearrange("p (a b) -> p a b", a=2)
                        tt("dve", r1, b1p, ex[:, 2, hs].unsqueeze(2).to_broadcast([128, 2, 129]), ALU.mult, [b1] + EX, [K_("r1")])
                        tt("dve", tmx, b0p, ex[:, 1, hs].unsqueeze(2).to_broadcast([128, 2, 129]), ALU.mult, [b0] + EX, [K_("tmx")])
                        tt("dve", rra[:, hs, :], tmx, r1, ALU.add, [K_("tmx"), K_("r1")], [K_("rra")])
                        yield
                    den = rra[:, :, 128]
                    ts("dve", dn[:, 0, :], den, -1.0, None, ALU.mult, None, [K_("rra")], [K_("dn")])
                    tt("dve", dn[:, 0, :], dn[:, 0, :], den, ALU.max, [K_("dn"), K_("rra")], [K_("dn")])
                    tt("dve", dn[:, 0, :], dn[:, 0, :], ex[:, 4, :], ALU.max, [K_("dn")] + EX, [K_("dn")])
                    recip(dn[:, 1, :], dn[:, 0, :], [K_("dn")], [K_("dn")])
                    tt("dve", hdir, rra[:, :, 0:128], dn[:, 1, :].unsqueeze(2).to_broadcast([128, 4, 128]), ALU.mult, [K_("rra"), K_("dn")], [K_("hdir")])
                    dma(hd[(d, seg)][c * 128:(c + 1) * 128, :].rearrange("p (h e) -> p h e", h=4), hdir, [K_("hdir")], [("h", d, seg, c)])
                    yield
                    for h in range(H):
                        bb = b0 if h < 2 else b1
                        hh = h % 2
                        mm(bb[:, hh * 129:(hh + 1) * 129], ktm[:, h, :], vw[:, h, :], True, True, [K_("ktm"), K_("vw")], [bb])
                    tt("dve", Ctmp, Caug[:], ex[:, 3, :].unsqueeze(2).to_broadcast([128, 4, 129]), ALU.mult, [Caug] + EX, [K_("Ctmp")])
                    yield
                    tt("dve", Caug[:, 0:2, :], Ctmp[:, 0:2, :], b0[:, 0:258].rearrange("p (a b) -> p a b", a=2), ALU.add, [K_("Ctmp"), b0], [Caug])
                    tt("dve", Caug[:, 2:4, :], Ctmp[:, 2:4, :], b1[:, 0:258].rearrange("p (a b) -> p a b", a=2), ALU.add, [K_("Ctmp"), b1], [Caug])
                    yield

            def s5_stream(d):
                bwd = d == 1
                nm = "s%d" % d
                K_ = lambda s_: nm + s_
                uc = [cv.f(384, "p (a b) -> p a b", a=3) for _ in range(2)]
                B2 = cv.f(1024, "p (r a b) -> p r a b", r=2, a=4); S2 = cv.f(1024, "p (r a b) -> p r a b", r=2, a=4)
                m1 = cv.f(1024, "p (r a b) -> p r a b", r=2, a=4); m2 = cv.f(1024, "p (r a b) -> p r a b", r=2, a=4)
                ycb = cv.f(384, "p (a b) -> p a b", a=3)
                bA = pbs[4 + 2 * d]; bB = pbs[5 + 2 * d]
                tab = tabd[d]; rho = rhod[d]; car = card[d]; TKd = TABK[d]
                chain = chains[d]
                last = 0 if bwd else 127

                def R3(ap):
                    return ap[:, :, ::-1] if bwd else ap

                def R4(ap):
                    return ap[:, :, :, ::-1] if bwd else ap

                def load(i):
                    seg, c = chain[i]
                    b_ = i % 2
                    tok = c * 128
                    dma(uc[b_], ud[seg][:, :, tok:tok + 128], [("u",) + unit_of(seg, c)], [K_("uc%d" % b_)])
                load(0)
                for i, (seg, c) in enumerate(chain):
                    b_ = i % 2
                    if i + 1 < len(chain):
                        load(i + 1)
                    u_ = uc[b_]; ku = K_("uc%d" % b_)
                    for j in range(3):
                        ks = slice(4 * j, 4 * j + 4)
                        for kk in range(4):
                            k = 4 * j + kk
                            mm(bA[:, kk * 128:(kk + 1) * 128], Bre[:, k, :], u_[:, j, :], True, True, ["Bre", ku], [bA])
                            mm(bB[:, kk * 128:(kk + 1) * 128], Bim[:, k, :], u_[:, j, :], True, True, ["Bim", ku], [bB])
                        yield
                        bAv = R3(bA[:, :].rearrange("p (a b) -> p a b", a=4)); bBv = R3(bB[:, :].rearrange("p (a b) -> p a b", a=4))
                        act(B2[:, 0], bAv, AF.Copy, [bA], [K_("B2")])
                        act(B2[:, 1], bBv, AF.Copy, [bB], [K_("B2")])
                        act(S2[:, 0], bBv, AF.Copy, [bB], [K_("S2")], scale=-1.0)
                        act(S2[:, 1], bAv, AF.Copy, [bA], [K_("S2")])
                        yield
                        bc = lambda t_: t_.unsqueeze(1).to_broadcast([128, 2, 4, 128])
                        tt("dve", m1, B2, bc(tab[:, 0, ks, :]), ALU.mult, [K_("B2"), TKd], [K_("m1")])
                        tt("dve", m2, S2, bc(tab[:, 1, ks, :]), ALU.mult, [K_("S2"), TKd], [K_("m2")])
                        tt("dve", m1, m1, m2, ALU.add, [K_("m1"), K_("m2")], [K_("m1")])
                        tt("dve", m1[:, :, :, 0], m1[:, :, :, 0], car[:, :, ks], ALU.add, [K_("m1"), car], [K_("m1")])
                        yield
                        rf = tab[:, 4, ks, :].rearrange("p a b -> p (a b)")
                        for r_ in range(2):
                            P.op("dve", lambda e, r_=r_, rf=rf: e.tensor_tensor_scan(B2[:, r_].rearrange("p a b -> p (a b)"), rf,
                                                                                     m1[:, r_].rearrange("p a b -> p (a b)"), 0.0, ALU.mult, ALU.add),
                                 reads=[TKd, K_("m1")], writes=[K_("B2")])
                        yield
                        act(S2[:, 0], B2[:, 1], AF.Copy, [K_("B2")], [K_("S2")], scale=-1.0)
                        act(S2[:, 1], B2[:, 0], AF.Copy, [K_("B2")], [K_("S2")])
                        tt("dve", m1, B2, bc(tab[:, 2, ks, :]), ALU.mult, [K_("B2"), TKd], [K_("m1")])
                        yield
                        tt("dve", m2, S2, bc(tab[:, 3, ks, :]), ALU.mult, [K_("S2"), TKd], [K_("m2")])
                        tt("dve", R4(B2), m1, m2, ALU.add, [K_("m1"), K_("m2")], [K_("B2")])
                        tt("dve", car[:, :, ks], B2[:, :, :, last], rho[:, ks].unsqueeze(1).to_broadcast([128, 2, 4]), ALU.mult, [K_("B2"), rho], [car])
                        yield
                        for kk in range(4):
                            k = 4 * j + kk
                            mm(bA[:, 0:128], Cre[:, k, :], B2[:, 0, kk, :], kk == 0, False, ["Cre", K_("B2")], [bA])
                            mm(bA[:, 0:128], Cimn[:, k, :], B2[:, 1, kk, :], False, kk == 3, ["Cimn", K_("B2")], [bA])
                        act(ycb[:, j, :], bA[:, 0:128], AF.Copy, [bA], [K_("ycb")])
                        yield
                    dma(yd_[(d, seg)][:, :, c * 128:(c + 1) * 128], ycb, [K_("ycb")], [("y", d, seg, c)])

            gens = [mlstm_stream(0), mlstm_stream(1), s5_stream(0), s5_stream(1)]
            alive = list(gens)
            while alive:
                for g_ in list(alive):
                    try:
                        next(g_)
                    except StopIteration:
                        alive.remove(g_)

        def phase3b(l, dst_of, is_last, units):
            P.barrier()
            cv = Carver()
            N3 = 512
            sqt_ = cv.f(KT * N3, "p (a b) -> p a b", a=KT); rs_ = cv.f(N3)
            slabs3 = [cv.b(SLABF) for _ in range(6)]
            a_t_ = cv.b(FT * N3, "p (a b) -> p a b", a=FT)
            sets = []
            for tag in ("A", "B"):
                sets.append({"tag": tag, "xt": cv.f(KT * N3, "p (a b) -> p a b", a=KT), "hT": cv.b(KT * N3, "p (a b) -> p a b", a=KT), "tmpA": cv.f(N3)})
            srr = [0]

            def lslab(w_ap, nk):
                i = srr[0] % 6
                srr[0] += 1
                key = "s3bslab%d" % i
                v = slabs3[i][:, 0:nk * 128].rearrange("p (k c) -> p k c", k=nk)
                dma(v, w_ap, [], [key])
                return v, key

            def unit_gen(seg, u, NT, S):
                tg = S["tag"]
                K_ = lambda n: "b" + tg + n
                xt, hT, tmpA = S["xt"][:, :, 0:NT], S["hT"][:, :, 0:NT], S["tmpA"][:, 0:NT]
                sqt = sqt_[:, :, 0:NT]; rs = rs_[:, 0:NT]; a_t = a_t_[:, :, 0:NT]
                sg = 1 if seg == "ctx" else 0
                t0 = u * N3

                def norm3(scale_fn, shift_fn, extra_r, dst, dkey):
                    act(sqt, xt, AF.Square, [K_("xt")], ["sqt3b"])
                    ps = PS()
                    for k in range(KT):
                        mm(ps[:, 0:NT], ones, sqt[:, k, :], k == 0, k == KT - 1, [cst, "sqt3b"], [ps])
                    act(rs, ps[:, 0:NT], AF.Sqrt, [ps], ["rs3b"], bias=EPS, scale=1.0 / D)
                    recip(rs, rs, ["rs3b"], ["rs3b"])
                    tt("dve", sqt, xt, rs.unsqueeze(1).to_broadcast([128, KT, NT]), ALU.mult, [K_("xt"), "rs3b"], ["sqt3b"])
                    for k in range(KT):
                        sh = shift_fn(k)
                        if sh is None:
                            act(dst[:, k, :], sqt[:, k, :], AF.Identity, ["sqt3b"] + extra_r, [dkey], scale=scale_fn(k))
                        else:
                            act(dst[:, k, :], sqt[:, k, :], AF.Identity, ["sqt3b"] + extra_r, [dkey], bias=sh, scale=scale_fn(k))
                dma(xt, x1d[seg][:, t0:t0 + NT].rearrange("(k p) t -> p k t", p=128), [], [K_("xt")])
                yield "y"
                norm3(lambda k: lcs[:, sg, 1, k:k + 1], lambda k: modv[:, 24 + k, sg:sg + 1], [lcs, modv], hT, K_("hT"))
                yield "S"
                for i in range(FT):
                    v1, s1 = lslab(wb["wfg"][l][:, i], KT)
                    pg = PS()
                    for k in range(KT):
                        mm(pg[:, 0:NT], v1[:, k, :], hT[:, k, :], k == 0, k == KT - 1, [s1, K_("hT")], [pg])
                    act(tmpA, pg[:, 0:NT], AF.Silu, [pg], [K_("tmpA")])
                    v2, s2 = lslab(wb["wfu"][l][:, i], KT)
                    pu = PS()
                    for k in range(KT):
                        mm(pu[:, 0:NT], v2[:, k, :], hT[:, k, :], k == 0, k == KT - 1, [s2, K_("hT")], [pu])
                    tt("dve", a_t[:, i, :], tmpA, pu[:, 0:NT], ALU.mult, [K_("tmpA"), pu], [("ba", i)])
                    yield "y"
                for jd in range(KT):
                    ps = PS()
                    for hf in range(2):
                        v3, s3 = lslab(wb["wfd"][l][:, jd, hf * 11:(hf + 1) * 11], 11)
                        for k in range(11):
                            i = hf * 11 + k
                            mm(ps[:, 0:NT], v3[:, k, :], a_t[:, i, :], i == 0, i == FT - 1, [s3, ("ba", i)], [ps])
                    stt("dve", xt[:, jd, :], ps[:, 0:NT], modv[:, 40 + jd, sg:sg + 1], xt[:, jd, :], ALU.mult, ALU.add, [ps, modv, K_("xt")], [K_("xt")])
                    yield "y"
                if is_last and seg == "lat":
                    norm3(lambda k: gfin[:, k:k + 1], lambda k: None, [gfin], sqt, "sqt3b")
                    dma(outT[:, t0:t0 + NT].rearrange("(k p) t -> p k t", p=128), sqt, ["sqt3b"], [("out", u)])
                else:
                    dma(dst_of[seg][:, t0:t0 + NT].rearrange("(k p) t -> p k t", p=128), xt, [K_("xt")], [("x2", seg, u)])
                yield "S"

            gens = [unit_gen(seg, u, nt_, sets[i % 2]) for i, (seg, u, nt_) in enumerate(units)]

            def to_stage(g):
                while next(g) != "S":
                    pass

            def weave(ga, gb):
                da = db = False
                while not (da and db):
                    if not da:
                        da = next(ga) == "S"
                    if not db:
                        db = next(gb) == "S"
            to_stage(gens[0])
            for i in range(len(gens)):
                if i + 1 < len(gens):
                    weave(gens[i], gens[i + 1])
                else:
                    to_stage(gens[i])

        def phase3(l, src_of, dst_of, is_last):
            P.barrier()
            cv = Carver()
            N3 = 512; NC3 = 4
            sqt_ = cv.f(KT * N3, "p (a b) -> p a b", a=KT); rs_ = cv.f(N3)
            slabs3 = [cv.b(SLABF) for _ in range(5)]
            gmh_row = cv.f(512); gsgu_row = cv.f(512); wsT = cv.f(512, "p (a b) -> p a b", a=4); bs_row = cv.f(512)
            wglu = cv.f(3 * 384, "p (a b) -> p a b", a=3)
            CK = ["p3c"]
            dma(gmh_row, din["gmh"][l].partition_broadcast(128), [], CK); dma(gsgu_row, din["gsgu"][l].partition_broadcast(128), [], CK)
            dma(bs_row, din["bs"][l].partition_broadcast(128), [], CK); dma(wsT, din["wsT"][l], [], CK); dma(wglu, din["wglu"][l], [], CK)
            w_l = wb["win"][l]

            SH = {}
            SH["merged"] = cv.b(KT * N3, "p (a b) -> p a b", a=KT); SH["tmpB"] = cv.f(N3)
            SH["sigo"] = cv.b(NC3 * 512, "p (c e) -> p c e", c=NC3); SH["vg"] = cv.f(NC3 * 512, "p (c e) -> p c e", c=NC3)
            SH["gsu"] = cv.b(4 * N3, "p (a b) -> p a b", a=4); SH["cb"] = cv.b(4 * N3, "p (a b) -> p a b", a=4)
            for n in ("cc", "yd"):
                SH[n] = cv.f(4 * N3, "p (a b) -> p a b", a=4)
            for n in ("ydb", "yaT", "ybT"):
                SH[n] = cv.b(4 * N3, "p (a b) -> p a b", a=4)
            SH["ysum"] = cv.f(3 * N3, "p (a b) -> p a b", a=3); SH["uT"] = cv.f(3 * N3, "p (a b) -> p a b", a=3)
            SH["hdir"] = cv.f(1024, "p (c e) -> p c e", c=2); SH["hbt"] = cv.f(1024, "p (c e) -> p c e", c=2); SH["hn"] = cv.f(1024, "p (c e) -> p c e", c=2)
            SH["ycb"] = cv.f(3 * N3, "p (a b) -> p a b", a=3); SH["sm4"] = cv.f(16)

            def alloc_set(tag):
                S = dict(SH)
                S["tag"] = tag
                S["xt"] = cv.f(KT * N3, "p (a b) -> p a b", a=KT); S["hT"] = cv.b(KT * N3, "p (a b) -> p a b", a=KT)
                S["tmpA"] = cv.f(N3); S["ys5T"] = cv.b(3 * N3, "p (a b) -> p a b", a=3)
                return S
            sets = [alloc_set("A"), alloc_set("B")]
            srr = [0]

            def lslab(w_ap, nk):
                i = srr[0] % 5
                srr[0] += 1
                key = "s3slab%d" % i
                v = slabs3[i][:, 0:nk * 128].rearrange("p (k c) -> p k c", k=nk)
                dma(v, w_ap, [], [key])
                return v, key

            def unit_gen(seg, u, NT, S):
                NCH_ = NT // 128
                tg = S["tag"]
                K_ = lambda n: (tg + n) if n in ("xt", "hT", "tmpA", "ys5T") else ("p3" + n)
                xt, hT, merged, tmpA, tmpB = S["xt"][:, :, 0:NT], S["hT"][:, :, 0:NT], S["merged"][:, :, 0:NT], S["tmpA"][:, 0:NT], S["tmpB"][:, 0:NT]
                sigo, vg = S["sigo"][:, 0:NCH_, :], S["vg"][:, 0:NCH_, :]
                gsu, cb, cc, yd, ydb = (S[n_][:, :, 0:NT] for n_ in ("gsu", "cb", "cc", "yd", "ydb"))
                yaT, ybT, ysum, ys5T, uT = (S[n_][:, :, 0:NT] for n_ in ("yaT", "ybT", "ysum", "ys5T", "uT"))
                hdir2, hbt2, hn2, ycb, sm16 = S["hdir"], S["hbt"], S["hn"], S["ycb"][:, :, 0:NT], S["sm4"]
                sqt = sqt_[:, :, 0:NT]; rs = rs_[:, 0:NT]
                sg = 1 if seg == "ctx" else 0
                t0 = u * N3
                rowlen = NT if seg == "ctx" else 64

                def norm3(scale_fn, shift_fn, extra_r, dst, dkey):
                    tt("pool", sqt, xt, xt, ALU.mult, [K_("xt")], ["sqt3"])
                    ps = PS()
                    for k in range(KT):
                        mm(ps[:, 0:NT], ones, sqt[:, k, :], k == 0, k == KT - 1, [cst, "sqt3"], [ps])
                    act(rs, ps[:, 0:NT], AF.Sqrt, [ps], ["rs3"], bias=EPS, scale=1.0 / D)
                    recip(rs, rs, ["rs3"], ["rs3"])
                    tt("dve", sqt, xt, rs.unsqueeze(1).to_broadcast([128, KT, NT]), ALU.mult, [K_("xt"), "rs3"], ["sqt3"])
                    for k in range(KT):
                        sh = shift_fn(k)
                        if sh is None:
                            act(dst[:, k, :], sqt[:, k, :], AF.Identity, ["sqt3"] + extra_r, [dkey], scale=scale_fn(k))
                        else:
                            act(dst[:, k, :], sqt[:, k, :], AF.Identity, ["sqt3"] + extra_r, [dkey], bias=sh, scale=scale_fn(k))

                def pfm(c0, ntiles, evac):
                    for j in range(ntiles):
                        v, s_ = lslab(w_l[:, c0 + j], KT)
                        ps = PS()
                        for k in range(KT):
                            mm(ps[:, 0:NT], v[:, k, :], hT[:, k, :], k == 0, k == KT - 1, [s_, K_("hT")], [ps])
                        evac(j, ps)
                        yield "y"

                def ptm(c0, ntiles, evac):
                    for j in range(ntiles):
                        v, s_ = lslab(w_l[:, c0 + j], KT)
                        for c in range(NCH_):
                            ps = PS()
                            for k in range(KT):
                                mm(ps[:, 0:128], hT[:, k, c * 128:(c + 1) * 128], v[:, k, :], k == 0, k == KT - 1, [s_, K_("hT")], [ps])
                            evac(j, c, ps)
                        yield "y"

                dma(xt, src_of[seg][:, t0:t0 + NT].rearrange("(k p) t -> p k t", p=128), [], [K_("xt")])
                dma(uT, ud[seg][:, :, t0:t0 + NT], [], [K_("uT")])
                dma(ysum, yd_[(0, seg)][:, :, t0:t0 + NT], [], [K_("ysum")])
                dma(ycb, yd_[(1, seg)][:, :, t0:t0 + NT], [], [K_("ycb")])
                yield "y"
                norm3(lambda k: lcs[:, sg, 0, k:k + 1], lambda k: modv[:, k, sg:sg + 1], [lcs, modv], hT, K_("hT"))
                yield "y"
                tt("dve", ysum, ysum, ycb, ALU.add, [K_("ysum"), K_("ycb")], [K_("ysum")])
                for j in range(3):
                    stt("dve", ysum[:, j, :], uT[:, j, :], s5d[:, j:j + 1], ysum[:, j, :], ALU.mult, ALU.add, [K_("uT"), s5d, K_("ysum")], [K_("ysum")])
                act(ysum, ysum, AF.Gelu_apprx_tanh, [K_("ysum")], [K_("ysum")])
                yield "S"
                yield from ptm(TO, 4, lambda j, c, ps: act(sigo[:, c, j * 128:(j + 1) * 128], ps[:, 0:128], AF.Sigmoid, [ps], [K_("sigo")]))
                yield from ptm(TSV, 4, lambda j, c, ps: act(vg[:, c, j * 128:(j + 1) * 128], ps[:, 0:128], AF.Gelu_apprx_tanh, [ps], [K_("vg")]))
                yield from pfm(TSU, 4, lambda j, ps: act(gsu[:, j, :], ps[:, 0:NT], AF.Gelu_apprx_tanh, [ps], [K_("gsu")]))
                yield from pfm(TCB, 4, lambda j, ps: act(cb[:, j, :], ps[:, 0:NT], AF.Copy, [ps], [K_("cb")]))
                yield from pfm(TCC, 4, lambda j, ps: act(cc[:, j, :], ps[:, 0:NT], AF.Copy, [ps], [K_("cc")]))
                yield from pfm(TCX, 4, lambda j, ps: tt("dve", cc[:, j, :], cc[:, j, :], ps[:, 0:NT], ALU.mult, [K_("cc"), ps], [K_("cc")]))
                for j in range(3):
                    ps = PS()
                    for k in range(3):
                        mm(ps[:, 0:NT], wglu[:, k, j * 128:(j + 1) * 128], ysum[:, k, :], k == 0, k == 2, CK + [K_("ysum")], [ps])
                    act(tmpA, ps[:, 0:NT], AF.Sigmoid, [ps, bglu], [K_("tmpA")], bias=bglu[:, j:j + 1])
                    tt("dve", ys5T[:, j, :], ysum[:, j, :], tmpA, ALU.mult, [K_("ysum"), K_("tmpA")], [K_("ys5T")])
                    yield "y"
                yield "S"
                conv3(yd, cc, wsconv, 0, rowlen, [K_("cc"), wsconv], [K_("yd")], NT)
                tt("pool", ydb, yd, cb, ALU.mult, [K_("yd"), K_("cb")], [K_("ydb")])
                yield "y"
                for c0 in range(0, NCH_, 2):
                    nb = min(2, NCH_ - c0)
                    csl = slice(c0, c0 + nb)
                    tsl = slice(c0 * 128, (c0 + nb) * 128)
                    tok0 = t0 + c0 * 128
                    hF = hdir2[:, 0:nb, :]; hB = hbt2[:, 0:nb, :]; hN = hn2[:, 0:nb, :]
                    dma(hF, hd[(0, seg)][tok0:tok0 + nb * 128, :].rearrange("(c p) e -> p c e", p=128), [], [K_("hdir")])
                    dma(hB, hd[(1, seg)][tok0:tok0 + nb * 128, :].rearrange("(c p) e -> p c e", p=128), [], [K_("hbt")])
                    tt("dve", hF, hF, hB, ALU.add, [K_("hdir"), K_("hbt")], [K_("hdir")])
                    act(hN, hF, AF.Square, [K_("hdir")], [K_("hn")])
                    ssq = sm16[:, 0:nb * 4]
                    red("dve", ssq, hN.rearrange("p c (h e) -> p (c h) e", h=4), ALU.add, [K_("hn")], [K_("sm4")])
                    act(ssq, ssq, AF.Sqrt, [K_("sm4")], [K_("sm4")], bias=EPS, scale=1.0 / 128)
                    recip(ssq, ssq, [K_("sm4")], [K_("sm4")])
                    yield "y"
                    tt("dve", hN.rearrange("p c (h e) -> p (c h) e", h=4), hF.rearrange("p c (h e) -> p (c h) e", h=4),
                       ssq.unsqueeze(2).to_broadcast([128, nb * 4, 128]), ALU.mult, [K_("hdir"), K_("sm4")], [K_("hn")])
                    tt("dve", hN, hN, gmh_row.unsqueeze(1).to_broadcast([128, nb, 512]), ALU.mult, [K_("hn")] + CK, [K_("hn")])
                    tt("dve", hN, hN, sigo[:, csl, :], ALU.mult, [K_("hn"), K_("sigo")], [K_("hn")])
                    vgs = vg[:, csl, :]
                    mu = sm16[:, 8:8 + nb]; vr = sm16[:, 12:12 + nb]
                    red("dve", mu, vgs, ALU.add, [K_("vg")], [K_("sm4")])
                    ts("dve", mu, mu, 1.0 / 512, None, ALU.mult, None, [K_("sm4")], [K_("sm4")])
                    tt("dve", vgs, vgs, mu.unsqueeze(2).to_broadcast([128, nb, 512]), ALU.subtract, [K_("vg"), K_("sm4")], [K_("vg")])
                    act(hB, vgs, AF.Square, [K_("vg")], [K_("hbt")])
                    yield "y"
                    for ci in range(nb):
                        c = c0 + ci
                        pT_ = PS()
                        for h in range(H):
                            tr(pT_[:, h * 128:(h + 1) * 128], hN[:, ci, h * 128:(h + 1) * 128], [K_("hn")], [pT_])
                        act(yaT[:, :, c * 128:(c + 1) * 128], pT_[:, :].rearrange("p (a b) -> p a b", a=4), AF.Copy, [pT_], [K_("yaT")])
                    red("dve", vr, hB, ALU.add, [K_("hbt")], [K_("sm4")])
                    act(vr, vr, AF.Sqrt, [K_("sm4")], [K_("sm4")], bias=EPS, scale=1.0 / 512)
                    recip(vr, vr, [K_("sm4")], [K_("sm4")])
                    tt("dve", vgs, vgs, vr.unsqueeze(2).to_broadcast([128, nb, 512]), ALU.mult, [K_("vg"), K_("sm4")], [K_("vg")])
                    tt("dve", vgs, vgs, gsgu_row.unsqueeze(1).to_broadcast([128, nb, 512]), ALU.mult, [K_("vg")] + CK, [K_("vg")])
                    yield "y"
                    for ci in range(nb):
                        c = c0 + ci
                        cs = slice(c * 128, (c + 1) * 128)
                        pM = PS()
                        for gi in range(4):
                            mm(pM[:, gi * 128:(gi + 1) * 128], vg[:, c, gi * 128:(gi + 1) * 128], wsT[:, gi, :], True, True, [K_("vg")] + CK, [pM])
                        hq = hF[:, ci, :].rearrange("p (a b) -> p a b", a=4)
                        tt("dve", hq, pM[:, :].rearrange("p (a b) -> p a b", a=4), bs_row.rearrange("p (a b) -> p a b", a=4), ALU.add, [pM] + CK, [K_("hdir")])
                        tt("dve", ybT[:, :, cs], hq, gsu[:, :, cs], ALU.mult, [K_("hdir"), K_("gsu")], [K_("ybT")])
                    yield "y"
                yield "S"
                branches = [(yaT, K_("yaT"), 4, wb["wupm"][l]), (ybT, K_("ybT"), 4, wb["wups"][l]), (ys5T, K_("ys5T"), 3, wb["wup5"][l]), (ydb, K_("ydb"), 4, wb["wupc"][l])]
                for jd in range(KT):
                    for b, (ybr, ykey, nkb, wup) in enumerate(branches):
                        vgt, sgt = lslab(w_l[:, TGATE + b * 8 + jd], KT)
                        pg = PS()
                        for k in range(KT):
                            mm(pg[:, 0:NT], vgt[:, k, :], hT[:, k, :], k == 0, k == KT - 1, [sgt, K_("hT")], [pg])
                        act(tmpA, pg[:, 0:NT], AF.Sigmoid, [pg], [K_("tmpA")])
                        vu, su_ = lslab(wup[:, jd], nkb)
                        pu = PS()
                        for k in range(nkb):
                            mm(pu[:, 0:NT], vu[:, k, :], ybr[:, k, :], k == 0, k == nkb - 1, [su_, ykey], [pu])
                        if b == 0:
                            tt("dve", tmpB, tmpA, pu[:, 0:NT], ALU.mult, [K_("tmpA"), pu], [K_("tmpB")])
                        else:
                            tt("dve", tmpA, tmpA, pu[:, 0:NT], ALU.mult, [K_("tmpA"), pu], [K_("tmpA")])
                            if b < 3:
                                tt("dve", tmpB, tmpB, tmpA, ALU.add, [K_("tmpA"), K_("tmpB")], [K_("tmpB")])
                            else:
                                tt("dve", merged[:, jd, :], tmpB, tmpA, ALU.add, [K_("tmpA"), K_("tmpB")], [K_("merged")])
                        yield "y"
                yield "S"
                for j in range(KT):
                    v, s_ = lslab(wb["wout"][l][:, j], KT)
                    ps = PS()
                    for k in range(KT):
                        mm(ps[:, 0:NT], v[:, k, :], merged[:, k, :], k == 0, k == KT - 1, [s_, K_("merged")], [ps])
                    stt("dve", xt[:, j, :], ps[:, 0:NT], modv[:, 16 + j, sg:sg + 1], xt[:, j, :], ALU.mult, ALU.add, [ps, modv, K_("xt")], [K_("xt")])
                    yield "y"
                dma(x1d[seg][:, t0:t0 + NT].rearrange("(k p) t -> p k t", p=128), xt, [K_("xt")], [("x1", seg, u)])
                yield "S"

            units = ([] if is_last else [("ctx", 0, CTX)]) + [("lat", u, N3) for u in range(T // N3)]
            gens = [unit_gen(seg, u, nt_, sets[i % 2]) for i, (seg, u, nt_) in enumerate(units)]

            def to_stage(g):
                while next(g) != "S":
                    pass

            def weave(ga, gb):
                da = db = False
                while not (da and db):
                    if not da:
                        da = next(ga) == "S"
                    if not db:
                        db = next(gb) == "S"
            for _ in range(2):
                to_stage(gens[0])
            for i in range(len(gens)):
                A = gens[i]
                Bn = gens[i + 1] if i + 1 < len(gens) else None
                for _ in range(2):
                    if Bn is None:
                        to_stage(A)
                    else:
                        weave(A, Bn)
                to_stage(A)
            phase3b(l, dst_of, is_last, units)

        NU = T // 512
        for l in range(n_layers):
            last = l == n_layers - 1
            if l == 0:
                cast_weights(0)
            layer_prep(l, first=(l == 0))
            if l + 1 < n_layers:
                cast_weights(l + 1)
            src_of = {"lat": din["xT"] if l == 0 else xs, "ctx": din["cT"] if l == 0 else csd}
            dst_of = {"lat": xs, "ctx": csd}
            phase1(l, src_of)
            phase2(l)
            phase3(l, src_of, dst_of, last)
        P.barrier()
        P.final_wait("sp", [("out", u) for u in range(NU)] + ["dbg_" + n for n in dbg_out])
        P.emit()
    return nc, dbg_out, P


def _kt(w):
    K_, C = w.shape
    return np.ascontiguousarray(w.reshape(K_ // 128, 128, C).transpose(1, 0, 2))


def _tile(w):
    K_, C = w.shape
    return np.ascontiguousarray(w.reshape(K_ // 128, 128, C // 128, 128).transpose(1, 2, 0, 3))


def _pcol(v):
    return np.ascontiguousarray(v.reshape(-1, 128).T)


def make_consts():
    c = np.zeros((128, 7, 128), np.float32)
    r = np.arange(128)[:, None]; cidx = np.arange(128)[None, :]
    c[:, 0] = (r == cidx); c[:, 1] = 1.0
    c[:, 2] = (r <= cidx); c[:, 3] = (r >= cidx)
    c[:, 4] = np.where(r <= cidx, 0.0, -BIG); c[:, 5] = np.where(r >= cidx, 0.0, -BIG)
    c[:, 6] = (cidx + 1)
    return c


def shared_inputs(inp):
    f = lambda a: np.asarray(a, np.float32)
    o = {"cst": make_consts()}
    o["wmod"] = np.stack([_kt(f(inp["w_mod"][l])) for l in range(L)])
    o["bmod"] = np.stack([_pcol(f(inp["b_mod"][l])) for l in range(L)])
    o["gnm"] = np.stack([_pcol(f(inp["g_norm_mix"][l])) for l in range(L)])
    o["gnf"] = np.stack([_pcol(f(inp["g_norm_ffn"][l])) for l in range(L)])
    o["gfin"] = _pcol(f(inp["g_final"]))
    def win_tiles(w):
        cols = []
        for c0, n in ((COL_Q, 4), (COL_K, 4), (COL_V, 4)):
            cols += [w[:, c0 + i * 128:c0 + (i + 1) * 128] for i in range(n)]
        gt = np.zeros((D, 128), np.float32); gt[:, :16] = w[:, COL_G:COL_G + 16]; cols.append(gt)
        for c0, n in ((COL_U, 3), (COL_O, 4), (COL_SU, 4), (COL_SV, 4), (COL_CB, 4), (COL_CC, 4), (COL_CX, 4), (COL_GATE, 32)):
            cols += [w[:, c0 + i * 128:c0 + (i + 1) * 128] for i in range(n)]
        t = np.stack(cols, 0)
        return np.ascontiguousarray(t.reshape(NWT, KT, 128, 128).transpose(2, 0, 1, 3))
    o["win"] = np.stack([win_tiles(f(inp["w_in"][l])) for l in range(L)])
    o["bg"] = f(inp["b_gates"]).reshape(L, 1, 16)
    wc = f(inp["w_conv_qk"])
    o["wconv"] = np.ascontiguousarray(wc.reshape(L, 2, 3, 4, 128).transpose(0, 4, 1, 2, 3).reshape(L, 128, 24))
    o["gmh"] = f(inp["g_mh"]).reshape(L, 1, 512)
    o["gsgu"] = f(inp["g_sgu"]).reshape(L, 1, 512)
    o["wsT"] = np.ascontiguousarray(f(inp["w_sgu"]).transpose(0, 3, 1, 2))
    o["bs"] = f(inp["b_sgu"]).reshape(L, 1, 512)

    def ptile(a):
        return np.ascontiguousarray(a.reshape(L, 2, 12, 128).transpose(0, 1, 3, 2))
    o["are"] = ptile(f(inp["s5_a_re"])); o["aim"] = ptile(f(inp["s5_a_im"]))
    o["ldt"] = ptile(np.repeat(f(inp["s5_log_dt"])[..., None], 64, axis=-1))

    def bpad(B):
        out = np.zeros((L, 128, 12, 128), np.float32)
        for g in range(24):
            k, gg = g // 2, g % 2
            r0 = (g % 8) * 16
            out[:, r0:r0 + 16, k, gg * 64:(gg + 1) * 64] = B[:, g].transpose(0, 2, 1)
        return out

    def cpad(C):
        out = np.zeros((L, 128, 12, 128), np.float32)
        for g in range(24):
            k, gg = g // 2, g % 2
            c0 = (g % 8) * 16
            out[:, gg * 64:(gg + 1) * 64, k, c0:c0 + 16] = C[:, g].transpose(0, 2, 1)
        return out
    o["Bre"] = bpad(f(inp["s5_b_re"])); o["Bim"] = bpad(f(inp["s5_b_im"]))
    o["Cre"] = cpad(f(inp["s5_c_re"])); o["Cim"] = cpad(f(inp["s5_c_im"]))
    o["s5d"] = np.stack([_pcol(f(inp["s5_d"][l])) for l in range(L)])
    o["bglu"] = np.stack([_pcol(f(inp["b_glu"][l])) for l in range(L)])
    o["wglu"] = np.stack([_kt(f(inp["w_glu"][l])) for l in range(L)])
    ws = f(inp["w_sconv"])
    o["wsconv"] = np.ascontiguousarray(ws.reshape(L, 3, 4, 128).transpose(0, 3, 1, 2).reshape(L, 128, 12))
    for nm, key in (("wupm", "w_up_mlstm"), ("wups", "w_up_sgu"), ("wup5", "w_up_s5"), ("wupc", "w_up_sconv"), ("wout", "w_out"),
                    ("wfg", "w_ffn_gate"), ("wfu", "w_ffn_up"), ("wfd", "w_ffn_down")):
        o[nm] = np.stack([_tile(f(inp[key][l])) for l in range(L)])
    return o


def core_inputs(inp, b, T):
    f = lambda a: np.asarray(a, np.float32)
    o = {}
    o["xT"] = np.ascontiguousarray(f(inp["x"][b, :T]).T)
    o["cT"] = np.ascontiguousarray(f(inp["ctx"][b]).T)
    sc = np.stack([_pcol(f(inp["c"][b])), _pcol(f(inp["c_ctx"]))], axis=-1)
    o["sc"] = np.ascontiguousarray(sc)
    return o


_CACHE = {}


def kernel(**inputs):
    B, T = inputs["x"].shape[0], inputs["x"].shape[1]
    if T not in _CACHE:
        _CACHE[T] = build_program(T)[0]
    nc = _CACHE[T]
    sh = shared_inputs(inputs)
    in_maps = []
    for b in range(B):
        m = dict(sh)
        m.update(core_inputs(inputs, b, T))
        in_maps.append(m)
    res = run_bass_kernel_spmd(nc, in_maps, core_ids=list(range(B)))
    out = np.stack([np.ascontiguousarray(r["outT"].T) for r in res.results], axis=0)
    return out.astype(np.float32)
```

```python
import contextlib
import numpy as np
import concourse.bass as bass
import concourse.mybir as mybir
from concourse.bass_utils import run_bass_kernel_spmd

F32 = mybir.dt.float32
I32 = mybir.dt.int32
BF16 = mybir.dt.bfloat16
ALU = mybir.AluOpType
AF = mybir.ActivationFunctionType
AX = mybir.AxisListType

ENG_NAMES = ("pe", "act", "dve", "pool", "sp")
N_DMA_SEMS = 10


class Prog:
    def __init__(self, nc, same_eng_sync=True):
        self.nc = nc
        self.same_eng_sync = same_eng_sync
        self.q = {e: [] for e in ENG_NAMES}
        self.cnt = {e: 0 for e in ENG_NAMES}
        self.waited = {e: {} for e in ENG_NAMES}
        self.last_w = {}
        self.readers = {}
        self.dma_rr = {e: 0 for e in ENG_NAMES}
        self.dma_cnt = {}
        self.sem_handles = {}
        self.pending = {e: {} for e in ENG_NAMES}
        self.n_instr = 0

    def _need(self, eng, needs, semkey, val):
        if semkey == ("c", "pe") and eng == "pe":
            return
        if (not self.same_eng_sync) and semkey == ("c", eng):
            return
        if self.waited[eng].get(semkey, 0) >= val:
            return
        if needs.get(semkey, 0) < val:
            needs[semkey] = val

    @staticmethod
    def _k(k):
        if isinstance(k, (str, int)):
            return k
        if isinstance(k, tuple):
            return tuple(Prog._k(x) for x in k)
        return k.name

    def barrier(self):
        snap = {("c", e): self.cnt[e] for e in ENG_NAMES if self.cnt[e]}
        snap.update(self.dma_cnt)
        for e in ENG_NAMES:
            for sk, v in snap.items():
                if self.pending[e].get(sk, 0) < v:
                    self.pending[e][sk] = v

    def _deps(self, eng, reads, writes):
        needs = {}
        if self.pending[eng]:
            for sk, v in self.pending[eng].items():
                self._need(eng, needs, sk, v)
            self.pending[eng] = {}
        for k in reads:
            lw = self.last_w.get(k)
            if lw:
                self._need(eng, needs, lw[0], lw[1])
        for k in writes:
            lw = self.last_w.get(k)
            if lw:
                self._need(eng, needs, lw[0], lw[1])
            for sk, v in self.readers.get(k, {}).items():
                self._need(eng, needs, sk, v)
        for sk, v in needs.items():
            self.waited[eng][sk] = v
        return list(needs.items())

    def _commit(self, semkey, val, reads, writes):
        for k in reads:
            self.readers.setdefault(k, {})[semkey] = val
        for k in writes:
            self.last_w[k] = (semkey, val)
            self.readers[k] = {}

    def op(self, eng, fn, reads=(), writes=()):
        reads = [self._k(k) for k in reads]
        writes = [self._k(k) for k in writes]
        waits = self._deps(eng, reads, writes)
        self.cnt[eng] += 1
        semkey = ("c", eng)
        self._commit(semkey, self.cnt[eng], reads, writes)
        self.q[eng].append((waits, fn, semkey, 1))
        self.n_instr += 1

    def dma(self, eng, out, in_, reads=(), writes=(), **kw):
        reads = [self._k(k) for k in reads]
        writes = [self._k(k) for k in writes]
        slot = self.dma_rr[eng]
        self.dma_rr[eng] = (slot + 1) % N_DMA_SEMS
        semkey = ("d", eng, slot)
        prev = self.dma_cnt.get(semkey, 0)
        waits = self._deps(eng, reads, writes)
        if prev and self.waited[eng].get(semkey, 0) < prev:
            waits.append((semkey, prev))
            self.waited[eng][semkey] = prev
        val = prev + 16
        self.dma_cnt[semkey] = val
        self._commit(semkey, val, reads, writes)

        def fn(e, out=out, in_=in_, kw=kw):
            return e.dma_start(out=out, in_=in_, **kw)
        self.q[eng].append((waits, fn, semkey, 16))
        self.n_instr += 1

    def final_wait(self, eng, keys):
        keys = [self._k(k) for k in keys]
        waits = self._deps(eng, keys, ())
        self.q[eng].append((waits, None, None, 0))

    def emit(self):
        nc = self.nc
        semkeys = set()
        for e in ENG_NAMES:
            for waits, fn, sk, inc in self.q[e]:
                if sk is not None:
                    semkeys.add(sk)
                for w, _ in waits:
                    semkeys.add(w)
        semkeys = sorted(semkeys, key=str)
        with contextlib.ExitStack() as st:
            for sk in semkeys:
                self.sem_handles[sk] = st.enter_context(nc.semaphore("s_" + "_".join(map(str, sk))))
            block = st.enter_context(nc.Block())

            def run(e_name):
                def body(eng):
                    for waits, fn, sk, inc in self.q[e_name]:
                        for wsk, v in waits:
                            eng.wait_ge(self.sem_handles[wsk], v)
                        if fn is not None:
                            fn(eng).then_inc(self.sem_handles[sk], inc)
                return body
            block.tensor(run("pe"))
            block.scalar(run("act"))
            block.vector(run("dve"))
            block.gpsimd(run("pool"))
            block.sync(run("sp"))


D = 1024
KT = 8
NT = 512
H = 4
CTX = 256
L = 2
FH = 2816
FT = 22
IN_DIM = 9104
COL_Q, COL_K, COL_V, COL_G, COL_U = 0, 512, 1024, 1536, 1552
COL_O, COL_SU, COL_SV, COL_CB, COL_CC, COL_CX, COL_GATE = 1936, 2448, 2960, 3472, 3984, 4496, 5008
EPS = 1e-6
BIG = 1.0e30
SLABF = 1408
NSLAB = 6
TQ, TK, TV, TG, TU, TO, TSU, TSV, TCB, TCC, TCX, TGATE = 0, 4, 8, 12, 13, 16, 20, 24, 28, 32, 36, 40
NWT = 72
PI = float(np.pi)
TWO_PI = float(2 * np.pi)

def input_shapes(T):
    return {
        "xT": [D, T], "cT": [D, CTX], "sc": [128, KT, 2], "cst": [128, 7, 128],
        "wmod": [L, 128, KT, 6 * D], "bmod": [L, 128, 48], "gnm": [L, 128, KT], "gnf": [L, 128, KT], "gfin": [128, KT],
        "win": [L, 128, NWT, KT, 128], "bg": [L, 1, 16], "wconv": [L, 128, 24], "gmh": [L, 1, 512], "gsgu": [L, 1, 512],
        "wsT": [L, 128, 4, 128], "bs": [L, 1, 512], "are": [L, 2, 128, 12], "aim": [L, 2, 128, 12], "ldt": [L, 2, 128, 12],
        "Bre": [L, 128, 12, 128], "Bim": [L, 128, 12, 128], "Cre": [L, 128, 12, 128], "Cim": [L, 128, 12, 128],
        "s5d": [L, 128, 3], "bglu": [L, 128, 3], "wglu": [L, 128, 3, 384], "wsconv": [L, 128, 12],
        "wupm": [L, 128, 8, 4, 128], "wups": [L, 128, 8, 4, 128], "wup5": [L, 128, 8, 3, 128], "wupc": [L, 128, 8, 4, 128],
        "wout": [L, 128, 8, KT, 128], "wfg": [L, 128, FT, KT, 128], "wfu": [L, 128, FT, KT, 128], "wfd": [L, 128, 8, FT, 128],
    }


def build_program(T, n_layers=L, dbg=()):
    nc = bass.Bass("TRN2", target_bir_lowering=False)
    din = {n: nc.dram_tensor(n, s, F32, kind="ExternalInput").ap() for n, s in input_shapes(T).items()}
    outT = nc.dram_tensor("outT", [D, T], F32, kind="ExternalOutput").ap()
    dbg_out = {}
    TS = {"lat": T, "ctx": CTX}
    xs = nc.dram_tensor("xs", [D, T], F32).ap()
    csd = nc.dram_tensor("csd", [D, CTX], F32).ap()
    qd = {sg: nc.dram_tensor("qd_" + sg, [128, 4, TS[sg]], F32).ap() for sg in TS}
    kd = {sg: nc.dram_tensor("kd_" + sg, [128, 4, TS[sg]], F32).ap() for sg in TS}
    vd = {sg: nc.dram_tensor("vd_" + sg, [TS[sg], 512], F32).ap() for sg in TS}
    gd = {sg: nc.dram_tensor("gd_" + sg, [TS[sg], 16], F32).ap() for sg in TS}
    ud = {sg: nc.dram_tensor("ud_" + sg, [128, 3, TS[sg]], F32).ap() for sg in TS}
    x1d = {sg: nc.dram_tensor("x1d_" + sg, [D, TS[sg]], F32).ap() for sg in TS}
    hd = {(d, sg): nc.dram_tensor(f"hd{d}_{sg}", [TS[sg], 512], F32).ap() for sg in TS for d in (0, 1)}
    yd_ = {(d, sg): nc.dram_tensor(f"yd{d}_{sg}", [128, 3, TS[sg]], F32).ap() for sg in TS for d in (0, 1)}
    BIGW = ("win", "wupm", "wups", "wup5", "wupc", "wout", "wfg", "wfu", "wfd")
    wb = {n: nc.dram_tensor("wb_" + n, input_shapes(T)[n], BF16).ap() for n in BIGW}

    with contextlib.ExitStack() as st:
        def sb(name, shape, dt=F32):
            return st.enter_context(nc.sbuf_tensor("sb_" + name, shape, dt))
        P = Prog(nc)
        cst = sb("cst", [128, 7, 128])
        ident, ones, Umat, Lmat, biasU, biasL, iota1 = (cst[:, i, :] for i in range(7))
        sc = sb("sc", [128, KT, 2]); modv = sb("modv", [128, 48, 2]); lcs = sb("lcs", [128, 2, 2, KT])
        gnm = sb("gnm", [128, KT]); gnf = sb("gnf", [128, KT]); gfin = sb("gfin", [128, KT]); bmod = sb("bmod", [128, 48])
        bg_row = sb("bg_row", [128, 16]); wconv = sb("wconv", [128, 24]); wsconv = sb("wsconv", [128, 12])
        s5d = sb("s5d", [128, 3]); bglu = sb("bglu", [128, 3])
        rhod = [sb(f"rho{d}", [128, 12]) for d in range(2)]
        card = [sb(f"car{d}", [128, 2, 12]) for d in range(2)]
        Caugd = [sb(f"Caug{d}", [128, H, 129]) for d in range(2)]
        mstd = [sb(f"mst{d}", [128, H]) for d in range(2)]
        s5p = sb("s5p", [128, 16, 12])
        NA = 50000
        ARENA = sb("ARENA", [128, NA])
        TABF = 3072 + 1536 + 32
        _tb = [NA - (2 - d) * TABF for d in range(2)]
        t16d = [ARENA[:, _tb[d]:_tb[d] + 3072].bitcast(BF16).rearrange("p (t a b) -> p t a b", t=4, a=12) for d in range(2)]
        rhoFd = [ARENA[:, _tb[d] + 3072:_tb[d] + 4608].rearrange("p (a b) -> p a b", a=12) for d in range(2)]
        rcsd = [ARENA[:, _tb[d] + 4608:_tb[d] + 4632].rearrange("p (a b) -> p a b", a=2) for d in range(2)]
        TABK = ["tab0", "tab1"]
        pbs = [st.enter_context(nc.psum_tensor(f"pb{i}", [128, 512], F32)) for i in range(8)]
        ps_rr = [0]

        def PS():
            t = pbs[ps_rr[0] % 8]
            ps_rr[0] += 1
            return t

        class Carver:
            def __init__(self, lim=None):
                self.off = 0
                self.lim = NA if lim is None else lim

            def f(self, n, pat=None, **kw):
                ap = ARENA[:, self.off:self.off + n]
                self.off += n
                assert self.off <= self.lim, self.off
                return ap.rearrange(pat, **kw) if pat else ap

            def b(self, n, pat=None, **kw):
                nf = (n + 1) // 2
                ap = ARENA[:, self.off:self.off + nf].bitcast(BF16)
                self.off += nf
                assert self.off <= self.lim, self.off
                return ap.rearrange(pat, **kw) if pat else ap

        def tt(eng, out, a, b, op, r, w):
            P.op(eng, lambda e: e.tensor_tensor(out, a, b, op), reads=r, writes=w)

        def ts(eng, out, a, s1, s2, op0, op1, r, w):
            if s2 is None:
                P.op(eng, lambda e: e.tensor_scalar(out, a, s1, None, op0=op0), reads=r, writes=w)
            else:
                P.op(eng, lambda e: e.tensor_scalar(out, a, s1, s2, op0=op0, op1=op1), reads=r, writes=w)

        def stt(eng, out, in0, scalar, in1, op0, op1, r, w):
            P.op(eng, lambda e: e.scalar_tensor_tensor(out, in0, scalar, in1, op0=op0, op1=op1), reads=r, writes=w)

        def act(out, in_, func, r, w, bias=0.0, scale=1.0):
            P.op("act", lambda e: e.activation(out, in_, func, bias=bias, scale=scale), reads=r, writes=w)

        def mm(ps, lhsT, rhs, start, stop, r, w):
            P.op("pe", lambda e: e.matmul(ps, lhsT, rhs, start=start, stop=stop), reads=r, writes=w)

        def tr(ps, in_, r, w):
            P.op("pe", lambda e: e.transpose(ps, in_, ident), reads=r + [cst], writes=w)

        def red(eng, out, in_, op, r, w):
            P.op(eng, lambda e: e.tensor_reduce(out, in_, axis=AX.X, op=op), reads=r, writes=w)

        def recip(out, in_, r, w):
            P.op("dve", lambda e: e.reciprocal(out, in_), reads=r, writes=w)

        def cp(eng, out, in_, r, w):
            P.op(eng, lambda e: e.tensor_copy(out, in_), reads=r, writes=w)

        def mset(eng, out, val, w):
            P.op(eng, lambda e: e.memset(out, val), writes=w)

        def dma(out, in_, r, w, eng="sp"):
            P.dma(eng, out, in_, reads=r, writes=w)

        def DBG(name, ap, shape, keys):
            if name in dbg and name not in dbg_out:
                t = nc.dram_tensor("dbg_" + name, shape, F32, kind="ExternalOutput").ap()
                dbg_out[name] = t
                dma(t, ap, keys, ["dbg_" + name])

        def cast_weights(l):
            for n in BIGW:
                src = din[n][l]; dstw = wb[n][l]
                nd = len(src.shape)
                letters = "abcd"[:nd - 1]
                pat = "p " + " ".join(letters) + " -> p (" + " ".join(letters) + ")"
                s2 = src.rearrange(pat); d2 = dstw.rearrange(pat)
                N = s2.shape[1]
                for c0 in range(0, N, 8192):
                    c1 = min(N, c0 + 8192)
                    dma(d2[:, c0:c1], s2[:, c0:c1], [], [("wb", n, l, c0)], eng="pool")

        def range_reduce(out, in_, ti, tm, keys_in, keys_out, kti, ktm):
            ts("dve", tm, in_, 1.0 / TWO_PI, None, ALU.mult, None, keys_in, [ktm])
            cp("dve", ti, tm, [ktm], [kti])
            cp("dve", tm, ti, [kti], [ktm])
            stt("dve", out, tm, -TWO_PI, in_, ALU.mult, ALU.add, [ktm] + keys_in, keys_out)
            P.op("dve", lambda e: e.tensor_single_scalar(tm, out, PI, op=ALU.is_gt), reads=keys_out, writes=[ktm])
            stt("dve", out, tm, -TWO_PI, out, ALU.mult, ALU.add, [ktm] + keys_out, keys_out)
            P.op("dve", lambda e: e.tensor_single_scalar(tm, out, -PI, op=ALU.is_lt), reads=keys_out, writes=[ktm])
            stt("dve", out, tm, TWO_PI, out, ALU.mult, ALU.add, [ktm] + keys_out, keys_out)
            ts("dve", out, out, PI, -PI, ALU.min, ALU.max, keys_out, keys_out)

        dma(cst[:], din["cst"], [], [cst])
        dma(sc[:], din["sc"], [], [sc])
        dma(gfin[:], din["gfin"], [], [gfin])
        act(sc[:], sc[:], AF.Silu, [sc], [sc])

        def layer_prep(l, first=False):
            if not first:
                P.barrier()
            cv = Carver(NA - 2 * TABF)
            wm = [cv.f(1024, "p (k c) -> p k c", k=KT) for _ in range(2)]
            ang = cv.f(1536, "p (a b) -> p a b", a=12); tsn = cv.f(1536, "p (a b) -> p a b", a=12); tcs = cv.f(1536, "p (a b) -> p a b", a=12)
            tmf = cv.f(1536); ti32 = cv.f(1536).bitcast(I32)
            Ts32 = cv.f(1536, "p (a b) -> p a b", a=12); Tc32 = cv.f(1536, "p (a b) -> p a b", a=12)
            for nm, t in (("gnm", gnm), ("gnf", gnf), ("bmod", bmod), ("wconv", wconv), ("wsconv", wsconv),
                          ("s5d", s5d), ("bglu", bglu)):
                dma(t[:], din[nm][l], [], [t])
            dma(bg_row[:], din["bg"][l].partition_broadcast(128), [], [bg_row])
            for ft in range(48):
                s = "wm%d" % (ft % 2)
                v = wm[ft % 2]
                dma(v, din["wmod"][l][:, :, ft * 128:(ft + 1) * 128], [], [s])
                ps = PS()
                for k in range(KT):
                    mm(ps[:, 0:2], v[:, k, :], sc[:, k, :], k == 0, k == KT - 1, [s, sc], [ps])
                ts("dve", modv[:, ft, :], ps[:, 0:2], bmod[:, ft:ft + 1], None, ALU.add, None, [ps, bmod], [modv])
            for sg in range(2):
                stt("dve", lcs[:, sg, 0, :], modv[:, 8:16, sg], 1.0, gnm[:], ALU.add, ALU.mult, [modv, gnm], [lcs])
                stt("dve", lcs[:, sg, 1, :], modv[:, 32:40, sg], 1.0, gnf[:], ALU.add, ALU.mult, [modv, gnf], [lcs])
            K = ["prep"]
            for d in range(2):
                t16 = t16d[d]; rhoF = rhoFd[d]; rcs = rcsd[d]; rho = rhod[d]; TK_ = TABK[d]
                are = s5p[:, 0, :]; aim = s5p[:, 1, :]; ldt = s5p[:, 2, :]
                dma(are, din["are"][l, d], [], K); dma(aim, din["aim"][l, d], [], K); dma(ldt, din["ldt"][l, d], [], K)
                dt_ = s5p[:, 3, :]; th = s5p[:, 4, :]; lar = s5p[:, 5, :]; sn = s5p[:, 6, :]; cs_ = s5p[:, 7, :]
                t0 = s5p[:, 8, :]; t1 = s5p[:, 9, :]; den = s5p[:, 10, :]; fr = s5p[:, 11, :]; fi = s5p[:, 12, :]
                nr = s5p[:, 13, :]; ni = s5p[:, 14, :]; t2 = s5p[:, 15, :]
                act(dt_, ldt, AF.Exp, K, K)
                tt("dve", th, dt_, aim, ALU.mult, K, K)
                tt("dve", lar, dt_, are, ALU.mult, K, K)
                act(rho[:], lar, AF.Exp, K, [rho] + K)
                range_reduce(t0, th, ti32[:, 0:12], t2, K, K, "ti32", "prep")
                act(sn, t0, AF.Sin, K, K)
                ts("dve", t1, th, PI / 2, None, ALU.add, None, K, K)
                range_reduce(t0, t1, ti32[:, 0:12], t2, K, K, "ti32", "prep")
                act(cs_, t0, AF.Sin, K, K)
                tt("dve", nr, rho[:], cs_, ALU.mult, [rho] + K, K)
                ts("dve", nr, nr, -1.0, None, ALU.add, None, K, K)
                tt("dve", ni, rho[:], sn, ALU.mult, [rho] + K, K)
                tt("dve", den, are, are, ALU.mult, K, K)
                tt("dve", t0, aim, aim, ALU.mult, K, K)
                tt("dve", den, den, t0, ALU.add, K, K)
                recip(den, den, K, K)
                tt("dve", t0, nr, are, ALU.mult, K, K); tt("dve", t1, ni, aim, ALU.mult, K, K)
                tt("dve", t0, t0, t1, ALU.add, K, K); tt("dve", fr, t0, den, ALU.mult, K, K)
                tt("dve", t0, ni, are, ALU.mult, K, K); tt("dve", t1, nr, aim, ALU.mult, K, K)
                tt("dve", t0, t0, t1, ALU.subtract, K, K); tt("dve", fi, t0, den, ALU.mult, K, K)
                thb = th.unsqueeze(2).to_broadcast([128, 12, 128])
                io = iota1.unsqueeze(1).to_broadcast([128, 12, 128])
                tt("dve", ang, io, thb, ALU.mult, [cst] + K, K)
                angf = ang.rearrange("p a b -> p (a b)")
                range_reduce(tsn.rearrange("p a b -> p (a b)"), angf, ti32, tmf, K, K, "ti32", "prep")
                act(Ts32, tsn, AF.Sin, K, K)
                ts("dve", angf, angf, PI / 2, None, ALU.add, None, K, K)
                range_reduce(tcs.rearrange("p a b -> p (a b)"), angf, ti32, tmf, K, K, "ti32", "prep")
                act(Tc32, tcs, AF.Sin, K, K)
                frb = fr.unsqueeze(2).to_broadcast([128, 12, 128]); fib = fi.unsqueeze(2).to_broadcast([128, 12, 128])
                tt("dve", ang, Tc32, frb, ALU.mult, K, K)
                tt("dve", tsn, Ts32, fib, ALU.mult, K, K)
                tt("dve", t16[:, 0, :, :], ang, tsn, ALU.add, K, [TK_])
                tt("dve", ang, Tc32, fib, ALU.mult, K, K)
                tt("dve", tsn, Ts32, frb, ALU.mult, K, K)
                tt("dve", t16[:, 1, :, :], ang, tsn, ALU.subtract, K, [TK_])
                cp("dve", t16[:, 2, :, :], Tc32, K, [TK_])
                cp("dve", t16[:, 3, :, :], Ts32, K, [TK_])
                cp("dve", rhoF, rho[:].unsqueeze(2).to_broadcast([128, 12, 128]), [rho], [TK_])
                mset("dve", rhoF[:, :, 0:1], 0.0, [TK_])
                tt("dve", rcs[:, 0, :], rho[:], Tc32[:, :, 127], ALU.mult, [rho] + K, [TK_])
                tt("dve", rcs[:, 1, :], rho[:], Ts32[:, :, 127], ALU.mult, [rho] + K, [TK_])
            P.barrier()

        def make_rowlocal(cv, nslab):
            V = {"nt": NT}
            V["xt_"] = cv.f(KT * NT, "p (a b) -> p a b", a=KT); V["sqt_"] = cv.f(KT * NT, "p (a b) -> p a b", a=KT)
            V["hT_"] = cv.b(KT * NT, "p (a b) -> p a b", a=KT); V["rs_"] = cv.f(NT)
            V["slabs"] = [cv.b(SLABF) for _ in range(nslab)]
            set_nt(V, NT)
            return V

        def set_nt(V, nt):
            V["nt"] = nt
            V["xt"] = V["xt_"][:, :, 0:nt]; V["sqt"] = V["sqt_"][:, :, 0:nt]; V["hT"] = V["hT_"][:, :, 0:nt]; V["rs"] = V["rs_"][:, 0:nt]

        slab_rr = [0]

        def load_slab(V, w_ap, nk):
            i = slab_rr[0] % len(V["slabs"])
            slab_rr[0] += 1
            s = V["slabs"][i]
            key = "slab%d" % i
            v = s[:, 0:nk * 128].rearrange("p (k c) -> p k c", k=nk)
            dma(v, w_ap, [], [key])
            return v, key

        def norm(V, scale_fn, shift_fn, extra_r, dst, dkey):
            xt, sqt, rs = V["xt"], V["sqt"], V["rs"]
            NT = V["nt"]
            tt("pool", sqt, xt, xt, ALU.mult, ["xt"], ["sqt"])
            ps = PS()
            for k in range(KT):
                mm(ps[:, 0:NT], ones, sqt[:, k, :], k == 0, k == KT - 1, [cst, "sqt"], [ps])
            act(rs, ps[:, 0:NT], AF.Sqrt, [ps], ["rs"], bias=EPS, scale=1.0 / D)
            recip(rs, rs, ["rs"], ["rs"])
            tt("dve", sqt, xt, rs.unsqueeze(1).to_broadcast([128, KT, NT]), ALU.mult, ["xt", "rs"], ["sqt"])
            for k in range(KT):
                sh = shift_fn(k)
                if sh is None:
                    act(dst[:, k, :], sqt[:, k, :], AF.Identity, ["sqt"] + extra_r, [dkey], scale=scale_fn(k))
                else:
                    act(dst[:, k, :], sqt[:, k, :], AF.Identity, ["sqt"] + extra_r, [dkey], bias=sh, scale=scale_fn(k))

        def proj_fm(V, w_l, c0, ntiles, nk, rhs_t, rhs_key, evac):
            NT = V["nt"]
            for j in range(ntiles):
                v, s = load_slab(V, w_l[:, c0 + j], nk)
                ps = PS()
                for k in range(nk):
                    mm(ps[:, 0:NT], v[:, k, :], rhs_t[:, k, :], k == 0, k == nk - 1, [s, rhs_key], [ps])
                evac(j, ps)

        def proj_tm(V, w_l, c0, ntiles, evac, ncols=128):
            hT = V["hT"]
            NCH = V["nt"] // 128
            for j in range(ntiles):
                v, s = load_slab(V, w_l[:, c0 + j], KT)
                for c in range(NCH):
                    ps = PS()
                    for k in range(KT):
                        mm(ps[:, 0:ncols], hT[:, k, c * 128:(c + 1) * 128], v[:, k, 0:ncols], k == 0, k == KT - 1, [s, "hT"], [ps])
                    evac(j, c, ps)

        def conv3(dst, src, wv, woff, rowlen, r, w, NT):
            nr_ = NT // rowlen
            for j in range(4):
                w0 = wv[:, woff + 0 + j:woff + 1 + j]; w1 = wv[:, woff + 4 + j:woff + 5 + j]; w2 = wv[:, woff + 8 + j:woff + 9 + j]
                d3 = dst[:, j, :].rearrange("p (r c) -> p r c", r=nr_); s3 = src[:, j, :].rearrange("p (r c) -> p r c", r=nr_)
                act(dst[:, j, :], src[:, j, :], AF.Identity, r, w, scale=w1)
                stt("dve", d3[:, :, 1:rowlen], s3[:, :, 0:rowlen - 1], w0, d3[:, :, 1:rowlen], ALU.mult, ALU.add, r + w, w)
                stt("dve", d3[:, :, 0:rowlen - 1], s3[:, :, 1:rowlen], w2, d3[:, :, 0:rowlen - 1], ALU.mult, ALU.add, r + w, w)

        def phase1(l, src_of):
            P.barrier()
            cv = Carver(NA - 2 * TABF)
            NTF = 512; NCF = 4
            sqt_ = cv.f(KT * NTF, "p (a b) -> p a b", a=KT); rs_ = cv.f(NTF)
            slabs1 = [cv.b(SLABF) for _ in range(9)]
            rawq_ = cv.f(4 * NTF, "p (a b) -> p a b", a=4); rawk_ = cv.f(4 * NTF, "p (a b) -> p a b", a=4)
            qT_ = cv.f(4 * NTF, "p (a b) -> p a b", a=4); kT_ = cv.f(4 * NTF, "p (a b) -> p a b", a=4)
            vtm_ = cv.f(NCF * 512, "p (c e) -> p c e", c=NCF); uT_ = cv.f(3 * NTF, "p (a b) -> p a b", a=3); graw_ = cv.f(NCF * 16, "p (c e) -> p c e", c=NCF)
            sets = [{"tag": tg, "xt": cv.f(KT * NTF, "p (a b) -> p a b", a=KT), "hT": cv.b(KT * NTF, "p (a b) -> p a b", a=KT)} for tg in ("A", "B")]
            w_l = wb["win"][l]
            srr = [0]

            def lslab(w_ap, nk):
                i = srr[0] % 9
                srr[0] += 1
                key = "s1slab%d" % i
                v = slabs1[i][:, 0:nk * 128].rearrange("p (k c) -> p k c", k=nk)
                dma(v, w_ap, [], [key])
                return v, key

            def unit_gen(seg, u, NT, S):
                NCH = NT // 128
                tg = S["tag"]
                kx, kh = "p1xt" + tg, "p1hT" + tg
                xt, hT = S["xt"][:, :, 0:NT], S["hT"][:, :, 0:NT]
                sqt = sqt_[:, :, 0:NT]; rs = rs_[:, 0:NT]
                rawq = rawq_[:, :, 0:NT]; rawk = rawk_[:, :, 0:NT]; qT = qT_[:, :, 0:NT]; kT = kT_[:, :, 0:NT]
                vtm = vtm_[:, 0:NCH, :]; uT = uT_[:, :, 0:NT]; graw = graw_[:, 0:NCH, :]
                sg = 1 if seg == "ctx" else 0
                t0 = u * NTF
                rowlen = NT if seg == "ctx" else 64
                dma(xt, src_of[seg][:, t0:t0 + NT].rearrange("(k p) t -> p k t", p=128), [], [kx])
                yield "y"
                act(sqt, xt, AF.Square, [kx], ["p1sqt"])
                ps = PS()
                for k in range(KT):
                    mm(ps[:, 0:NT], ones, sqt[:, k, :], k == 0, k == KT - 1, [cst, "p1sqt"], [ps])
                act(rs, ps[:, 0:NT], AF.Sqrt, [ps], ["p1rs"], bias=EPS, scale=1.0 / D)
                recip(rs, rs, ["p1rs"], ["p1rs"])
                yield "y"
                tt("dve", sqt, xt, rs.unsqueeze(1).to_broadcast([128, KT, NT]), ALU.mult, [kx, "p1rs"], ["p1sqt"])
                for k in range(KT):
                    act(hT[:, k, :], sqt[:, k, :], AF.Identity, ["p1sqt", lcs, modv], [kh], bias=modv[:, k, sg:sg + 1], scale=lcs[:, sg, 0, k:k + 1])
                yield "S"

                def pfm(c0, ntiles, dst, dkey):
                    for j in range(ntiles):
                        v, s_ = lslab(w_l[:, c0 + j], KT)
                        ps = PS()
                        for k in range(KT):
                            mm(ps[:, 0:NT], v[:, k, :], hT[:, k, :], k == 0, k == KT - 1, [s_, kh], [ps])
                        act(dst[:, j, :], ps[:, 0:NT], AF.Copy, [ps], [dkey])
                        yield "y"

                def ptm(c0, ntiles, evac, ncols=128):
                    for j in range(ntiles):
                        v, s_ = lslab(w_l[:, c0 + j], KT)
                        for c in range(NCH):
                            ps = PS()
                            for k in range(KT):
                                mm(ps[:, 0:ncols], hT[:, k, c * 128:(c + 1) * 128], v[:, k, 0:ncols], k == 0, k == KT - 1, [s_, kh], [ps])
                            evac(j, c, ps)
                        yield "y"
                yield from pfm(TQ, 4, rawq, "p1rawq")
                yield from pfm(TK, 4, rawk, "p1rawk")
                conv3(qT, rawq, wconv, 0, rowlen, ["p1rawq", wconv], ["p1qT"], NT)
                act(qT, qT, AF.Silu, ["p1qT"], ["p1qT"])
                dma(qd[seg][:, :, t0:t0 + NT], qT, ["p1qT"], [("q", seg, u)])
                yield "y"
                yield from ptm(TV, 4, lambda j, c, ps: act(vtm[:, c, j * 128:(j + 1) * 128], ps[:, 0:128], AF.Copy, [ps], ["p1vtm"]))
                dma(vd[seg][t0:t0 + NT, :].rearrange("(c p) e -> p c e", p=128), vtm, ["p1vtm"], [("v", seg, u)])
                conv3(kT, rawk, wconv, 12, rowlen, ["p1rawk", wconv], ["p1kT"], NT)
                act(kT, kT, AF.Silu, ["p1kT"], ["p1kT"], )
                ts("dve", kT, kT, float(128 ** -0.5), None, ALU.mult, None, ["p1kT"], ["p1kT"])
                dma(kd[seg][:, :, t0:t0 + NT], kT, ["p1kT"], [("k", seg, u)])
                yield "y"
                yield from ptm(TG, 1, lambda j, c, ps: tt("dve", graw[:, c, :], ps[:, 0:16], bg_row[:], ALU.add, [ps, bg_row], ["p1graw"]), ncols=16)
                dma(gd[seg][t0:t0 + NT, :].rearrange("(c p) e -> p c e", p=128), graw, ["p1graw"], [("g", seg, u)])
                yield from pfm(TU, 3, uT, "p1uT")
                dma(ud[seg][:, :, t0:t0 + NT], uT, ["p1uT"], [("u", seg, u)])
                yield "S"

            units = [("ctx", 0, CTX)] + [("lat", u, NTF) for u in range(T // NTF)]
            gens = [unit_gen(seg, u, nt_, sets[i % 2]) for i, (seg, u, nt_) in enumerate(units)]

            def to_stage(g):
                while next(g) != "S":
                    pass

            def weave(ga, gb):
                da = db = False
                while not (da and db):
                    if not da:
                        da = next(ga) == "S"
                    if not db:
                        db = next(gb) == "S"
            to_stage(gens[0])
            for i in range(len(gens)):
                if i + 1 < len(gens):
                    weave(gens[i], gens[i + 1])
                else:
                    to_stage(gens[i])

        def phase2(l):
            P.barrier()
            cv = Carver(NA - 2 * TABF)
            nlat = T // 128
            chains = {0: [("ctx", c) for c in range(CTX // 128)] + [("lat", c) for c in range(nlat)],
                      1: [("ctx", c) for c in range(CTX // 128 - 1, -1, -1)] + [("lat", c) for c in range(nlat - 1, -1, -1)]}
            for d in range(2):
                mset("pool", Caugd[d][:], 0.0, [Caugd[d]]); mset("pool", mstd[d][:], 0.0, [mstd[d]]); mset("pool", card[d][:], 0.0, [card[d]])

            def unit_of(seg, c):
                return (seg, 0)

            Bre = cv.b(1536, "p (a b) -> p a b", a=12); Bim = cv.b(1536, "p (a b) -> p a b", a=12)
            Cre = cv.b(1536, "p (a b) -> p a b", a=12); Cimn = cv.b(1536, "p (a b) -> p a b", a=12)
            dma(Bre, din["Bre"][l], [], ["Bre"], eng="pool"); dma(Bim, din["Bim"][l], [], ["Bim"], eng="pool")
            dma(Cre, din["Cre"][l], [], ["Cre"], eng="pool"); dma(Cimn, din["Cim"][l], [], ["Cimn"], eng="pool")
            ts("dve", Cimn, Cimn, -1.0, None, ALU.mult, None, ["Cimn"], ["Cimn"])

            def mlstm_stream(d):
                bwd = d == 1
                nm = "m%d" % d
                qc = [cv.f(512, "p (a b) -> p a b", a=4) for _ in range(2)]; kc = [cv.f(512, "p (a b) -> p a b", a=4) for _ in range(2)]
                va = [cv.f(516, "p (a b) -> p a b", a=4) for _ in range(2)]; gc = [cv.f(16) for _ in range(2)]
                gsm = cv.f(64, "p (a b) -> p a b", a=16); ex = cv.f(20, "p (a b) -> p a b", a=5)
                gdt = cv.f(512, "p (a b) -> p a b", a=4); tmt = cv.f(512, "p (a b) -> p a b", a=4)
                vw = cv.f(516, "p (a b) -> p a b", a=4); PTa = cv.f(512, "p (a b) -> p a b", a=4); ktm = cv.f(512, "p (a b) -> p a b", a=4)
                r1 = cv.f(258, "p (a b) -> p a b", a=2); tmx = cv.f(258, "p (a b) -> p a b", a=2); rra = cv.f(516, "p (a b) -> p a b", a=4)
                dn = cv.f(16, "p (a b) -> p a b", a=4); hdir = cv.f(512, "p (a b) -> p a b", a=4); Ctmp = cv.f(516, "p (a b) -> p a b", a=4)
                b0 = pbs[2 * d]; b1 = pbs[2 * d + 1]
                Caug = Caugd[d]; mst = mstd[d]
                io, fo = (8, 12) if bwd else (0, 4)
                Tri = Lmat if bwd else Umat
                biasM = biasU if bwd else biasL
                K_ = lambda s_: nm + s_
                chain = chains[d]
                for b_ in range(2):
                    mset("pool", va[b_][:, :, 128:129], 1.0, [K_("va%d" % b_)])

                def load(i):
                    seg, c = chain[i]
                    b_ = i % 2
                    tok = c * 128
                    su = unit_of(seg, c)
                    dma(qc[b_], qd[seg][:, :, tok:tok + 128], [("q",) + su], [K_("qc%d" % b_)])
                    dma(kc[b_], kd[seg][:, :, tok:tok + 128], [("k",) + su], [K_("kc%d" % b_)])
                    dma(va[b_][:, :, 0:128], vd[seg][tok:tok + 128, :].rearrange("p (h e) -> p h e", h=4), [("v",) + su], [K_("va%d" % b_)])
                    dma(gc[b_], gd[seg][tok:tok + 128, :], [("g",) + su], [K_("gc%d" % b_)])
                load(0)
                for i, (seg, c) in enumerate(chain):
                    b_ = i % 2
                    if i + 1 < len(chain):
                        load(i + 1)
                    q_, k_, v_, g_ = qc[b_], kc[b_], va[b_], gc[b_]
                    kq, kk_, kv, kg = K_("qc%d" % b_), K_("kc%d" % b_), K_("va%d" % b_), K_("gc%d" % b_)
                    G = [K_("gsm")]; EX = [K_("ex")]
                    e1 = gsm[:, 0, :]; sp = gsm[:, 1, :]; g = gsm[:, 2, :]; gmax = gsm[:, 3, :]; cmax = gsm[:, 4, :]
                    Mx = gsm[:, 5, :]; M = gsm[:, 6, :]
                    act(e1, g_[:, fo:fo + 4], AF.Exp, [kg], G, scale=-1.0)
                    act(sp, e1, AF.Ln, G, G, bias=1.0)
                    mm(b0[:, 0:4], Tri, sp, True, True, [cst] + G, [b0])
                    mm(b0[:, 4:8], ones, sp, True, True, [cst] + G, [b0])
                    yield
                    tt("dve", g, g_[:, io:io + 4], b0[:, 0:4], ALU.add, [kg, b0], G)
                    tt("dve", gdt, ident.unsqueeze(1).to_broadcast([128, 4, 128]), g.unsqueeze(2).to_broadcast([128, 4, 128]), ALU.mult, [cst] + G, [K_("gdt")])
                    mm(b1[:, :], ones, gdt.rearrange("p a b -> p (a b)"), True, True, [cst, K_("gdt")], [b1])
                    yield
                    b1v = b1[:, :].rearrange("p (a b) -> p a b", a=4)
                    red("dve", gmax, b1v, ALU.max, [b1], G)
                    tt("dve", tmt, b1v, biasM.unsqueeze(1).to_broadcast([128, 4, 128]), ALU.add, [b1, cst], [K_("tmt")])
                    red("dve", cmax, tmt, ALU.max, [K_("tmt")], G)
                    tt("dve", Mx, mst[:], gmax, ALU.max, [mst] + G, G)
                    tt("dve", M, mst[:], cmax, ALU.max, [mst] + G, G)
                    tt("dve", ex[:, 0, :], g, Mx, ALU.subtract, G, EX)
                    tt("dve", ex[:, 1, :], Mx, M, ALU.subtract, G, EX)
                    tt("dve", ex[:, 2, :], mst[:], M, ALU.subtract, [mst] + G, EX)
                    tt("dve", ex[:, 3, :], mst[:], Mx, ALU.subtract, [mst] + G, EX)
                    tt("dve", ex[:, 4, :], b0[:, 0:4], M, ALU.subtract, [b0] + G, EX)
                    act(ex, ex, AF.Exp, EX, EX)
                    tt("dve", mst[:], Mx, b0[:, 4:8], ALU.subtract, G + [b0], [mst])
                    yield
                    tt("dve", vw, v_, ex[:, 0, :].unsqueeze(2).to_broadcast([128, 4, 129]), ALU.mult, [kv] + EX, [K_("vw")])
                    for h in range(H):
                        mm(b0[:, h * 128:(h + 1) * 128], k_[:, h, :], q_[:, h, :], True, True, [kk_, kq], [b0])
                    for h in range(H):
                        tr(b1[:, h * 128:(h + 1) * 128], k_[:, h, :], [kk_], [b1])
                    yield
                    maskPT = (Lmat if bwd else Umat).unsqueeze(1).to_broadcast([128, 4, 128])
                    tt("dve", PTa, b0[:, :].rearrange("p (a b) -> p a b", a=4), maskPT, ALU.mult, [b0, cst], [K_("PTa")])
                    act(ktm, b1[:, :].rearrange("p (a b) -> p a b", a=4), AF.Copy, [b1], [K_("ktm")])
                    yield
                    for pr in range(2):
                        for hh in range(2):
                            h = 2 * pr + hh
                            mm(b0[:, hh * 129:(hh + 1) * 129], PTa[:, h, :], vw[:, h, :], True, True, [K_("PTa"), K_("vw")], [b0])
                            mm(b1[:, hh * 129:(hh + 1) * 129], q_[:, h, :], Caug[:, h, :], True, True, [kq, Caug], [b1])
                        yield
                        hs = slice(2 * pr, 2 * pr + 2)
                        b0p = b0[:, 0:258].rearrange("p (a b) -> p a b", a=2); b1p = b1[:, 0:258].rearrange("p (a b) -> p a b", a=2)
                        tt("dve", r1, b1p, ex[:, 2, hs].unsqueeze(2).to_broadcast([128, 2, 129]), ALU.mult, [b1] + EX, [K_("r1")])
                        tt("dve", tmx, b0p, ex[:, 1, hs].unsqueeze(2).to_broadcast([128, 2, 129]), ALU.mult, [b0] + EX, [K_("tmx")])
                        tt("dve", rra[:, hs, :], tmx, r1, ALU.add, [K_("tmx"), K_("r1")], [K_("rra")])
                        yield
                    den = rra[:, :, 128]
                    ts("dve", dn[:, 0, :], den, -1.0, None, ALU.mult, None, [K_("rra")], [K_("dn")])
                    tt("dve", dn[:, 0, :], dn[:, 0, :], den, ALU.max, [K_("dn"), K_("rra")], [K_("dn")])
                    tt("dve", dn[:, 0, :], dn[:, 0, :], ex[:, 4, :], ALU.max, [K_("dn")] + EX, [K_("dn")])
                    recip(dn[:, 1, :], dn[:, 0, :], [K_("dn")], [K_("dn")])
                    tt("dve", hdir, rra[:, :, 0:128], dn[:, 1, :].unsqueeze(2).to_broadcast([128, 4, 128]), ALU.mult, [K_("rra"), K_("dn")], [K_("hdir")])
                    dma(hd[(d, seg)][c * 128:(c + 1) * 128, :].rearrange("p (h e) -> p h e", h=4), hdir, [K_("hdir")], [("h", d, seg, c)])
                    yield
                    for h in range(H):
                        bb = b0 if h < 2 else b1
                        hh = h % 2
                        mm(bb[:, hh * 129:(hh + 1) * 129], ktm[:, h, :], vw[:, h, :], True, True, [K_("ktm"), K_("vw")], [bb])
                    tt("dve", Ctmp, Caug[:], ex[:, 3, :].unsqueeze(2).to_broadcast([128, 4, 129]), ALU.mult, [Caug] + EX, [K_("Ctmp")])
                    yield
                    tt("dve", Caug[:, 0:2, :], Ctmp[:, 0:2, :], b0[:, 0:258].rearrange("p (a b) -> p a b", a=2), ALU.add, [K_("Ctmp"), b0], [Caug])
                    tt("dve", Caug[:, 2:4, :], Ctmp[:, 2:4, :], b1[:, 0:258].rearrange("p (a b) -> p a b", a=2), ALU.add, [K_("Ctmp"), b1], [Caug])
                    yield

            def s5_stream(d):
                bwd = d == 1
                nm = "s%d" % d
                K_ = lambda s_: nm + s_
                uc = [cv.b(384, "p (a b) -> p a b", a=3) for _ in range(2)]
                B2 = cv.b(1024, "p (r a b) -> p r a b", r=2, a=4); S2 = cv.b(1024, "p (r a b) -> p r a b", r=2, a=4)
                m1 = cv.b(1024, "p (r a b) -> p r a b", r=2, a=4); m2 = cv.b(1024, "p (r a b) -> p r a b", r=2, a=4)
                vf = cv.f(1024, "p (r a b) -> p r a b", r=2, a=4); Y = cv.f(1024, "p (r a b) -> p r a b", r=2, a=4)
                tq = cv.f(16, "p (a b) -> p a b", a=4)
                ycb = cv.f(384, "p (a b) -> p a b", a=3)
                bA = pbs[4 + 2 * d]; bB = pbs[5 + 2 * d]
                t16 = t16d[d]; rhoF = rhoFd[d]; rcs = rcsd[d]; car = card[d]; TKd = TABK[d]
                chain = chains[d]

                def R3(ap):
                    return ap[:, :, ::-1] if bwd else ap

                def R4(ap):
                    return ap[:, :, :, ::-1] if bwd else ap

                def load(i):
                    seg, c = chain[i]
                    b_ = i % 2
                    tok = c * 128
                    dma(uc[b_], ud[seg][:, :, tok:tok + 128], [("u",) + unit_of(seg, c)], [K_("uc%d" % b_)], eng="pool")
                load(0)
                for i, (seg, c) in enumerate(chain):
                    b_ = i % 2
                    if i + 1 < len(chain):
                        load(i + 1)
                    u_ = uc[b_]; ku = K_("uc%d" % b_)
                    for j in range(3):
                        ks = slice(4 * j, 4 * j + 4)
                        for kk in range(4):
                            k = 4 * j + kk
                            mm(bA[:, kk * 128:(kk + 1) * 128], Bre[:, k, :], u_[:, j, :], True, True, ["Bre", ku], [bA])
                            mm(bB[:, kk * 128:(kk + 1) * 128], Bim[:, k, :], u_[:, j, :], True, True, ["Bim", ku], [bB])
                        yield
                        bAv = R3(bA[:, :].rearrange("p (a b) -> p a b", a=4)); bBv = R3(bB[:, :].rearrange("p (a b) -> p a b", a=4))
                        act(B2[:, 0], bAv, AF.Copy, [bA], [K_("B2")])
                        act(B2[:, 1], bBv, AF.Copy, [bB], [K_("B2")])
                        act(S2[:, 0], bBv, AF.Copy, [bB], [K_("S2")], scale=-1.0)
                        act(S2[:, 1], bAv, AF.Copy, [bA], [K_("S2")])
                        yield
                        bc = lambda t_: t_.unsqueeze(1).to_broadcast([128, 2, 4, 128])
                        tt("dve", m1, B2, bc(t16[:, 0, ks, :]), ALU.mult, [K_("B2"), TKd], [K_("m1")])
                        tt("dve", m2, S2, bc(t16[:, 1, ks, :]), ALU.mult, [K_("S2"), TKd], [K_("m2")])
                        tt("dve", vf, m1, m2, ALU.add, [K_("m1"), K_("m2")], [K_("vf")])
                        tt("dve", vf[:, :, :, 0], vf[:, :, :, 0], car[:, :, ks], ALU.add, [K_("vf"), car], [K_("vf")])
                        yield
                        rf = rhoF[:, ks, :].rearrange("p a b -> p (a b)")
                        for r_ in range(2):
                            P.op("dve", lambda e, r_=r_, rf=rf: e.tensor_tensor_scan(Y[:, r_].rearrange("p a b -> p (a b)"), rf,
                                                                                     vf[:, r_].rearrange("p a b -> p (a b)"), 0.0, ALU.mult, ALU.add),
                                 reads=[TKd, K_("vf")], writes=[K_("Y")])
                        yield
                        act(B2, Y, AF.Copy, [K_("Y")], [K_("B2")])
                        act(S2[:, 0], Y[:, 1], AF.Copy, [K_("Y")], [K_("S2")], scale=-1.0)
                        act(S2[:, 1], Y[:, 0], AF.Copy, [K_("Y")], [K_("S2")])
                        yr = Y[:, 0, :, 127]; yi = Y[:, 1, :, 127]; rc = rcs[:, 0, ks]; rs_ = rcs[:, 1, ks]
                        tt("dve", tq[:, 0, :], yr, rc, ALU.mult, [K_("Y"), TKd], [K_("tq")])
                        tt("dve", tq[:, 1, :], yi, rs_, ALU.mult, [K_("Y"), TKd], [K_("tq")])
                        tt("dve", tq[:, 2, :], yi, rc, ALU.mult, [K_("Y"), TKd], [K_("tq")])
                        tt("dve", tq[:, 3, :], yr, rs_, ALU.mult, [K_("Y"), TKd], [K_("tq")])
                        tt("dve", car[:, 0, ks], tq[:, 0, :], tq[:, 1, :], ALU.subtract, [K_("tq")], [car])
                        tt("dve", car[:, 1, ks], tq[:, 2, :], tq[:, 3, :], ALU.add, [K_("tq")], [car])
                        yield
                        tt("dve", m1, B2, bc(t16[:, 2, ks, :]), ALU.mult, [K_("B2"), TKd], [K_("m1")])
                        tt("dve", m2, S2, bc(t16[:, 3, ks, :]), ALU.mult, [K_("S2"), TKd], [K_("m2")])
                        tt("dve", R4(B2), m1, m2, ALU.add, [K_("m1"), K_("m2")], [K_("B2")])
                        yield
                        for kk in range(4):
                            k = 4 * j + kk
                            mm(bA[:, 0:128], Cre[:, k, :], B2[:, 0, kk, :], kk == 0, False, ["Cre", K_("B2")], [bA])
                            mm(bA[:, 0:128], Cimn[:, k, :], B2[:, 1, kk, :], False, kk == 3, ["Cimn", K_("B2")], [bA])
                        act(ycb[:, j, :], bA[:, 0:128], AF.Copy, [bA], [K_("ycb")])
                        yield
                    dma(yd_[(d, seg)][:, :, c * 128:(c + 1) * 128], ycb, [K_("ycb")], [("y", d, seg, c)])

            gens = [mlstm_stream(0), mlstm_stream(1), s5_stream(0), s5_stream(1)]
            alive = list(gens)
            while alive:
                for g_ in list(alive):
                    try:
                        next(g_)
                    except StopIteration:
                        alive.remove(g_)

        def phase3b(l, dst_of, is_last, units):
            P.barrier()
            cv = Carver()
            N3 = 512
            sqt_ = cv.f(KT * N3, "p (a b) -> p a b", a=KT); rs_ = cv.f(N3)
            slabs3 = [cv.b(SLABF) for _ in range(10)]
            a_t_ = cv.b(FT * N3, "p (a b) -> p a b", a=FT)
            sets = []
            for tag in ("A", "B"):
                sets.append({"tag": tag, "xt": cv.f(KT * N3, "p (a b) -> p a b", a=KT), "hT": cv.b(KT * N3, "p (a b) -> p a b", a=KT), "tmpA": cv.f(N3)})
            srr = [0]

            def lslab(w_ap, nk):
                i = srr[0] % 10
                srr[0] += 1
                key = "s3bslab%d" % i
                v = slabs3[i][:, 0:nk * 128].rearrange("p (k c) -> p k c", k=nk)
                dma(v, w_ap, [], [key])
                return v, key

            def unit_gen(seg, u, NT, S):
                tg = S["tag"]
                K_ = lambda n: "b" + tg + n
                xt, hT, tmpA = S["xt"][:, :, 0:NT], S["hT"][:, :, 0:NT], S["tmpA"][:, 0:NT]
                sqt = sqt_[:, :, 0:NT]; rs = rs_[:, 0:NT]; a_t = a_t_[:, :, 0:NT]
                sg = 1 if seg == "ctx" else 0
                t0 = u * N3

                def norm3(scale_fn, shift_fn, extra_r, dst, dkey):
                    act(sqt, xt, AF.Square, [K_("xt")], ["sqt3b"])
                    ps = PS()
                    for k in range(KT):
                        mm(ps[:, 0:NT], ones, sqt[:, k, :], k == 0, k == KT - 1, [cst, "sqt3b"], [ps])
                    act(rs, ps[:, 0:NT], AF.Sqrt, [ps], ["rs3b"], bias=EPS, scale=1.0 / D)
                    recip(rs, rs, ["rs3b"], ["rs3b"])
                    tt("dve", sqt, xt, rs.unsqueeze(1).to_broadcast([128, KT, NT]), ALU.mult, [K_("xt"), "rs3b"], ["sqt3b"])
                    for k in range(KT):
                        sh = shift_fn(k)
                        if sh is None:
                            act(dst[:, k, :], sqt[:, k, :], AF.Identity, ["sqt3b"] + extra_r, [dkey], scale=scale_fn(k))
                        else:
                            act(dst[:, k, :], sqt[:, k, :], AF.Identity, ["sqt3b"] + extra_r, [dkey], bias=sh, scale=scale_fn(k))
                dma(xt, x1d[seg][:, t0:t0 + NT].rearrange("(k p) t -> p k t", p=128), [], [K_("xt")])
                yield "y"
                norm3(lambda k: lcs[:, sg, 1, k:k + 1], lambda k: modv[:, 24 + k, sg:sg + 1], [lcs, modv], hT, K_("hT"))
                yield "S"
                for i in range(FT):
                    v1, s1 = lslab(wb["wfg"][l][:, i], KT)
                    pg = PS()
                    for k in range(KT):
                        mm(pg[:, 0:NT], v1[:, k, :], hT[:, k, :], k == 0, k == KT - 1, [s1, K_("hT")], [pg])
                    act(tmpA, pg[:, 0:NT], AF.Silu, [pg], [K_("tmpA")])
                    v2, s2 = lslab(wb["wfu"][l][:, i], KT)
                    pu = PS()
                    for k in range(KT):
                        mm(pu[:, 0:NT], v2[:, k, :], hT[:, k, :], k == 0, k == KT - 1, [s2, K_("hT")], [pu])
                    tt("dve", a_t[:, i, :], tmpA, pu[:, 0:NT], ALU.mult, [K_("tmpA"), pu], [("ba", i)])
                    yield "y"
                for jd in range(KT):
                    ps = PS()
                    for hf in range(2):
                        v3, s3 = lslab(wb["wfd"][l][:, jd, hf * 11:(hf + 1) * 11], 11)
                        for k in range(11):
                            i = hf * 11 + k
                            mm(ps[:, 0:NT], v3[:, k, :], a_t[:, i, :], i == 0, i == FT - 1, [s3, ("ba", i)], [ps])
                    stt("dve", xt[:, jd, :], ps[:, 0:NT], modv[:, 40 + jd, sg:sg + 1], xt[:, jd, :], ALU.mult, ALU.add, [ps, modv, K_("xt")], [K_("xt")])
                    yield "y"
                if is_last and seg == "lat":
                    norm3(lambda k: gfin[:, k:k + 1], lambda k: None, [gfin], sqt, "sqt3b")
                    dma(outT[:, t0:t0 + NT].rearrange("(k p) t -> p k t", p=128), sqt, ["sqt3b"], [("out", u)])
                else:
                    dma(dst_of[seg][:, t0:t0 + NT].rearrange("(k p) t -> p k t", p=128), xt, [K_("xt")], [("x2", seg, u)])
                yield "S"

            gens = [unit_gen(seg, u, nt_, sets[i % 2]) for i, (seg, u, nt_) in enumerate(units)]

            def to_stage(g):
                while next(g) != "S":
                    pass

            def weave(ga, gb):
                da = db = False
                while not (da and db):
                    if not da:
                        da = next(ga) == "S"
                    if not db:
                        db = next(gb) == "S"
            to_stage(gens[0])
            for i in range(len(gens)):
                if i + 1 < len(gens):
                    weave(gens[i], gens[i + 1])
                else:
                    to_stage(gens[i])

        def phase3(l, src_of, dst_of, is_last):
            P.barrier()
            cv = Carver()
            N3 = 512; NC3 = 4
            sqt_ = cv.f(KT * N3, "p (a b) -> p a b", a=KT); rs_ = cv.f(N3)
            slabs3 = [cv.b(SLABF) for _ in range(5)]
            gmh_row = cv.f(512); gsgu_row = cv.f(512); wsT = cv.f(512, "p (a b) -> p a b", a=4); bs_row = cv.f(512)
            wglu = cv.f(3 * 384, "p (a b) -> p a b", a=3)
            CK = ["p3c"]
            dma(gmh_row, din["gmh"][l].partition_broadcast(128), [], CK); dma(gsgu_row, din["gsgu"][l].partition_broadcast(128), [], CK)
            dma(bs_row, din["bs"][l].partition_broadcast(128), [], CK); dma(wsT, din["wsT"][l], [], CK); dma(wglu, din["wglu"][l], [], CK)
            w_l = wb["win"][l]

            SH = {}
            SH["merged"] = cv.b(KT * N3, "p (a b) -> p a b", a=KT); SH["tmpB"] = cv.f(N3)
            SH["sigo"] = cv.b(NC3 * 512, "p (c e) -> p c e", c=NC3); SH["vg"] = cv.f(NC3 * 512, "p (c e) -> p c e", c=NC3)
            SH["gsu"] = cv.b(4 * N3, "p (a b) -> p a b", a=4); SH["cb"] = cv.b(4 * N3, "p (a b) -> p a b", a=4)
            for n in ("cc", "yd"):
                SH[n] = cv.f(4 * N3, "p (a b) -> p a b", a=4)
            for n in ("ydb", "yaT", "ybT"):
                SH[n] = cv.b(4 * N3, "p (a b) -> p a b", a=4)
            SH["ysum"] = cv.f(3 * N3, "p (a b) -> p a b", a=3); SH["uT"] = cv.f(3 * N3, "p (a b) -> p a b", a=3)
            SH["hdir"] = cv.f(1024, "p (c e) -> p c e", c=2); SH["hbt"] = cv.f(1024, "p (c e) -> p c e", c=2); SH["hn"] = cv.f(1024, "p (c e) -> p c e", c=2)
            SH["ycb"] = cv.f(3 * N3, "p (a b) -> p a b", a=3); SH["sm4"] = cv.f(16)

            def alloc_set(tag):
                S = dict(SH)
                S["tag"] = tag
                S["xt"] = cv.f(KT * N3, "p (a b) -> p a b", a=KT); S["hT"] = cv.b(KT * N3, "p (a b) -> p a b", a=KT)
                S["tmpA"] = cv.f(N3); S["ys5T"] = cv.b(3 * N3, "p (a b) -> p a b", a=3)
                return S
            sets = [alloc_set("A"), alloc_set("B")]
            srr = [0]

            def lslab(w_ap, nk):
                i = srr[0] % 5
                srr[0] += 1
                key = "s3slab%d" % i
                v = slabs3[i][:, 0:nk * 128].rearrange("p (k c) -> p k c", k=nk)
                dma(v, w_ap, [], [key])
                return v, key

            def unit_gen(seg, u, NT, S):
                NCH_ = NT // 128
                tg = S["tag"]
                K_ = lambda n: (tg + n) if n in ("xt", "hT", "tmpA", "ys5T") else ("p3" + n)
                xt, hT, merged, tmpA, tmpB = S["xt"][:, :, 0:NT], S["hT"][:, :, 0:NT], S["merged"][:, :, 0:NT], S["tmpA"][:, 0:NT], S["tmpB"][:, 0:NT]
                sigo, vg = S["sigo"][:, 0:NCH_, :], S["vg"][:, 0:NCH_, :]
                gsu, cb, cc, yd, ydb = (S[n_][:, :, 0:NT] for n_ in ("gsu", "cb", "cc", "yd", "ydb"))
                yaT, ybT, ysum, ys5T, uT = (S[n_][:, :, 0:NT] for n_ in ("yaT", "ybT", "ysum", "ys5T", "uT"))
                hdir2, hbt2, hn2, ycb, sm16 = S["hdir"], S["hbt"], S["hn"], S["ycb"][:, :, 0:NT], S["sm4"]
                sqt = sqt_[:, :, 0:NT]; rs = rs_[:, 0:NT]
                sg = 1 if seg == "ctx" else 0
                t0 = u * N3
                rowlen = NT if seg == "ctx" else 64

                def norm3(scale_fn, shift_fn, extra_r, dst, dkey):
                    tt("pool", sqt, xt, xt, ALU.mult, [K_("xt")], ["sqt3"])
                    ps = PS()
                    for k in range(KT):
                        mm(ps[:, 0:NT], ones, sqt[:, k, :], k == 0, k == KT - 1, [cst, "sqt3"], [ps])
                    act(rs, ps[:, 0:NT], AF.Sqrt, [ps], ["rs3"], bias=EPS, scale=1.0 / D)
                    recip(rs, rs, ["rs3"], ["rs3"])
                    tt("dve", sqt, xt, rs.unsqueeze(1).to_broadcast([128, KT, NT]), ALU.mult, [K_("xt"), "rs3"], ["sqt3"])
                    for k in range(KT):
                        sh = shift_fn(k)
                        if sh is None:
                            act(dst[:, k, :], sqt[:, k, :], AF.Identity, ["sqt3"] + extra_r, [dkey], scale=scale_fn(k))
                        else:
                            act(dst[:, k, :], sqt[:, k, :], AF.Identity, ["sqt3"] + extra_r, [dkey], bias=sh, scale=scale_fn(k))

                def pfm(c0, ntiles, evac):
                    for j in range(ntiles):
                        v, s_ = lslab(w_l[:, c0 + j], KT)
                        ps = PS()
                        for k in range(KT):
                            mm(ps[:, 0:NT], v[:, k, :], hT[:, k, :], k == 0, k == KT - 1, [s_, K_("hT")], [ps])
                        evac(j, ps)
                        yield "y"

                def ptm(c0, ntiles, evac):
                    for j in range(ntiles):
                        v, s_ = lslab(w_l[:, c0 + j], KT)
                        for c in range(NCH_):
                            ps = PS()
                            for k in range(KT):
                                mm(ps[:, 0:128], hT[:, k, c * 128:(c + 1) * 128], v[:, k, :], k == 0, k == KT - 1, [s_, K_("hT")], [ps])
                            evac(j, c, ps)
                        yield "y"

                dma(xt, src_of[seg][:, t0:t0 + NT].rearrange("(k p) t -> p k t", p=128), [], [K_("xt")])
                dma(uT, ud[seg][:, :, t0:t0 + NT], [], [K_("uT")])
                dma(ysum, yd_[(0, seg)][:, :, t0:t0 + NT], [], [K_("ysum")])
                dma(ycb, yd_[(1, seg)][:, :, t0:t0 + NT], [], [K_("ycb")])
                yield "y"
                norm3(lambda k: lcs[:, sg, 0, k:k + 1], lambda k: modv[:, k, sg:sg + 1], [lcs, modv], hT, K_("hT"))
                yield "y"
                tt("dve", ysum, ysum, ycb, ALU.add, [K_("ysum"), K_("ycb")], [K_("ysum")])
                for j in range(3):
                    stt("dve", ysum[:, j, :], uT[:, j, :], s5d[:, j:j + 1], ysum[:, j, :], ALU.mult, ALU.add, [K_("uT"), s5d, K_("ysum")], [K_("ysum")])
                act(ysum, ysum, AF.Gelu_apprx_tanh, [K_("ysum")], [K_("ysum")])
                yield "S"
                yield from ptm(TO, 4, lambda j, c, ps: act(sigo[:, c, j * 128:(j + 1) * 128], ps[:, 0:128], AF.Sigmoid, [ps], [K_("sigo")]))
                yield from ptm(TSV, 4, lambda j, c, ps: act(vg[:, c, j * 128:(j + 1) * 128], ps[:, 0:128], AF.Gelu_apprx_tanh, [ps], [K_("vg")]))
                yield from pfm(TSU, 4, lambda j, ps: act(gsu[:, j, :], ps[:, 0:NT], AF.Gelu_apprx_tanh, [ps], [K_("gsu")]))
                yield from pfm(TCB, 4, lambda j, ps: act(cb[:, j, :], ps[:, 0:NT], AF.Copy, [ps], [K_("cb")]))
                yield from pfm(TCC, 4, lambda j, ps: act(cc[:, j, :], ps[:, 0:NT], AF.Copy, [ps], [K_("cc")]))
                yield from pfm(TCX, 4, lambda j, ps: tt("dve", cc[:, j, :], cc[:, j, :], ps[:, 0:NT], ALU.mult, [K_("cc"), ps], [K_("cc")]))
                for j in range(3):
                    ps = PS()
                    for k in range(3):
                        mm(ps[:, 0:NT], wglu[:, k, j * 128:(j + 1) * 128], ysum[:, k, :], k == 0, k == 2, CK + [K_("ysum")], [ps])
                    act(tmpA, ps[:, 0:NT], AF.Sigmoid, [ps, bglu], [K_("tmpA")], bias=bglu[:, j:j + 1])
                    tt("dve", ys5T[:, j, :], ysum[:, j, :], tmpA, ALU.mult, [K_("ysum"), K_("tmpA")], [K_("ys5T")])
                    yield "y"
                yield "S"
                conv3(yd, cc, wsconv, 0, rowlen, [K_("cc"), wsconv], [K_("yd")], NT)
                tt("pool", ydb, yd, cb, ALU.mult, [K_("yd"), K_("cb")], [K_("ydb")])
                yield "y"
                for c0 in range(0, NCH_, 2):
                    nb = min(2, NCH_ - c0)
                    csl = slice(c0, c0 + nb)
                    tsl = slice(c0 * 128, (c0 + nb) * 128)
                    tok0 = t0 + c0 * 128
                    hF = hdir2[:, 0:nb, :]; hB = hbt2[:, 0:nb, :]; hN = hn2[:, 0:nb, :]
                    dma(hF, hd[(0, seg)][tok0:tok0 + nb * 128, :].rearrange("(c p) e -> p c e", p=128), [], [K_("hdir")])
                    dma(hB, hd[(1, seg)][tok0:tok0 + nb * 128, :].rearrange("(c p) e -> p c e", p=128), [], [K_("hbt")])
                    tt("dve", hF, hF, hB, ALU.add, [K_("hdir"), K_("hbt")], [K_("hdir")])
                    act(hN, hF, AF.Square, [K_("hdir")], [K_("hn")])
                    ssq = sm16[:, 0:nb * 4]
                    red("dve", ssq, hN.rearrange("p c (h e) -> p (c h) e", h=4), ALU.add, [K_("hn")], [K_("sm4")])
                    act(ssq, ssq, AF.Sqrt, [K_("sm4")], [K_("sm4")], bias=EPS, scale=1.0 / 128)
                    recip(ssq, ssq, [K_("sm4")], [K_("sm4")])
                    yield "y"
                    tt("dve", hN.rearrange("p c (h e) -> p (c h) e", h=4), hF.rearrange("p c (h e) -> p (c h) e", h=4),
                       ssq.unsqueeze(2).to_broadcast([128, nb * 4, 128]), ALU.mult, [K_("hdir"), K_("sm4")], [K_("hn")])
                    tt("dve", hN, hN, gmh_row.unsqueeze(1).to_broadcast([128, nb, 512]), ALU.mult, [K_("hn")] + CK, [K_("hn")])
                    tt("dve", hN, hN, sigo[:, csl, :], ALU.mult, [K_("hn"), K_("sigo")], [K_("hn")])
                    vgs = vg[:, csl, :]
                    mu = sm16[:, 8:8 + nb]; vr = sm16[:, 12:12 + nb]
                    red("dve", mu, vgs, ALU.add, [K_("vg")], [K_("sm4")])
                    ts("dve", mu, mu, 1.0 / 512, None, ALU.mult, None, [K_("sm4")], [K_("sm4")])
                    tt("dve", vgs, vgs, mu.unsqueeze(2).to_broadcast([128, nb, 512]), ALU.subtract, [K_("vg"), K_("sm4")], [K_("vg")])
                    act(hB, vgs, AF.Square, [K_("vg")], [K_("hbt")])
                    yield "y"
                    for ci in range(nb):
                        c = c0 + ci
                        pT_ = PS()
                        for h in range(H):
                            tr(pT_[:, h * 128:(h + 1) * 128], hN[:, ci, h * 128:(h + 1) * 128], [K_("hn")], [pT_])
                        act(yaT[:, :, c * 128:(c + 1) * 128], pT_[:, :].rearrange("p (a b) -> p a b", a=4), AF.Copy, [pT_], [K_("yaT")])
                    red("dve", vr, hB, ALU.add, [K_("hbt")], [K_("sm4")])
                    act(vr, vr, AF.Sqrt, [K_("sm4")], [K_("sm4")], bias=EPS, scale=1.0 / 512)
                    recip(vr, vr, [K_("sm4")], [K_("sm4")])
                    tt("dve", vgs, vgs, vr.unsqueeze(2).to_broadcast([128, nb, 512]), ALU.mult, [K_("vg"), K_("sm4")], [K_("vg")])
                    tt("dve", vgs, vgs, gsgu_row.unsqueeze(1).to_broadcast([128, nb, 512]), ALU.mult, [K_("vg")] + CK, [K_("vg")])
                    yield "y"
                    for ci in range(nb):
                        c = c0 + ci
                        cs = slice(c * 128, (c + 1) * 128)
                        pM = PS()
                        for gi in range(4):
                            mm(pM[:, gi * 128:(gi + 1) * 128], vg[:, c, gi * 128:(gi + 1) * 128], wsT[:, gi, :], True, True, [K_("vg")] + CK, [pM])
                        hq = hF[:, ci, :].rearrange("p (a b) -> p a b", a=4)
                        tt("dve", hq, pM[:, :].rearrange("p (a b) -> p a b", a=4), bs_row.rearrange("p (a b) -> p a b", a=4), ALU.add, [pM] + CK, [K_("hdir")])
                        tt("dve", ybT[:, :, cs], hq, gsu[:, :, cs], ALU.mult, [K_("hdir"), K_("gsu")], [K_("ybT")])
                    yield "y"
                yield "S"
                branches = [(yaT, K_("yaT"), 4, wb["wupm"][l]), (ybT, K_("ybT"), 4, wb["wups"][l]), (ys5T, K_("ys5T"), 3, wb["wup5"][l]), (ydb, K_("ydb"), 4, wb["wupc"][l])]
                for jd in range(KT):
                    for b, (ybr, ykey, nkb, wup) in enumerate(branches):
                        vgt, sgt = lslab(w_l[:, TGATE + b * 8 + jd], KT)
                        pg = PS()
                        for k in range(KT):
                            mm(pg[:, 0:NT], vgt[:, k, :], hT[:, k, :], k == 0, k == KT - 1, [sgt, K_("hT")], [pg])
                        act(tmpA, pg[:, 0:NT], AF.Sigmoid, [pg], [K_("tmpA")])
                        vu, su_ = lslab(wup[:, jd], nkb)
                        pu = PS()
                        for k in range(nkb):
                            mm(pu[:, 0:NT], vu[:, k, :], ybr[:, k, :], k == 0, k == nkb - 1, [su_, ykey], [pu])
                        if b == 0:
                            tt("dve", tmpB, tmpA, pu[:, 0:NT], ALU.mult, [K_("tmpA"), pu], [K_("tmpB")])
                        else:
                            tt("dve", tmpA, tmpA, pu[:, 0:NT], ALU.mult, [K_("tmpA"), pu], [K_("tmpA")])
                            if b < 3:
                                tt("dve", tmpB, tmpB, tmpA, ALU.add, [K_("tmpA"), K_("tmpB")], [K_("tmpB")])
                            else:
                                tt("dve", merged[:, jd, :], tmpB, tmpA, ALU.add, [K_("tmpA"), K_("tmpB")], [K_("merged")])
                        yield "y"
                yield "S"
                for j in range(KT):
                    v, s_ = lslab(wb["wout"][l][:, j], KT)
                    ps = PS()
                    for k in range(KT):
                        mm(ps[:, 0:NT], v[:, k, :], merged[:, k, :], k == 0, k == KT - 1, [s_, K_("merged")], [ps])
                    stt("dve", xt[:, j, :], ps[:, 0:NT], modv[:, 16 + j, sg:sg + 1], xt[:, j, :], ALU.mult, ALU.add, [ps, modv, K_("xt")], [K_("xt")])
                    yield "y"
                dma(x1d[seg][:, t0:t0 + NT].rearrange("(k p) t -> p k t", p=128), xt, [K_("xt")], [("x1", seg, u)])
                yield "S"

            units = ([] if is_last else [("ctx", 0, CTX)]) + [("lat", u, N3) for u in range(T // N3)]
            gens = [unit_gen(seg, u, nt_, sets[i % 2]) for i, (seg, u, nt_) in enumerate(units)]

            def to_stage(g):
                while next(g) != "S":
                    pass

            def weave(ga, gb):
                da = db = False
                while not (da and db):
                    if not da:
                        da = next(ga) == "S"
                    if not db:
                        db = next(gb) == "S"
            for _ in range(2):
                to_stage(gens[0])
            for i in range(len(gens)):
                A = gens[i]
                Bn = gens[i + 1] if i + 1 < len(gens) else None
                for _ in range(2):
                    if Bn is None:
                        to_stage(A)
                    else:
                        weave(A, Bn)
                to_stage(A)
            phase3b(l, dst_of, is_last, units)

        NU = T // 512
        for l in range(n_layers):
            last = l == n_layers - 1
            if l == 0:
                cast_weights(0)
            layer_prep(l, first=(l == 0))
            if l + 1 < n_layers:
                cast_weights(l + 1)
            src_of = {"lat": din["xT"] if l == 0 else xs, "ctx": din["cT"] if l == 0 else csd}
            dst_of = {"lat": xs, "ctx": csd}
            phase1(l, src_of)
            phase2(l)
            phase3(l, src_of, dst_of, last)
        P.barrier()
        P.final_wait("sp", [("out", u) for u in range(NU)] + ["dbg_" + n for n in dbg_out])
        P.emit()
    return nc, dbg_out, P


def _kt(w):
    K_, C = w.shape
    return np.ascontiguousarray(w.reshape(K_ // 128, 128, C).transpose(1, 0, 2))


def _tile(w):
    K_, C = w.shape
    return np.ascontiguousarray(w.reshape(K_ // 128, 128, C // 128, 128).transpose(1, 2, 0, 3))


def _pcol(v):
    return np.ascontiguousarray(v.reshape(-1, 128).T)


def make_consts():
    c = np.zeros((128, 7, 128), np.float32)
    r = np.arange(128)[:, None]; cidx = np.arange(128)[None, :]
    c[:, 0] = (r == cidx); c[:, 1] = 1.0
    c[:, 2] = (r <= cidx); c[:, 3] = (r >= cidx)
    c[:, 4] = np.where(r <= cidx, 0.0, -BIG); c[:, 5] = np.where(r >= cidx, 0.0, -BIG)
    c[:, 6] = (cidx + 1)
    return c


def shared_inputs(inp):
    f = lambda a: np.asarray(a, np.float32)
    o = {"cst": make_consts()}
    o["wmod"] = np.stack([_kt(f(inp["w_mod"][l])) for l in range(L)])
    o["bmod"] = np.stack([_pcol(f(inp["b_mod"][l])) for l in range(L)])
    o["gnm"] = np.stack([_pcol(f(inp["g_norm_mix"][l])) for l in range(L)])
    o["gnf"] = np.stack([_pcol(f(inp["g_norm_ffn"][l])) for l in range(L)])
    o["gfin"] = _pcol(f(inp["g_final"]))
    def win_tiles(w):
        cols = []
        for c0, n in ((COL_Q, 4), (COL_K, 4), (COL_V, 4)):
            cols += [w[:, c0 + i * 128:c0 + (i + 1) * 128] for i in range(n)]
        gt = np.zeros((D, 128), np.float32); gt[:, :16] = w[:, COL_G:COL_G + 16]; cols.append(gt)
        for c0, n in ((COL_U, 3), (COL_O, 4), (COL_SU, 4), (COL_SV, 4), (COL_CB, 4), (COL_CC, 4), (COL_CX, 4), (COL_GATE, 32)):
            cols += [w[:, c0 + i * 128:c0 + (i + 1) * 128] for i in range(n)]
        t = np.stack(cols, 0)
        return np.ascontiguousarray(t.reshape(NWT, KT, 128, 128).transpose(2, 0, 1, 3))
    o["win"] = np.stack([win_tiles(f(inp["w_in"][l])) for l in range(L)])
    o["bg"] = f(inp["b_gates"]).reshape(L, 1, 16)
    wc = f(inp["w_conv_qk"])
    o["wconv"] = np.ascontiguousarray(wc.reshape(L, 2, 3, 4, 128).transpose(0, 4, 1, 2, 3).reshape(L, 128, 24))
    o["gmh"] = f(inp["g_mh"]).reshape(L, 1, 512)
    o["gsgu"] = f(inp["g_sgu"]).reshape(L, 1, 512)
    o["wsT"] = np.ascontiguousarray(f(inp["w_sgu"]).transpose(0, 3, 1, 2))
    o["bs"] = f(inp["b_sgu"]).reshape(L, 1, 512)

    def ptile(a):
        return np.ascontiguousarray(a.reshape(L, 2, 12, 128).transpose(0, 1, 3, 2))
    o["are"] = ptile(f(inp["s5_a_re"])); o["aim"] = ptile(f(inp["s5_a_im"]))
    o["ldt"] = ptile(np.repeat(f(inp["s5_log_dt"])[..., None], 64, axis=-1))

    def bpad(B):
        out = np.zeros((L, 128, 12, 128), np.float32)
        for g in range(24):
            k, gg = g // 2, g % 2
            r0 = (g % 8) * 16
            out[:, r0:r0 + 16, k, gg * 64:(gg + 1) * 64] = B[:, g].transpose(0, 2, 1)
        return out

    def cpad(C):
        out = np.zeros((L, 128, 12, 128), np.float32)
        for g in range(24):
            k, gg = g // 2, g % 2
            c0 = (g % 8) * 16
            out[:, gg * 64:(gg + 1) * 64, k, c0:c0 + 16] = C[:, g].transpose(0, 2, 1)
        return out
    o["Bre"] = bpad(f(inp["s5_b_re"])); o["Bim"] = bpad(f(inp["s5_b_im"]))
    o["Cre"] = cpad(f(inp["s5_c_re"])); o["Cim"] = cpad(f(inp["s5_c_im"]))
    o["s5d"] = np.stack([_pcol(f(inp["s5_d"][l])) for l in range(L)])
    o["bglu"] = np.stack([_pcol(f(inp["b_glu"][l])) for l in range(L)])
    o["wglu"] = np.stack([_kt(f(inp["w_glu"][l])) for l in range(L)])
    ws = f(inp["w_sconv"])
    o["wsconv"] = np.ascontiguousarray(ws.reshape(L, 3, 4, 128).transpose(0, 3, 1, 2).reshape(L, 128, 12))
    for nm, key in (("wupm", "w_up_mlstm"), ("wups", "w_up_sgu"), ("wup5", "w_up_s5"), ("wupc", "w_up_sconv"), ("wout", "w_out"),
                    ("wfg", "w_ffn_gate"), ("wfu", "w_ffn_up"), ("wfd", "w_ffn_down")):
        o[nm] = np.stack([_tile(f(inp[key][l])) for l in range(L)])
    return o


def core_inputs(inp, b, T):
    f = lambda a: np.asarray(a, np.float32)
    o = {}
    o["xT"] = np.ascontiguousarray(f(inp["x"][b, :T]).T)
    o["cT"] = np.ascontiguousarray(f(inp["ctx"][b]).T)
    sc = np.stack([_pcol(f(inp["c"][b])), _pcol(f(inp["c_ctx"]))], axis=-1)
    o["sc"] = np.ascontiguousarray(sc)
    return o


_CACHE = {}


def kernel(**inputs):
    B, T = inputs["x"].shape[0], inputs["x"].shape[1]
    if T not in _CACHE:
        _CACHE[T] = build_program(T)[0]
    nc = _CACHE[T]
    sh = shared_inputs(inputs)
    in_maps = []
    for b in range(B):
        m = dict(sh)
        m.update(core_inputs(inputs, b, T))
        in_maps.append(m)
    res = run_bass_kernel_spmd(nc, in_maps, core_ids=list(range(B)))
    out = np.stack([np.ascontiguousarray(r["outT"].T) for r in res.results], axis=0)
    return out.astype(np.float32)
```

```python
import contextlib
import numpy as np
import concourse.bass as bass
import concourse.mybir as mybir
from concourse.bass_utils import run_bass_kernel_spmd

F32 = mybir.dt.float32
I32 = mybir.dt.int32
BF16 = mybir.dt.bfloat16
ALU = mybir.AluOpType
AF = mybir.ActivationFunctionType
AX = mybir.AxisListType

ENG_NAMES = ("pe", "act", "dve", "pool", "sp")
N_DMA_SEMS = 10


class Prog:
    def __init__(self, nc, same_eng_sync=True):
        self.nc = nc
        self.same_eng_sync = same_eng_sync
        self.q = {e: [] for e in ENG_NAMES}
        self.cnt = {e: 0 for e in ENG_NAMES}
        self.waited = {e: {} for e in ENG_NAMES}
        self.last_w = {}
        self.readers = {}
        self.dma_rr = {e: 0 for e in ENG_NAMES}
        self.dma_cnt = {}
        self.sem_handles = {}
        self.pending = {e: {} for e in ENG_NAMES}
        self.n_instr = 0

    def _need(self, eng, needs, semkey, val):
        if semkey == ("c", "pe") and eng == "pe":
            return
        if (not self.same_eng_sync) and semkey == ("c", eng):
            return
        if self.waited[eng].get(semkey, 0) >= val:
            return
        if needs.get(semkey, 0) < val:
            needs[semkey] = val

    @staticmethod
    def _k(k):
        if isinstance(k, (str, int)):
            return k
        if isinstance(k, tuple):
            return tuple(Prog._k(x) for x in k)
        return k.name

    def barrier(self):
        snap = {("c", e): self.cnt[e] for e in ENG_NAMES if self.cnt[e]}
        snap.update(self.dma_cnt)
        for e in ENG_NAMES:
            for sk, v in snap.items():
                if self.pending[e].get(sk, 0) < v:
                    self.pending[e][sk] = v

    def _deps(self, eng, reads, writes):
        needs = {}
        if self.pending[eng]:
            for sk, v in self.pending[eng].items():
                self._need(eng, needs, sk, v)
            self.pending[eng] = {}
        for k in reads:
            lw = self.last_w.get(k)
            if lw:
                self._need(eng, needs, lw[0], lw[1])
        for k in writes:
            lw = self.last_w.get(k)
            if lw:
                self._need(eng, needs, lw[0], lw[1])
            for sk, v in self.readers.get(k, {}).items():
                self._need(eng, needs, sk, v)
        for sk, v in needs.items():
            self.waited[eng][sk] = v
        return list(needs.items())

    def _commit(self, semkey, val, reads, writes):
        for k in reads:
            self.readers.setdefault(k, {})[semkey] = val
        for k in writes:
            self.last_w[k] = (semkey, val)
            self.readers[k] = {}

    def op(self, eng, fn, reads=(), writes=()):
        reads = [self._k(k) for k in reads]
        writes = [self._k(k) for k in writes]
        waits = self._deps(eng, reads, writes)
        self.cnt[eng] += 1
        semkey = ("c", eng)
        self._commit(semkey, self.cnt[eng], reads, writes)
        self.q[eng].append((waits, fn, semkey, 1))
        self.n_instr += 1

    def dma(self, eng, out, in_, reads=(), writes=(), **kw):
        reads = [self._k(k) for k in reads]
        writes = [self._k(k) for k in writes]
        slot = self.dma_rr[eng]
        self.dma_rr[eng] = (slot + 1) % N_DMA_SEMS
        semkey = ("d", eng, slot)
        prev = self.dma_cnt.get(semkey, 0)
        waits = self._deps(eng, reads, writes)
        if prev and self.waited[eng].get(semkey, 0) < prev:
            waits.append((semkey, prev))
            self.waited[eng][semkey] = prev
        val = prev + 16
        self.dma_cnt[semkey] = val
        self._commit(semkey, val, reads, writes)

        def fn(e, out=out, in_=in_, kw=kw):
            return e.dma_start(out=out, in_=in_, **kw)
        self.q[eng].append((waits, fn, semkey, 16))
        self.n_instr += 1

    def final_wait(self, eng, keys):
        keys = [self._k(k) for k in keys]
        waits = self._deps(eng, keys, ())
        self.q[eng].append((waits, None, None, 0))

    def emit(self):
        nc = self.nc
        semkeys = set()
        for e in ENG_NAMES:
            for waits, fn, sk, inc in self.q[e]:
                if sk is not None:
                    semkeys.add(sk)
                for w, _ in waits:
                    semkeys.add(w)
        semkeys = sorted(semkeys, key=str)
        with contextlib.ExitStack() as st:
            for sk in semkeys:
                self.sem_handles[sk] = st.enter_context(nc.semaphore("s_" + "_".join(map(str, sk))))
            block = st.enter_context(nc.Block())

            def run(e_name):
                def body(eng):
                    for waits, fn, sk, inc in self.q[e_name]:
                        for wsk, v in waits:
                            eng.wait_ge(self.sem_handles[wsk], v)
                        if fn is not None:
                            fn(eng).then_inc(self.sem_handles[sk], inc)
                return body
            block.tensor(run("pe"))
            block.scalar(run("act"))
            block.vector(run("dve"))
            block.gpsimd(run("pool"))
            block.sync(run("sp"))


D = 1024
KT = 8
NT = 512
H = 4
CTX = 256
L = 2
FH = 2816
FT = 22
IN_DIM = 9104
COL_Q, COL_K, COL_V, COL_G, COL_U = 0, 512, 1024, 1536, 1552
COL_O, COL_SU, COL_SV, COL_CB, COL_CC, COL_CX, COL_GATE = 1936, 2448, 2960, 3472, 3984, 4496, 5008
EPS = 1e-6
BIG = 1.0e30
SLABF = 1408
NSLAB = 6
TQ, TK, TV, TG, TU, TO, TSU, TSV, TCB, TCC, TCX, TGATE = 0, 4, 8, 12, 13, 16, 20, 24, 28, 32, 36, 40
NWT = 72
PI = float(np.pi)
TWO_PI = float(2 * np.pi)

def input_shapes(T):
    return {
        "xT": [D, T], "cT": [D, CTX], "sc": [128, KT, 2], "cst": [128, 7, 128],
        "wmod": [L, 128, KT, 6 * D], "bmod": [L, 128, 48], "gnm": [L, 128, KT], "gnf": [L, 128, KT], "gfin": [128, KT],
        "win": [L, 128, NWT, KT, 128], "bg": [L, 1, 16], "wconv": [L, 128, 24], "gmh": [L, 1, 512], "gsgu": [L, 1, 512],
        "wsT": [L, 128, 4, 128], "bs": [L, 1, 512], "are": [L, 2, 128, 12], "aim": [L, 2, 128, 12], "ldt": [L, 2, 128, 12],
        "Bre": [L, 128, 12, 128], "Bim": [L, 128, 12, 128], "Cre": [L, 128, 12, 128], "Cim": [L, 128, 12, 128],
        "s5d": [L, 128, 3], "bglu": [L, 128, 3], "wglu": [L, 128, 3, 384], "wsconv": [L, 128, 12],
        "wupm": [L, 128, 8, 4, 128], "wups": [L, 128, 8, 4, 128], "wup5": [L, 128, 8, 3, 128], "wupc": [L, 128, 8, 4, 128],
        "wout": [L, 128, 8, KT, 128], "wfg": [L, 128, FT, KT, 128], "wfu": [L, 128, FT, KT, 128], "wfd": [L, 128, 8, FT, 128],
    }


def build_program(T, n_layers=L, dbg=()):
    nc = bass.Bass("TRN2", target_bir_lowering=False)
    din = {n: nc.dram_tensor(n, s, F32, kind="ExternalInput").ap() for n, s in input_shapes(T).items()}
    outT = nc.dram_tensor("outT", [D, T], F32, kind="ExternalOutput").ap()
    dbg_out = {}
    TS = {"lat": T, "ctx": CTX}
    xs = nc.dram_tensor("xs", [D, T], F32).ap()
    csd = nc.dram_tensor("csd", [D, CTX], F32).ap()
    qd = {sg: nc.dram_tensor("qd_" + sg, [128, 4, TS[sg]], F32).ap() for sg in TS}
    kd = {sg: nc.dram_tensor("kd_" + sg, [128, 4, TS[sg]], F32).ap() for sg in TS}
    vd = {sg: nc.dram_tensor("vd_" + sg, [TS[sg], 512], F32).ap() for sg in TS}
    gd = {sg: nc.dram_tensor("gd_" + sg, [TS[sg], 16], F32).ap() for sg in TS}
    ud = {sg: nc.dram_tensor("ud_" + sg, [128, 3, TS[sg]], F32).ap() for sg in TS}
    x1d = {sg: nc.dram_tensor("x1d_" + sg, [D, TS[sg]], F32).ap() for sg in TS}
    hd = {(d, sg): nc.dram_tensor(f"hd{d}_{sg}", [TS[sg], 512], F32).ap() for sg in TS for d in (0, 1)}
    yd_ = {(d, sg): nc.dram_tensor(f"yd{d}_{sg}", [128, 3, TS[sg]], F32).ap() for sg in TS for d in (0, 1)}
    BIGW = ("win", "wupm", "wups", "wup5", "wupc", "wout", "wfg", "wfu", "wfd")
    wb = {n: nc.dram_tensor("wb_" + n, input_shapes(T)[n], BF16).ap() for n in BIGW}

    with contextlib.ExitStack() as st:
        def sb(name, shape, dt=F32):
            return st.enter_context(nc.sbuf_tensor("sb_" + name, shape, dt))
        P = Prog(nc)
        cst = sb("cst", [128, 7, 128])
        ident, ones, Umat, Lmat, biasU, biasL, iota1 = (cst[:, i, :] for i in range(7))
        sc = sb("sc", [128, KT, 2]); modv = sb("modv", [128, 48, 2]); lcs = sb("lcs", [128, 2, 2, KT])
        gnm = sb("gnm", [128, KT]); gnf = sb("gnf", [128, KT]); gfin = sb("gfin", [128, KT]); bmod = sb("bmod", [128, 48])
        bg_row = sb("bg_row", [128, 16]); wconv = sb("wconv", [128, 24]); wsconv = sb("wsconv", [128, 12])
        s5d = sb("s5d", [128, 3]); bglu = sb("bglu", [128, 3])
        rhod = [sb(f"rho{d}", [128, 12]) for d in range(2)]
        card = [sb(f"car{d}", [128, 2, 12]) for d in range(2)]
        Caugd = [sb(f"Caug{d}", [128, H, 129]) for d in range(2)]
        mstd = [sb(f"mst{d}", [128, H]) for d in range(2)]
        s5p = sb("s5p", [128, 16, 12])
        NA = 50000
        ARENA = sb("ARENA", [128, NA])
        TABF = 3072 + 1536 + 32
        _tb = [NA - (2 - d) * TABF for d in range(2)]
        t16d = [ARENA[:, _tb[d]:_tb[d] + 3072].bitcast(BF16).rearrange("p (t a b) -> p t a b", t=4, a=12) for d in range(2)]
        rhoFd = [ARENA[:, _tb[d] + 3072:_tb[d] + 4608].rearrange("p (a b) -> p a b", a=12) for d in range(2)]
        rcsd = [ARENA[:, _tb[d] + 4608:_tb[d] + 4632].rearrange("p (a b) -> p a b", a=2) for d in range(2)]
        TABK = ["tab0", "tab1"]
        pbs = [st.enter_context(nc.psum_tensor(f"pb{i}", [128, 512], F32)) for i in range(8)]
        ps_rr = [0]

        def PS():
            t = pbs[ps_rr[0] % 8]
            ps_rr[0] += 1
            return t

        class Carver:
            def __init__(self, lim=None):
                self.off = 0
                self.lim = NA if lim is None else lim

            def f(self, n, pat=None, **kw):
                ap = ARENA[:, self.off:self.off + n]
                self.off += n
                assert self.off <= self.lim, self.off
                return ap.rearrange(pat, **kw) if pat else ap

            def b(self, n, pat=None, **kw):
                nf = (n + 1) // 2
                ap = ARENA[:, self.off:self.off + nf].bitcast(BF16)
                self.off += nf
                assert self.off <= self.lim, self.off
                return ap.rearrange(pat, **kw) if pat else ap

        def tt(eng, out, a, b, op, r, w):
            P.op(eng, lambda e: e.tensor_tensor(out, a, b, op), reads=r, writes=w)

        def ts(eng, out, a, s1, s2, op0, op1, r, w):
            if s2 is None:
                P.op(eng, lambda e: e.tensor_scalar(out, a, s1, None, op0=op0), reads=r, writes=w)
            else:
                P.op(eng, lambda e: e.tensor_scalar(out, a, s1, s2, op0=op0, op1=op1), reads=r, writes=w)

        def stt(eng, out, in0, scalar, in1, op0, op1, r, w):
            P.op(eng, lambda e: e.scalar_tensor_tensor(out, in0, scalar, in1, op0=op0, op1=op1), reads=r, writes=w)

        def act(out, in_, func, r, w, bias=0.0, scale=1.0):
            P.op("act", lambda e: e.activation(out, in_, func, bias=bias, scale=scale), reads=r, writes=w)

        def mm(ps, lhsT, rhs, start, stop, r, w):
            P.op("pe", lambda e: e.matmul(ps, lhsT, rhs, start=start, stop=stop), reads=r, writes=w)

        def tr(ps, in_, r, w):
            P.op("pe", lambda e: e.transpose(ps, in_, ident), reads=r + [cst], writes=w)

        def red(eng, out, in_, op, r, w):
            P.op(eng, lambda e: e.tensor_reduce(out, in_, axis=AX.X, op=op), reads=r, writes=w)

        def recip(out, in_, r, w):
            P.op("dve", lambda e: e.reciprocal(out, in_), reads=r, writes=w)

        def cp(eng, out, in_, r, w):
            P.op(eng, lambda e: e.tensor_copy(out, in_), reads=r, writes=w)

        def mset(eng, out, val, w):
            P.op(eng, lambda e: e.memset(out, val), writes=w)

        def dma(out, in_, r, w, eng="sp"):
            P.dma(eng, out, in_, reads=r, writes=w)

        def DBG(name, ap, shape, keys):
            if name in dbg and name not in dbg_out:
                t = nc.dram_tensor("dbg_" + name, shape, F32, kind="ExternalOutput").ap()
                dbg_out[name] = t
                dma(t, ap, keys, ["dbg_" + name])

        def cast_weights(l):
            for n in BIGW:
                src = din[n][l]; dstw = wb[n][l]
                nd = len(src.shape)
                letters = "abcd"[:nd - 1]
                pat = "p " + " ".join(letters) + " -> p (" + " ".join(letters) + ")"
                s2 = src.rearrange(pat); d2 = dstw.rearrange(pat)
                N = s2.shape[1]
                for c0 in range(0, N, 8192):
                    c1 = min(N, c0 + 8192)
                    dma(d2[:, c0:c1], s2[:, c0:c1], [], [("wb", n, l, c0)], eng="pool")

        def range_reduce(out, in_, ti, tm, keys_in, keys_out, kti, ktm):
            ts("dve", tm, in_, 1.0 / TWO_PI, None, ALU.mult, None, keys_in, [ktm])
            cp("dve", ti, tm, [ktm], [kti])
            cp("dve", tm, ti, [kti], [ktm])
            stt("dve", out, tm, -TWO_PI, in_, ALU.mult, ALU.add, [ktm] + keys_in, keys_out)
            P.op("dve", lambda e: e.tensor_single_scalar(tm, out, PI, op=ALU.is_gt), reads=keys_out, writes=[ktm])
            stt("dve", out, tm, -TWO_PI, out, ALU.mult, ALU.add, [ktm] + keys_out, keys_out)
            P.op("dve", lambda e: e.tensor_single_scalar(tm, out, -PI, op=ALU.is_lt), reads=keys_out, writes=[ktm])
            stt("dve", out, tm, TWO_PI, out, ALU.mult, ALU.add, [ktm] + keys_out, keys_out)
            ts("dve", out, out, PI, -PI, ALU.min, ALU.max, keys_out, keys_out)

        dma(cst[:], din["cst"], [], [cst])
        dma(sc[:], din["sc"], [], [sc])
        dma(gfin[:], din["gfin"], [], [gfin])
        act(sc[:], sc[:], AF.Silu, [sc], [sc])

        def layer_prep(l, first=False):
            if not first:
                P.barrier()
            cv = Carver(NA - 2 * TABF)
            wm = [cv.f(1024, "p (k c) -> p k c", k=KT) for _ in range(2)]
            ang = cv.f(1536, "p (a b) -> p a b", a=12); tsn = cv.f(1536, "p (a b) -> p a b", a=12); tcs = cv.f(1536, "p (a b) -> p a b", a=12)
            tmf = cv.f(1536); ti32 = cv.f(1536).bitcast(I32)
            Ts32 = cv.f(1536, "p (a b) -> p a b", a=12); Tc32 = cv.f(1536, "p (a b) -> p a b", a=12)
            for nm, t in (("gnm", gnm), ("gnf", gnf), ("bmod", bmod), ("wconv", wconv), ("wsconv", wsconv),
                          ("s5d", s5d), ("bglu", bglu)):
                dma(t[:], din[nm][l], [], [t])
            dma(bg_row[:], din["bg"][l].partition_broadcast(128), [], [bg_row])
            for ft in range(48):
                s = "wm%d" % (ft % 2)
                v = wm[ft % 2]
                dma(v, din["wmod"][l][:, :, ft * 128:(ft + 1) * 128], [], [s])
                ps = PS()
                for k in range(KT):
                    mm(ps[:, 0:2], v[:, k, :], sc[:, k, :], k == 0, k == KT - 1, [s, sc], [ps])
                ts("dve", modv[:, ft, :], ps[:, 0:2], bmod[:, ft:ft + 1], None, ALU.add, None, [ps, bmod], [modv])
            for sg in range(2):
                stt("dve", lcs[:, sg, 0, :], modv[:, 8:16, sg], 1.0, gnm[:], ALU.add, ALU.mult, [modv, gnm], [lcs])
                stt("dve", lcs[:, sg, 1, :], modv[:, 32:40, sg], 1.0, gnf[:], ALU.add, ALU.mult, [modv, gnf], [lcs])
            K = ["prep"]
            for d in range(2):
                t16 = t16d[d]; rhoF = rhoFd[d]; rcs = rcsd[d]; rho = rhod[d]; TK_ = TABK[d]
                are = s5p[:, 0, :]; aim = s5p[:, 1, :]; ldt = s5p[:, 2, :]
                dma(are, din["are"][l, d], [], K); dma(aim, din["aim"][l, d], [], K); dma(ldt, din["ldt"][l, d], [], K)
                dt_ = s5p[:, 3, :]; th = s5p[:, 4, :]; lar = s5p[:, 5, :]; sn = s5p[:, 6, :]; cs_ = s5p[:, 7, :]
                t0 = s5p[:, 8, :]; t1 = s5p[:, 9, :]; den = s5p[:, 10, :]; fr = s5p[:, 11, :]; fi = s5p[:, 12, :]
                nr = s5p[:, 13, :]; ni = s5p[:, 14, :]; t2 = s5p[:, 15, :]
                act(dt_, ldt, AF.Exp, K, K)
                tt("dve", th, dt_, aim, ALU.mult, K, K)
                tt("dve", lar, dt_, are, ALU.mult, K, K)
                act(rho[:], lar, AF.Exp, K, [rho] + K)
                range_reduce(t0, th, ti32[:, 0:12], t2, K, K, "ti32", "prep")
                act(sn, t0, AF.Sin, K, K)
                ts("dve", t1, th, PI / 2, None, ALU.add, None, K, K)
                range_reduce(t0, t1, ti32[:, 0:12], t2, K, K, "ti32", "prep")
                act(cs_, t0, AF.Sin, K, K)
                tt("dve", nr, rho[:], cs_, ALU.mult, [rho] + K, K)
                ts("dve", nr, nr, -1.0, None, ALU.add, None, K, K)
                tt("dve", ni, rho[:], sn, ALU.mult, [rho] + K, K)
                tt("dve", den, are, are, ALU.mult, K, K)
                tt("dve", t0, aim, aim, ALU.mult, K, K)
                tt("dve", den, den, t0, ALU.add, K, K)
                recip(den, den, K, K)
                tt("dve", t0, nr, are, ALU.mult, K, K); tt("dve", t1, ni, aim, ALU.mult, K, K)
                tt("dve", t0, t0, t1, ALU.add, K, K); tt("dve", fr, t0, den, ALU.mult, K, K)
                tt("dve", t0, ni, are, ALU.mult, K, K); tt("dve", t1, nr, aim, ALU.mult, K, K)
                tt("dve", t0, t0, t1, ALU.subtract, K, K); tt("dve", fi, t0, den, ALU.mult, K, K)
                thb = th.unsqueeze(2).to_broadcast([128, 12, 128])
                io = iota1.unsqueeze(1).to_broadcast([128, 12, 128])
                tt("dve", ang, io, thb, ALU.mult, [cst] + K, K)
                angf = ang.rearrange("p a b -> p (a b)")
                range_reduce(tsn.rearrange("p a b -> p (a b)"), angf, ti32, tmf, K, K, "ti32", "prep")
                act(Ts32, tsn, AF.Sin, K, K)
                ts("dve", angf, angf, PI / 2, None, ALU.add, None, K, K)
                range_reduce(tcs.rearrange("p a b -> p (a b)"), angf, ti32, tmf, K, K, "ti32", "prep")
                act(Tc32, tcs, AF.Sin, K, K)
                frb = fr.unsqueeze(2).to_broadcast([128, 12, 128]); fib = fi.unsqueeze(2).to_broadcast([128, 12, 128])
                tt("dve", ang, Tc32, frb, ALU.mult, K, K)
                tt("dve", tsn, Ts32, fib, ALU.mult, K, K)
                tt("dve", t16[:, 0, :, :], ang, tsn, ALU.add, K, [TK_])
                tt("dve", ang, Tc32, fib, ALU.mult, K, K)
                tt("dve", tsn, Ts32, frb, ALU.mult, K, K)
                tt("dve", t16[:, 1, :, :], ang, tsn, ALU.subtract, K, [TK_])
                cp("dve", t16[:, 2, :, :], Tc32, K, [TK_])
                cp("dve", t16[:, 3, :, :], Ts32, K, [TK_])
                cp("dve", rhoF, rho[:].unsqueeze(2).to_broadcast([128, 12, 128]), [rho], [TK_])
                mset("dve", rhoF[:, :, 0:1], 0.0, [TK_])
                tt("dve", rcs[:, 0, :], rho[:], Tc32[:, :, 127], ALU.mult, [rho] + K, [TK_])
                tt("dve", rcs[:, 1, :], rho[:], Ts32[:, :, 127], ALU.mult, [rho] + K, [TK_])
            P.barrier()

        def make_rowlocal(cv, nslab):
            V = {"nt": NT}
            V["xt_"] = cv.f(KT * NT, "p (a b) -> p a b", a=KT); V["sqt_"] = cv.f(KT * NT, "p (a b) -> p a b", a=KT)
            V["hT_"] = cv.b(KT * NT, "p (a b) -> p a b", a=KT); V["rs_"] = cv.f(NT)
            V["slabs"] = [cv.b(SLABF) for _ in range(nslab)]
            set_nt(V, NT)
            return V

        def set_nt(V, nt):
            V["nt"] = nt
            V["xt"] = V["xt_"][:, :, 0:nt]; V["sqt"] = V["sqt_"][:, :, 0:nt]; V["hT"] = V["hT_"][:, :, 0:nt]; V["rs"] = V["rs_"][:, 0:nt]

        slab_rr = [0]

        def load_slab(V, w_ap, nk):
            i = slab_rr[0] % len(V["slabs"])
            slab_rr[0] += 1
            s = V["slabs"][i]
            key = "slab%d" % i
            v = s[:, 0:nk * 128].rearrange("p (k c) -> p k c", k=nk)
            dma(v, w_ap, [], [key])
            return v, key

        def norm(V, scale_fn, shift_fn, extra_r, dst, dkey):
            xt, sqt, rs = V["xt"], V["sqt"], V["rs"]
            NT = V["nt"]
            tt("pool", sqt, xt, xt, ALU.mult, ["xt"], ["sqt"])
            ps = PS()
            for k in range(KT):
                mm(ps[:, 0:NT], ones, sqt[:, k, :], k == 0, k == KT - 1, [cst, "sqt"], [ps])
            act(rs, ps[:, 0:NT], AF.Sqrt, [ps], ["rs"], bias=EPS, scale=1.0 / D)
            recip(rs, rs, ["rs"], ["rs"])
            tt("dve", sqt, xt, rs.unsqueeze(1).to_broadcast([128, KT, NT]), ALU.mult, ["xt", "rs"], ["sqt"])
            for k in range(KT):
                sh = shift_fn(k)
                if sh is None:
                    act(dst[:, k, :], sqt[:, k, :], AF.Identity, ["sqt"] + extra_r, [dkey], scale=scale_fn(k))
                else:
                    act(dst[:, k, :], sqt[:, k, :], AF.Identity, ["sqt"] + extra_r, [dkey], bias=sh, scale=scale_fn(k))

        def proj_fm(V, w_l, c0, ntiles, nk, rhs_t, rhs_key, evac):
            NT = V["nt"]
            for j in range(ntiles):
                v, s = load_slab(V, w_l[:, c0 + j], nk)
                ps = PS()
                for k in range(nk):
                    mm(ps[:, 0:NT], v[:, k, :], rhs_t[:, k, :], k == 0, k == nk - 1, [s, rhs_key], [ps])
                evac(j, ps)

        def proj_tm(V, w_l, c0, ntiles, evac, ncols=128):
            hT = V["hT"]
            NCH = V["nt"] // 128
            for j in range(ntiles):
                v, s = load_slab(V, w_l[:, c0 + j], KT)
                for c in range(NCH):
                    ps = PS()
                    for k in range(KT):
                        mm(ps[:, 0:ncols], hT[:, k, c * 128:(c + 1) * 128], v[:, k, 0:ncols], k == 0, k == KT - 1, [s, "hT"], [ps])
                    evac(j, c, ps)

        def conv3(dst, src, wv, woff, rowlen, r, w, NT):
            nr_ = NT // rowlen
            for j in range(4):
                w0 = wv[:, woff + 0 + j:woff + 1 + j]; w1 = wv[:, woff + 4 + j:woff + 5 + j]; w2 = wv[:, woff + 8 + j:woff + 9 + j]
                d3 = dst[:, j, :].rearrange("p (r c) -> p r c", r=nr_); s3 = src[:, j, :].rearrange("p (r c) -> p r c", r=nr_)
                act(dst[:, j, :], src[:, j, :], AF.Identity, r, w, scale=w1)
                stt("dve", d3[:, :, 1:rowlen], s3[:, :, 0:rowlen - 1], w0, d3[:, :, 1:rowlen], ALU.mult, ALU.add, r + w, w)
                stt("dve", d3[:, :, 0:rowlen - 1], s3[:, :, 1:rowlen], w2, d3[:, :, 0:rowlen - 1], ALU.mult, ALU.add, r + w, w)

        def phase1(l, src_of):
            P.barrier()
            cv = Carver(NA - 2 * TABF)
            NTF = 512; NCF = 4
            sqt_ = cv.f(KT * NTF, "p (a b) -> p a b", a=KT); rs_ = cv.f(NTF)
            slabs1 = [cv.b(SLABF) for _ in range(9)]
            rawq_ = cv.f(4 * NTF, "p (a b) -> p a b", a=4); rawk_ = cv.f(4 * NTF, "p (a b) -> p a b", a=4)
            qT_ = cv.f(4 * NTF, "p (a b) -> p a b", a=4); kT_ = cv.f(4 * NTF, "p (a b) -> p a b", a=4)
            vtm_ = cv.f(NCF * 512, "p (c e) -> p c e", c=NCF); uT_ = cv.f(3 * NTF, "p (a b) -> p a b", a=3); graw_ = cv.f(NCF * 16, "p (c e) -> p c e", c=NCF)
            sets = [{"tag": tg, "xt": cv.f(KT * NTF, "p (a b) -> p a b", a=KT), "hT": cv.b(KT * NTF, "p (a b) -> p a b", a=KT)} for tg in ("A", "B")]
            w_l = wb["win"][l]
            srr = [0]

            def lslab(w_ap, nk):
                i = srr[0] % 9
                srr[0] += 1
                key = "s1slab%d" % i
                v = slabs1[i][:, 0:nk * 128].rearrange("p (k c) -> p k c", k=nk)
                dma(v, w_ap, [], [key])
                return v, key

            def unit_gen(seg, u, NT, S):
                NCH = NT // 128
                tg = S["tag"]
                kx, kh = "p1xt" + tg, "p1hT" + tg
                xt, hT = S["xt"][:, :, 0:NT], S["hT"][:, :, 0:NT]
                sqt = sqt_[:, :, 0:NT]; rs = rs_[:, 0:NT]
                rawq = rawq_[:, :, 0:NT]; rawk = rawk_[:, :, 0:NT]; qT = qT_[:, :, 0:NT]; kT = kT_[:, :, 0:NT]
                vtm = vtm_[:, 0:NCH, :]; uT = uT_[:, :, 0:NT]; graw = graw_[:, 0:NCH, :]
                sg = 1 if seg == "ctx" else 0
                t0 = u * NTF
                rowlen = NT if seg == "ctx" else 64
                dma(xt, src_of[seg][:, t0:t0 + NT].rearrange("(k p) t -> p k t", p=128), [], [kx])
                yield "y"
                act(sqt, xt, AF.Square, [kx], ["p1sqt"])
                ps = PS()
                for k in range(KT):
                    mm(ps[:, 0:NT], ones, sqt[:, k, :], k == 0, k == KT - 1, [cst, "p1sqt"], [ps])
                act(rs, ps[:, 0:NT], AF.Sqrt, [ps], ["p1rs"], bias=EPS, scale=1.0 / D)
                recip(rs, rs, ["p1rs"], ["p1rs"])
                yield "y"
                tt("dve", sqt, xt, rs.unsqueeze(1).to_broadcast([128, KT, NT]), ALU.mult, [kx, "p1rs"], ["p1sqt"])
                for k in range(KT):
                    act(hT[:, k, :], sqt[:, k, :], AF.Identity, ["p1sqt", lcs, modv], [kh], bias=modv[:, k, sg:sg + 1], scale=lcs[:, sg, 0, k:k + 1])
                yield "S"

                def pfm(c0, ntiles, dst, dkey):
                    for j in range(ntiles):
                        v, s_ = lslab(w_l[:, c0 + j], KT)
                        ps = PS()
                        for k in range(KT):
                            mm(ps[:, 0:NT], v[:, k, :], hT[:, k, :], k == 0, k == KT - 1, [s_, kh], [ps])
                        act(dst[:, j, :], ps[:, 0:NT], AF.Copy, [ps], [dkey])
                        yield "y"

                def ptm(c0, ntiles, evac, ncols=128):
                    for j in range(ntiles):
                        v, s_ = lslab(w_l[:, c0 + j], KT)
                        for c in range(NCH):
                            ps = PS()
                            for k in range(KT):
                                mm(ps[:, 0:ncols], hT[:, k, c * 128:(c + 1) * 128], v[:, k, 0:ncols], k == 0, k == KT - 1, [s_, kh], [ps])
                            evac(j, c, ps)
                        yield "y"
                yield from pfm(TQ, 4, rawq, "p1rawq")
                yield from pfm(TK, 4, rawk, "p1rawk")
                conv3(qT, rawq, wconv, 0, rowlen, ["p1rawq", wconv], ["p1qT"], NT)
                act(qT, qT, AF.Silu, ["p1qT"], ["p1qT"])
                dma(qd[seg][:, :, t0:t0 + NT], qT, ["p1qT"], [("q", seg, u)])
                yield "y"
                yield from ptm(TV, 4, lambda j, c, ps: act(vtm[:, c, j * 128:(j + 1) * 128], ps[:, 0:128], AF.Copy, [ps], ["p1vtm"]))
                dma(vd[seg][t0:t0 + NT, :].rearrange("(c p) e -> p c e", p=128), vtm, ["p1vtm"], [("v", seg, u)])
                conv3(kT, rawk, wconv, 12, rowlen, ["p1rawk", wconv], ["p1kT"], NT)
                act(kT, kT, AF.Silu, ["p1kT"], ["p1kT"], )
                ts("dve", kT, kT, float(128 ** -0.5), None, ALU.mult, None, ["p1kT"], ["p1kT"])
                dma(kd[seg][:, :, t0:t0 + NT], kT, ["p1kT"], [("k", seg, u)])
                yield "y"
                yield from ptm(TG, 1, lambda j, c, ps: tt("dve", graw[:, c, :], ps[:, 0:16], bg_row[:], ALU.add, [ps, bg_row], ["p1graw"]), ncols=16)
                dma(gd[seg][t0:t0 + NT, :].rearrange("(c p) e -> p c e", p=128), graw, ["p1graw"], [("g", seg, u)])
                yield from pfm(TU, 3, uT, "p1uT")
                dma(ud[seg][:, :, t0:t0 + NT], uT, ["p1uT"], [("u", seg, u)])
                yield "S"

            units = [("ctx", 0, CTX)] + [("lat", u, NTF) for u in range(T // NTF)]
            gens = [unit_gen(seg, u, nt_, sets[i % 2]) for i, (seg, u, nt_) in enumerate(units)]

            def to_stage(g):
                while next(g) != "S":
                    pass

            def weave(ga, gb):
                da = db = False
                while not (da and db):
                    if not da:
                        da = next(ga) == "S"
                    if not db:
                        db = next(gb) == "S"
            to_stage(gens[0])
            for i in range(len(gens)):
                if i + 1 < len(gens):
                    weave(gens[i], gens[i + 1])
                else:
                    to_stage(gens[i])

        def phase2(l):
            P.barrier()
            cv = Carver(NA - 2 * TABF)
            nlat = T // 128
            chains = {0: [("ctx", c) for c in range(CTX // 128)] + [("lat", c) for c in range(nlat)],
                      1: [("ctx", c) for c in range(CTX // 128 - 1, -1, -1)] + [("lat", c) for c in range(nlat - 1, -1, -1)]}
            for d in range(2):
                mset("pool", Caugd[d][:], 0.0, [Caugd[d]]); mset("pool", mstd[d][:], 0.0, [mstd[d]]); mset("pool", card[d][:], 0.0, [card[d]])

            def unit_of(seg, c):
                return (seg, 0)

            Bre = cv.b(1536, "p (a b) -> p a b", a=12); Bim = cv.b(1536, "p (a b) -> p a b", a=12)
            Cre = cv.b(1536, "p (a b) -> p a b", a=12); Cimn = cv.b(1536, "p (a b) -> p a b", a=12)
            dma(Bre, din["Bre"][l], [], ["Bre"], eng="pool"); dma(Bim, din["Bim"][l], [], ["Bim"], eng="pool")
            dma(Cre, din["Cre"][l], [], ["Cre"], eng="pool"); dma(Cimn, din["Cim"][l], [], ["Cimn"], eng="pool")
            ts("dve", Cimn, Cimn, -1.0, None, ALU.mult, None, ["Cimn"], ["Cimn"])

            def mlstm_stream(d):
                bwd = d == 1
                nm = "m%d" % d
                qc = [cv.f(512, "p (a b) -> p a b", a=4) for _ in range(2)]; kc = [cv.f(512, "p (a b) -> p a b", a=4) for _ in range(2)]
                va = [cv.f(516, "p (a b) -> p a b", a=4) for _ in range(2)]; gc = [cv.f(16) for _ in range(2)]
                gsm = cv.f(64, "p (a b) -> p a b", a=16); ex = cv.f(20, "p (a b) -> p a b", a=5)
                gdt = cv.f(512, "p (a b) -> p a b", a=4); tmt = cv.f(512, "p (a b) -> p a b", a=4)
                vw = cv.f(516, "p (a b) -> p a b", a=4); PTa = cv.f(512, "p (a b) -> p a b", a=4); ktm = cv.f(512, "p (a b) -> p a b", a=4)
                r1 = cv.f(258, "p (a b) -> p a b", a=2); tmx = cv.f(258, "p (a b) -> p a b", a=2); rra = cv.f(516, "p (a b) -> p a b", a=4)
                dn = cv.f(16, "p (a b) -> p a b", a=4); hdir = cv.f(512, "p (a b) -> p a b", a=4); Ctmp = cv.f(516, "p (a b) -> p a b", a=4)
                b0 = pbs[2 * d]; b1 = pbs[2 * d + 1]
                Caug = Caugd[d]; mst = mstd[d]
                io, fo = (8, 12) if bwd else (0, 4)
                Tri = Lmat if bwd else Umat
                biasM = biasU if bwd else biasL
                K_ = lambda s_: nm + s_
                chain = chains[d]
                for b_ in range(2):
                    mset("pool", va[b_][:, :, 128:129], 1.0, [K_("va%d" % b_)])

                def load(i):
                    seg, c = chain[i]
                    b_ = i % 2
                    tok = c * 128
                    su = unit_of(seg, c)
                    dma(qc[b_], qd[seg][:, :, tok:tok + 128], [("q",) + su], [K_("qc%d" % b_)])
                    dma(kc[b_], kd[seg][:, :, tok:tok + 128], [("k",) + su], [K_("kc%d" % b_)])
                    dma(va[b_][:, :, 0:128], vd[seg][tok:tok + 128, :].rearrange("p (h e) -> p h e", h=4), [("v",) + su], [K_("va%d" % b_)])
                    dma(gc[b_], gd[seg][tok:tok + 128, :], [("g",) + su], [K_("gc%d" % b_)])
                load(0)
                for i, (seg, c) in enumerate(chain):
                    b_ = i % 2
                    if i + 1 < len(chain):
                        load(i + 1)
                    q_, k_, v_, g_ = qc[b_], kc[b_], va[b_], gc[b_]
                    kq, kk_, kv, kg = K_("qc%d" % b_), K_("kc%d" % b_), K_("va%d" % b_), K_("gc%d" % b_)
                    G = [K_("gsm")]; EX = [K_("ex")]
                    e1 = gsm[:, 0, :]; sp = gsm[:, 1, :]; g = gsm[:, 2, :]; gmax = gsm[:, 3, :]; cmax = gsm[:, 4, :]
                    Mx = gsm[:, 5, :]; M = gsm[:, 6, :]
                    act(e1, g_[:, fo:fo + 4], AF.Exp, [kg], G, scale=-1.0)
                    act(sp, e1, AF.Ln, G, G, bias=1.0)
                    mm(b0[:, 0:4], Tri, sp, True, True, [cst] + G, [b0])
                    mm(b0[:, 4:8], ones, sp, True, True, [cst] + G, [b0])
                    yield
                    tt("dve", g, g_[:, io:io + 4], b0[:, 0:4], ALU.add, [kg, b0], G)
                    tt("dve", gdt, ident.unsqueeze(1).to_broadcast([128, 4, 128]), g.unsqueeze(2).to_broadcast([128, 4, 128]), ALU.mult, [cst] + G, [K_("gdt")])
                    mm(b1[:, :], ones, gdt.rearrange("p a b -> p (a b)"), True, True, [cst, K_("gdt")], [b1])
                    yield
                    b1v = b1[:, :].rearrange("p (a b) -> p a b", a=4)
                    red("dve", gmax, b1v, ALU.max, [b1], G)
                    tt("dve", tmt, b1v, biasM.unsqueeze(1).to_broadcast([128, 4, 128]), ALU.add, [b1, cst], [K_("tmt")])
                    red("dve", cmax, tmt, ALU.max, [K_("tmt")], G)
                    tt("dve", Mx, mst[:], gmax, ALU.max, [mst] + G, G)
                    tt("dve", M, mst[:], cmax, ALU.max, [mst] + G, G)
                    tt("dve", ex[:, 0, :], g, Mx, ALU.subtract, G, EX)
                    tt("dve", ex[:, 1, :], Mx, M, ALU.subtract, G, EX)
                    tt("dve", ex[:, 2, :], mst[:], M, ALU.subtract, [mst] + G, EX)
                    tt("dve", ex[:, 3, :], mst[:], Mx, ALU.subtract, [mst] + G, EX)
                    tt("dve", ex[:, 4, :], b0[:, 0:4], M, ALU.subtract, [b0] + G, EX)
                    act(ex, ex, AF.Exp, EX, EX)
                    tt("dve", mst[:], Mx, b0[:, 4:8], ALU.subtract, G + [b0], [mst])
                    yield
                    tt("dve", vw, v_, ex[:, 0, :].unsqueeze(2).to_broadcast([128, 4, 129]), ALU.mult, [kv] + EX, [K_("vw")])
                    for h in range(H):
                        mm(b0[:, h * 128:(h + 1) * 128], k_[:, h, :], q_[:, h, :], True, True, [kk_, kq], [b0])
                    for h in range(H):
                        tr(b1[:, h * 128:(h + 1) * 128], k_[:, h, :], [kk_], [b1])
                    yield
                    maskPT = (Lmat if bwd else Umat).unsqueeze(1).to_broadcast([128, 4, 128])
                    tt("dve", PTa, b0[:, :].rearrange("p (a b) -> p a b", a=4), maskPT, ALU.mult, [b0, cst], [K_("PTa")])
                    act(ktm, b1[:, :].rearrange("p (a b) -> p a b", a=4), AF.Copy, [b1], [K_("ktm")])
                    yield
                    for pr in range(2):
                        for hh in range(2):
                            h = 2 * pr + hh
                            mm(b0[:, hh * 129:(hh + 1) * 129], PTa[:, h, :], vw[:, h, :], True, True, [K_("PTa"), K_("vw")], [b0])
                            mm(b1[:, hh * 129:(hh + 1) * 129], q_[:, h, :], Caug[:, h, :], True, True, [kq, Caug], [b1])
                        yield
                        hs = slice(2 * pr, 2 * pr + 2)
                        b0p = b0[:, 0:258].rearrange("p (a b) -> p a b", a=2); b1p = b1[:, 0:258].rearrange("p (a b) -> p a b", a=2)
                        tt("dve", r1, b1p, ex[:, 2, hs].unsqueeze(2).to_broadcast([128, 2, 129]), ALU.mult, [b1] + EX, [K_("r1")])
                        tt("dve", tmx, b0p, ex[:, 1, hs].unsqueeze(2).to_broadcast([128, 2, 129]), ALU.mult, [b0] + EX, [K_("tmx")])
                        tt("dve", rra[:, hs, :], tmx, r1, ALU.add, [K_("tmx"), K_("r1")], [K_("rra")])
                        yield
                    den = rra[:, :, 128]
                    ts("dve", dn[:, 0, :], den, -1.0, None, ALU.mult, None, [K_("rra")], [K_("dn")])
                    tt("dve", dn[:, 0, :], dn[:, 0, :], den, ALU.max, [K_("dn"), K_("rra")], [K_("dn")])
                    tt("dve", dn[:, 0, :], dn[:, 0, :], ex[:, 4, :], ALU.max, [K_("dn")] + EX, [K_("dn")])
                    recip(dn[:, 1, :], dn[:, 0, :], [K_("dn")], [K_("dn")])
                    tt("dve", hdir, rra[:, :, 0:128], dn[:, 1, :].unsqueeze(2).to_broadcast([128, 4, 128]), ALU.mult, [K_("rra"), K_("dn")], [K_("hdir")])
                    dma(hd[(d, seg)][c * 128:(c + 1) * 128, :].rearrange("p (h e) -> p h e", h=4), hdir, [K_("hdir")], [("h", d, seg, c)])
                    yield
                    for h in range(H):
                        bb = b0 if h < 2 else b1
                        hh = h % 2
                        mm(bb[:, hh * 129:(hh + 1) * 129], ktm[:, h, :], vw[:, h, :], True, True, [K_("ktm"), K_("vw")], [bb])
                    tt("dve", Ctmp, Caug[:], ex[:, 3, :].unsqueeze(2).to_broadcast([128, 4, 129]), ALU.mult, [Caug] + EX, [K_("Ctmp")])
                    yield
                    tt("dve", Caug[:, 0:2, :], Ctmp[:, 0:2, :], b0[:, 0:258].rearrange("p (a b) -> p a b", a=2), ALU.add, [K_("Ctmp"), b0], [Caug])
                    tt("dve", Caug[:, 2:4, :], Ctmp[:, 2:4, :], b1[:, 0:258].rearrange("p (a b) -> p a b", a=2), ALU.add, [K_("Ctmp"), b1], [Caug])
                    yield

            def s5_stream(d):
                bwd = d == 1
                nm = "s%d" % d
                K_ = lambda s_: nm + s_
                uc = [cv.b(384, "p (a b) -> p a b", a=3) for _ in range(2)]
                B2 = cv.b(1024, "p (r a b) -> p r a b", r=2, a=4); S2 = cv.b(1024, "p (r a b) -> p r a b", r=2, a=4)
                m1 = cv.b(1024, "p (r a b) -> p r a b", r=2, a=4); m2 = cv.b(1024, "p (r a b) -> p r a b", r=2, a=4)
                vf = cv.f(1024, "p (r a b) -> p r a b", r=2, a=4); Y = cv.f(1024, "p (r a b) -> p r a b", r=2, a=4)
                tq = cv.f(16, "p (a b) -> p a b", a=4)
                ycb = cv.f(384, "p (a b) -> p a b", a=3)
                bA = pbs[4 + 2 * d]; bB = pbs[5 + 2 * d]
                t16 = t16d[d]; rhoF = rhoFd[d]; rcs = rcsd[d]; car = card[d]; TKd = TABK[d]
                chain = chains[d]

                def R3(ap):
                    return ap[:, :, ::-1] if bwd else ap

                def R4(ap):
                    return ap[:, :, :, ::-1] if bwd else ap

                def load(i):
                    seg, c = chain[i]
                    b_ = i % 2
                    tok = c * 128
                    dma(uc[b_], ud[seg][:, :, tok:tok + 128], [("u",) + unit_of(seg, c)], [K_("uc%d" % b_)], eng="pool")
                load(0)
                for i, (seg, c) in enumerate(chain):
                    b_ = i % 2
                    if i + 1 < len(chain):
                        load(i + 1)
                    u_ = uc[b_]; ku = K_("uc%d" % b_)
                    for j in range(3):
                        ks = slice(4 * j, 4 * j + 4)
                        for kk in range(4):
                            k = 4 * j + kk
                            mm(bA[:, kk * 128:(kk + 1) * 128], Bre[:, k, :], u_[:, j, :], True, True, ["Bre", ku], [bA])
                            mm(bB[:, kk * 128:(kk + 1) * 128], Bim[:, k, :], u_[:, j, :], True, True, ["Bim", ku], [bB])
                        yield
                        bAv = R3(bA[:, :].rearrange("p (a b) -> p a b", a=4)); bBv = R3(bB[:, :].rearrange("p (a b) -> p a b", a=4))
                        act(B2[:, 0], bAv, AF.Copy, [bA], [K_("B2")])
                        act(B2[:, 1], bBv, AF.Copy, [bB], [K_("B2")])
                        act(S2[:, 0], bBv, AF.Copy, [bB], [K_("S2")], scale=-1.0)
                        act(S2[:, 1], bAv, AF.Copy, [bA], [K_("S2")])
                        yield
                        bc = lambda t_: t_.unsqueeze(1).to_broadcast([128, 2, 4, 128])
                        tt("dve", m1, B2, bc(t16[:, 0, ks, :]), ALU.mult, [K_("B2"), TKd], [K_("m1")])
                        tt("dve", m2, S2, bc(t16[:, 1, ks, :]), ALU.mult, [K_("S2"), TKd], [K_("m2")])
                        tt("dve", vf, m1, m2, ALU.add, [K_("m1"), K_("m2")], [K_("vf")])
                        tt("dve", vf[:, :, :, 0], vf[:, :, :, 0], car[:, :, ks], ALU.add, [K_("vf"), car], [K_("vf")])
                        yield
                        rf = rhoF[:, ks, :].rearrange("p a b -> p (a b)")
                        for r_ in range(2):
                            P.op("dve", lambda e, r_=r_, rf=rf: e.tensor_tensor_scan(Y[:, r_].rearrange("p a b -> p (a b)"), rf,
                                                                                     vf[:, r_].rearrange("p a b -> p (a b)"), 0.0, ALU.mult, ALU.add),
                                 reads=[TKd, K_("vf")], writes=[K_("Y")])
                        yield
                        act(B2, Y, AF.Copy, [K_("Y")], [K_("B2")])
                        act(S2[:, 0], Y[:, 1], AF.Copy, [K_("Y")], [K_("S2")], scale=-1.0)
                        act(S2[:, 1], Y[:, 0], AF.Copy, [K_("Y")], [K_("S2")])
                        yr = Y[:, 0, :, 127]; yi = Y[:, 1, :, 127]; rc = rcs[:, 0, ks]; rs_ = rcs[:, 1, ks]
                        tt("dve", tq[:, 0, :], yr, rc, ALU.mult, [K_("Y"), TKd], [K_("tq")])
                        tt("dve", tq[:, 1, :], yi, rs_, ALU.mult, [K_("Y"), TKd], [K_("tq")])
                        tt("dve", tq[:, 2, :], yi, rc, ALU.mult, [K_("Y"), TKd], [K_("tq")])
                        tt("dve", tq[:, 3, :], yr, rs_, ALU.mult, [K_("Y"), TKd], [K_("tq")])
                        tt("dve", car[:, 0, ks], tq[:, 0, :], tq[:, 1, :], ALU.subtract, [K_("tq")], [car])
                        tt("dve", car[:, 1, ks], tq[:, 2, :], tq[:, 3, :], ALU.add, [K_("tq")], [car])
                        yield
                        tt("dve", m1, B2, bc(t16[:, 2, ks, :]), ALU.mult, [K_("B2"), TKd], [K_("m1")])
                        tt("dve", m2, S2, bc(t16[:, 3, ks, :]), ALU.mult, [K_("S2"), TKd], [K_("m2")])
                        tt("dve", R4(B2), m1, m2, ALU.add, [K_("m1"), K_("m2")], [K_("B2")])
                        yield
                        for kk in range(4):
                            k = 4 * j + kk
                            mm(bA[:, 0:128], Cre[:, k, :], B2[:, 0, kk, :], kk == 0, False, ["Cre", K_("B2")], [bA])
                            mm(bA[:, 0:128], Cimn[:, k, :], B2[:, 1, kk, :], False, kk == 3, ["Cimn", K_("B2")], [bA])
                        act(ycb[:, j, :], bA[:, 0:128], AF.Copy, [bA], [K_("ycb")])
                        yield
                    dma(yd_[(d, seg)][:, :, c * 128:(c + 1) * 128], ycb, [K_("ycb")], [("y", d, seg, c)])

            gens = [mlstm_stream(0), mlstm_stream(1), s5_stream(0), s5_stream(1)]
            alive = list(gens)
            while alive:
                for g_ in list(alive):
                    try:
                        next(g_)
                    except StopIteration:
                        alive.remove(g_)

        def phase3b(l, dst_of, is_last, units):
            P.barrier()
            cv = Carver()
            N3 = 512
            sqt_ = cv.f(KT * N3, "p (a b) -> p a b", a=KT); rs_ = cv.f(N3)
            slabs3 = [cv.b(SLABF) for _ in range(10)]
            a_t_ = cv.b(FT * N3, "p (a b) -> p a b", a=FT)
            sets = []
            for tag in ("A", "B"):
                sets.append({"tag": tag, "xt": cv.f(KT * N3, "p (a b) -> p a b", a=KT), "hT": cv.b(KT * N3, "p (a b) -> p a b", a=KT), "tmpA": cv.f(N3)})
            srr = [0]

            def lslab(w_ap, nk):
                i = srr[0] % 10
                srr[0] += 1
                key = "s3bslab%d" % i
                v = slabs3[i][:, 0:nk * 128].rearrange("p (k c) -> p k c", k=nk)
                dma(v, w_ap, [], [key])
                return v, key

            def unit_gen(seg, u, NT, S):
                tg = S["tag"]
                K_ = lambda n: "b" + tg + n
                xt, hT, tmpA = S["xt"][:, :, 0:NT], S["hT"][:, :, 0:NT], S["tmpA"][:, 0:NT]
                sqt = sqt_[:, :, 0:NT]; rs = rs_[:, 0:NT]; a_t = a_t_[:, :, 0:NT]
                sg = 1 if seg == "ctx" else 0
                t0 = u * N3

                def norm3(scale_fn, shift_fn, extra_r, dst, dkey):
                    act(sqt, xt, AF.Square, [K_("xt")], ["sqt3b"])
                    ps = PS()
                    for k in range(KT):
                        mm(ps[:, 0:NT], ones, sqt[:, k, :], k == 0, k == KT - 1, [cst, "sqt3b"], [ps])
                    act(rs, ps[:, 0:NT], AF.Sqrt, [ps], ["rs3b"], bias=EPS, scale=1.0 / D)
                    recip(rs, rs, ["rs3b"], ["rs3b"])
                    tt("dve", sqt, xt, rs.unsqueeze(1).to_broadcast([128, KT, NT]), ALU.mult, [K_("xt"), "rs3b"], ["sqt3b"])
                    for k in range(KT):
                        sh = shift_fn(k)
                        if sh is None:
                            act(dst[:, k, :], sqt[:, k, :], AF.Identity, ["sqt3b"] + extra_r, [dkey], scale=scale_fn(k))
                        else:
                            act(dst[:, k, :], sqt[:, k, :], AF.Identity, ["sqt3b"] + extra_r, [dkey], bias=sh, scale=scale_fn(k))
                dma(xt, x1d[seg][:, t0:t0 + NT].rearrange("(k p) t -> p k t", p=128), [], [K_("xt")])
                yield "y"
                norm3(lambda k: lcs[:, sg, 1, k:k + 1], lambda k: modv[:, 24 + k, sg:sg + 1], [lcs, modv], hT, K_("hT"))
                yield "S"
                for i in range(FT):
                    v1, s1 = lslab(wb["wfg"][l][:, i], KT)
                    pg = PS()
                    for k in range(KT):
                        mm(pg[:, 0:NT], v1[:, k, :], hT[:, k, :], k == 0, k == KT - 1, [s1, K_("hT")], [pg])
                    act(tmpA, pg[:, 0:NT], AF.Silu, [pg], [K_("tmpA")])
                    v2, s2 = lslab(wb["wfu"][l][:, i], KT)
                    pu = PS()
                    for k in range(KT):
                        mm(pu[:, 0:NT], v2[:, k, :], hT[:, k, :], k == 0, k == KT - 1, [s2, K_("hT")], [pu])
                    tt("dve", a_t[:, i, :], tmpA, pu[:, 0:NT], ALU.mult, [K_("tmpA"), pu], [("ba", i)])
                    yield "y"
                for jd in range(KT):
                    ps = PS()
                    for hf in range(2):
                        v3, s3 = lslab(wb["wfd"][l][:, jd, hf * 11:(hf + 1) * 11], 11)
                        for k in range(11):
                            i = hf * 11 + k
                            mm(ps[:, 0:NT], v3[:, k, :], a_t[:, i, :], i == 0, i == FT - 1, [s3, ("ba", i)], [ps])
                    stt("dve", xt[:, jd, :], ps[:, 0:NT], modv[:, 40 + jd, sg:sg + 1], xt[:, jd, :], ALU.mult, ALU.add, [ps, modv, K_("xt")], [K_("xt")])
                    yield "y"
                if is_last and seg == "lat":
                    norm3(lambda k: gfin[:, k:k + 1], lambda k: None, [gfin], sqt, "sqt3b")
                    dma(outT[:, t0:t0 + NT].rearrange("(k p) t -> p k t", p=128), sqt, ["sqt3b"], [("out", u)])
                else:
                    dma(dst_of[seg][:, t0:t0 + NT].rearrange("(k p) t -> p k t", p=128), xt, [K_("xt")], [("x2", seg, u)])
                yield "S"

            gens = [unit_gen(seg, u, nt_, sets[i % 2]) for i, (seg, u, nt_) in enumerate(units)]

            def to_stage(g):
                while next(g) != "S":
                    pass

            def weave(ga, gb):
                da = db = False
                while not (da and db):
                    if not da:
                        da = next(ga) == "S"
                    if not db:
                        db = next(gb) == "S"
            to_stage(gens[0])
            for i in range(len(gens)):
                if i + 1 < len(gens):
                    weave(gens[i], gens[i + 1])
                else:
                    to_stage(gens[i])

        def phase3(l, src_of, dst_of, is_last):
            P.barrier()
            cv = Carver()
            N3 = 512; NC3 = 4
            sqt_ = cv.f(KT * N3, "p (a b) -> p a b", a=KT); rs_ = cv.f(N3)
            slabs3 = [cv.b(1024) for _ in range(8)]
            gmh_row = cv.f(512); gsgu_row = cv.f(512); wsT = cv.f(512, "p (a b) -> p a b", a=4); bs_row = cv.f(512)
            wglu = cv.f(3 * 384, "p (a b) -> p a b", a=3)
            CK = ["p3c"]
            dma(gmh_row, din["gmh"][l].partition_broadcast(128), [], CK); dma(gsgu_row, din["gsgu"][l].partition_broadcast(128), [], CK)
            dma(bs_row, din["bs"][l].partition_broadcast(128), [], CK); dma(wsT, din["wsT"][l], [], CK); dma(wglu, din["wglu"][l], [], CK)
            w_l = wb["win"][l]

            SH = {}
            SH["merged"] = cv.b(KT * N3, "p (a b) -> p a b", a=KT); SH["tmpB"] = cv.f(N3)
            SH["sigo"] = cv.b(NC3 * 512, "p (c e) -> p c e", c=NC3); SH["vg"] = cv.f(NC3 * 512, "p (c e) -> p c e", c=NC3)
            SH["gsu"] = cv.b(4 * N3, "p (a b) -> p a b", a=4); SH["cb"] = cv.b(4 * N3, "p (a b) -> p a b", a=4)
            for n in ("cc", "yd"):
                SH[n] = cv.f(4 * N3, "p (a b) -> p a b", a=4)
            for n in ("ydb", "yaT", "ybT"):
                SH[n] = cv.b(4 * N3, "p (a b) -> p a b", a=4)
            SH["ysum"] = cv.f(3 * N3, "p (a b) -> p a b", a=3); SH["uT"] = cv.f(3 * N3, "p (a b) -> p a b", a=3)
            SH["hdir"] = cv.f(1024, "p (c e) -> p c e", c=2); SH["hbt"] = cv.f(1024, "p (c e) -> p c e", c=2); SH["hn"] = cv.f(1024, "p (c e) -> p c e", c=2)
            SH["ycb"] = cv.f(3 * N3, "p (a b) -> p a b", a=3); SH["sm4"] = cv.f(16)

            def alloc_set(tag):
                S = dict(SH)
                S["tag"] = tag
                S["xt"] = cv.f(KT * N3, "p (a b) -> p a b", a=KT); S["hT"] = cv.b(KT * N3, "p (a b) -> p a b", a=KT)
                S["tmpA"] = cv.f(N3); S["ys5T"] = cv.b(3 * N3, "p (a b) -> p a b", a=3)
                return S
            sets = [alloc_set("A"), alloc_set("B")]
            srr = [0]

            def lslab(w_ap, nk):
                i = srr[0] % 8
                srr[0] += 1
                key = "s3slab%d" % i
                v = slabs3[i][:, 0:nk * 128].rearrange("p (k c) -> p k c", k=nk)
                dma(v, w_ap, [], [key])
                return v, key

            def unit_gen(seg, u, NT, S):
                NCH_ = NT // 128
                tg = S["tag"]
                K_ = lambda n: (tg + n) if n in ("xt", "hT", "tmpA", "ys5T") else ("p3" + n)
                xt, hT, merged, tmpA, tmpB = S["xt"][:, :, 0:NT], S["hT"][:, :, 0:NT], S["merged"][:, :, 0:NT], S["tmpA"][:, 0:NT], S["tmpB"][:, 0:NT]
                sigo, vg = S["sigo"][:, 0:NCH_, :], S["vg"][:, 0:NCH_, :]
                gsu, cb, cc, yd, ydb = (S[n_][:, :, 0:NT] for n_ in ("gsu", "cb", "cc", "yd", "ydb"))
                yaT, ybT, ysum, ys5T, uT = (S[n_][:, :, 0:NT] for n_ in ("yaT", "ybT", "ysum", "ys5T", "uT"))
                hdir2, hbt2, hn2, ycb, sm16 = S["hdir"], S["hbt"], S["hn"], S["ycb"][:, :, 0:NT], S["sm4"]
                sqt = sqt_[:, :, 0:NT]; rs = rs_[:, 0:NT]
                sg = 1 if seg == "ctx" else 0
                t0 = u * N3
                rowlen = NT if seg == "ctx" else 64

                def norm3(scale_fn, shift_fn, extra_r, dst, dkey):
                    tt("pool", sqt, xt, xt, ALU.mult, [K_("xt")], ["sqt3"])
                    ps = PS()
                    for k in range(KT):
                        mm(ps[:, 0:NT], ones, sqt[:, k, :], k == 0, k == KT - 1, [cst, "sqt3"], [ps])
                    act(rs, ps[:, 0:NT], AF.Sqrt, [ps], ["rs3"], bias=EPS, scale=1.0 / D)
                    recip(rs, rs, ["rs3"], ["rs3"])
                    tt("dve", sqt, xt, rs.unsqueeze(1).to_broadcast([128, KT, NT]), ALU.mult, [K_("xt"), "rs3"], ["sqt3"])
                    for k in range(KT):
                        sh = shift_fn(k)
                        if sh is None:
                            act(dst[:, k, :], sqt[:, k, :], AF.Identity, ["sqt3"] + extra_r, [dkey], scale=scale_fn(k))
                        else:
                            act(dst[:, k, :], sqt[:, k, :], AF.Identity, ["sqt3"] + extra_r, [dkey], bias=sh, scale=scale_fn(k))

                def pfm(c0, ntiles, evac):
                    for j in range(ntiles):
                        v, s_ = lslab(w_l[:, c0 + j], KT)
                        ps = PS()
                        for k in range(KT):
                            mm(ps[:, 0:NT], v[:, k, :], hT[:, k, :], k == 0, k == KT - 1, [s_, K_("hT")], [ps])
                        evac(j, ps)
                        yield "y"

                def ptm(c0, ntiles, evac):
                    for j in range(ntiles):
                        v, s_ = lslab(w_l[:, c0 + j], KT)
                        for c in range(NCH_):
                            ps = PS()
                            for k in range(KT):
                                mm(ps[:, 0:128], hT[:, k, c * 128:(c + 1) * 128], v[:, k, :], k == 0, k == KT - 1, [s_, K_("hT")], [ps])
                            evac(j, c, ps)
                        yield "y"

                dma(xt, src_of[seg][:, t0:t0 + NT].rearrange("(k p) t -> p k t", p=128), [], [K_("xt")])
                dma(uT, ud[seg][:, :, t0:t0 + NT], [], [K_("uT")])
                dma(ysum, yd_[(0, seg)][:, :, t0:t0 + NT], [], [K_("ysum")])
                dma(ycb, yd_[(1, seg)][:, :, t0:t0 + NT], [], [K_("ycb")])
                yield "y"
                norm3(lambda k: lcs[:, sg, 0, k:k + 1], lambda k: modv[:, k, sg:sg + 1], [lcs, modv], hT, K_("hT"))
                yield "y"
                tt("dve", ysum, ysum, ycb, ALU.add, [K_("ysum"), K_("ycb")], [K_("ysum")])
                for j in range(3):
                    stt("dve", ysum[:, j, :], uT[:, j, :], s5d[:, j:j + 1], ysum[:, j, :], ALU.mult, ALU.add, [K_("uT"), s5d, K_("ysum")], [K_("ysum")])
                act(ysum, ysum, AF.Gelu_apprx_tanh, [K_("ysum")], [K_("ysum")])
                yield "S"
                yield from ptm(TO, 4, lambda j, c, ps: act(sigo[:, c, j * 128:(j + 1) * 128], ps[:, 0:128], AF.Sigmoid, [ps], [K_("sigo")]))
                yield from ptm(TSV, 4, lambda j, c, ps: act(vg[:, c, j * 128:(j + 1) * 128], ps[:, 0:128], AF.Gelu_apprx_tanh, [ps], [K_("vg")]))
                yield from pfm(TSU, 4, lambda j, ps: act(gsu[:, j, :], ps[:, 0:NT], AF.Gelu_apprx_tanh, [ps], [K_("gsu")]))
                yield from pfm(TCB, 4, lambda j, ps: act(cb[:, j, :], ps[:, 0:NT], AF.Copy, [ps], [K_("cb")]))
                yield from pfm(TCC, 4, lambda j, ps: act(cc[:, j, :], ps[:, 0:NT], AF.Copy, [ps], [K_("cc")]))
                yield from pfm(TCX, 4, lambda j, ps: tt("dve", cc[:, j, :], cc[:, j, :], ps[:, 0:NT], ALU.mult, [K_("cc"), ps], [K_("cc")]))
                for j in range(3):
                    ps = PS()
                    for k in range(3):
                        mm(ps[:, 0:NT], wglu[:, k, j * 128:(j + 1) * 128], ysum[:, k, :], k == 0, k == 2, CK + [K_("ysum")], [ps])
                    act(tmpA, ps[:, 0:NT], AF.Sigmoid, [ps, bglu], [K_("tmpA")], bias=bglu[:, j:j + 1])
                    tt("dve", ys5T[:, j, :], ysum[:, j, :], tmpA, ALU.mult, [K_("ysum"), K_("tmpA")], [K_("ys5T")])
                    yield "y"
                yield "S"
                conv3(yd, cc, wsconv, 0, rowlen, [K_("cc"), wsconv], [K_("yd")], NT)
                tt("pool", ydb, yd, cb, ALU.mult, [K_("yd"), K_("cb")], [K_("ydb")])
                yield "y"
                for c0 in range(0, NCH_, 2):
                    nb = min(2, NCH_ - c0)
                    csl = slice(c0, c0 + nb)
                    tsl = slice(c0 * 128, (c0 + nb) * 128)
                    tok0 = t0 + c0 * 128
                    hF = hdir2[:, 0:nb, :]; hB = hbt2[:, 0:nb, :]; hN = hn2[:, 0:nb, :]
                    dma(hF, hd[(0, seg)][tok0:tok0 + nb * 128, :].rearrange("(c p) e -> p c e", p=128), [], [K_("hdir")])
                    dma(hB, hd[(1, seg)][tok0:tok0 + nb * 128, :].rearrange("(c p) e -> p c e", p=128), [], [K_("hbt")])
                    tt("dve", hF, hF, hB, ALU.add, [K_("hdir"), K_("hbt")], [K_("hdir")])
                    act(hN, hF, AF.Square, [K_("hdir")], [K_("hn")])
                    ssq = sm16[:, 0:nb * 4]
                    red("dve", ssq, hN.rearrange("p c (h e) -> p (c h) e", h=4), ALU.add, [K_("hn")], [K_("sm4")])
                    act(ssq, ssq, AF.Sqrt, [K_("sm4")], [K_("sm4")], bias=EPS, scale=1.0 / 128)
                    recip(ssq, ssq, [K_("sm4")], [K_("sm4")])
                    yield "y"
                    tt("dve", hN.rearrange("p c (h e) -> p (c h) e", h=4), hF.rearrange("p c (h e) -> p (c h) e", h=4),
                       ssq.unsqueeze(2).to_broadcast([128, nb * 4, 128]), ALU.mult, [K_("hdir"), K_("sm4")], [K_("hn")])
                    tt("dve", hN, hN, gmh_row.unsqueeze(1).to_broadcast([128, nb, 512]), ALU.mult, [K_("hn")] + CK, [K_("hn")])
                    tt("dve", hN, hN, sigo[:, csl, :], ALU.mult, [K_("hn"), K_("sigo")], [K_("hn")])
                    vgs = vg[:, csl, :]
                    mu = sm16[:, 8:8 + nb]; vr = sm16[:, 12:12 + nb]
                    red("dve", mu, vgs, ALU.add, [K_("vg")], [K_("sm4")])
                    ts("dve", mu, mu, 1.0 / 512, None, ALU.mult, None, [K_("sm4")], [K_("sm4")])
                    tt("dve", vgs, vgs, mu.unsqueeze(2).to_broadcast([128, nb, 512]), ALU.subtract, [K_("vg"), K_("sm4")], [K_("vg")])
                    act(hB, vgs, AF.Square, [K_("vg")], [K_("hbt")])
                    yield "y"
                    for ci in range(nb):
                        c = c0 + ci
                        pT_ = PS()
                        for h in range(H):
                            tr(pT_[:, h * 128:(h + 1) * 128], hN[:, ci, h * 128:(h + 1) * 128], [K_("hn")], [pT_])
                        act(yaT[:, :, c * 128:(c + 1) * 128], pT_[:, :].rearrange("p (a b) -> p a b", a=4), AF.Copy, [pT_], [K_("yaT")])
                    red("dve", vr, hB, ALU.add, [K_("hbt")], [K_("sm4")])
                    act(vr, vr, AF.Sqrt, [K_("sm4")], [K_("sm4")], bias=EPS, scale=1.0 / 512)
                    recip(vr, vr, [K_("sm4")], [K_("sm4")])
                    tt("dve", vgs, vgs, vr.unsqueeze(2).to_broadcast([128, nb, 512]), ALU.mult, [K_("vg"), K_("sm4")], [K_("vg")])
                    tt("dve", vgs, vgs, gsgu_row.unsqueeze(1).to_broadcast([128, nb, 512]), ALU.mult, [K_("vg")] + CK, [K_("vg")])
                    yield "y"
                    for ci in range(nb):
                        c = c0 + ci
                        cs = slice(c * 128, (c + 1) * 128)
                        pM = PS()
                        for gi in range(4):
                            mm(pM[:, gi * 128:(gi + 1) * 128], vg[:, c, gi * 128:(gi + 1) * 128], wsT[:, gi, :], True, True, [K_("vg")] + CK, [pM])
                        hq = hF[:, ci, :].rearrange("p (a b) -> p a b", a=4)
                        tt("dve", hq, pM[:, :].rearrange("p (a b) -> p a b", a=4), bs_row.rearrange("p (a b) -> p a b", a=4), ALU.add, [pM] + CK, [K_("hdir")])
                        tt("dve", ybT[:, :, cs], hq, gsu[:, :, cs], ALU.mult, [K_("hdir"), K_("gsu")], [K_("ybT")])
                    yield "y"
                yield "S"
                branches = [(yaT, K_("yaT"), 4, wb["wupm"][l]), (ybT, K_("ybT"), 4, wb["wups"][l]), (ys5T, K_("ys5T"), 3, wb["wup5"][l]), (ydb, K_("ydb"), 4, wb["wupc"][l])]
                for jd in range(KT):
                    for b, (ybr, ykey, nkb, wup) in enumerate(branches):
                        vgt, sgt = lslab(w_l[:, TGATE + b * 8 + jd], KT)
                        pg = PS()
                        for k in range(KT):
                            mm(pg[:, 0:NT], vgt[:, k, :], hT[:, k, :], k == 0, k == KT - 1, [sgt, K_("hT")], [pg])
                        act(tmpA, pg[:, 0:NT], AF.Sigmoid, [pg], [K_("tmpA")])
                        vu, su_ = lslab(wup[:, jd], nkb)
                        pu = PS()
                        for k in range(nkb):
                            mm(pu[:, 0:NT], vu[:, k, :], ybr[:, k, :], k == 0, k == nkb - 1, [su_, ykey], [pu])
                        if b == 0:
                            tt("dve", tmpB, tmpA, pu[:, 0:NT], ALU.mult, [K_("tmpA"), pu], [K_("tmpB")])
                        else:
                            tt("dve", tmpA, tmpA, pu[:, 0:NT], ALU.mult, [K_("tmpA"), pu], [K_("tmpA")])
                            if b < 3:
                                tt("dve", tmpB, tmpB, tmpA, ALU.add, [K_("tmpA"), K_("tmpB")], [K_("tmpB")])
                            else:
                                tt("dve", merged[:, jd, :], tmpB, tmpA, ALU.add, [K_("tmpA"), K_("tmpB")], [K_("merged")])
                        yield "y"
                yield "S"
                for j in range(KT):
                    v, s_ = lslab(wb["wout"][l][:, j], KT)
                    ps = PS()
                    for k in range(KT):
                        mm(ps[:, 0:NT], v[:, k, :], merged[:, k, :], k == 0, k == KT - 1, [s_, K_("merged")], [ps])
                    stt("dve", xt[:, j, :], ps[:, 0:NT], modv[:, 16 + j, sg:sg + 1], xt[:, j, :], ALU.mult, ALU.add, [ps, modv, K_("xt")], [K_("xt")])
                    yield "y"
                dma(x1d[seg][:, t0:t0 + NT].rearrange("(k p) t -> p k t", p=128), xt, [K_("xt")], [("x1", seg, u)])
                yield "S"

            units = ([] if is_last else [("ctx", 0, CTX)]) + [("lat", u, N3) for u in range(T // N3)]
            gens = [unit_gen(seg, u, nt_, sets[i % 2]) for i, (seg, u, nt_) in enumerate(units)]

            def to_stage(g):
                while next(g) != "S":
                    pass

            def weave(ga, gb):
                da = db = False
                while not (da and db):
                    if not da:
                        da = next(ga) == "S"
                    if not db:
                        db = next(gb) == "S"
            for _ in range(2):
                to_stage(gens[0])
            for i in range(len(gens)):
                A = gens[i]
                Bn = gens[i + 1] if i + 1 < len(gens) else None
                for _ in range(2):
                    if Bn is None:
                        to_stage(A)
                    else:
                        weave(A, Bn)
                to_stage(A)
            phase3b(l, dst_of, is_last, units)

        NU = T // 512
        for l in range(n_layers):
            last = l == n_layers - 1
            if l == 0:
                cast_weights(0)
            layer_prep(l, first=(l == 0))
            if l + 1 < n_layers:
                cast_weights(l + 1)
            src_of = {"lat": din["xT"] if l == 0 else xs, "ctx": din["cT"] if l == 0 else csd}
            dst_of = {"lat": xs, "ctx": csd}
            phase1(l, src_of)
            phase2(l)
            phase3(l, src_of, dst_of, last)
        P.barrier()
        P.final_wait("sp", [("out", u) for u in range(NU)] + ["dbg_" + n for n in dbg_out])
        P.emit()
    return nc, dbg_out, P


def _kt(w):
    K_, C = w.shape
    return np.ascontiguousarray(w.reshape(K_ // 128, 128, C).transpose(1, 0, 2))


def _tile(w):
    K_, C = w.shape
    return np.ascontiguousarray(w.reshape(K_ // 128, 128, C // 128, 128).transpose(1, 2, 0, 3))


def _pcol(v):
    return np.ascontiguousarray(v.reshape(-1, 128).T)


def make_consts():
    c = np.zeros((128, 7, 128), np.float32)
    r = np.arange(128)[:, None]; cidx = np.arange(128)[None, :]
    c[:, 0] = (r == cidx); c[:, 1] = 1.0
    c[:, 2] = (r <= cidx); c[:, 3] = (r >= cidx)
    c[:, 4] = np.where(r <= cidx, 0.0, -BIG); c[:, 5] = np.where(r >= cidx, 0.0, -BIG)
    c[:, 6] = (cidx + 1)
    return c


def shared_inputs(inp):
    f = lambda a: np.asarray(a, np.float32)
    o = {"cst": make_consts()}
    o["wmod"] = np.stack([_kt(f(inp["w_mod"][l])) for l in range(L)])
    o["bmod"] = np.stack([_pcol(f(inp["b_mod"][l])) for l in range(L)])
    o["gnm"] = np.stack([_pcol(f(inp["g_norm_mix"][l])) for l in range(L)])
    o["gnf"] = np.stack([_pcol(f(inp["g_norm_ffn"][l])) for l in range(L)])
    o["gfin"] = _pcol(f(inp["g_final"]))
    def win_tiles(w):
        cols = []
        for c0, n in ((COL_Q, 4), (COL_K, 4), (COL_V, 4)):
            cols += [w[:, c0 + i * 128:c0 + (i + 1) * 128] for i in range(n)]
        gt = np.zeros((D, 128), np.float32); gt[:, :16] = w[:, COL_G:COL_G + 16]; cols.append(gt)
        for c0, n in ((COL_U, 3), (COL_O, 4), (COL_SU, 4), (COL_SV, 4), (COL_CB, 4), (COL_CC, 4), (COL_CX, 4), (COL_GATE, 32)):
            cols += [w[:, c0 + i * 128:c0 + (i + 1) * 128] for i in range(n)]
        t = np.stack(cols, 0)
        return np.ascontiguousarray(t.reshape(NWT, KT, 128, 128).transpose(2, 0, 1, 3))
    o["win"] = np.stack([win_tiles(f(inp["w_in"][l])) for l in range(L)])
    o["bg"] = f(inp["b_gates"]).reshape(L, 1, 16)
    wc = f(inp["w_conv_qk"])
    o["wconv"] = np.ascontiguousarray(wc.reshape(L, 2, 3, 4, 128).transpose(0, 4, 1, 2, 3).reshape(L, 128, 24))
    o["gmh"] = f(inp["g_mh"]).reshape(L, 1, 512)
    o["gsgu"] = f(inp["g_sgu"]).reshape(L, 1, 512)
    o["wsT"] = np.ascontiguousarray(f(inp["w_sgu"]).transpose(0, 3, 1, 2))
    o["bs"] = f(inp["b_sgu"]).reshape(L, 1, 512)

    def ptile(a):
        return np.ascontiguousarray(a.reshape(L, 2, 12, 128).transpose(0, 1, 3, 2))
    o["are"] = ptile(f(inp["s5_a_re"])); o["aim"] = ptile(f(inp["s5_a_im"]))
    o["ldt"] = ptile(np.repeat(f(inp["s5_log_dt"])[..., None], 64, axis=-1))

    def bpad(B):
        out = np.zeros((L, 128, 12, 128), np.float32)
        for g in range(24):
            k, gg = g // 2, g % 2
            r0 = (g % 8) * 16
            out[:, r0:r0 + 16, k, gg * 64:(gg + 1) * 64] = B[:, g].transpose(0, 2, 1)
        return out

    def cpad(C):
        out = np.zeros((L, 128, 12, 128), np.float32)
        for g in range(24):
            k, gg = g // 2, g % 2
            c0 = (g % 8) * 16
            out[:, gg * 64:(gg + 1) * 64, k, c0:c0 + 16] = C[:, g].transpose(0, 2, 1)
        return out
    o["Bre"] = bpad(f(inp["s5_b_re"])); o["Bim"] = bpad(f(inp["s5_b_im"]))
    o["Cre"] = cpad(f(inp["s5_c_re"])); o["Cim"] = cpad(f(inp["s5_c_im"]))
    o["s5d"] = np.stack([_pcol(f(inp["s5_d"][l])) for l in range(L)])
    o["bglu"] = np.stack([_pcol(f(inp["b_glu"][l])) for l in range(L)])
    o["wglu"] = np.stack([_kt(f(inp["w_glu"][l])) for l in range(L)])
    ws = f(inp["w_sconv"])
    o["wsconv"] = np.ascontiguousarray(ws.reshape(L, 3, 4, 128).transpose(0, 3, 1, 2).reshape(L, 128, 12))
    for nm, key in (("wupm", "w_up_mlstm"), ("wups", "w_up_sgu"), ("wup5", "w_up_s5"), ("wupc", "w_up_sconv"), ("wout", "w_out"),
                    ("wfg", "w_ffn_gate"), ("wfu", "w_ffn_up"), ("wfd", "w_ffn_down")):
        o[nm] = np.stack([_tile(f(inp[key][l])) for l in range(L)])
    return o


def core_inputs(inp, b, T):
    f = lambda a: np.asarray(a, np.float32)
    o = {}
    o["xT"] = np.ascontiguousarray(f(inp["x"][b, :T]).T)
    o["cT"] = np.ascontiguousarray(f(inp["ctx"][b]).T)
    sc = np.stack([_pcol(f(inp["c"][b])), _pcol(f(inp["c_ctx"]))], axis=-1)
    o["sc"] = np.ascontiguousarray(sc)
    return o


_CACHE = {}


def kernel(**inputs):
    B, T = inputs["x"].shape[0], inputs["x"].shape[1]
    if T not in _CACHE:
        _CACHE[T] = build_program(T)[0]
    nc = _CACHE[T]
    sh = shared_inputs(inputs)
    in_maps = []
    for b in range(B):
        m = dict(sh)
        m.update(core_inputs(inputs, b, T))
        in_maps.append(m)
    res = run_bass_kernel_spmd(nc, in_maps, core_ids=list(range(B)))
    out = np.stack([np.ascontiguousarray(r["outT"].T) for r in res.results], axis=0)
    return out.astype(np.float32)
```
